# Optimizing a Trainium2 kernel written in Bass

```python
import jax
import jax.numpy as jnp
from jax import lax
import numpy as np

D_MODEL = 1024
BATCH = 8
SEQ = 2048
DEPTH = 4

CHUNK = 64
EPS = 1e-6
N_MOD = 6

D_HGRN = 512
D_S5 = 256
D_LRU = 256
D_MIX = D_HGRN + D_S5 + D_LRU

HGRN_HEADS = 4
HGRN_DK = D_HGRN // HGRN_HEADS

S5_GROUP = 16
S5_GROUPS = D_S5 // S5_GROUP
S5_STATE = 64
S5_DT_MIN = 0.001
S5_DT_MAX = 0.1

LRU_BLOCKS = 4
LRU_BLOCK = D_LRU // LRU_BLOCKS
CONV_W = 4
LRU_C = 8.0

PEER_HEADS = 8
N_KEYS = 128
N_EXPERTS = N_KEYS * N_KEYS
PEER_DQ = 256
PEER_TOPK = 16
PEER_BLOCK = 128

IN_COLS = 4 * D_HGRN + D_S5 + 2 * D_LRU
IN_SPLITS = (D_HGRN, 2 * D_HGRN, 3 * D_HGRN, 4 * D_HGRN, 4 * D_HGRN + D_S5, 4 * D_HGRN + D_S5 + D_LRU)

kernel_name = 'hybrid_hgrn2_s5_rglru_peer_adaln'


def _normalize(y):
    y32 = y.astype(jnp.float32)
    return y32 * lax.rsqrt(jnp.mean(y32 * y32, axis=-1, keepdims=True) + EPS)


def rms_norm(x, g):
    return (_normalize(x) * g.astype(jnp.float32)).astype(x.dtype)


def hgrn2_mixer(q, f_pre, i, g, lb):
    f32 = jnp.float32
    bsz, seq, _ = q.shape
    nc = seq // CHUNK
    lb = lb.astype(f32)
    qa = jax.nn.silu(q.astype(f32))
    f = lb + (1.0 - lb) * jax.nn.sigmoid(f_pre.astype(f32))
    logf = jnp.log(f)
    k = 1.0 - f
    v = i.astype(f32)

    def to_chunks(t):
        return t.reshape(bsz, nc, CHUNK, HGRN_HEADS, HGRN_DK).transpose(1, 0, 3, 2, 4)

    qc, kc, vc, gc = to_chunks(qa), to_chunks(k), to_chunks(v), to_chunks(logf)
    causal = jnp.tril(jnp.ones((CHUNK, CHUNK), dtype=bool))[None, None, :, :, None]

    def step(state, inp):
        qb, kb, vb, gb = inp
        b = jnp.cumsum(gb, axis=2)
        diff = b[:, :, :, None, :] - b[:, :, None, :, :]
        decay = jnp.exp(jnp.where(causal, diff, -jnp.inf))
        scores = jnp.einsum('bhtd,bhsd,bhtsd->bhts', qb, kb, decay)
        o = (jnp.einsum('bhts,bhsv->bhtv', scores, vb)
             + jnp.einsum('bhtd,bhdv->bhtv', qb * jnp.exp(b), state))
        b_last = b[:, :, -1:, :]
        state = (state * jnp.exp(b_last[:, :, 0, :, None])
                 + jnp.einsum('bhsd,bhsv->bhdv', kb * jnp.exp(b_last - b), vb))
        return state, o

    s0 = jnp.zeros((bsz, HGRN_HEADS, HGRN_DK, HGRN_DK), f32)
    _, o = lax.scan(step, s0, (qc, kc, vc, gc))
    o = o.transpose(1, 0, 3, 2, 4).reshape(bsz, seq, HGRN_HEADS, HGRN_DK)
    o = _normalize(o).reshape(bsz, seq, D_HGRN)
    return o * jax.nn.silu(g.astype(f32))


def s5_mixer(u, a_re, a_im, b_re, b_im, c_re, c_im, log_dt, d_skip, w_glu, b_glu):
    f32 = jnp.float32
    bsz, seq, _ = u.shape
    u32 = u.astype(f32).reshape(bsz, seq, S5_GROUPS, S5_GROUP)
    lam_re = jnp.minimum(a_re.astype(f32), -1e-4)
    lam_im = a_im.astype(f32)
    dt = jnp.exp(log_dt.astype(f32))[:, None]
    mag = jnp.exp(lam_re * dt)
    ab_re = mag * jnp.cos(lam_im * dt)
    ab_im = mag * jnp.sin(lam_im * dt)
    den = lam_re * lam_re + lam_im * lam_im
    nr = ab_re - 1.0
    z_re = (nr * lam_re + ab_im * lam_im) / den
    z_im = (ab_im * lam_re - nr * lam_im) / den
    br = b_re.astype(f32)
    bi = b_im.astype(f32)
    bb_re = z_re[..., None] * br - z_im[..., None] * bi
    bb_im = z_re[..., None] * bi + z_im[..., None] * br
    bu_re = jnp.einsum('blgc,gpc->blgp', u32, bb_re)
    bu_im = jnp.einsum('blgc,gpc->blgp', u32, bb_im)
    a_full_re = jnp.broadcast_to(ab_re, bu_re.shape)
    a_full_im = jnp.broadcast_to(ab_im, bu_im.shape)

    def combine(e1, e2):
        a1r, a1i, b1r, b1i = e1
        a2r, a2i, b2r, b2i = e2
        return (a2r * a1r - a2i * a1i,
                a2r * a1i + a2i * a1r,
                a2r * b1r - a2i * b1i + b2r,
                a2r * b1i + a2i * b1r + b2i)

    _, _, xr, xi = lax.associative_scan(combine, (a_full_re, a_full_im, bu_re, bu_im), axis=1)
    y = (jnp.einsum('blgp,gcp->blgc', xr, c_re.astype(f32))
         - jnp.einsum('blgp,gcp->blgc', xi, c_im.astype(f32)))
    y = (y + d_skip.astype(f32).reshape(S5_GROUPS, S5_GROUP) * u32).reshape(bsz, seq, D_S5)
    z = jax.nn.gelu(y)
    return z * jax.nn.sigmoid(z @ w_glu.astype(f32) + b_glu.astype(f32))


def rglru_mixer(xb, zb, conv_w, conv_b, w_a, b_a, w_x, b_x, lam):
    f32 = jnp.float32
    bsz, seq, _ = xb.shape
    x32 = xb.astype(f32)
    rhs = conv_w.astype(f32)[:, None, :]
    xc = lax.conv_general_dilated(x32, rhs, window_strides=(1,), padding=[(CONV_W - 1, 0)],
                                  dimension_numbers=('NWC', 'WIO', 'NWC'),
                                  feature_group_count=D_LRU) + conv_b.astype(f32)
    xblk = xc.reshape(bsz, seq, LRU_BLOCKS, LRU_BLOCK)
    r = jax.nn.sigmoid(jnp.einsum('blhi,hij->blhj', xblk, w_a.astype(f32)).reshape(bsz, seq, D_LRU)
                       + b_a.astype(f32))
    ig = jax.nn.sigmoid(jnp.einsum('blhi,hij->blhj', xblk, w_x.astype(f32)).reshape(bsz, seq, D_LRU)
                        + b_x.astype(f32))
    log_a = -LRU_C * r * jax.nn.softplus(-lam.astype(f32))
    a = jnp.exp(log_a)
    bterm = jnp.sqrt(-jnp.expm1(2.0 * log_a)) * (ig * xc)

    def combine(e1, e2):
        a1, b1 = e1
        a2, b2 = e2
        return a1 * a2, a2 * b1 + b2

    _, h = lax.associative_scan(combine, (a, bterm), axis=1)
    return h * jax.nn.gelu(zb.astype(f32))


def peer_ffn(h, w_q, sub_keys, u_tab, v_tab):
    f32 = jnp.float32
    bsz, seq, dm = h.shape
    ntok = bsz * seq
    ht = h.reshape(ntok, dm)
    q = (ht @ w_q).astype(f32).reshape(ntok, PEER_HEADS, 2, PEER_DQ // 2)
    s = jnp.einsum('thpd,pkd->thpk', q, sub_keys.astype(f32))
    s_top, i_top = lax.top_k(s, PEER_TOPK)
    cand = s_top[:, :, 0, :, None] + s_top[:, :, 1, None, :]
    cand_idx = i_top[:, :, 0, :, None] * N_KEYS + i_top[:, :, 1, None, :]
    best, pos = lax.top_k(cand.reshape(ntok, PEER_HEADS, PEER_TOPK * PEER_TOPK), PEER_TOPK)
    experts = jnp.take_along_axis(cand_idx.reshape(ntok, PEER_HEADS, PEER_TOPK * PEER_TOPK), pos, axis=-1)
    gates = jax.nn.softmax(best, axis=-1)
    nb = ntok // PEER_BLOCK

    def block(args):
        hb, eb, gb = args
        act = jax.nn.gelu(jnp.einsum('td,thkd->thk', hb.astype(f32), u_tab[eb].astype(f32)))
        return jnp.einsum('thk,thkd->td', gb * act, v_tab[eb].astype(f32))

    out = lax.map(block, (ht.reshape(nb, PEER_BLOCK, dm),
                          experts.reshape(nb, PEER_BLOCK, PEER_HEADS, PEER_TOPK),
                          gates.reshape(nb, PEER_BLOCK, PEER_HEADS, PEER_TOPK)))
    return out.reshape(bsz, seq, dm).astype(h.dtype)


def setup_inputs(seed: int = 0) -> dict:
    key = jax.random.key(seed)
    ks = iter(jax.random.split(key, 40))
    nrm = lambda shape, scale: jax.random.normal(next(ks), shape, jnp.float32) * scale
    gain = lambda shape: 1.0 + nrm(shape, 0.05)
    inp = {}
    inp['x'] = nrm((BATCH, SEQ, D_MODEL), 1.0)
    inp['c'] = nrm((BATCH, D_MODEL), 1.0)
    inp['w_mod'] = nrm((DEPTH, D_MODEL, N_MOD * D_MODEL), 0.5 * D_MODEL ** -0.5)
    inp['b_mod'] = nrm((DEPTH, N_MOD * D_MODEL), 0.02)
    inp['g_mix'] = gain((DEPTH, D_MODEL))
    inp['w_in'] = nrm((DEPTH, D_MODEL, IN_COLS), D_MODEL ** -0.5)
    inp['hgrn_lb_logits'] = nrm((DEPTH, D_HGRN), 0.1)
    inp['s5_a_re'] = -0.5 + nrm((DEPTH, S5_GROUPS, S5_STATE), 0.01)
    inp['s5_a_im'] = (jnp.pi * jnp.arange(S5_STATE, dtype=jnp.float32))[None, None, :] + nrm((DEPTH, S5_GROUPS, S5_STATE), 0.01)
    inp['s5_b_re'] = nrm((DEPTH, S5_GROUPS, S5_STATE, S5_GROUP), (2.0 * S5_GROUP) ** -0.5)
    inp['s5_b_im'] = nrm((DEPTH, S5_GROUPS, S5_STATE, S5_GROUP), (2.0 * S5_GROUP) ** -0.5)
    inp['s5_c_re'] = nrm((DEPTH, S5_GROUPS, S5_GROUP, S5_STATE), (2.0 * S5_STATE) ** -0.5)
    inp['s5_c_im'] = nrm((DEPTH, S5_GROUPS, S5_GROUP, S5_STATE), (2.0 * S5_STATE) ** -0.5)
    inp['s5_log_dt'] = jax.random.uniform(next(ks), (DEPTH, S5_GROUPS), jnp.float32,
                                          np.log(S5_DT_MIN), np.log(S5_DT_MAX))
    inp['s5_d'] = nrm((DEPTH, D_S5), 1.0)
    inp['s5_w_glu'] = nrm((DEPTH, D_S5, D_S5), D_S5 ** -0.5)
    inp['s5_b_glu'] = nrm((DEPTH, D_S5), 0.02)
    inp['lru_conv_w'] = nrm((DEPTH, CONV_W, D_LRU), CONV_W ** -0.5)
    inp['lru_conv_b'] = nrm((DEPTH, D_LRU), 0.02)
    inp['lru_w_a'] = nrm((DEPTH, LRU_BLOCKS, LRU_BLOCK, LRU_BLOCK), LRU_BLOCK ** -0.5)
    inp['lru_b_a'] = nrm((DEPTH, D_LRU), 0.02)
    inp['lru_w_x'] = nrm((DEPTH, LRU_BLOCKS, LRU_BLOCK, LRU_BLOCK), LRU_BLOCK ** -0.5)
    inp['lru_b_x'] = nrm((DEPTH, D_LRU), 0.02)
    a_c = jax.random.uniform(next(ks), (DEPTH, D_LRU), jnp.float32, 0.9, 0.999)
    sig = a_c ** (1.0 / LRU_C)
    inp['lru_lambda'] = jnp.log(sig) - jnp.log1p(-sig)
    inp['g_branch'] = gain((DEPTH, D_MIX))
    inp['w_out'] = nrm((DEPTH, D_MIX, D_MODEL), D_MIX ** -0.5)
    inp['g_ffn'] = gain((DEPTH, D_MODEL))
    inp['peer_w_q'] = nrm((DEPTH, D_MODEL, PEER_HEADS * PEER_DQ), D_MODEL ** -0.5)
    inp['peer_sub_keys'] = nrm((DEPTH, 2, N_KEYS, PEER_DQ // 2), (PEER_DQ // 2) ** -0.5)
    inp['peer_u'] = nrm((DEPTH, N_EXPERTS, D_MODEL), D_MODEL ** -0.5)
    inp['peer_v'] = nrm((DEPTH, N_EXPERTS, D_MODEL), (PEER_HEADS * PEER_TOPK) ** -0.5)
    inp['g_final'] = gain((D_MODEL,))
    return inp


def reference(x, c, w_mod, b_mod, g_mix, w_in, hgrn_lb_logits, s5_a_re, s5_a_im, s5_b_re, s5_b_im,
              s5_c_re, s5_c_im, s5_log_dt, s5_d, s5_w_glu, s5_b_glu, lru_conv_w, lru_conv_b,
              lru_w_a, lru_b_a, lru_w_x, lru_b_x, lru_lambda, g_branch, w_out, g_ffn,
              peer_w_q, peer_sub_keys, peer_u, peer_v, g_final):
    cond = jax.nn.silu(c)
    lb_soft = jax.nn.softmax(hgrn_lb_logits.astype(jnp.float32), axis=0)
    lb_all = jnp.cumsum(lb_soft, axis=0) - lb_soft[0:1]
    for l in range(DEPTH):
        mod = (cond @ w_mod[l] + b_mod[l])[:, None, :]
        sh1, sc1, gt1, sh2, sc2, gt2 = jnp.split(mod, N_MOD, axis=-1)
        h = rms_norm(x, g_mix[l]) * (1.0 + sc1) + sh1
        p = h @ w_in[l]
        q_a, f_a, i_a, g_a, u_b, x_c, z_c = jnp.split(p, IN_SPLITS, axis=-1)
        y_a = hgrn2_mixer(q_a, f_a, i_a, g_a, lb_all[l])
        y_b = s5_mixer(u_b, s5_a_re[l], s5_a_im[l], s5_b_re[l], s5_b_im[l], s5_c_re[l], s5_c_im[l],
                       s5_log_dt[l], s5_d[l], s5_w_glu[l], s5_b_glu[l])
        y_c = rglru_mixer(x_c, z_c, lru_conv_w[l], lru_conv_b[l], lru_w_a[l], lru_b_a[l],
                          lru_w_x[l], lru_b_x[l], lru_lambda[l])
        y = jnp.concatenate([_normalize(y_a), _normalize(y_b), _normalize(y_c)], axis=-1)
        y = (y * g_branch[l].astype(jnp.float32)).astype(x.dtype)
        x = x + gt1 * (y @ w_out[l])
        h2 = rms_norm(x, g_ffn[l]) * (1.0 + sc2) + sh2
        x = x + gt2 * peer_ffn(h2, peer_w_q[l], peer_sub_keys[l], peer_u[l], peer_v[l])
    return rms_norm(x, g_final)
```

```python
import numpy as np
from contextlib import ExitStack
import concourse.bass as bass
import concourse.mybir as mybir
from concourse.bass_utils import run_bass_kernel_spmd

F32 = mybir.dt.float32
BF16 = mybir.dt.bfloat16
U32 = mybir.dt.uint32
I32 = mybir.dt.int32
AF = mybir.ActivationFunctionType
ALU = mybir.AluOpType
AX = mybir.AxisListType
F32R = mybir.dt.float32r

ENG = ['sync', 'act', 'pool', 'pe', 'dve']


class FW:
    def __init__(self, nc, ctx, slots=None):
        self.nc = nc
        self.ctx = ctx
        self.q = {e: [] for e in ENG}
        self.sems = {}
        self.cnt = {}
        for e in ENG:
            self.sems[e] = ctx.enter_context(nc.semaphore('s_' + e))
            self.cnt[e] = 0
        slots = slots or {'sync': 8, 'act': 4, 'pool': 8}
        self.slots = {}
        self.rr = {}
        for e, n in slots.items():
            self.slots[e] = []
            self.rr[e] = 0
            for i in range(n):
                nm = 'd_%s%d' % (e, i)
                self.sems[nm] = ctx.enter_context(nc.semaphore(nm))
                self.cnt[nm] = 0
                self.slots[e].append(nm)
        self.seen = {e: {} for e in ENG}
        self.lastw = {}
        self.readers = {}
        self.nins = {e: 0 for e in ENG}

    def sb(self, name, shape, dtype):
        self.uid = getattr(self, 'uid', 0) + 1
        if not hasattr(self, 'names'):
            self.names = {}
        self.names.setdefault(name, []).append('%s_u%d' % (name, self.uid))
        return self.ctx.enter_context(self.nc.sbuf_tensor('%s_u%d' % (name, self.uid), list(shape), dtype))

    def ps(self, name, shape, dtype):
        return self.ctx.enter_context(self.nc.psum_tensor(name, list(shape), dtype))

    def _wait(self, eng, s, v):
        if self.mute:
            return
        if self.seen[eng].get(s, 0) < v:
            self.seen[eng][s] = v
            sem = self.sems[s]
            self.q[eng].append(lambda e, sem=sem, v=v: e.wait_ge(sem, v))

    def _wait_deps(self, eng, reads, writes):
        deps = {}
        for k in list(reads) + list(writes):
            ev = self.lastw.get(k)
            if ev is not None and deps.get(ev[0], 0) < ev[1]:
                deps[ev[0]] = ev[1]
        for k in writes:
            for s, v in self.readers.get(k, {}).items():
                if deps.get(s, 0) < v:
                    deps[s] = v
        for s, v in deps.items():
            self._wait(eng, s, v)

    def _record(self, ev, reads, writes):
        s, v = ev
        ws = set(writes)
        for k in ws:
            self.lastw[k] = ev
            self.readers[k] = {}
        for k in reads:
            if k in ws:
                continue
            r = self.readers.setdefault(k, {})
            if r.get(s, 0) < v:
                r[s] = v

    mute = False

    def op(self, eng, fn, reads=(), writes=()):
        if self.mute:
            return
        self._wait_deps(eng, reads, writes)
        self.cnt[eng] += 1
        v = self.cnt[eng]
        sem = self.sems[eng]
        self.q[eng].append(lambda e, fn=fn, sem=sem: fn(e).then_inc(sem, 1))
        self.nins[eng] += 1
        self._record((eng, v), reads, writes)

    def dma(self, eng, out, in_, reads=(), writes=(), indirect=None, **kw):
        if self.mute:
            return
        self._wait_deps(eng, reads, writes)
        sl = self.slots[eng]
        slot = sl[self.rr[eng] % len(sl)]
        self.rr[eng] += 1
        if self.cnt[slot] > 0:
            self._wait(eng, slot, self.cnt[slot])
        self.cnt[slot] += 16
        v = self.cnt[slot]
        sem = self.sems[slot]
        if indirect is None:
            self.q[eng].append(lambda e, out=out, in_=in_, sem=sem, kw=kw:
                               e.dma_start(out=out, in_=in_, **kw).then_inc(sem, 16))
        else:
            self.q[eng].append(lambda e, out=out, in_=in_, sem=sem, ind=indirect, kw=kw:
                               e.indirect_dma_start(out=out, out_offset=None, in_=in_, in_offset=ind, **kw).then_inc(sem, 16))
        self.nins[eng] += 1
        self._record((slot, v), reads, writes)

    def finish(self, out_keys):
        self._wait_deps('sync', out_keys, [])
        q = self.q
        with self.nc.Block() as block:
            @block.sync
            def _(e):
                for f in q['sync']:
                    f(e)

            @block.scalar
            def _(e):
                for f in q['act']:
                    f(e)

            @block.gpsimd
            def _(e):
                for f in q['pool']:
                    f(e)

            @block.tensor
            def _(e):
                for f in q['pe']:
                    f(e)

            @block.vector
            def _(e):
                for f in q['dve']:
                    f(e)


import os
PIPE = 1
JF = 1
T = 2048
D = 1024
NT = 16
EPS = 1e-6
TWO_PI = float(2 * np.pi)
CW1 = 6.28125
CW2 = 0.0019353071795864769
GK = 1.5957691216057308

CSTW = 384 + 32 + 512 + 2048
NSMALL = 16 + 24 + 128 * 4 + 2 + 2 + 2 + 8 + 2 + 2 + 2 + 2 + 8 + 8
O_LBL = 0
O_ARS = 16
O_AIS = 24
O_LDS = 32
O_ARR = 40
O_AIR = 168
O_BTR = 296
O_BTI = 424
O_LDR = 552
O_S5D = 554
O_BGL = 556
O_CW = 558
O_CB = 566
O_BA = 568
O_BX = 570
O_LAM = 572
O_GBR = 574
O_RMK = 582


class _Stop(Exception):
    pass


def build(n_layers=4, dbg=None, do_peer=True, stop=None):
    nc = bass.Bass("TRN2", target_bir_lowering=False)

    def din(name, shape, dt=F32):
        return nc.dram_tensor(name, list(shape), dt, kind="ExternalInput").ap()

    x_d = din("x", [T, D])
    c_d = din("c", [128, 8])
    wmod_d = din("w_mod", [4, D, 6 * D])
    bmod_d = din("b_mod", [4, 6 * D])
    gmix_d = din("g_mix", [4, D])
    win_d = din("w_in", [4, D, 2816])
    small_d = din("small", [4, 128, NSMALL])
    crt_d = din("crt", [4, 128, 8, 128])
    cit_d = din("cit", [4, 128, 8, 128])
    wglu_d = din("w_glu", [4, 256, 256])
    wa_d = din("wa_bd", [4, 128, 2, 128])
    wx_d = din("wx_bd", [4, 128, 2, 128])
    gbrow_d = din("g_branch", [4, D])
    wout_d = din("w_out", [4, D, D])
    gffn_d = din("g_ffn", [4, D])
    wq_d = din("peer_w_q", [4, D, 2048])
    skt_d = din("skt", [4, 128, 2, 128])
    pu_d = [din("peer_u%d" % l_, [16384, D]) for l_ in range(4)]
    pv_d = [din("peer_v%d" % l_, [16384, D]) for l_ in range(4)]
    gfin_d = din("g_final", [1, D])
    consts_d = din("consts", [128, CSTW])
    out_d = nc.dram_tensor("out", [T, D], F32, kind="ExternalOutput").ap()
    ub_d = [nc.dram_tensor("ub_scr%d" % l_, [16384, D], BF16, kind="Internal").ap() for l_ in range(4)]
    vb_d = [nc.dram_tensor("vb_scr%d" % l_, [16384, D], BF16, kind="Internal").ap() for l_ in range(4)]

    with ExitStack() as ctx:
        fw = FW(nc, ctx, slots={'sync': 8, 'act': 2, 'pool': 10})
        build.fw = fw

        def TS(eng, out, in0, s1, s2, op0, op1, r, w):
            fw.op(eng, lambda e: e.tensor_scalar(out=out, in0=in0, scalar1=s1, scalar2=s2, op0=op0, op1=op1), r, w)

        def TT(eng, out, in0, in1, op, r, w):
            fw.op(eng, lambda e: e.tensor_tensor(out=out, in0=in0, in1=in1, op=op), r, w)

        def STT(out, in0, scalar, in1, op0, op1, r, w):
            fw.op('dve', lambda e: e.scalar_tensor_tensor(out=out, in0=in0, scalar=scalar, in1=in1, op0=op0, op1=op1), r, w)

        def ACT(out, in_, func, r, w, scale=1.0, bias=0.0, accum=None):
            r = list(r) + ['cst', 'sm']
            if accum is None:
                fw.op('act', lambda e: e.activation(out=out, in_=in_, func=func, scale=scale, bias=bias), r, w)
            else:
                fw.op('act', lambda e: e.activation(out=out, in_=in_, func=func, scale=scale, bias=bias, accum_out=accum), r, w)

        def CP(eng, out, in_, r, w):
            if eng == 'act':
                fw.op(eng, lambda e: e.activation(out=out, in_=in_, func=AF.Copy), r, w)
            else:
                fw.op(eng, lambda e: e.tensor_copy(out=out, in_=in_), r, w)

        def SCAN(out, d0, d1, init, r, w):
            fw.op('dve', lambda e: e.tensor_tensor_scan(out=out, data0=d0, data1=d1, initial=init, op0=ALU.mult, op1=ALU.add), r, w)

        def MMG(mms, r, w):
            def f(e, mms=mms):
                ins = None
                for mm in mms:
                    (o, l, rh, st, sp) = mm[:5]
                    if len(mm) > 5:
                        ins = e.matmul(o, l, rh, start=st, stop=sp, skip_group_check=True)
                    else:
                        ins = e.matmul(o, l, rh, start=st, stop=sp)
                return ins
            fw.op('pe', f, r, w)

        def TRG(trs, ident, r, w):
            def f(e, trs=trs):
                ins = None
                for (o, i) in trs:
                    ins = e.transpose(o, i, ident)
                return ins
            fw.op('pe', f, r, w)

        def barrier():
            allsems = list(fw.cnt.items())
            for e in ENG:
                for s, v in allsems:
                    if v > 0:
                        fw._wait(e, s, v)

        xs = fw.sb("xs", [128, NT, D], F32)
        cst = fw.sb("cst", [128, 928], F32)
        identf = cst[:, 0:128]
        onesf = cst[:, 128:256]
        maskbd = cst[:, 256:384]
        halfpi = cst[:, 384:385]
        epscol = cst[:, 385:386]
        onecol = cst[:, 386:387]
        io16 = cst[:, 400:416]
        tau = cst[:, 416:416 + 512]
        rmask_t = fw.sb("rmask", [128, 2048], BF16)
        rmask = rmask_t[:]
        identb = fw.sb("identb", [128, 128], BF16)
        condr = fw.sb("condr", [128, 8, 128], F32)
        cond = fw.sb("cond", [128, 8], F32)
        PA = fw.ps("PA", [128, 2048], F32)
        PB = fw.ps("PB", [128, 2048], F32)
        PAk = ['PA0', 'PA1', 'PA2', 'PA3']
        PBk = ['PB0', 'PB1', 'PB2', 'PB3']

        fw.dma('sync', cst[:, 0:928], consts_d[:, 0:928], writes=['cst'])
        fw.dma('pool', rmask_t[:], consts_d[:, 928:928 + 2048], writes=['cst'])
        for i in range(NT):
            fw.dma('sync', xs[:, i, :], x_d[i * 128:(i + 1) * 128, :], writes=['x%d' % i])
        fw.dma('sync', cond[:], c_d[:, :], writes=['cond'])
        CP('dve', identb[:], identf, ['cst'], ['identb'])
        ACT(cond[:], cond[:], AF.Silu, ['cond'], ['cond'])
        CP('dve', condr[:], cond[:, :].unsqueeze(2).to_broadcast([128, 8, 128]), ['cond'], ['condr'])

        def mod_tile(l, j, out_tile, okey, stg, bm):
            wv = wmod_d[l].rearrange("(p kc) n -> p kc n", kc=8)
            fw.dma('sync', bm[:], bmod_d[l:l + 1, j * 1024:(j + 1) * 1024].to_broadcast([128, 1024]), writes=['bm'])
            for half in range(2):
                c0 = j * 1024 + half * 512
                fw.dma('sync', stg[:], wv[:, :, c0:c0 + 512], writes=['stg'])
                MMG([(PA[:, 0:512], condr[:, kc, :], stg[:, kc, :], kc == 0, kc == 7) for kc in range(8)],
                    ['condr', 'stg'], ['PA0'])
                TT('dve', out_tile[:, half * 512:(half + 1) * 512], PA[:, 0:512], bm[:, half * 512:(half + 1) * 512], ALU.add,
                   ['PA0', 'bm'], [okey])

        def gelu_inplace(eng_t, t, tmp, key, tkey):
            ACT(tmp, t, AF.Square, [key], [tkey])
            TS('dve', tmp, tmp, 0.044715, 1.0, ALU.mult, ALU.add, [tkey], [tkey])
            TT('dve', tmp, tmp, t, ALU.mult, [tkey, key], [tkey])
            ACT(tmp, tmp, AF.Sigmoid, [tkey], [tkey], scale=GK)
            TT('dve', t, t, tmp, ALU.mult, [key, tkey], [key])

        def rstd_from_ss(ss_ap, n, key):
            ACT(ss_ap, ss_ap, AF.Sqrt, [key], [key], scale=1.0 / n, bias=epscol)
            fw.op('dve', lambda e: e.reciprocal(out=ss_ap, in_=ss_ap), [key], [key])

        def norm_to_T(A, B, hT, hkeys, keep=None):
            pass

        for l in range(n_layers):
          try:
              with ExitStack() as actx:
                  octx = fw.ctx
                  fw.ctx = actx
                  hT = fw.sb("hT", [128, 8, T], BF16)
                  yT = fw.sb("yT", [128, 8, T], BF16)
                  sm = fw.sb("sm", [128, NSMALL], F32)
                  rsd = fw.sb("rsd", [128, 3, NT], F32)
                  fw.dma('sync', sm[:], small_d[l], writes=['sm'])
                  if do_peer:
                      for (src, dst, key) in [(pu_d[l], ub_d[l], 'ubd'), (pv_d[l], vb_d[l], 'vbd')]:
                          for c in range(16):
                              fw.dma('pool', dst[c * 1024:(c + 1) * 1024, :].rearrange("(p r) d -> p r d", p=128),
                                     src[c * 1024:(c + 1) * 1024, :].rearrange("(p r) d -> p r d", p=128), writes=[key + str(c)])
                  with ExitStack() as sctx:
                      fw.ctx = sctx
                      stg = fw.sb("stg", [128, 8, 512], F32)
                      bm = fw.sb("bm", [128, 1024], F32)
                      A1 = fw.sb("A1", [128, 1024], F32)
                      B1 = fw.sb("B1", [128, 1024], F32)
                      gm = fw.sb("gm", [128, 1024], F32)
                      ss = fw.sb("ss", [128, NT], F32)
                      junk = fw.sb("junk", [128, 1024], BF16)
                      tmpf = fw.sb("tmpf", [128, 1024], F32)
                      hb = fw.sb("hb", [128, 1024], BF16)
                      mod_tile(l, 0, B1, 'B1', stg, bm)
                      mod_tile(l, 1, A1, 'A1', stg, bm)
                      fw.dma('sync', gm[:], gmix_d[l:l + 1, :].to_broadcast([128, 1024]), writes=['gm'])
                      STT(A1[:], A1[:], 1.0, gm[:], ALU.add, ALU.mult, ['A1', 'gm'], ['A1'])
                      for i in range(NT):
                          ACT(junk[:], xs[:, i, :], AF.Square, ['x%d' % i], ['junk', 'ss'], accum=ss[:, i:i + 1])
                      rstd_from_ss(ss[:], float(D), 'ss')
                      PAb = PA[:, 0:512].bitcast(BF16)
                      for i in range(NT):
                          STT(tmpf[:], xs[:, i, :], ss[:, i:i + 1], A1[:], ALU.mult, ALU.mult, ['x%d' % i, 'ss', 'A1'], ['tmpf'])
                          TT('dve', hb[:], tmpf[:], B1[:], ALU.add, ['tmpf', 'B1'], ['hb'])
                          TRG([(PAb[:, kc * 128:(kc + 1) * 128], hb[:, kc * 128:(kc + 1) * 128]) for kc in range(8)],
                              identb[:], ['hb', 'identb'], ['PA0'])
                          CP('act', hT[:, :, i * 128:(i + 1) * 128], PAb.rearrange("p (k t) -> p k t", k=8), ['PA0'], ['hT'])
                      barrier()
                  fw.ctx = actx
                  if stop == 'norm':
                      fw.mute = True
                  lbt = fw.sb("lbt", [128, 16], F32)
                  lbz = fw.sb("lbz", [128, 4], F32)
                  lb = fw.sb("lb", [128, 4], F32)
                  oml = fw.sb("oml", [128, 4], F32)
                  ACT(lbt[:], sm[:, O_LBL:O_LBL + 16], AF.Exp, ['sm'], ['lbt'])
                  lbv = lbt[:].rearrange("p (h l) -> p h l", h=4)
                  fw.op('dve', lambda e: e.tensor_reduce(out=lbz[:], in_=lbv, op=ALU.add, axis=AX.X), ['lbt'], ['lbz'])
                  fw.op('dve', lambda e: e.reciprocal(out=lbz[:], in_=lbz[:]), ['lbz'], ['lbz'])
                  fw.op('dve', lambda e: e.memset(lb[:], 0.0), [], ['lb'])
                  for j in range(1, l + 1):
                      TT('dve', lb[:], lb[:], lbv[:, :, j], ALU.add, ['lb', 'lbt'], ['lb'])
                  TT('dve', lb[:], lb[:], lbz[:], ALU.mult, ['lb', 'lbz'], ['lb'])
                  TS('dve', oml[:], lb[:], -1.0, 1.0, ALU.mult, ALU.add, ['lb'], ['oml'])
                  winv = win_d[l].rearrange("(kc p) n -> p kc n", p=128)

                  with ExitStack() as sctx:
                      fw.ctx = sctx
                      wh = fw.sb("wh", [128, 8, 512], BF16)
                      t1 = fw.sb("t1", [128, T], F32)
                      t2 = fw.sb("t2", [128, T], F32)
                      t3 = fw.sb("t3", [128, T], F32)
                      kt = fw.sb("kt", [128, T], BF16)
                      kh = fw.sb("kh", [128, T], BF16)
                      qt = fw.sb("qt", [128, T], BF16)
                      khk = fw.sb("khk", [128, NT, 128], BF16)
                      khkz = fw.sb("khkz", [128, NT, 128], BF16)
                      qz = fw.sb("qz", [128, T], BF16)
                      fw.op('dve', lambda e: e.memset(qz[:], 0.0), [], ['qz'])
                      zer = fw.sb("zer", [128, 128], BF16)
                      fw.op('dve', lambda e: e.memset(zer[:], 0.0), [], ['zer'])
                      el = fw.sb("el", [128, 64], F32)
                      S = fw.sb("S", [128, 128], F32)
                      Sb = fw.sb("Sb", [128, 128], BF16)
                      vb = fw.sb("vb", [128, 128], BF16)
                      gs = fw.sb("gs", [128, 128], F32)
                      scm = fw.sb("scm", [128, 128], BF16)
                      yh = fw.sb("yh", [128, 128], F32)
                      yhb = fw.sb("yhb", [128, 128], BF16)
                      jk = fw.sb("jk", [128, 128], F32)
                      ssh = fw.sb("ssh", [128, 2], F32)
                      ssa = fw.sb("ssa", [128, 4, NT], F32)
                      gbb = fw.sb("gbb", [128, 512], F32)
                      fw.dma('sync', gbb[:], gbrow_d[l:l + 1, 0:512].to_broadcast([128, 512]), writes=['gbb'])
                      PAb = PA[:, 0:1024].bitcast(BF16)
                      for h in range(4):
                          for j, c0 in enumerate([h * 128, 512 + h * 128, 1024 + h * 128, 1536 + h * 128]):
                              fw.dma('pool', wh[:, :, j * 128:(j + 1) * 128], winv[:, :, c0:c0 + 128], writes=['wh'])
                          for tb in range(4):
                              MMG([(PA[:, tb * 512:(tb + 1) * 512], wh[:, kc, 128:256], hT[:, kc, tb * 512:(tb + 1) * 512], kc == 0, kc == 7)
                                   for kc in range(8)], ['wh', 'hT'], [PAk[tb]])
                              ACT(t1[:, tb * 512:(tb + 1) * 512], PA[:, tb * 512:(tb + 1) * 512], AF.Sigmoid, [PAk[tb]], ['t1'])
                          TS('dve', t1[:], t1[:], oml[:, h:h + 1], lb[:, h:h + 1], ALU.mult, ALU.add, ['t1', 'oml', 'lb'], ['t1'])
                          if stop == 'hg_a':
                              fw.mute = True
                          ACT(t2[:], t1[:], AF.Ln, ['t1'], ['t2'])
                          SCAN(t3[:], rmask, t2[:], 0.0, ['cst', 't2'], ['t3'])
                          if stop == 'hg_b':
                              fw.mute = True
                          TS('dve', t1[:], t1[:], -1.0, 1.0, ALU.mult, ALU.add, ['t1'], ['t1'])
                          ACT(t2[:], t3[:], AF.Exp, ['t3'], ['t2'], scale=-1.0)
                          TT('dve', kt[:], t1[:], t2[:], ALU.mult, ['t1', 't2'], ['kt'])
                          b3 = t3[:].rearrange("p (c s) -> p c s", s=32)
                          TT('dve', t2[:].rearrange("p (c s) -> p c s", s=32), b3[:, :, 31:32].to_broadcast([128, 64, 32]), b3, ALU.subtract,
                             ['t3'], ['t2'])
                          ACT(t2[:], t2[:], AF.Exp, ['t2'], ['t2'])
                          TT('dve', kh[:], t1[:], t2[:], ALU.mult, ['t1', 't2'], ['kh'])
                          ACT(t3[:], t3[:], AF.Exp, ['t3'], ['t3'])
                          CP('dve', el[:], t3[:].rearrange("p (c s) -> p c s", s=32)[:, :, 31], ['t3'], ['el'])
                          if stop == 'hg_c':
                              fw.mute = True
                          for tb in range(4):
                              MMG([(PB[:, tb * 512:(tb + 1) * 512], wh[:, kc, 0:128], hT[:, kc, tb * 512:(tb + 1) * 512], kc == 0, kc == 7)
                                   for kc in range(8)], ['wh', 'hT'], [PBk[tb]])
                              ACT(t1[:, tb * 512:(tb + 1) * 512], PB[:, tb * 512:(tb + 1) * 512], AF.Silu, [PBk[tb]], ['t1'])
                          TT('dve', qt[:], t1[:], t3[:], ALU.mult, ['t1', 't3'], ['qt'])
                          if stop == 'hg_c1':
                              fw.mute = True
                          CP('dve', qz[:].rearrange("p (i t) -> p i t", t=128)[:, :, 96:128], qt[:].rearrange("p (i t) -> p i t", t=128)[:, :, 96:128],
                             ['qt'], ['qz'])
                          if stop == 'hg_c2':
                              fw.mute = True
                          for half in range(2):
                              TRG([(PAb[:, j * 128:(j + 1) * 128], kh[:, (half * 8 + j) * 128:(half * 8 + j + 1) * 128]) for j in range(8)],
                                  identb[:], ['kh', 'identb'], ['PA0'])
                              CP('act', khk[:, half * 8:(half + 1) * 8, :], PAb[:, 0:1024].rearrange("p (j d) -> p j d", j=8), ['PA0'], ['khk'])
                              if stop == 'hg_c3':
                                  fw.mute = True
                              CP('act', khkz[64:128, half * 8:(half + 1) * 8, :], PAb[64:128, 0:1024].rearrange("p (j d) -> p j d", j=8), ['PA0'], ['khkz'])
                              if stop == 'hg_c4':
                                  fw.mute = True
                              fw.op('dve', lambda e, half=half: e.memset(khkz[64:96, half * 8:(half + 1) * 8, :], 0.0), [], ['khkz'])
                              if stop == 'hg_c5':
                                  fw.mute = True
                          fw.op('dve', lambda e: e.memset(S[:], 0.0), [], ['S'])
                          if stop == 'hg_c7':
                              fw.mute = True
                          fw.op('dve', lambda e: e.memset(Sb[:], 0.0), [], ['Sb'])
                          if stop == 'hg_d':
                              fw.mute = True
                          for i in range(NT):
                              tsl = slice(i * 128, (i + 1) * 128)
                              MMG([(PB[:, 0:256], hT[:, kc, tsl], wh[:, kc, 256:512], kc == 0, kc == 7) for kc in range(8)],
                                  ['hT', 'wh'], ['PB0'])
                              CP('act', vb[:], PB[:, 0:128], ['PB0'], ['vb'])
                              ACT(gs[:], PB[:, 128:256], AF.Silu, ['PB0'], ['gs'])
                              MMG([(PB[:, 512:640], kt[:, tsl], qt[:, tsl], True, True)], ['kt', 'qt'], ['PB1'])
                              TT('dve', scm[:], PB[:, 512:640], maskbd, ALU.mult, ['PB1', 'cst'], ['scm'])
                              if stop == 'hg_e':
                                  fw.mute = True
                              MMG([(PB[:, 1024:1152], scm[:], vb[:], True, False)], ['scm', 'vb'], ['PB2'])
                              for j in range(4):
                                  rs_ = slice(32 * j, 32 * j + 32)
                                  if j < 3:
                                      MMG([(PB[rs_, 1024:1152], qt[:, i * 128 + 32 * j:i * 128 + 32 * j + 32], Sb[:], False, False, 1)],
                                          ['qt', 'Sb'], ['PB2'])
                                      MMG([(PB[:, 1536:1664], khk[rs_, i, :], vb[rs_, :], True, True)], ['khk', 'vb'], ['PB3'])
                                  else:
                                      MMG([(PB[64:128, 1024:1152], qz[:, i * 128 + 64:i * 128 + 128], Sb[:], False, False, 1)],
                                          ['qz', 'Sb'], ['PB2'])
                                      MMG([(PB[:, 1536:1664], khkz[64:128, i, :], vb[64:128, :], True, True)], ['khkz', 'vb'], ['PB3'])
                                  STT(S[:], S[:], el[:, 4 * i + j:4 * i + j + 1], PB[:, 1536:1664], ALU.mult, ALU.add, ['S', 'el', 'PB3'], ['S'])
                                  CP('act', Sb[:], S[:], ['S'], ['Sb'])
                              MMG([(PB[:, 1024:1152], zer[:], vb[:], False, True)], ['zer', 'vb'], ['PB2'])
                              if stop == 'hg_f':
                                  fw.mute = True
                              ACT(jk[:], PB[:, 1024:1152], AF.Square, ['PB2'], ['jk', 'ssh'], accum=ssh[:, 0:1])
                              rstd_from_ss(ssh[:, 0:1], 128.0, 'ssh')
                              STT(yh[:], PB[:, 1024:1152], ssh[:, 0:1], gs[:], ALU.mult, ALU.mult, ['PB2', 'ssh', 'gs'], ['yh'])
                              ACT(jk[:], yh[:], AF.Square, ['yh'], ['jk', 'ssa'], accum=ssa[:, h, i:i + 1])
                              TT('dve', yhb[:], yh[:], gbb[:, h * 128:(h + 1) * 128], ALU.mult, ['yh', 'gbb'], ['yhb'])
                              TRG([(PAb[:, 1024:1152], yhb[:])], identb[:], ['yhb', 'identb'], ['PA1'])
                              CP('act', yT[:, h, tsl], PAb[:, 1024:1152], ['PA1'], ['yT%d' % h])
                      TT('dve', ssa[:, 0, :], ssa[:, 0, :], ssa[:, 1, :], ALU.add, ['ssa'], ['ssa'])
                      TT('dve', ssa[:, 2, :], ssa[:, 2, :], ssa[:, 3, :], ALU.add, ['ssa'], ['ssa'])
                      TT('dve', rsd[:, 0, :], ssa[:, 0, :], ssa[:, 2, :], ALU.add, ['ssa'], ['rsd0'])
                      rstd_from_ss(rsd[:, 0, :], 512.0, 'rsd0')
                      barrier()
                  fw.ctx = actx
                  if stop == 'hgrn':
                      fw.mute = True

                  with ExitStack() as sctx:
                      fw.ctx = sctx
                      wl = fw.sb("wl", [128, 8, 256], BF16)
                      wab = fw.sb("wab", [128, 2, 128], BF16)
                      wxb = fw.sb("wxb", [128, 2, 128], BF16)
                      xraw = fw.sb("xraw", [128, 3 + T], F32)
                      xc = fw.sb("xc", [128, T], F32)
                      xcb = fw.sb("xcb", [128, T], BF16)
                      ta = fw.sb("ta", [128, T], F32)
                      tb_ = fw.sb("tb_", [128, T], F32)
                      tr = fw.sb("tr", [128, T], F32)
                      ti = fw.sb("ti", [128, T], F32)
                      c8 = fw.sb("c8", [128, 2], F32)
                      c16 = fw.sb("c16", [128, 2], F32)
                      ssc = fw.sb("ssc", [128, 2, NT], F32)
                      fw.dma('pool', wab[:], wa_d[l], writes=['wab'])
                      fw.dma('pool', wxb[:], wx_d[l], writes=['wxb'])
                      ACT(c8[:], sm[:, O_LAM:O_LAM + 2], AF.Exp, ['sm'], ['c8'], scale=-1.0)
                      ACT(c8[:], c8[:], AF.Ln, ['c8'], ['c8'], bias=onecol)
                      TS('dve', c16[:], c8[:], -16.0, None, ALU.mult, ALU.bypass, ['c8'], ['c16'])
                      TS('dve', c8[:], c8[:], -8.0, None, ALU.mult, ALU.bypass, ['c8'], ['c8'])
                      fw.op('dve', lambda e: e.memset(xraw[:, 0:3], 0.0), [], ['xraw'])
                      for hc in range(2):
                          for j, c0 in enumerate([2304 + hc * 128, 2560 + hc * 128]):
                              fw.dma('pool', wl[:, :, j * 128:(j + 1) * 128], winv[:, :, c0:c0 + 128], writes=['wl'])
                          for tb in range(4):
                              MMG([(PA[:, tb * 512:(tb + 1) * 512], wl[:, kc, 0:128], hT[:, kc, tb * 512:(tb + 1) * 512], kc == 0, kc == 7)
                                   for kc in range(8)], ['wl', 'hT'], [PAk[tb]])
                              CP('act', xraw[:, 3 + tb * 512:3 + (tb + 1) * 512], PA[:, tb * 512:(tb + 1) * 512], [PAk[tb]], ['xraw'])
                          cw = sm[:, O_CW + hc * 4:O_CW + hc * 4 + 4]
                          TS('dve', xc[:], xraw[:, 3:3 + T], cw[:, 3:4], sm[:, O_CB + hc:O_CB + hc + 1], ALU.mult, ALU.add, ['xraw', 'sm'], ['xc'])
                          for w_ in range(3):
                              STT(xc[:], xraw[:, w_:w_ + T], cw[:, w_:w_ + 1], xc[:], ALU.mult, ALU.add, ['xraw', 'sm', 'xc'], ['xc'])
                          CP('act', xcb[:], xc[:], ['xc'], ['xcb'])
                          for tb in range(4):
                              sl = slice(tb * 512, (tb + 1) * 512)
                              MMG([(PB[:, sl], wab[:, hc, :], xcb[:, sl], True, True)], ['wab', 'xcb'], [PBk[tb]])
                              ACT(tr[:, sl], PB[:, sl], AF.Sigmoid, [PBk[tb]], ['tr'], bias=sm[:, O_BA + hc:O_BA + hc + 1])
                          for tb in range(4):
                              sl = slice(tb * 512, (tb + 1) * 512)
                              MMG([(PA[:, sl], wxb[:, hc, :], xcb[:, sl], True, True)], ['wxb', 'xcb'], [PAk[tb]])
                              ACT(ti[:, sl], PA[:, sl], AF.Sigmoid, [PAk[tb]], ['ti'], bias=sm[:, O_BX + hc:O_BX + hc + 1])
                          ACT(ta[:], tr[:], AF.Exp, ['tr', 'c8'], ['ta'], scale=c8[:, hc:hc + 1])
                          ACT(tb_[:], tr[:], AF.Exp, ['tr', 'c16'], ['tb_'], scale=c16[:, hc:hc + 1])
                          TS('dve', tb_[:], tb_[:], -1.0, 1.0, ALU.mult, ALU.add, ['tb_'], ['tb_'])
                          ACT(tb_[:], tb_[:], AF.Sqrt, ['tb_'], ['tb_'])
                          TT('dve', tb_[:], tb_[:], ti[:], ALU.mult, ['tb_', 'ti'], ['tb_'])
                          TT('dve', tb_[:], tb_[:], xc[:], ALU.mult, ['tb_', 'xc'], ['tb_'])
                          SCAN(tr[:], ta[:], tb_[:], 0.0, ['ta', 'tb_'], ['tr'])
                          for tb in range(4):
                              sl = slice(tb * 512, (tb + 1) * 512)
                              MMG([(PB[:, sl], wl[:, kc, 128:256], hT[:, kc, sl], kc == 0, kc == 7) for kc in range(8)],
                                  ['wl', 'hT'], [PBk[tb]])
                              CP('act', ti[:, sl], PB[:, sl], [PBk[tb]], ['ti'])
                          gelu_inplace('dve', ti[:], ta[:], 'ti', 'ta')
                          TT('dve', tb_[:], tr[:], ti[:], ALU.mult, ['tr', 'ti'], ['tb_'])
                          TS('dve', yT[:, 6 + hc, :], tb_[:], sm[:, O_GBR + 6 + hc:O_GBR + 7 + hc], None, ALU.mult, ALU.bypass,
                             ['tb_', 'sm'], ['yT%d' % (6 + hc)])
                          ACT(ta[:], tb_[:], AF.Square, ['tb_'], ['ta'])
                          MMG([(PA[:, 2 * i:2 * i + 2], ta[:, i * 128:(i + 1) * 128], onesf[:, 0:2], True, True) for i in range(NT)],
                              ['ta', 'cst'], ['PA0'])
                          CP('dve', ssc[:, hc, :], PA[:, 0:2 * NT].rearrange("p (i two) -> p i two", two=2)[:, :, 0], ['PA0'], ['ssc'])
                      TT('dve', rsd[:, 2, :], ssc[:, 0, :], ssc[:, 1, :], ALU.add, ['ssc'], ['rsd2'])
                      rstd_from_ss(rsd[:, 2, :], 256.0, 'rsd2')
                      barrier()
                  fw.ctx = actx
                  if stop == 'lru':
                      fw.mute = True

                  with ExitStack() as sctx:
                      fw.ctx = sctx
                      LP = 512
                      NP_ = T // LP
                      ws5 = fw.sb("ws5", [128, 8, 256], BF16)
                      crt = fw.sb("crt", [128, 8, 128], BF16)
                      cit = fw.sb("cit", [128, 8, 128], BF16)
                      wglu = fw.sb("wglu", [128, 2, 256], BF16)
                      ub = fw.sb("ub", [128, 2, T], BF16)
                      zb = fw.sb("zb", [128, 2, T], BF16)
                      lhb = fw.sb("lhb", [128, 8, 2, 128], BF16)
                      pst = fw.sb("pst", [128, 5, 8], F32)
                      prp = fw.sb("prp", [128, 12, 128], F32)
                      pri = fw.sb("pri", [128, 128], I32)
                      car = fw.sb("car", [128, 8, 2], F32)
                      tcos = fw.sb("tcos", [128, LP], F32)
                      tsin = fw.sb("tsin", [128, LP], F32)
                      tki = fw.sb("tki", [128, LP], I32)
                      s1 = fw.sb("s1", [128, LP], F32)
                      s2 = fw.sb("s2", [128, LP], F32)
                      swr = fw.sb("swr", [128, LP], F32)
                      swi = fw.sb("swi", [128, LP], F32)
                      szr = fw.sb("szr", [128, LP], F32)
                      szi = fw.sb("szi", [128, LP], F32)
                      xrb = fw.sb("xrb", [128, LP], BF16)
                      xib = fw.sb("xib", [128, LP], BF16)
                      yb = fw.sb("yb", [128, 512], F32)
                      yb2 = fw.sb("yb2", [128, 512], F32)
                      ssb = fw.sb("ssb", [128, 2, NT], F32)
                      fw.dma('pool', crt[:], crt_d[l], writes=['crt'])
                      fw.dma('pool', cit[:], cit_d[l], writes=['cit'])
                      fw.dma('pool', wglu[:], wglu_d[l].rearrange("(kc p) n -> p kc n", p=128), writes=['wglu'])
                      fw.dma('pool', ws5[:], winv[:, :, 2048:2304], writes=['ws5'])

                      def sincos(ang, ki, sin_o, cos_o, tmp, keys):
                          ka, kk, ks, kc_, kt_ = keys
                          TS('dve', ki, ang, 1.0 / TWO_PI, None, ALU.mult, ALU.bypass, [ka], [kk])
                          STT(ang, ki, -CW1, ang, ALU.mult, ALU.add, [kk, ka], [ka])
                          STT(ang, ki, -CW2, ang, ALU.mult, ALU.add, [kk, ka], [ka])
                          TS('dve', ang, ang, -3.14159, 3.14159, ALU.max, ALU.min, [ka], [ka])
                          ACT(sin_o, ang, AF.Sin, [ka], [ks])
                          STT(tmp, ang, -1.0, ang, ALU.mult, ALU.max, [ka], [kt_])
                          ACT(cos_o, tmp, AF.Sin, [kt_, 'cst'], [kc_], scale=-1.0, bias=halfpi)

                      lamre, dts, rmag, theta = pst[:, 0, :], pst[:, 1, :], pst[:, 2, :], pst[:, 3, :]
                      TS('dve', lamre, sm[:, O_ARS:O_ARS + 8], -1e-4, None, ALU.min, ALU.bypass, ['sm'], ['pst'])
                      ACT(dts, sm[:, O_LDS:O_LDS + 8], AF.Exp, ['sm'], ['pst'])
                      TT('dve', rmag, lamre, dts, ALU.mult, ['pst'], ['pst'])
                      ACT(rmag, rmag, AF.Exp, ['pst'], ['pst'])
                      TT('dve', theta, sm[:, O_AIS:O_AIS + 8], dts, ALU.mult, ['sm', 'pst'], ['pst'])
                      R = lambda k: prp[:, k, :]
                      lam_r, lam_i = R(0), R(1)
                      TS('dve', lam_r, sm[:, O_ARR:O_ARR + 128], -1e-4, None, ALU.min, ALU.bypass, ['sm'], ['prp'])
                      CP('dve', lam_i, sm[:, O_AIR:O_AIR + 128], ['sm'], ['prp'])
                      ACT(prp[:, 11, 0:2], sm[:, O_LDR:O_LDR + 2], AF.Exp, ['sm'], ['prp'])
                      for hg in range(2):
                          cs = slice(hg * 64, (hg + 1) * 64)
                          TS('dve', prp[:, 2, cs], prp[:, 0, cs], prp[:, 11, hg:hg + 1], None, ALU.mult, ALU.bypass, ['prp'], ['prp'])
                          TS('dve', prp[:, 3, cs], prp[:, 1, cs], prp[:, 11, hg:hg + 1], None, ALU.mult, ALU.bypass, ['prp'], ['prp'])
                      ACT(R(2), R(2), AF.Exp, ['prp'], ['prp'])
                      sincos(R(3), pri[:], R(4), R(5), R(6), ['prp', 'pri', 'prp', 'prp', 'prp'])
                      TT('dve', R(5), R(5), R(2), ALU.mult, ['prp'], ['prp'])
                      TT('dve', R(4), R(4), R(2), ALU.mult, ['prp'], ['prp'])
                      TS('dve', R(5), R(5), -1.0, None, ALU.add, ALU.bypass, ['prp'], ['prp'])
                      TT('dve', R(2), lam_r, lam_r, ALU.mult, ['prp'], ['prp'])
                      TT('dve', R(3), lam_i, lam_i, ALU.mult, ['prp'], ['prp'])
                      TT('dve', R(2), R(2), R(3), ALU.add, ['prp'], ['prp'])
                      fw.op('dve', lambda e: e.reciprocal(out=R(2), in_=R(2)), ['prp'], ['prp'])
                      TT('dve', R(6), R(5), lam_r, ALU.mult, ['prp'], ['prp'])
                      TT('dve', R(7), R(4), lam_i, ALU.mult, ['prp'], ['prp'])
                      TT('dve', R(6), R(6), R(7), ALU.add, ['prp'], ['prp'])
                      TT('dve', R(6), R(6), R(2), ALU.mult, ['prp'], ['prp'])
                      TT('dve', R(7), R(4), lam_r, ALU.mult, ['prp'], ['prp'])
                      TT('dve', R(8), R(5), lam_i, ALU.mult, ['prp'], ['prp'])
                      TT('dve', R(7), R(7), R(8), ALU.subtract, ['prp'], ['prp'])
                      TT('dve', R(7), R(7), R(2), ALU.mult, ['prp'], ['prp'])
                      btr, bti = sm[:, O_BTR:O_BTR + 128], sm[:, O_BTI:O_BTI + 128]
                      TT('dve', R(8), R(6), btr, ALU.mult, ['prp', 'sm'], ['prp'])
                      TT('dve', R(9), R(7), bti, ALU.mult, ['prp', 'sm'], ['prp'])
                      TT('dve', R(8), R(8), R(9), ALU.subtract, ['prp'], ['prp'])
                      TT('dve', R(9), R(6), bti, ALU.mult, ['prp', 'sm'], ['prp'])
                      TT('dve', R(10), R(7), btr, ALU.mult, ['prp', 'sm'], ['prp'])
                      TT('dve', R(9), R(9), R(10), ALU.add, ['prp'], ['prp'])
                      for gp in range(8):
                          hc = gp // 4
                          for gi in range(2):
                              gl = (2 * gp + gi) % 8
                              for ri, src in enumerate([8, 9]):
                                  TS('dve', lhb[:, gp, ri, gi * 64:(gi + 1) * 64], prp[:, src, hc * 64:(hc + 1) * 64],
                                     sm[:, O_RMK + gl:O_RMK + gl + 1], None, ALU.mult, ALU.bypass, ['prp', 'sm'], ['lhb'])
                      for hc in range(2):
                          for tb in range(4):
                              sl = slice(tb * 512, (tb + 1) * 512)
                              MMG([(PA[:, sl], ws5[:, kc, hc * 128:(hc + 1) * 128], hT[:, kc, sl], kc == 0, kc == 7) for kc in range(8)],
                                  ['ws5', 'hT'], [PAk[tb]])
                              CP('act', ub[:, hc, sl], PA[:, sl], [PAk[tb]], ['ub'])
                      fw.op('dve', lambda e: e.memset(car[:], 0.0), [], ['car'])
                      for hc in range(2):
                          for pc in range(NP_):
                              psl = slice(pc * LP, (pc + 1) * LP)
                              for gq in range(4):
                                  gp = hc * 4 + gq
                                  if True:
                                      TS('dve', s1[:], tau[:, 0:LP], theta[:, gp:gp + 1], None, ALU.mult, ALU.bypass, ['cst', 'pst'], ['s1'])
                                      sincos(s1[:], tki[:], tsin[:], tcos[:], s2[:], ['s1', 'tki', 'tsin', 'tcos', 's2'])
                                  MMG([(PA[:, 0:LP], lhb[:, gp, 0, :], ub[:, hc, psl], True, True),
                                       (PA[:, 512:512 + LP], lhb[:, gp, 1, :], ub[:, hc, psl], True, True)], ['lhb', 'ub'], ['PA0', 'PA1'])
                                  bur, bui = PA[:, 0:LP], PA[:, 512:512 + LP]
                                  TT('dve', s1[:], bur, tcos[:], ALU.mult, ['PA0', 'tcos'], ['s1'])
                                  TT('dve', s2[:], bui, tsin[:], ALU.mult, ['PA1', 'tsin'], ['s2'])
                                  TT('dve', swr[:], s1[:], s2[:], ALU.add, ['s1', 's2'], ['swr'])
                                  TT('dve', s1[:], bui, tcos[:], ALU.mult, ['PA1', 'tcos'], ['s1'])
                                  TT('dve', s2[:], bur, tsin[:], ALU.mult, ['PA0', 'tsin'], ['s2'])
                                  TT('dve', swi[:], s1[:], s2[:], ALU.subtract, ['s1', 's2'], ['swi'])
                                  rb_ = rmag[:, gp:gp + 1].to_broadcast([128, LP])
                                  SCAN(szr[:], rb_, swr[:], car[:, gp, 0:1], ['pst', 'swr', 'car'], ['szr'])
                                  SCAN(szi[:], rb_, swi[:], car[:, gp, 1:2], ['pst', 'swi', 'car'], ['szi'])
                                  TT('dve', s1[:], szr[:], tcos[:], ALU.mult, ['szr', 'tcos'], ['s1'])
                                  TT('dve', s2[:], szi[:], tsin[:], ALU.mult, ['szi', 'tsin'], ['s2'])
                                  TT('dve', swr[:], s1[:], s2[:], ALU.subtract, ['s1', 's2'], ['swr'])
                                  TT('dve', s1[:], szr[:], tsin[:], ALU.mult, ['szr', 'tsin'], ['s1'])
                                  TT('dve', s2[:], szi[:], tcos[:], ALU.mult, ['szi', 'tcos'], ['s2'])
                                  TT('dve', swi[:], s1[:], s2[:], ALU.add, ['s1', 's2'], ['swi'])
                                  CP('act', car[:, gp, 0:1], swr[:, LP - 1:LP], ['swr'], ['car'])
                                  CP('act', car[:, gp, 1:2], swi[:, LP - 1:LP], ['swi'], ['car'])
                                  CP('act', xrb[:], swr[:], ['swr'], ['xrb'])
                                  ACT(xib[:], swi[:], AF.Copy, ['swi'], ['xib'], scale=-1.0)
                                  MMG([(PB[:, 0:LP], crt[:, gp, :], xrb[:], gq == 0, False),
                                       (PB[:, 0:LP], cit[:, gp, :], xib[:], False, gq == 3)], ['crt', 'cit', 'xrb', 'xib'], ['PB0'])
                              STT(yb[:], ub[:, hc, psl], sm[:, O_S5D + hc:O_S5D + hc + 1], PB[:, 0:LP], ALU.mult, ALU.add,
                                  ['ub', 'sm', 'PB0'], ['yb'])
                              gelu_inplace('dve', yb[:], yb2[:], 'yb', 'yb2')
                              CP('act', zb[:, hc, psl], yb[:], ['yb'], ['zb'])
                      for oc in range(2):
                          for tb in range(4):
                              sl = slice(tb * 512, (tb + 1) * 512)
                              MMG([(PA[:, sl], wglu[:, kc, oc * 128:(oc + 1) * 128], zb[:, kc, sl], kc == 0, kc == 1) for kc in range(2)],
                                  ['wglu', 'zb'], [PAk[tb]])
                              ACT(yb[:], PA[:, sl], AF.Sigmoid, [PAk[tb]], ['yb'], bias=sm[:, O_BGL + oc:O_BGL + oc + 1])
                              TT('dve', yb[:], yb[:], zb[:, oc, sl], ALU.mult, ['yb', 'zb'], ['yb'])
                              TS('dve', yT[:, 4 + oc, sl], yb[:], sm[:, O_GBR + 4 + oc:O_GBR + 5 + oc], None, ALU.mult, ALU.bypass,
                                 ['yb', 'sm'], ['yT%d' % (4 + oc)])
                              ACT(yb2[:], yb[:], AF.Square, ['yb'], ['yb2'])
                              MMG([(PB[:, 2 * (tb * 4 + j):2 * (tb * 4 + j) + 2], yb2[:, j * 128:(j + 1) * 128], onesf[:, 0:2], True, True)
                                   for j in range(4)], ['yb2', 'cst'], ['PB0'])
                          CP('dve', ssb[:, oc, :], PB[:, 0:2 * NT].rearrange("p (i two) -> p i two", two=2)[:, :, 0], ['PB0'], ['ssb'])
                      TT('dve', rsd[:, 1, :], ssb[:, 0, :], ssb[:, 1, :], ALU.add, ['ssb'], ['rsd1'])
                      rstd_from_ss(rsd[:, 1, :], 256.0, 'rsd1')
                      barrier()
                  fw.ctx = actx
                  if stop == 's5':
                      fw.mute = True

                  with ExitStack() as sctx:
                      fw.ctx = sctx
                      stg = fw.sb("stg", [128, 8, 512], F32)
                      bm = fw.sb("bm", [128, 1024], F32)
                      gt1 = fw.sb("gt1", [128, 1024], F32)
                      wo = fw.sb("wo", [128, 8, 1024], BF16)
                      tmpo = fw.sb("tmpo", [128, 1024], F32)
                      wov = wout_d[l].rearrange("(kc p) n -> p kc n", p=128)
                      for kc in range(8):
                          fw.dma('pool', wo[:, kc, :], wov[:, kc, :], writes=['wo'])
                      mod_tile(l, 2, gt1, 'gt1', stg, bm)
                      for i in range(NT):
                          tsl = slice(i * 128, (i + 1) * 128)
                          for (P_, off, kcs, keys) in [(PA, 0, [0, 1, 2, 3], ['PA0', 'PA1']), (PA, 1024, [4, 5], ['PA2', 'PA3']),
                                                       (PB, 0, [6, 7], ['PB0', 'PB1'])]:
                              for nh in range(2):
                                  MMG([(P_[:, off + nh * 512:off + (nh + 1) * 512], yT[:, kc, tsl], wo[:, kc, nh * 512:(nh + 1) * 512],
                                        kc == kcs[0], kc == kcs[-1]) for kc in kcs],
                                      ['yT%d' % kc for kc in kcs] + ['wo'], [keys[nh]])
                          TS('dve', tmpo[:], PA[:, 0:1024], rsd[:, 0, i:i + 1], None, ALU.mult, ALU.bypass, ['PA0', 'PA1', 'rsd0'], ['tmpo'])
                          STT(tmpo[:], PA[:, 1024:2048], rsd[:, 1, i:i + 1], tmpo[:], ALU.mult, ALU.add, ['PA2', 'PA3', 'rsd1', 'tmpo'], ['tmpo'])
                          STT(tmpo[:], PB[:, 0:1024], rsd[:, 2, i:i + 1], tmpo[:], ALU.mult, ALU.add, ['PB0', 'PB1', 'rsd2', 'tmpo'], ['tmpo'])
                          TT('dve', tmpo[:], tmpo[:], gt1[:], ALU.mult, ['tmpo', 'gt1'], ['tmpo'])
                          TT('dve', xs[:, i, :], xs[:, i, :], tmpo[:], ALU.add, ['x%d' % i, 'tmpo'], ['x%d' % i])
                      barrier()
                  fw.ctx = octx
              barrier()
              if not do_peer:
                  continue
              with ExitStack() as bctx:
                  octx = fw.ctx
                  fw.ctx = bctx
                  A2 = fw.sb("A2", [128, 1024], F32)
                  B2 = fw.sb("B2", [128, 1024], F32)
                  gt2 = fw.sb("gt2", [128, 1024], F32)
                  with ExitStack() as mctx:
                      fw.ctx = mctx
                      stg = fw.sb("stg", [128, 8, 512], F32)
                      bm = fw.sb("bm", [128, 1024], F32)
                      gm = fw.sb("gm", [128, 1024], F32)
                      mod_tile(l, 3, B2, 'B2', stg, bm)
                      mod_tile(l, 4, A2, 'A2', stg, bm)
                      mod_tile(l, 5, gt2, 'gt2', stg, bm)
                      fw.dma('sync', gm[:], gffn_d[l:l + 1, :].to_broadcast([128, 1024]), writes=['gm'])
                      STT(A2[:], A2[:], 1.0, gm[:], ALU.add, ALU.mult, ['A2', 'gm'], ['A2'])
                      barrier()
                  fw.ctx = bctx
                  NB = 16
                  ss = fw.sb("ss", [128, NT], F32)
                  junk = fw.sb("junk", [128, 1024], BF16)
                  wq = fw.sb("wq", [128, 8, 2048], BF16)
                  skt = fw.sb("skt", [128, 2, 128], BF16)
                  h2 = fw.sb("h2", [128, 1024], F32)
                  h2b = fw.sb("h2b", [128, 1024], BF16)
                  h2T = fw.sb("h2T", [128, 8, 128], BF16)
                  qTb = fw.sb("qTb", [128, 16, 128], BF16)
                  big = fw.sb("big", [128, 2048], F32)
                  sc = big[:].rearrange("p (a b) -> p a b", a=16)
                  cand = big[:].rearrange("p (h c) -> p h c", h=8)
                  eq = big[:].rearrange("p (h k a) -> p h k a", h=8, k=16)
                  top = fw.sb("top", [128, 16, 16], F32)
                  tix = fw.sb("tix", [128, 16, 16], U32)
                  tixf = fw.sb("tixf", [128, 16, 16], F32)
                  best = fw.sb("best", [128, 8, 16], F32)
                  pos = fw.sb("pos", [128, 8, 16], U32)
                  posf = fw.sb("posf", [128, 8, 16], F32)
                  ai = fw.sb("ai", [128, 8, 16], I32)
                  af = fw.sb("af", [128, 8, 16], F32)
                  bf = fw.sb("bf", [128, 8, 16], F32)
                  isel = fw.sb("isel", [128, 8, 16], F32)
                  jsel = fw.sb("jsel", [128, 8, 16], F32)
                  eidx2 = fw.sb("eidx", [128, 2, 128], I32)
                  gat = fw.sb("gat", [128, 8, 16], F32)
                  gz_ = fw.sb("gz_", [128, 8], F32)
                  actp = fw.sb("actp", [128, 128], F32)
                  actt = fw.sb("actt", [128, 128], F32)
                  wgt = fw.sb("wgt", [128, 128], F32)
                  acc = fw.sb("acc", [128, 1024], F32)
                  jf = fw.sb("jf", [128, 1024], F32) if JF else acc
                  gb = [fw.sb("gb%d" % j, [128, 1024], BF16) for j in range(NB)]
                  gv = gb
                  gvc = gbc = [0]
                  NBV = NB
                  wqv = wq_d[l].rearrange("(kc p) n -> p kc n", p=128)
                  for kc in range(8):
                      fw.dma('pool', wq[:, kc, :], wqv[:, kc, :], writes=['wq'])
                  fw.dma('pool', skt[:], skt_d[l], writes=['skt'])
                  for i in range(NT):
                      ACT(junk[:], xs[:, i, :], AF.Square, ['x%d' % i], ['junk', 'ss'], accum=ss[:, i:i + 1])
                  rstd_from_ss(ss[:], float(D), 'ss')
                  PAb = PA[:, 0:512].bitcast(BF16)
                  topv = top[:].rearrange("p (h two) k -> p h two k", two=2)
                  tixv = tixf[:].rearrange("p (h two) k -> p h two k", two=2)
                  dg = [fw.sb("dg%d" % k, [128, 128], BF16) for k in range(4)]
                  ubk = ['ubd%d' % c for c in range(16)]
                  vbk = ['vbd%d' % c for c in range(16)]

                  def idx_phase(i):
                      xk = 'x%d' % i
                      ek = 'eidx%d' % (i % 2)
                      eidx = eidx2[:, i % 2, :]
                      STT(jf[:], xs[:, i, :], ss[:, i:i + 1], A2[:], ALU.mult, ALU.mult, [xk, 'ss', 'A2'], ['acc', 'jf'])
                      TT('dve', h2[:], jf[:], B2[:], ALU.add, ['acc', 'jf', 'B2'], ['h2'])
                      CP('act', h2b[:], h2[:], ['h2'], ['h2b'])
                      TRG([(PAb[:, kc * 128:(kc + 1) * 128], h2b[:, kc * 128:(kc + 1) * 128]) for kc in range(8)],
                          identb[:], ['h2b', 'identb'], ['PA0'])
                      CP('act', h2T[:], PAb.rearrange("p (k t) -> p k t", k=8), ['PA0'], ['h2T'])
                      for half in range(2):
                          for hq2 in range(2):
                              hq = half * 2 + hq2
                              MMG([(PB[:, hq2 * 512 + j * 128:hq2 * 512 + (j + 1) * 128], wq[:, kc, (hq * 4 + j) * 128:(hq * 4 + j + 1) * 128], h2T[:, kc, :],
                                    kc == 0, kc == 7) for j in range(4) for kc in range(8)], ['wq', 'h2T'], [PBk[hq2]])
                          CP('act', qTb[:, half * 8:(half + 1) * 8, :].rearrange("p a b -> p (a b)"), PB[:, 0:1024], PBk[0:2], ['qTb'])
                      for hq in range(4):
                          MMG([(PA[:, hq * 512 + j * 128:hq * 512 + (j + 1) * 128], qTb[:, hq * 4 + j, :], skt[:, (hq * 4 + j) % 2, :], True, True)
                               for j in range(4)], ['qTb', 'skt'], [PAk[hq]])
                      CP('dve', big[:], PA[:, :], PAk, ['big'])
                      for hp in range(16):
                          fw.op('dve', lambda e, hp=hp: e.max(out=top[:, hp, 0:8], in_=sc[:, hp, :]), ['big'], ['top'])
                          fw.op('dve', lambda e, hp=hp: e.max_index(out=tix[:, hp, 0:8], in_max=top[:, hp, 0:8], in_values=sc[:, hp, :]), ['big', 'top'], ['tix'])
                          fw.op('dve', lambda e, hp=hp: e.match_replace(out=sc[:, hp, :], in_to_replace=top[:, hp, 0:8], in_values=sc[:, hp, :], imm_value=-1e30),
                                ['big', 'top'], ['big'])
                          fw.op('dve', lambda e, hp=hp: e.max(out=top[:, hp, 8:16], in_=sc[:, hp, :]), ['big'], ['top'])
                          fw.op('dve', lambda e, hp=hp: e.max_index(out=tix[:, hp, 8:16], in_max=top[:, hp, 8:16], in_values=sc[:, hp, :]), ['big', 'top'], ['tix'])
                      CP('dve', tixf[:], tix[:], ['tix'], ['tixf'])
                      TT('dve', cand.rearrange("p h (a b) -> p h a b", a=16), topv[:, :, 0, :].unsqueeze(3).to_broadcast([128, 8, 16, 16]),
                         topv[:, :, 1, :].unsqueeze(2).to_broadcast([128, 8, 16, 16]), ALU.add, ['top'], ['big'])
                      for h in range(8):
                          fw.op('dve', lambda e, h=h: e.max(out=best[:, h, 0:8], in_=cand[:, h, :]), ['big'], ['best'])
                          fw.op('dve', lambda e, h=h: e.max_index(out=pos[:, h, 0:8], in_max=best[:, h, 0:8], in_values=cand[:, h, :]), ['big', 'best'], ['pos'])
                          fw.op('dve', lambda e, h=h: e.match_replace(out=cand[:, h, :], in_to_replace=best[:, h, 0:8], in_values=cand[:, h, :], imm_value=-1e30),
                                ['big', 'best'], ['big'])
                          fw.op('dve', lambda e, h=h: e.max(out=best[:, h, 8:16], in_=cand[:, h, :]), ['big'], ['best'])
                          fw.op('dve', lambda e, h=h: e.max_index(out=pos[:, h, 8:16], in_max=best[:, h, 8:16], in_values=cand[:, h, :]), ['big', 'best'], ['pos'])
                      CP('dve', posf[:], pos[:], ['pos'], ['posf'])
                      TS('dve', ai[:], posf[:], 1.0 / 16.0, -7.5 / 16.0, ALU.mult, ALU.add, ['posf'], ['ai'])
                      CP('dve', af[:], ai[:], ['ai'], ['af'])
                      STT(bf[:], af[:], -16.0, posf[:], ALU.mult, ALU.add, ['af', 'posf'], ['bf'])
                      io_b = io16.unsqueeze(1).unsqueeze(1).to_broadcast([128, 8, 16, 16])
                      for (src, tv, dst, dk) in [(af, 0, isel, 'isel'), (bf, 1, jsel, 'jsel')]:
                          TT('dve', eq, src[:].unsqueeze(3).to_broadcast([128, 8, 16, 16]), io_b, ALU.is_equal, ['af', 'bf', 'cst'], ['big'])
                          TT('dve', eq, eq, tixv[:, :, tv, :].unsqueeze(2).to_broadcast([128, 8, 16, 16]), ALU.mult, ['big', 'tixf'], ['big'])
                          fw.op('dve', lambda e, dst=dst: e.tensor_reduce(out=dst[:], in_=eq, axis=AX.X, op=ALU.add), ['big'], [dk])
                      STT(isel[:], isel[:], 128.0, jsel[:], ALU.mult, ALU.add, ['isel', 'jsel'], ['isel'])
                      CP('dve', eidx, isel[:].rearrange("p h k -> p (h k)"), ['isel'], [ek])
                      TT('dve', gat[:], best[:], best[:, :, 0:1].to_broadcast([128, 8, 16]), ALU.subtract, ['best'], ['gat'])
                      ACT(gat[:], gat[:], AF.Exp, ['gat'], ['gat'])
                      fw.op('dve', lambda e: e.tensor_reduce(out=gz_[:], in_=gat[:], axis=AX.X, op=ALU.add), ['gat'], ['gz_'])
                      fw.op('dve', lambda e: e.reciprocal(out=gz_[:], in_=gz_[:]), ['gz_'], ['gz_'])
                      TT('dve', gat[:], gat[:], gz_[:].unsqueeze(2).to_broadcast([128, 8, 16]), ALU.mult, ['gat', 'gz_'], ['gat'])

                  def u_phase(i):
                      ek = 'eidx%d' % (i % 2)
                      for n in range(128):
                          j = gbc[0] % NB
                          gbc[0] += 1
                          fw.dma('pool', gb[j][:], ub_d[l], reads=[ek] + ubk, writes=['gb%d' % j],
                                 indirect=bass.IndirectOffsetOnAxis(ap=eidx2[:, i % 2, n:n + 1], axis=0))
                          fw.op('dve', lambda e, j=j, n=n: e.scalar_tensor_tensor(out=(jf[:] if JF else junk[:]), in0=gb[j][:], scalar=1.0, in1=h2[:],
                                                                                op0=ALU.mult, op1=ALU.mult, accum_out=actp[:, n:n + 1]),
                                ['gb%d' % j, 'h2'], ['junk', 'jf', 'actp'])
                      gelu_inplace('dve', actp[:], actt[:], 'actp', 'actt')
                      TT('dve', wgt[:], actp[:], gat[:].rearrange("p h k -> p (h k)"), ALU.mult, ['actp', 'gat'], ['wgt'])

                  def v_phase(i):
                      xk = 'x%d' % i
                      ek = 'eidx%d' % (i % 2)
                      for n in range(128):
                          j = gvc[0] % NBV
                          gvc[0] += 1
                          k = n % 4
                          fw.dma('pool', gv[j][:], vb_d[l], reads=[ek] + vbk, writes=['gb%d' % j],
                                 indirect=bass.IndirectOffsetOnAxis(ap=eidx2[:, i % 2, n:n + 1], axis=0))
                          fw.op('act', lambda e, k=k, n=n: e.activation(out=dg[k][:], in_=identf, func=AF.Copy, scale=wgt[:, n:n + 1]),
                                ['cst', 'wgt'], ['dg%d' % k])
                          MMG([(PB[:, 1024:1536], dg[k][:], gv[j][:, 0:512], n == 0, n == 127),
                               (PB[:, 1536:2048], dg[k][:], gv[j][:, 512:1024], n == 0, n == 127)],
                              ['dg%d' % k, 'gb%d' % j], ['PB2', 'PB3'])
                      TT('dve', acc[:], PB[:, 1024:2048], gt2[:], ALU.mult, ['PB2', 'PB3', 'gt2'], ['acc'])
                      TT('dve', xs[:, i, :], xs[:, i, :], acc[:], ALU.add, [xk, 'acc'], [xk])

                  if PIPE:
                      idx_phase(0)
                      for i in range(NT):
                          u_phase(i)
                          if i + 1 < NT:
                              idx_phase(i + 1)
                          v_phase(i)
                  else:
                      for i in range(NT):
                          idx_phase(i)
                          u_phase(i)
                          v_phase(i)
                  barrier()
                  fw.ctx = octx
              barrier()
          except _Stop:
            fw.ctx = ctx
            barrier()
            break

        fw.mute = False
        barrier()
        with ExitStack() as fctx:
            fw.ctx = fctx
            gf = fw.sb("gf", [128, 1024], F32)
            ss = fw.sb("ss", [128, NT], F32)
            junk = fw.sb("junk", [128, 1024], BF16)
            ob = [fw.sb("ob%d" % j, [128, 1024], F32) for j in range(2)]
            fw.dma('sync', gf[:], gfin_d[0:1, :].to_broadcast([128, 1024]), writes=['gf'])
            for i in range(NT):
                ACT(junk[:], xs[:, i, :], AF.Square, ['x%d' % i], ['junk', 'ss'], accum=ss[:, i:i + 1])
            rstd_from_ss(ss[:], float(D), 'ss')
            outk = []
            for i in range(NT):
                j = i % 2
                STT(ob[j][:], xs[:, i, :], ss[:, i:i + 1], gf[:], ALU.mult, ALU.mult, ['x%d' % i, 'ss', 'gf'], ['ob%d' % j])
                fw.dma('sync', out_d[i * 128:(i + 1) * 128, :], ob[j][:], reads=['ob%d' % j], writes=['out%d' % i])
                outk.append('out%d' % i)
            fw.finish(outk)
    return nc


def _consts():
    c = np.zeros((128, CSTW), np.float32)
    c[:, 0:128] = np.eye(128, dtype=np.float32)
    c[:, 128:256] = 1.0
    s = np.arange(128)[:, None]
    t = np.arange(128)[None, :]
    c[:, 256:384] = ((s // 32 == t // 32) & (s <= t)).astype(np.float32)
    c[:, 384] = np.pi / 2
    c[:, 385] = EPS
    c[:, 386] = 1.0
    c[:, 400:416] = np.arange(16, dtype=np.float32)[None, :]
    c[:, 416:928] = np.arange(1, 513, dtype=np.float32)[None, :]
    rm = np.ones(2048, np.float32)
    rm[::32] = 0.0
    c[:, 928:928 + 2048] = rm[None, :]
    return c


def _layouts(inp):
    f = lambda k: np.asarray(inp[k], dtype=np.float32)
    small = np.zeros((4, 128, NSMALL), np.float32)
    crt = np.zeros((4, 128, 8, 128), np.float32)
    cit = np.zeros((4, 128, 8, 128), np.float32)
    wa = np.zeros((4, 128, 2, 128), np.float32)
    wx = np.zeros((4, 128, 2, 128), np.float32)
    lbl = f('hgrn_lb_logits').reshape(4, 4, 128).transpose(2, 1, 0).reshape(128, 16)
    st = lambda a: a.reshape(8, 2, 64).transpose(1, 2, 0).reshape(128, 8)
    rep = lambda a: np.broadcast_to(a.reshape(2, 8, 1, 64), (2, 8, 16, 64)).transpose(1, 2, 0, 3).reshape(128, 128)
    col2 = lambda v: v.reshape(2, 128).T
    rmk = np.zeros((128, 8), np.float32)
    for gl in range(8):
        rmk[gl * 16:(gl + 1) * 16, gl] = 1.0
    for l in range(4):
        small[l, :, O_LBL:O_LBL + 16] = lbl
        small[l, :, O_ARS:O_ARS + 8] = st(f('s5_a_re')[l])
        small[l, :, O_AIS:O_AIS + 8] = st(f('s5_a_im')[l])
        ld = f('s5_log_dt')[l]
        small[l, :, O_LDS:O_LDS + 8] = st(np.broadcast_to(ld[:, None], (16, 64)).copy())
        small[l, :, O_ARR:O_ARR + 128] = rep(f('s5_a_re')[l])
        small[l, :, O_AIR:O_AIR + 128] = rep(f('s5_a_im')[l])
        small[l, :, O_BTR:O_BTR + 128] = f('s5_b_re')[l].reshape(2, 8, 64, 16).transpose(1, 3, 0, 2).reshape(128, 128)
        small[l, :, O_BTI:O_BTI + 128] = f('s5_b_im')[l].reshape(2, 8, 64, 16).transpose(1, 3, 0, 2).reshape(128, 128)
        small[l, :, O_LDR:O_LDR + 2] = np.broadcast_to(ld.reshape(2, 8, 1), (2, 8, 16)).transpose(1, 2, 0).reshape(128, 2)
        small[l, :, O_S5D:O_S5D + 2] = col2(f('s5_d')[l])
        small[l, :, O_BGL:O_BGL + 2] = col2(f('s5_b_glu')[l])
        small[l, :, O_CW:O_CW + 8] = f('lru_conv_w')[l].reshape(4, 2, 128).transpose(2, 1, 0).reshape(128, 8)
        small[l, :, O_CB:O_CB + 2] = col2(f('lru_conv_b')[l])
        small[l, :, O_BA:O_BA + 2] = col2(f('lru_b_a')[l])
        small[l, :, O_BX:O_BX + 2] = col2(f('lru_b_x')[l])
        small[l, :, O_LAM:O_LAM + 2] = col2(f('lru_lambda')[l])
        small[l, :, O_GBR:O_GBR + 8] = f('g_branch')[l].reshape(8, 128).T
        small[l, :, O_RMK:O_RMK + 8] = rmk
        for g in range(16):
            gp, gi, gl = g // 2, g % 2, g % 8
            crt[l, gi * 64:(gi + 1) * 64, gp, gl * 16:(gl + 1) * 16] = f('s5_c_re')[l, g].T
            cit[l, gi * 64:(gi + 1) * 64, gp, gl * 16:(gl + 1) * 16] = f('s5_c_im')[l, g].T
        for h in range(4):
            hc, o = h // 2, (h % 2) * 64
            wa[l, o:o + 64, hc, o:o + 64] = f('lru_w_a')[l, h]
            wx[l, o:o + 64, hc, o:o + 64] = f('lru_w_x')[l, h]
    skt = np.ascontiguousarray(f('peer_sub_keys').transpose(0, 3, 1, 2))
    return dict(small=small, crt=crt, cit=cit, wa_bd=wa, wx_bd=wx, skt=skt)


def make_in_maps(inp, cores):
    f = lambda k: np.ascontiguousarray(np.asarray(inp[k], dtype=np.float32))
    lay = _layouts(inp)
    shared = dict(w_mod=f('w_mod'), b_mod=f('b_mod'), g_mix=f('g_mix'), w_in=f('w_in'), w_glu=f('s5_w_glu'),
                  g_branch=f('g_branch'), w_out=f('w_out'), g_ffn=f('g_ffn'), peer_w_q=f('peer_w_q'),
                  g_final=f('g_final').reshape(1, D), consts=_consts())
    shared.update(lay)
    pu, pv = f('peer_u'), f('peer_v')
    for l_ in range(4):
        shared['peer_u%d' % l_] = pu[l_]
        shared['peer_v%d' % l_] = pv[l_]
    x = f('x')
    c = f('c')
    maps = []
    for b in cores:
        m = dict(shared)
        m['x'] = np.ascontiguousarray(x[b])
        m['c'] = np.ascontiguousarray(c[b].reshape(128, 8))
        maps.append(m)
    return maps


def kernel(**inputs):
    nc = build()
    maps = make_in_maps(inputs, list(range(8)))
    res = run_bass_kernel_spmd(nc, maps, core_ids=list(range(8)))
    return np.stack([np.asarray(r['out'], dtype=np.float32) for r in res.results], axis=0)
```

```python
import numpy as np
from contextlib import ExitStack
import concourse.bass as bass
import concourse.mybir as mybir
from concourse.bass_utils import run_bass_kernel_spmd

F32 = mybir.dt.float32
BF16 = mybir.dt.bfloat16
U32 = mybir.dt.uint32
I32 = mybir.dt.int32
AF = mybir.ActivationFunctionType
ALU = mybir.AluOpType
AX = mybir.AxisListType
F32R = mybir.dt.float32r

ENG = ['sync', 'act', 'pool', 'pe', 'dve']


class FW:
    def __init__(self, nc, ctx, slots=None):
        self.nc = nc
        self.ctx = ctx
        self.q = {e: [] for e in ENG}
        self.sems = {}
        self.cnt = {}
        for e in ENG:
            self.sems[e] = ctx.enter_context(nc.semaphore('s_' + e))
            self.cnt[e] = 0
        slots = slots or {'sync': 8, 'act': 4, 'pool': 8}
        self.slots = {}
        self.rr = {}
        for e, n in slots.items():
            self.slots[e] = []
            self.rr[e] = 0
            for i in range(n):
                nm = 'd_%s%d' % (e, i)
                self.sems[nm] = ctx.enter_context(nc.semaphore(nm))
                self.cnt[nm] = 0
                self.slots[e].append(nm)
        self.seen = {e: {} for e in ENG}
        self.lastw = {}
        self.readers = {}
        self.nins = {e: 0 for e in ENG}

    def sb(self, name, shape, dtype):
        self.uid = getattr(self, 'uid', 0) + 1
        if not hasattr(self, 'names'):
            self.names = {}
        self.names.setdefault(name, []).append('%s_u%d' % (name, self.uid))
        return self.ctx.enter_context(self.nc.sbuf_tensor('%s_u%d' % (name, self.uid), list(shape), dtype))

    def ps(self, name, shape, dtype):
        return self.ctx.enter_context(self.nc.psum_tensor(name, list(shape), dtype))

    def _wait(self, eng, s, v):
        if self.mute:
            return
        if self.seen[eng].get(s, 0) < v:
            self.seen[eng][s] = v
            sem = self.sems[s]
            self.q[eng].append(lambda e, sem=sem, v=v: e.wait_ge(sem, v))

    def _wait_deps(self, eng, reads, writes):
        deps = {}
        for k in list(reads) + list(writes):
            ev = self.lastw.get(k)
            if ev is not None and deps.get(ev[0], 0) < ev[1]:
                deps[ev[0]] = ev[1]
        for k in writes:
            for s, v in self.readers.get(k, {}).items():
                if deps.get(s, 0) < v:
                    deps[s] = v
        for s, v in deps.items():
            self._wait(eng, s, v)

    def _record(self, ev, reads, writes):
        s, v = ev
        ws = set(writes)
        for k in ws:
            self.lastw[k] = ev
            self.readers[k] = {}
        for k in reads:
            if k in ws:
                continue
            r = self.readers.setdefault(k, {})
            if r.get(s, 0) < v:
                r[s] = v

    mute = False

    def op(self, eng, fn, reads=(), writes=()):
        if self.mute:
            return
        self._wait_deps(eng, reads, writes)
        self.cnt[eng] += 1
        v = self.cnt[eng]
        sem = self.sems[eng]
        self.q[eng].append(lambda e, fn=fn, sem=sem: fn(e).then_inc(sem, 1))
        self.nins[eng] += 1
        self._record((eng, v), reads, writes)

    def dma(self, eng, out, in_, reads=(), writes=(), indirect=None, **kw):
        if self.mute:
            return
        self._wait_deps(eng, reads, writes)
        sl = self.slots[eng]
        slot = sl[self.rr[eng] % len(sl)]
        self.rr[eng] += 1
        if self.cnt[slot] > 0:
            self._wait(eng, slot, self.cnt[slot])
        self.cnt[slot] += 16
        v = self.cnt[slot]
        sem = self.sems[slot]
        if indirect is None:
            self.q[eng].append(lambda e, out=out, in_=in_, sem=sem, kw=kw:
                               e.dma_start(out=out, in_=in_, **kw).then_inc(sem, 16))
        else:
            self.q[eng].append(lambda e, out=out, in_=in_, sem=sem, ind=indirect, kw=kw:
                               e.indirect_dma_start(out=out, out_offset=None, in_=in_, in_offset=ind, **kw).then_inc(sem, 16))
        self.nins[eng] += 1
        self._record((slot, v), reads, writes)

    def finish(self, out_keys):
        self._wait_deps('sync', out_keys, [])
        q = self.q
        with self.nc.Block() as block:
            @block.sync
            def _(e):
                for f in q['sync']:
                    f(e)

            @block.scalar
            def _(e):
                for f in q['act']:
                    f(e)

            @block.gpsimd
            def _(e):
                for f in q['pool']:
                    f(e)

            @block.tensor
            def _(e):
                for f in q['pe']:
                    f(e)

            @block.vector
            def _(e):
                for f in q['dve']:
                    f(e)


import os
PIPE = 1
JF = 1
T = 2048
D = 1024
NT = 16
EPS = 1e-6
TWO_PI = float(2 * np.pi)
CW1 = 6.28125
CW2 = 0.0019353071795864769
GK = 1.5957691216057308

CSTW = 384 + 32 + 512 + 2048
NSMALL = 16 + 24 + 128 * 4 + 2 + 2 + 2 + 8 + 2 + 2 + 2 + 2 + 8 + 8
O_LBL = 0
O_ARS = 16
O_AIS = 24
O_LDS = 32
O_ARR = 40
O_AIR = 168
O_BTR = 296
O_BTI = 424
O_LDR = 552
O_S5D = 554
O_BGL = 556
O_CW = 558
O_CB = 566
O_BA = 568
O_BX = 570
O_LAM = 572
O_GBR = 574
O_RMK = 582


class _Stop(Exception):
    pass


def build(n_layers=4, dbg=None, do_peer=True, stop=None):
    nc = bass.Bass("TRN2", target_bir_lowering=False)

    def din(name, shape, dt=F32):
        return nc.dram_tensor(name, list(shape), dt, kind="ExternalInput").ap()

    x_d = din("x", [T, D])
    c_d = din("c", [128, 8])
    wmod_d = din("w_mod", [4, D, 6 * D])
    bmod_d = din("b_mod", [4, 6 * D])
    gmix_d = din("g_mix", [4, D])
    win_d = din("w_in", [4, D, 2816])
    small_d = din("small", [4, 128, NSMALL])
    crt_d = din("crt", [4, 128, 8, 128])
    cit_d = din("cit", [4, 128, 8, 128])
    wglu_d = din("w_glu", [4, 256, 256])
    wa_d = din("wa_bd", [4, 128, 2, 128])
    wx_d = din("wx_bd", [4, 128, 2, 128])
    gbrow_d = din("g_branch", [4, D])
    wout_d = din("w_out", [4, D, D])
    gffn_d = din("g_ffn", [4, D])
    wq_d = din("peer_w_q", [4, D, 2048])
    skt_d = din("skt", [4, 128, 2, 128])
    pu_d = [din("peer_u%d" % l_, [16384, D]) for l_ in range(4)]
    pv_d = [din("peer_v%d" % l_, [16384, D]) for l_ in range(4)]
    gfin_d = din("g_final", [1, D])
    consts_d = din("consts", [128, CSTW])
    out_d = nc.dram_tensor("out", [T, D], F32, kind="ExternalOutput").ap()
    ub_d = [nc.dram_tensor("ub_scr%d" % l_, [16384, D], BF16, kind="Internal").ap() for l_ in range(4)]
    vb_d = [nc.dram_tensor("vb_scr%d" % l_, [16384, D], BF16, kind="Internal").ap() for l_ in range(4)]

    with ExitStack() as ctx:
        fw = FW(nc, ctx, slots={'sync': 8, 'act': 2, 'pool': 10})
        build.fw = fw

        def TS(eng, out, in0, s1, s2, op0, op1, r, w):
            fw.op(eng, lambda e: e.tensor_scalar(out=out, in0=in0, scalar1=s1, scalar2=s2, op0=op0, op1=op1), r, w)

        def TT(eng, out, in0, in1, op, r, w):
            fw.op(eng, lambda e: e.tensor_tensor(out=out, in0=in0, in1=in1, op=op), r, w)

        def STT(out, in0, scalar, in1, op0, op1, r, w):
            fw.op('dve', lambda e: e.scalar_tensor_tensor(out=out, in0=in0, scalar=scalar, in1=in1, op0=op0, op1=op1), r, w)

        def ACT(out, in_, func, r, w, scale=1.0, bias=0.0, accum=None):
            r = list(r) + ['cst', 'sm']
            if accum is None:
                fw.op('act', lambda e: e.activation(out=out, in_=in_, func=func, scale=scale, bias=bias), r, w)
            else:
                fw.op('act', lambda e: e.activation(out=out, in_=in_, func=func, scale=scale, bias=bias, accum_out=accum), r, w)

        def CP(eng, out, in_, r, w):
            if eng == 'act':
                fw.op(eng, lambda e: e.activation(out=out, in_=in_, func=AF.Copy), r, w)
            else:
                fw.op(eng, lambda e: e.tensor_copy(out=out, in_=in_), r, w)

        def SCAN(out, d0, d1, init, r, w):
            fw.op('dve', lambda e: e.tensor_tensor_scan(out=out, data0=d0, data1=d1, initial=init, op0=ALU.mult, op1=ALU.add), r, w)

        def MMG(mms, r, w):
            def f(e, mms=mms):
                ins = None
                for mm in mms:
                    (o, l, rh, st, sp) = mm[:5]
                    if len(mm) > 5:
                        ins = e.matmul(o, l, rh, start=st, stop=sp, skip_group_check=True)
                    else:
                        ins = e.matmul(o, l, rh, start=st, stop=sp)
                return ins
            fw.op('pe', f, r, w)

        def TRG(trs, ident, r, w):
            def f(e, trs=trs):
                ins = None
                for (o, i) in trs:
                    ins = e.transpose(o, i, ident)
                return ins
            fw.op('pe', f, r, w)

        def barrier():
            allsems = list(fw.cnt.items())
            for e in ENG:
                for s, v in allsems:
                    if v > 0:
                        fw._wait(e, s, v)

        xs = fw.sb("xs", [128, NT, D], F32)
        cst = fw.sb("cst", [128, 928], F32)
        identf = cst[:, 0:128]
        onesf = cst[:, 128:256]
        maskbd = cst[:, 256:384]
        halfpi = cst[:, 384:385]
        epscol = cst[:, 385:386]
        onecol = cst[:, 386:387]
        io16 = cst[:, 400:416]
        tau = cst[:, 416:416 + 512]
        rmask_t = fw.sb("rmask", [128, 2048], BF16)
        rmask = rmask_t[:]
        identb = fw.sb("identb", [128, 128], BF16)
        condr = fw.sb("condr", [128, 8, 128], F32)
        cond = fw.sb("cond", [128, 8], F32)
        PA = fw.ps("PA", [128, 2048], F32)
        PB = fw.ps("PB", [128, 2048], F32)
        PAk = ['PA0', 'PA1', 'PA2', 'PA3']
        PBk = ['PB0', 'PB1', 'PB2', 'PB3']

        fw.dma('sync', cst[:, 0:928], consts_d[:, 0:928], writes=['cst'])
        fw.dma('pool', rmask_t[:], consts_d[:, 928:928 + 2048], writes=['cst'])
        for i in range(NT):
            fw.dma('sync', xs[:, i, :], x_d[i * 128:(i + 1) * 128, :], writes=['x%d' % i])
        fw.dma('sync', cond[:], c_d[:, :], writes=['cond'])
        CP('dve', identb[:], identf, ['cst'], ['identb'])
        ACT(cond[:], cond[:], AF.Silu, ['cond'], ['cond'])
        CP('dve', condr[:], cond[:, :].unsqueeze(2).to_broadcast([128, 8, 128]), ['cond'], ['condr'])

        def mod_tile(l, j, out_tile, okey, stg, bm):
            wv = wmod_d[l].rearrange("(p kc) n -> p kc n", kc=8)
            fw.dma('sync', bm[:], bmod_d[l:l + 1, j * 1024:(j + 1) * 1024].to_broadcast([128, 1024]), writes=['bm'])
            for half in range(2):
                c0 = j * 1024 + half * 512
                fw.dma('sync', stg[:], wv[:, :, c0:c0 + 512], writes=['stg'])
                MMG([(PA[:, 0:512], condr[:, kc, :], stg[:, kc, :], kc == 0, kc == 7) for kc in range(8)],
                    ['condr', 'stg'], ['PA0'])
                TT('dve', out_tile[:, half * 512:(half + 1) * 512], PA[:, 0:512], bm[:, half * 512:(half + 1) * 512], ALU.add,
                   ['PA0', 'bm'], [okey])

        def gelu_inplace(eng_t, t, tmp, key, tkey):
            ACT(tmp, t, AF.Square, [key], [tkey])
            TS('dve', tmp, tmp, 0.044715, 1.0, ALU.mult, ALU.add, [tkey], [tkey])
            TT('dve', tmp, tmp, t, ALU.mult, [tkey, key], [tkey])
            ACT(tmp, tmp, AF.Sigmoid, [tkey], [tkey], scale=GK)
            TT('dve', t, t, tmp, ALU.mult, [key, tkey], [key])

        def rstd_from_ss(ss_ap, n, key):
            ACT(ss_ap, ss_ap, AF.Sqrt, [key], [key], scale=1.0 / n, bias=epscol)
            fw.op('dve', lambda e: e.reciprocal(out=ss_ap, in_=ss_ap), [key], [key])

        def norm_to_T(A, B, hT, hkeys, keep=None):
            pass

        for l in range(n_layers):
          try:
              with ExitStack() as actx:
                  octx = fw.ctx
                  fw.ctx = actx
                  hT = fw.sb("hT", [128, 8, T], BF16)
                  yT = fw.sb("yT", [128, 8, T], BF16)
                  sm = fw.sb("sm", [128, NSMALL], F32)
                  rsd = fw.sb("rsd", [128, 3, NT], F32)
                  fw.dma('sync', sm[:], small_d[l], writes=['sm'])
                  if do_peer:
                      for (src, dst, key) in [(pu_d[l], ub_d[l], 'ubd'), (pv_d[l], vb_d[l], 'vbd')]:
                          for c in range(16):
                              fw.dma('pool', dst[c * 1024:(c + 1) * 1024, :].rearrange("(p r) d -> p r d", p=128),
                                     src[c * 1024:(c + 1) * 1024, :].rearrange("(p r) d -> p r d", p=128), writes=[key + str(c)])
                  with ExitStack() as sctx:
                      fw.ctx = sctx
                      stg = fw.sb("stg", [128, 8, 512], F32)
                      bm = fw.sb("bm", [128, 1024], F32)
                      A1 = fw.sb("A1", [128, 1024], F32)
                      B1 = fw.sb("B1", [128, 1024], F32)
                      gm = fw.sb("gm", [128, 1024], F32)
                      ss = fw.sb("ss", [128, NT], F32)
                      junk = fw.sb("junk", [128, 1024], BF16)
                      tmpf = fw.sb("tmpf", [128, 1024], F32)
                      hb = fw.sb("hb", [128, 1024], BF16)
                      mod_tile(l, 0, B1, 'B1', stg, bm)
                      mod_tile(l, 1, A1, 'A1', stg, bm)
                      fw.dma('sync', gm[:], gmix_d[l:l + 1, :].to_broadcast([128, 1024]), writes=['gm'])
                      STT(A1[:], A1[:], 1.0, gm[:], ALU.add, ALU.mult, ['A1', 'gm'], ['A1'])
                      for i in range(NT):
                          ACT(junk[:], xs[:, i, :], AF.Square, ['x%d' % i], ['junk', 'ss'], accum=ss[:, i:i + 1])
                      rstd_from_ss(ss[:], float(D), 'ss')
                      PAb = PA[:, 0:512].bitcast(BF16)
                      for i in range(NT):
                          STT(tmpf[:], xs[:, i, :], ss[:, i:i + 1], A1[:], ALU.mult, ALU.mult, ['x%d' % i, 'ss', 'A1'], ['tmpf'])
                          TT('dve', hb[:], tmpf[:], B1[:], ALU.add, ['tmpf', 'B1'], ['hb'])
                          TRG([(PAb[:, kc * 128:(kc + 1) * 128], hb[:, kc * 128:(kc + 1) * 128]) for kc in range(8)],
                              identb[:], ['hb', 'identb'], ['PA0'])
                          CP('act', hT[:, :, i * 128:(i + 1) * 128], PAb.rearrange("p (k t) -> p k t", k=8), ['PA0'], ['hT'])
                      barrier()
                  fw.ctx = actx
                  if stop == 'norm':
                      fw.mute = True
                  lbt = fw.sb("lbt", [128, 16], F32)
                  lbz = fw.sb("lbz", [128, 4], F32)
                  lb = fw.sb("lb", [128, 4], F32)
                  oml = fw.sb("oml", [128, 4], F32)
                  ACT(lbt[:], sm[:, O_LBL:O_LBL + 16], AF.Exp, ['sm'], ['lbt'])
                  lbv = lbt[:].rearrange("p (h l) -> p h l", h=4)
                  fw.op('dve', lambda e: e.tensor_reduce(out=lbz[:], in_=lbv, op=ALU.add, axis=AX.X), ['lbt'], ['lbz'])
                  fw.op('dve', lambda e: e.reciprocal(out=lbz[:], in_=lbz[:]), ['lbz'], ['lbz'])
                  fw.op('dve', lambda e: e.memset(lb[:], 0.0), [], ['lb'])
                  for j in range(1, l + 1):
                      TT('dve', lb[:], lb[:], lbv[:, :, j], ALU.add, ['lb', 'lbt'], ['lb'])
                  TT('dve', lb[:], lb[:], lbz[:], ALU.mult, ['lb', 'lbz'], ['lb'])
                  TS('dve', oml[:], lb[:], -1.0, 1.0, ALU.mult, ALU.add, ['lb'], ['oml'])
                  winv = win_d[l].rearrange("(kc p) n -> p kc n", p=128)

                  with ExitStack() as sctx:
                      fw.ctx = sctx
                      wh = fw.sb("wh", [128, 8, 512], BF16)
                      t1 = fw.sb("t1", [128, T], F32)
                      t2 = fw.sb("t2", [128, T], F32)
                      t3 = fw.sb("t3", [128, T], F32)
                      kt = fw.sb("kt", [128, T], BF16)
                      kh = fw.sb("kh", [128, T], BF16)
                      qt = fw.sb("qt", [128, T], BF16)
                      khk = fw.sb("khk", [128, NT, 128], BF16)
                      khkz = fw.sb("khkz", [128, NT, 128], BF16)
                      qz = fw.sb("qz", [128, T], BF16)
                      fw.op('dve', lambda e: e.memset(qz[:], 0.0), [], ['qz'])
                      zer = fw.sb("zer", [128, 128], BF16)
                      fw.op('dve', lambda e: e.memset(zer[:], 0.0), [], ['zer'])
                      el = fw.sb("el", [128, 64], F32)
                      S = fw.sb("S", [128, 128], F32)
                      Sb = fw.sb("Sb", [128, 128], BF16)
                      vb = fw.sb("vb", [128, 128], BF16)
                      gs = fw.sb("gs", [128, 128], F32)
                      scm = fw.sb("scm", [128, 128], BF16)
                      yh = fw.sb("yh", [128, 128], F32)
                      yhb = fw.sb("yhb", [128, 128], BF16)
                      jk = fw.sb("jk", [128, 128], F32)
                      ssh = fw.sb("ssh", [128, 2], F32)
                      ssa = fw.sb("ssa", [128, 4, NT], F32)
                      gbb = fw.sb("gbb", [128, 512], F32)
                      fw.dma('sync', gbb[:], gbrow_d[l:l + 1, 0:512].to_broadcast([128, 512]), writes=['gbb'])
                      PAb = PA[:, 0:1024].bitcast(BF16)
                      for h in range(4):
                          for j, c0 in enumerate([h * 128, 512 + h * 128, 1024 + h * 128, 1536 + h * 128]):
                              fw.dma('pool', wh[:, :, j * 128:(j + 1) * 128], winv[:, :, c0:c0 + 128], writes=['wh'])
                          for tb in range(4):
                              MMG([(PA[:, tb * 512:(tb + 1) * 512], wh[:, kc, 128:256], hT[:, kc, tb * 512:(tb + 1) * 512], kc == 0, kc == 7)
                                   for kc in range(8)], ['wh', 'hT'], [PAk[tb]])
                              ACT(t1[:, tb * 512:(tb + 1) * 512], PA[:, tb * 512:(tb + 1) * 512], AF.Sigmoid, [PAk[tb]], ['t1'])
                          TS('dve', t1[:], t1[:], oml[:, h:h + 1], lb[:, h:h + 1], ALU.mult, ALU.add, ['t1', 'oml', 'lb'], ['t1'])
                          if stop == 'hg_a':
                              fw.mute = True
                          ACT(t2[:], t1[:], AF.Ln, ['t1'], ['t2'])
                          SCAN(t3[:], rmask, t2[:], 0.0, ['cst', 't2'], ['t3'])
                          if stop == 'hg_b':
                              fw.mute = True
                          TS('dve', t1[:], t1[:], -1.0, 1.0, ALU.mult, ALU.add, ['t1'], ['t1'])
                          ACT(t2[:], t3[:], AF.Exp, ['t3'], ['t2'], scale=-1.0)
                          TT('dve', kt[:], t1[:], t2[:], ALU.mult, ['t1', 't2'], ['kt'])
                          b3 = t3[:].rearrange("p (c s) -> p c s", s=32)
                          TT('dve', t2[:].rearrange("p (c s) -> p c s", s=32), b3[:, :, 31:32].to_broadcast([128, 64, 32]), b3, ALU.subtract,
                             ['t3'], ['t2'])
                          ACT(t2[:], t2[:], AF.Exp, ['t2'], ['t2'])
                          TT('dve', kh[:], t1[:], t2[:], ALU.mult, ['t1', 't2'], ['kh'])
                          ACT(t3[:], t3[:], AF.Exp, ['t3'], ['t3'])
                          CP('dve', el[:], t3[:].rearrange("p (c s) -> p c s", s=32)[:, :, 31], ['t3'], ['el'])
                          if stop == 'hg_c':
                              fw.mute = True
                          for tb in range(4):
                              MMG([(PB[:, tb * 512:(tb + 1) * 512], wh[:, kc, 0:128], hT[:, kc, tb * 512:(tb + 1) * 512], kc == 0, kc == 7)
                                   for kc in range(8)], ['wh', 'hT'], [PBk[tb]])
                              ACT(t1[:, tb * 512:(tb + 1) * 512], PB[:, tb * 512:(tb + 1) * 512], AF.Silu, [PBk[tb]], ['t1'])
                          TT('dve', qt[:], t1[:], t3[:], ALU.mult, ['t1', 't3'], ['qt'])
                          if stop == 'hg_c1':
                              fw.mute = True
                          CP('dve', qz[:].rearrange("p (i t) -> p i t", t=128)[:, :, 96:128], qt[:].rearrange("p (i t) -> p i t", t=128)[:, :, 96:128],
                             ['qt'], ['qz'])
                          if stop == 'hg_c2':
                              fw.mute = True
                          for half in range(2):
                              TRG([(PAb[:, j * 128:(j + 1) * 128], kh[:, (half * 8 + j) * 128:(half * 8 + j + 1) * 128]) for j in range(8)],
                                  identb[:], ['kh', 'identb'], ['PA0'])
                              CP('act', khk[:, half * 8:(half + 1) * 8, :], PAb[:, 0:1024].rearrange("p (j d) -> p j d", j=8), ['PA0'], ['khk'])
                              if stop == 'hg_c3':
                                  fw.mute = True
                              CP('act', khkz[64:128, half * 8:(half + 1) * 8, :], PAb[64:128, 0:1024].rearrange("p (j d) -> p j d", j=8), ['PA0'], ['khkz'])
                              if stop == 'hg_c4':
                                  fw.mute = True
                              fw.op('dve', lambda e, half=half: e.memset(khkz[64:96, half * 8:(half + 1) * 8, :], 0.0), [], ['khkz'])
                              if stop == 'hg_c5':
                                  fw.mute = True
                          fw.op('dve', lambda e: e.memset(S[:], 0.0), [], ['S'])
                          if stop == 'hg_c7':
                              fw.mute = True
                          fw.op('dve', lambda e: e.memset(Sb[:], 0.0), [], ['Sb'])
                          if stop == 'hg_d':
                              fw.mute = True
                          for i in range(NT):
                              tsl = slice(i * 128, (i + 1) * 128)
                              MMG([(PB[:, 0:256], hT[:, kc, tsl], wh[:, kc, 256:512], kc == 0, kc == 7) for kc in range(8)],
                                  ['hT', 'wh'], ['PB0'])
                              CP('act', vb[:], PB[:, 0:128], ['PB0'], ['vb'])
                              ACT(gs[:], PB[:, 128:256], AF.Silu, ['PB0'], ['gs'])
                              MMG([(PB[:, 512:640], kt[:, tsl], qt[:, tsl], True, True)], ['kt', 'qt'], ['PB1'])
                              TT('dve', scm[:], PB[:, 512:640], maskbd, ALU.mult, ['PB1', 'cst'], ['scm'])
                              if stop == 'hg_e':
                                  fw.mute = True
                              MMG([(PB[:, 1024:1152], scm[:], vb[:], True, False)], ['scm', 'vb'], ['PB2'])
                              for j in range(4):
                                  rs_ = slice(32 * j, 32 * j + 32)
                                  if j < 3:
                                      MMG([(PB[rs_, 1024:1152], qt[:, i * 128 + 32 * j:i * 128 + 32 * j + 32], Sb[:], False, False, 1)],
                                          ['qt', 'Sb'], ['PB2'])
                                      MMG([(PB[:, 1536:1664], khk[rs_, i, :], vb[rs_, :], True, True)], ['khk', 'vb'], ['PB3'])
                                  else:
                                      MMG([(PB[64:128, 1024:1152], qz[:, i * 128 + 64:i * 128 + 128], Sb[:], False, False, 1)],
                                          ['qz', 'Sb'], ['PB2'])
                                      MMG([(PB[:, 1536:1664], khkz[64:128, i, :], vb[64:128, :], True, True)], ['khkz', 'vb'], ['PB3'])
                                  STT(S[:], S[:], el[:, 4 * i + j:4 * i + j + 1], PB[:, 1536:1664], ALU.mult, ALU.add, ['S', 'el', 'PB3'], ['S'])
                                  CP('act', Sb[:], S[:], ['S'], ['Sb'])
                              MMG([(PB[:, 1024:1152], zer[:], vb[:], False, True)], ['zer', 'vb'], ['PB2'])
                              if stop == 'hg_f':
                                  fw.mute = True
                              ACT(jk[:], PB[:, 1024:1152], AF.Square, ['PB2'], ['jk', 'ssh'], accum=ssh[:, 0:1])
                              rstd_from_ss(ssh[:, 0:1], 128.0, 'ssh')
                              STT(yh[:], PB[:, 1024:1152], ssh[:, 0:1], gs[:], ALU.mult, ALU.mult, ['PB2', 'ssh', 'gs'], ['yh'])
                              ACT(jk[:], yh[:], AF.Square, ['yh'], ['jk', 'ssa'], accum=ssa[:, h, i:i + 1])
                              TT('dve', yhb[:], yh[:], gbb[:, h * 128:(h + 1) * 128], ALU.mult, ['yh', 'gbb'], ['yhb'])
                              TRG([(PAb[:, 1024:1152], yhb[:])], identb[:], ['yhb', 'identb'], ['PA1'])
                              CP('act', yT[:, h, tsl], PAb[:, 1024:1152], ['PA1'], ['yT%d' % h])
                      TT('dve', ssa[:, 0, :], ssa[:, 0, :], ssa[:, 1, :], ALU.add, ['ssa'], ['ssa'])
                      TT('dve', ssa[:, 2, :], ssa[:, 2, :], ssa[:, 3, :], ALU.add, ['ssa'], ['ssa'])
                      TT('dve', rsd[:, 0, :], ssa[:, 0, :], ssa[:, 2, :], ALU.add, ['ssa'], ['rsd0'])
                      rstd_from_ss(rsd[:, 0, :], 512.0, 'rsd0')
                      barrier()
                  fw.ctx = actx
                  if stop == 'hgrn':
                      fw.mute = True

                  with ExitStack() as sctx:
                      fw.ctx = sctx
                      wl = fw.sb("wl", [128, 8, 256], BF16)
                      wab = fw.sb("wab", [128, 2, 128], BF16)
                      wxb = fw.sb("wxb", [128, 2, 128], BF16)
                      xraw = fw.sb("xraw", [128, 3 + T], F32)
                      xc = fw.sb("xc", [128, T], F32)
                      xcb = fw.sb("xcb", [128, T], BF16)
                      ta = fw.sb("ta", [128, T], F32)
                      tb_ = fw.sb("tb_", [128, T], F32)
                      tr = fw.sb("tr", [128, T], F32)
                      ti = fw.sb("ti", [128, T], F32)
                      c8 = fw.sb("c8", [128, 2], F32)
                      c16 = fw.sb("c16", [128, 2], F32)
                      ssc = fw.sb("ssc", [128, 2, NT], F32)
                      fw.dma('pool', wab[:], wa_d[l], writes=['wab'])
                      fw.dma('pool', wxb[:], wx_d[l], writes=['wxb'])
                      ACT(c8[:], sm[:, O_LAM:O_LAM + 2], AF.Exp, ['sm'], ['c8'], scale=-1.0)
                      ACT(c8[:], c8[:], AF.Ln, ['c8'], ['c8'], bias=onecol)
                      TS('dve', c16[:], c8[:], -16.0, None, ALU.mult, ALU.bypass, ['c8'], ['c16'])
                      TS('dve', c8[:], c8[:], -8.0, None, ALU.mult, ALU.bypass, ['c8'], ['c8'])
                      fw.op('dve', lambda e: e.memset(xraw[:, 0:3], 0.0), [], ['xraw'])
                      for hc in range(2):
                          for j, c0 in enumerate([2304 + hc * 128, 2560 + hc * 128]):
                              fw.dma('pool', wl[:, :, j * 128:(j + 1) * 128], winv[:, :, c0:c0 + 128], writes=['wl'])
                          for tb in range(4):
                              MMG([(PA[:, tb * 512:(tb + 1) * 512], wl[:, kc, 0:128], hT[:, kc, tb * 512:(tb + 1) * 512], kc == 0, kc == 7)
                                   for kc in range(8)], ['wl', 'hT'], [PAk[tb]])
                              CP('act', xraw[:, 3 + tb * 512:3 + (tb + 1) * 512], PA[:, tb * 512:(tb + 1) * 512], [PAk[tb]], ['xraw'])
                          cw = sm[:, O_CW + hc * 4:O_CW + hc * 4 + 4]
                          TS('dve', xc[:], xraw[:, 3:3 + T], cw[:, 3:4], sm[:, O_CB + hc:O_CB + hc + 1], ALU.mult, ALU.add, ['xraw', 'sm'], ['xc'])
                          for w_ in range(3):
                              STT(xc[:], xraw[:, w_:w_ + T], cw[:, w_:w_ + 1], xc[:], ALU.mult, ALU.add, ['xraw', 'sm', 'xc'], ['xc'])
                          CP('act', xcb[:], xc[:], ['xc'], ['xcb'])
                          for tb in range(4):
                              sl = slice(tb * 512, (tb + 1) * 512)
                              MMG([(PB[:, sl], wab[:, hc, :], xcb[:, sl], True, True)], ['wab', 'xcb'], [PBk[tb]])
                              ACT(tr[:, sl], PB[:, sl], AF.Sigmoid, [PBk[tb]], ['tr'], bias=sm[:, O_BA + hc:O_BA + hc + 1])
                          for tb in range(4):
                              sl = slice(tb * 512, (tb + 1) * 512)
                              MMG([(PA[:, sl], wxb[:, hc, :], xcb[:, sl], True, True)], ['wxb', 'xcb'], [PAk[tb]])
                              ACT(ti[:, sl], PA[:, sl], AF.Sigmoid, [PAk[tb]], ['ti'], bias=sm[:, O_BX + hc:O_BX + hc + 1])
                          ACT(ta[:], tr[:], AF.Exp, ['tr', 'c8'], ['ta'], scale=c8[:, hc:hc + 1])
                          ACT(tb_[:], tr[:], AF.Exp, ['tr', 'c16'], ['tb_'], scale=c16[:, hc:hc + 1])
                          TS('dve', tb_[:], tb_[:], -1.0, 1.0, ALU.mult, ALU.add, ['tb_'], ['tb_'])
                          ACT(tb_[:], tb_[:], AF.Sqrt, ['tb_'], ['tb_'])
                          TT('dve', tb_[:], tb_[:], ti[:], ALU.mult, ['tb_', 'ti'], ['tb_'])
                          TT('dve', tb_[:], tb_[:], xc[:], ALU.mult, ['tb_', 'xc'], ['tb_'])
                          SCAN(tr[:], ta[:], tb_[:], 0.0, ['ta', 'tb_'], ['tr'])
                          for tb in range(4):
                              sl = slice(tb * 512, (tb + 1) * 512)
                              MMG([(PB[:, sl], wl[:, kc, 128:256], hT[:, kc, sl], kc == 0, kc == 7) for kc in range(8)],
                                  ['wl', 'hT'], [PBk[tb]])
                              CP('act', ti[:, sl], PB[:, sl], [PBk[tb]], ['ti'])
                          gelu_inplace('dve', ti[:], ta[:], 'ti', 'ta')
                          TT('dve', tb_[:], tr[:], ti[:], ALU.mult, ['tr', 'ti'], ['tb_'])
                          TS('dve', yT[:, 6 + hc, :], tb_[:], sm[:, O_GBR + 6 + hc:O_GBR + 7 + hc], None, ALU.mult, ALU.bypass,
                             ['tb_', 'sm'], ['yT%d' % (6 + hc)])
                          ACT(ta[:], tb_[:], AF.Square, ['tb_'], ['ta'])
                          MMG([(PA[:, 2 * i:2 * i + 2], ta[:, i * 128:(i + 1) * 128], onesf[:, 0:2], True, True) for i in range(NT)],
                              ['ta', 'cst'], ['PA0'])
                          CP('dve', ssc[:, hc, :], PA[:, 0:2 * NT].rearrange("p (i two) -> p i two", two=2)[:, :, 0], ['PA0'], ['ssc'])
                      TT('dve', rsd[:, 2, :], ssc[:, 0, :], ssc[:, 1, :], ALU.add, ['ssc'], ['rsd2'])
                      rstd_from_ss(rsd[:, 2, :], 256.0, 'rsd2')
                      barrier()
                  fw.ctx = actx
                  if stop == 'lru':
                      fw.mute = True

                  with ExitStack() as sctx:
                      fw.ctx = sctx
                      LP = 512
                      NP_ = T // LP
                      ws5 = fw.sb("ws5", [128, 8, 256], BF16)
                      crt = fw.sb("crt", [128, 8, 128], BF16)
                      cit = fw.sb("cit", [128, 8, 128], BF16)
                      wglu = fw.sb("wglu", [128, 2, 256], BF16)
                      ub = fw.sb("ub", [128, 2, T], BF16)
                      zb = fw.sb("zb", [128, 2, T], BF16)
                      lhb = fw.sb("lhb", [128, 8, 2, 128], BF16)
                      pst = fw.sb("pst", [128, 5, 8], F32)
                      prp = fw.sb("prp", [128, 12, 128], F32)
                      pri = fw.sb("pri", [128, 128], I32)
                      car = fw.sb("car", [128, 8, 2], F32)
                      tcos = fw.sb("tcos", [128, LP], F32)
                      tsin = fw.sb("tsin", [128, LP], F32)
                      tki = fw.sb("tki", [128, LP], I32)
                      s1 = fw.sb("s1", [128, LP], F32)
                      s2 = fw.sb("s2", [128, LP], F32)
                      swr = fw.sb("swr", [128, LP], F32)
                      swi = fw.sb("swi", [128, LP], F32)
                      szr = fw.sb("szr", [128, LP], F32)
                      szi = fw.sb("szi", [128, LP], F32)
                      xrb = fw.sb("xrb", [128, LP], BF16)
                      xib = fw.sb("xib", [128, LP], BF16)
                      yb = fw.sb("yb", [128, 512], F32)
                      yb2 = fw.sb("yb2", [128, 512], F32)
                      ssb = fw.sb("ssb", [128, 2, NT], F32)
                      fw.dma('pool', crt[:], crt_d[l], writes=['crt'])
                      fw.dma('pool', cit[:], cit_d[l], writes=['cit'])
                      fw.dma('pool', wglu[:], wglu_d[l].rearrange("(kc p) n -> p kc n", p=128), writes=['wglu'])
                      fw.dma('pool', ws5[:], winv[:, :, 2048:2304], writes=['ws5'])

                      def sincos(ang, ki, sin_o, cos_o, tmp, keys):
                          ka, kk, ks, kc_, kt_ = keys
                          TS('dve', ki, ang, 1.0 / TWO_PI, None, ALU.mult, ALU.bypass, [ka], [kk])
                          STT(ang, ki, -CW1, ang, ALU.mult, ALU.add, [kk, ka], [ka])
                          STT(ang, ki, -CW2, ang, ALU.mult, ALU.add, [kk, ka], [ka])
                          TS('dve', ang, ang, -3.14159, 3.14159, ALU.max, ALU.min, [ka], [ka])
                          ACT(sin_o, ang, AF.Sin, [ka], [ks])
                          STT(tmp, ang, -1.0, ang, ALU.mult, ALU.max, [ka], [kt_])
                          ACT(cos_o, tmp, AF.Sin, [kt_, 'cst'], [kc_], scale=-1.0, bias=halfpi)

                      lamre, dts, rmag, theta = pst[:, 0, :], pst[:, 1, :], pst[:, 2, :], pst[:, 3, :]
                      TS('dve', lamre, sm[:, O_ARS:O_ARS + 8], -1e-4, None, ALU.min, ALU.bypass, ['sm'], ['pst'])
                      ACT(dts, sm[:, O_LDS:O_LDS + 8], AF.Exp, ['sm'], ['pst'])
                      TT('dve', rmag, lamre, dts, ALU.mult, ['pst'], ['pst'])
                      ACT(rmag, rmag, AF.Exp, ['pst'], ['pst'])
                      TT('dve', theta, sm[:, O_AIS:O_AIS + 8], dts, ALU.mult, ['sm', 'pst'], ['pst'])
                      R = lambda k: prp[:, k, :]
                      lam_r, lam_i = R(0), R(1)
                      TS('dve', lam_r, sm[:, O_ARR:O_ARR + 128], -1e-4, None, ALU.min, ALU.bypass, ['sm'], ['prp'])
                      CP('dve', lam_i, sm[:, O_AIR:O_AIR + 128], ['sm'], ['prp'])
                      ACT(prp[:, 11, 0:2], sm[:, O_LDR:O_LDR + 2], AF.Exp, ['sm'], ['prp'])
                      for hg in range(2):
                          cs = slice(hg * 64, (hg + 1) * 64)
                          TS('dve', prp[:, 2, cs], prp[:, 0, cs], prp[:, 11, hg:hg + 1], None, ALU.mult, ALU.bypass, ['prp'], ['prp'])
                          TS('dve', prp[:, 3, cs], prp[:, 1, cs], prp[:, 11, hg:hg + 1], None, ALU.mult, ALU.bypass, ['prp'], ['prp'])
                      ACT(R(2), R(2), AF.Exp, ['prp'], ['prp'])
                      sincos(R(3), pri[:], R(4), R(5), R(6), ['prp', 'pri', 'prp', 'prp', 'prp'])
                      TT('dve', R(5), R(5), R(2), ALU.mult, ['prp'], ['prp'])
                      TT('dve', R(4), R(4), R(2), ALU.mult, ['prp'], ['prp'])
                      TS('dve', R(5), R(5), -1.0, None, ALU.add, ALU.bypass, ['prp'], ['prp'])
                      TT('dve', R(2), lam_r, lam_r, ALU.mult, ['prp'], ['prp'])
                      TT('dve', R(3), lam_i, lam_i, ALU.mult, ['prp'], ['prp'])
                      TT('dve', R(2), R(2), R(3), ALU.add, ['prp'], ['prp'])
                      fw.op('dve', lambda e: e.reciprocal(out=R(2), in_=R(2)), ['prp'], ['prp'])
                      TT('dve', R(6), R(5), lam_r, ALU.mult, ['prp'], ['prp'])
                      TT('dve', R(7), R(4), lam_i, ALU.mult, ['prp'], ['prp'])
                      TT('dve', R(6), R(6), R(7), ALU.add, ['prp'], ['prp'])
                      TT('dve', R(6), R(6), R(2), ALU.mult, ['prp'], ['prp'])
                      TT('dve', R(7), R(4), lam_r, ALU.mult, ['prp'], ['prp'])
                      TT('dve', R(8), R(5), lam_i, ALU.mult, ['prp'], ['prp'])
                      TT('dve', R(7), R(7), R(8), ALU.subtract, ['prp'], ['prp'])
                      TT('dve', R(7), R(7), R(2), ALU.mult, ['prp'], ['prp'])
                      btr, bti = sm[:, O_BTR:O_BTR + 128], sm[:, O_BTI:O_BTI + 128]
                      TT('dve', R(8), R(6), btr, ALU.mult, ['prp', 'sm'], ['prp'])
                      TT('dve', R(9), R(7), bti, ALU.mult, ['prp', 'sm'], ['prp'])
                      TT('dve', R(8), R(8), R(9), ALU.subtract, ['prp'], ['prp'])
                      TT('dve', R(9), R(6), bti, ALU.mult, ['prp', 'sm'], ['prp'])
                      TT('dve', R(10), R(7), btr, ALU.mult, ['prp', 'sm'], ['prp'])
                      TT('dve', R(9), R(9), R(10), ALU.add, ['prp'], ['prp'])
                      for gp in range(8):
                          hc = gp // 4
                          for gi in range(2):
                              gl = (2 * gp + gi) % 8
                              for ri, src in enumerate([8, 9]):
                                  TS('dve', lhb[:, gp, ri, gi * 64:(gi + 1) * 64], prp[:, src, hc * 64:(hc + 1) * 64],
                                     sm[:, O_RMK + gl:O_RMK + gl + 1], None, ALU.mult, ALU.bypass, ['prp', 'sm'], ['lhb'])
                      for hc in range(2):
                          for tb in range(4):
                              sl = slice(tb * 512, (tb + 1) * 512)
                              MMG([(PA[:, sl], ws5[:, kc, hc * 128:(hc + 1) * 128], hT[:, kc, sl], kc == 0, kc == 7) for kc in range(8)],
                                  ['ws5', 'hT'], [PAk[tb]])
                              CP('act', ub[:, hc, sl], PA[:, sl], [PAk[tb]], ['ub'])
                      fw.op('dve', lambda e: e.memset(car[:], 0.0), [], ['car'])
                      for hc in range(2):
                          for pc in range(NP_):
                              psl = slice(pc * LP, (pc + 1) * LP)
                              for gq in range(4):
                                  gp = hc * 4 + gq
                                  if True:
                                      TS('dve', s1[:], tau[:, 0:LP], theta[:, gp:gp + 1], None, ALU.mult, ALU.bypass, ['cst', 'pst'], ['s1'])
                                      sincos(s1[:], tki[:], tsin[:], tcos[:], s2[:], ['s1', 'tki', 'tsin', 'tcos', 's2'])
                                  MMG([(PA[:, 0:LP], lhb[:, gp, 0, :], ub[:, hc, psl], True, True),
                                       (PA[:, 512:512 + LP], lhb[:, gp, 1, :], ub[:, hc, psl], True, True)], ['lhb', 'ub'], ['PA0', 'PA1'])
                                  bur, bui = PA[:, 0:LP], PA[:, 512:512 + LP]
                                  TT('dve', s1[:], bur, tcos[:], ALU.mult, ['PA0', 'tcos'], ['s1'])
                                  TT('dve', s2[:], bui, tsin[:], ALU.mult, ['PA1', 'tsin'], ['s2'])
                                  TT('dve', swr[:], s1[:], s2[:], ALU.add, ['s1', 's2'], ['swr'])
                                  TT('dve', s1[:], bui, tcos[:], ALU.mult, ['PA1', 'tcos'], ['s1'])
                                  TT('dve', s2[:], bur, tsin[:], ALU.mult, ['PA0', 'tsin'], ['s2'])
                                  TT('dve', swi[:], s1[:], s2[:], ALU.subtract, ['s1', 's2'], ['swi'])
                                  rb_ = rmag[:, gp:gp + 1].to_broadcast([128, LP])
                                  SCAN(szr[:], rb_, swr[:], car[:, gp, 0:1], ['pst', 'swr', 'car'], ['szr'])
                                  SCAN(szi[:], rb_, swi[:], car[:, gp, 1:2], ['pst', 'swi', 'car'], ['szi'])
                                  TT('dve', s1[:], szr[:], tcos[:], ALU.mult, ['szr', 'tcos'], ['s1'])
                                  TT('dve', s2[:], szi[:], tsin[:], ALU.mult, ['szi', 'tsin'], ['s2'])
                                  TT('dve', swr[:], s1[:], s2[:], ALU.subtract, ['s1', 's2'], ['swr'])
                                  TT('dve', s1[:], szr[:], tsin[:], ALU.mult, ['szr', 'tsin'], ['s1'])
                                  TT('dve', s2[:], szi[:], tcos[:], ALU.mult, ['szi', 'tcos'], ['s2'])
                                  TT('dve', swi[:], s1[:], s2[:], ALU.add, ['s1', 's2'], ['swi'])
                                  CP('act', car[:, gp, 0:1], swr[:, LP - 1:LP], ['swr'], ['car'])
                                  CP('act', car[:, gp, 1:2], swi[:, LP - 1:LP], ['swi'], ['car'])
                                  CP('act', xrb[:], swr[:], ['swr'], ['xrb'])
                                  ACT(xib[:], swi[:], AF.Copy, ['swi'], ['xib'], scale=-1.0)
                                  MMG([(PB[:, 0:LP], crt[:, gp, :], xrb[:], gq == 0, False),
                                       (PB[:, 0:LP], cit[:, gp, :], xib[:], False, gq == 3)], ['crt', 'cit', 'xrb', 'xib'], ['PB0'])
                              STT(yb[:], ub[:, hc, psl], sm[:, O_S5D + hc:O_S5D + hc + 1], PB[:, 0:LP], ALU.mult, ALU.add,
                                  ['ub', 'sm', 'PB0'], ['yb'])
                              gelu_inplace('dve', yb[:], yb2[:], 'yb', 'yb2')
                              CP('act', zb[:, hc, psl], yb[:], ['yb'], ['zb'])
                      for oc in range(2):
                          for tb in range(4):
                              sl = slice(tb * 512, (tb + 1) * 512)
                              MMG([(PA[:, sl], wglu[:, kc, oc * 128:(oc + 1) * 128], zb[:, kc, sl], kc == 0, kc == 1) for kc in range(2)],
                                  ['wglu', 'zb'], [PAk[tb]])
                              ACT(yb[:], PA[:, sl], AF.Sigmoid, [PAk[tb]], ['yb'], bias=sm[:, O_BGL + oc:O_BGL + oc + 1])
                              TT('dve', yb[:], yb[:], zb[:, oc, sl], ALU.mult, ['yb', 'zb'], ['yb'])
                              TS('dve', yT[:, 4 + oc, sl], yb[:], sm[:, O_GBR + 4 + oc:O_GBR + 5 + oc], None, ALU.mult, ALU.bypass,
                                 ['yb', 'sm'], ['yT%d' % (4 + oc)])
                              ACT(yb2[:], yb[:], AF.Square, ['yb'], ['yb2'])
                              MMG([(PB[:, 2 * (tb * 4 + j):2 * (tb * 4 + j) + 2], yb2[:, j * 128:(j + 1) * 128], onesf[:, 0:2], True, True)
                                   for j in range(4)], ['yb2', 'cst'], ['PB0'])
                          CP('dve', ssb[:, oc, :], PB[:, 0:2 * NT].rearrange("p (i two) -> p i two", two=2)[:, :, 0], ['PB0'], ['ssb'])
                      TT('dve', rsd[:, 1, :], ssb[:, 0, :], ssb[:, 1, :], ALU.add, ['ssb'], ['rsd1'])
                      rstd_from_ss(rsd[:, 1, :], 256.0, 'rsd1')
                      barrier()
                  fw.ctx = actx
                  if stop == 's5':
                      fw.mute = True

                  with ExitStack() as sctx:
                      fw.ctx = sctx
                      stg = fw.sb("stg", [128, 8, 512], F32)
                      bm = fw.sb("bm", [128, 1024], F32)
                      gt1 = fw.sb("gt1", [128, 1024], F32)
                      wo = fw.sb("wo", [128, 8, 1024], BF16)
                      tmpo = fw.sb("tmpo", [128, 1024], F32)
                      wov = wout_d[l].rearrange("(kc p) n -> p kc n", p=128)
                      for kc in range(8):
                          fw.dma('pool', wo[:, kc, :], wov[:, kc, :], writes=['wo'])
                      mod_tile(l, 2, gt1, 'gt1', stg, bm)
                      for i in range(NT):
                          tsl = slice(i * 128, (i + 1) * 128)
                          for (P_, off, kcs, keys) in [(PA, 0, [0, 1, 2, 3], ['PA0', 'PA1']), (PA, 1024, [4, 5], ['PA2', 'PA3']),
                                                       (PB, 0, [6, 7], ['PB0', 'PB1'])]:
                              for nh in range(2):
                                  MMG([(P_[:, off + nh * 512:off + (nh + 1) * 512], yT[:, kc, tsl], wo[:, kc, nh * 512:(nh + 1) * 512],
                                        kc == kcs[0], kc == kcs[-1]) for kc in kcs],
                                      ['yT%d' % kc for kc in kcs] + ['wo'], [keys[nh]])
                          TS('dve', tmpo[:], PA[:, 0:1024], rsd[:, 0, i:i + 1], None, ALU.mult, ALU.bypass, ['PA0', 'PA1', 'rsd0'], ['tmpo'])
                          STT(tmpo[:], PA[:, 1024:2048], rsd[:, 1, i:i + 1], tmpo[:], ALU.mult, ALU.add, ['PA2', 'PA3', 'rsd1', 'tmpo'], ['tmpo'])
                          STT(tmpo[:], PB[:, 0:1024], rsd[:, 2, i:i + 1], tmpo[:], ALU.mult, ALU.add, ['PB0', 'PB1', 'rsd2', 'tmpo'], ['tmpo'])
                          TT('dve', tmpo[:], tmpo[:], gt1[:], ALU.mult, ['tmpo', 'gt1'], ['tmpo'])
                          TT('dve', xs[:, i, :], xs[:, i, :], tmpo[:], ALU.add, ['x%d' % i, 'tmpo'], ['x%d' % i])
                      barrier()
                  fw.ctx = octx
              barrier()
              if not do_peer:
                  continue
              with ExitStack() as bctx:
                  octx = fw.ctx
                  fw.ctx = bctx
                  A2 = fw.sb("A2", [128, 1024], F32)
                  B2 = fw.sb("B2", [128, 1024], F32)
                  gt2 = fw.sb("gt2", [128, 1024], F32)
                  with ExitStack() as mctx:
                      fw.ctx = mctx
                      stg = fw.sb("stg", [128, 8, 512], F32)
                      bm = fw.sb("bm", [128, 1024], F32)
                      gm = fw.sb("gm", [128, 1024], F32)
                      mod_tile(l, 3, B2, 'B2', stg, bm)
                      mod_tile(l, 4, A2, 'A2', stg, bm)
                      mod_tile(l, 5, gt2, 'gt2', stg, bm)
                      fw.dma('sync', gm[:], gffn_d[l:l + 1, :].to_broadcast([128, 1024]), writes=['gm'])
                      STT(A2[:], A2[:], 1.0, gm[:], ALU.add, ALU.mult, ['A2', 'gm'], ['A2'])
                      barrier()
                  fw.ctx = bctx
                  NB = 16
                  ss = fw.sb("ss", [128, NT], F32)
                  junk = fw.sb("junk", [128, 1024], BF16)
                  wq = fw.sb("wq", [128, 8, 2048], BF16)
                  skt = fw.sb("skt", [128, 2, 128], BF16)
                  h2 = fw.sb("h2", [128, 1024], F32)
                  h2b = fw.sb("h2b", [128, 1024], BF16)
                  h2T = fw.sb("h2T", [128, 8, 128], BF16)
                  qTb = fw.sb("qTb", [128, 16, 128], BF16)
                  big = fw.sb("big", [128, 2048], F32)
                  sc = big[:].rearrange("p (a b) -> p a b", a=16)
                  cand = big[:].rearrange("p (h c) -> p h c", h=8)
                  eq = big[:].rearrange("p (h k a) -> p h k a", h=8, k=16)
                  top = fw.sb("top", [128, 16, 16], F32)
                  tix = fw.sb("tix", [128, 16, 16], U32)
                  tixf = fw.sb("tixf", [128, 16, 16], F32)
                  best = fw.sb("best", [128, 8, 16], F32)
                  pos = fw.sb("pos", [128, 8, 16], U32)
                  posf = fw.sb("posf", [128, 8, 16], F32)
                  ai = fw.sb("ai", [128, 8, 16], I32)
                  af = fw.sb("af", [128, 8, 16], F32)
                  bf = fw.sb("bf", [128, 8, 16], F32)
                  isel = fw.sb("isel", [128, 8, 16], F32)
                  jsel = fw.sb("jsel", [128, 8, 16], F32)
                  eidx2 = fw.sb("eidx", [128, 2, 128], I32)
                  gat = fw.sb("gat", [128, 8, 16], F32)
                  gz_ = fw.sb("gz_", [128, 8], F32)
                  actp = fw.sb("actp", [128, 128], F32)
                  actt = fw.sb("actt", [128, 128], F32)
                  wgt = fw.sb("wgt", [128, 128], F32)
                  acc = fw.sb("acc", [128, 1024], F32)
                  jf = fw.sb("jf", [128, 1024], F32) if JF else acc
                  gb = [fw.sb("gb%d" % j, [128, 1024], BF16) for j in range(NB)]
                  gv = gb
                  gvc = gbc = [0]
                  NBV = NB
                  wqv = wq_d[l].rearrange("(kc p) n -> p kc n", p=128)
                  for kc in range(8):
                      fw.dma('pool', wq[:, kc, :], wqv[:, kc, :], writes=['wq'])
                  fw.dma('pool', skt[:], skt_d[l], writes=['skt'])
                  for i in range(NT):
                      ACT(junk[:], xs[:, i, :], AF.Square, ['x%d' % i], ['junk', 'ss'], accum=ss[:, i:i + 1])
                  rstd_from_ss(ss[:], float(D), 'ss')
                  PAb = PA[:, 0:512].bitcast(BF16)
                  topv = top[:].rearrange("p (h two) k -> p h two k", two=2)
                  tixv = tixf[:].rearrange("p (h two) k -> p h two k", two=2)
                  dg = [fw.sb("dg%d" % k, [128, 128], BF16) for k in range(4)]
                  ubk = ['ubd%d' % c for c in range(16)]
                  vbk = ['vbd%d' % c for c in range(16)]

                  def idx_a(i):
                      xk = 'x%d' % i
                      ek = 'eidx%d' % (i % 2)
                      eidx = eidx2[:, i % 2, :]
                      STT(jf[:], xs[:, i, :], ss[:, i:i + 1], A2[:], ALU.mult, ALU.mult, [xk, 'ss', 'A2'], ['acc', 'jf'])
                      TT('dve', h2[:], jf[:], B2[:], ALU.add, ['acc', 'jf', 'B2'], ['h2'])
                      CP('act', h2b[:], h2[:], ['h2'], ['h2b'])
                      TRG([(PAb[:, kc * 128:(kc + 1) * 128], h2b[:, kc * 128:(kc + 1) * 128]) for kc in range(8)],
                          identb[:], ['h2b', 'identb'], ['PA0'])
                      CP('act', h2T[:], PAb.rearrange("p (k t) -> p k t", k=8), ['PA0'], ['h2T'])
                      for half in range(2):
                          for hq2 in range(2):
                              hq = half * 2 + hq2
                              MMG([(PB[:, hq2 * 512 + j * 128:hq2 * 512 + (j + 1) * 128], wq[:, kc, (hq * 4 + j) * 128:(hq * 4 + j + 1) * 128], h2T[:, kc, :],
                                    kc == 0, kc == 7) for j in range(4) for kc in range(8)], ['wq', 'h2T'], [PBk[hq2]])
                          CP('act', qTb[:, half * 8:(half + 1) * 8, :].rearrange("p a b -> p (a b)"), PB[:, 0:1024], PBk[0:2], ['qTb'])
                      for hq in range(4):
                          MMG([(PA[:, hq * 512 + j * 128:hq * 512 + (j + 1) * 128], qTb[:, hq * 4 + j, :], skt[:, (hq * 4 + j) % 2, :], True, True)
                               for j in range(4)], ['qTb', 'skt'], [PAk[hq]])
                      CP('dve', big[:], PA[:, :], PAk, ['big'])

                  def idx_b(i):
                      ek = 'eidx%d' % (i % 2)
                      eidx = eidx2[:, i % 2, :]
                      for hp in range(16):
                          fw.op('dve', lambda e, hp=hp: e.max(out=top[:, hp, 0:8], in_=sc[:, hp, :]), ['big'], ['top'])
                          fw.op('dve', lambda e, hp=hp: e.max_index(out=tix[:, hp, 0:8], in_max=top[:, hp, 0:8], in_values=sc[:, hp, :]), ['big', 'top'], ['tix'])
                          fw.op('dve', lambda e, hp=hp: e.match_replace(out=sc[:, hp, :], in_to_replace=top[:, hp, 0:8], in_values=sc[:, hp, :], imm_value=-1e30),
                                ['big', 'top'], ['big'])
                          fw.op('dve', lambda e, hp=hp: e.max(out=top[:, hp, 8:16], in_=sc[:, hp, :]), ['big'], ['top'])
                          fw.op('dve', lambda e, hp=hp: e.max_index(out=tix[:, hp, 8:16], in_max=top[:, hp, 8:16], in_values=sc[:, hp, :]), ['big', 'top'], ['tix'])
                      CP('dve', tixf[:], tix[:], ['tix'], ['tixf'])
                      TT('dve', cand.rearrange("p h (a b) -> p h a b", a=16), topv[:, :, 0, :].unsqueeze(3).to_broadcast([128, 8, 16, 16]),
                         topv[:, :, 1, :].unsqueeze(2).to_broadcast([128, 8, 16, 16]), ALU.add, ['top'], ['big'])
                      for h in range(8):
                          fw.op('dve', lambda e, h=h: e.max(out=best[:, h, 0:8], in_=cand[:, h, :]), ['big'], ['best'])
                          fw.op('dve', lambda e, h=h: e.max_index(out=pos[:, h, 0:8], in_max=best[:, h, 0:8], in_values=cand[:, h, :]), ['big', 'best'], ['pos'])
                          fw.op('dve', lambda e, h=h: e.match_replace(out=cand[:, h, :], in_to_replace=best[:, h, 0:8], in_values=cand[:, h, :], imm_value=-1e30),
                                ['big', 'best'], ['big'])
                          fw.op('dve', lambda e, h=h: e.max(out=best[:, h, 8:16], in_=cand[:, h, :]), ['big'], ['best'])
                          fw.op('dve', lambda e, h=h: e.max_index(out=pos[:, h, 8:16], in_max=best[:, h, 8:16], in_values=cand[:, h, :]), ['big', 'best'], ['pos'])
                      CP('dve', posf[:], pos[:], ['pos'], ['posf'])
                      TS('dve', ai[:], posf[:], 1.0 / 16.0, -7.5 / 16.0, ALU.mult, ALU.add, ['posf'], ['ai'])
                      CP('dve', af[:], ai[:], ['ai'], ['af'])
                      STT(bf[:], af[:], -16.0, posf[:], ALU.mult, ALU.add, ['af', 'posf'], ['bf'])
                      io_b = io16.unsqueeze(1).unsqueeze(1).to_broadcast([128, 8, 16, 16])
                      for (src, tv, dst, dk) in [(af, 0, isel, 'isel'), (bf, 1, jsel, 'jsel')]:
                          TT('dve', eq, src[:].unsqueeze(3).to_broadcast([128, 8, 16, 16]), io_b, ALU.is_equal, ['af', 'bf', 'cst'], ['big'])
                          TT('dve', eq, eq, tixv[:, :, tv, :].unsqueeze(2).to_broadcast([128, 8, 16, 16]), ALU.mult, ['big', 'tixf'], ['big'])
                          fw.op('dve', lambda e, dst=dst: e.tensor_reduce(out=dst[:], in_=eq, axis=AX.X, op=ALU.add), ['big'], [dk])
                      STT(isel[:], isel[:], 128.0, jsel[:], ALU.mult, ALU.add, ['isel', 'jsel'], ['isel'])
                      CP('dve', eidx, isel[:].rearrange("p h k -> p (h k)"), ['isel'], [ek])

                  def u_phase(i):
                      ek = 'eidx%d' % (i % 2)
                      for n in range(128):
                          j = gbc[0] % NB
                          gbc[0] += 1
                          fw.dma('pool', gb[j][:], ub_d[l], reads=[ek] + ubk, writes=['gb%d' % j],
                                 indirect=bass.IndirectOffsetOnAxis(ap=eidx2[:, i % 2, n:n + 1], axis=0))
                          fw.op('dve', lambda e, j=j, n=n: e.scalar_tensor_tensor(out=(jf[:] if JF else junk[:]), in0=gb[j][:], scalar=1.0, in1=h2[:],
                                                                                op0=ALU.mult, op1=ALU.mult, accum_out=actp[:, n:n + 1]),
                                ['gb%d' % j, 'h2'], ['junk', 'jf', 'actp'])
                      TT('dve', gat[:], best[:], best[:, :, 0:1].to_broadcast([128, 8, 16]), ALU.subtract, ['best'], ['gat'])
                      ACT(gat[:], gat[:], AF.Exp, ['gat'], ['gat'])
                      fw.op('dve', lambda e: e.tensor_reduce(out=gz_[:], in_=gat[:], axis=AX.X, op=ALU.add), ['gat'], ['gz_'])
                      fw.op('dve', lambda e: e.reciprocal(out=gz_[:], in_=gz_[:]), ['gz_'], ['gz_'])
                      TT('dve', gat[:], gat[:], gz_[:].unsqueeze(2).to_broadcast([128, 8, 16]), ALU.mult, ['gat', 'gz_'], ['gat'])

                      gelu_inplace('dve', actp[:], actt[:], 'actp', 'actt')
                      TT('dve', wgt[:], actp[:], gat[:].rearrange("p h k -> p (h k)"), ALU.mult, ['actp', 'gat'], ['wgt'])

                  def v_phase(i):
                      xk = 'x%d' % i
                      ek = 'eidx%d' % (i % 2)
                      for n in range(128):
                          j = gvc[0] % NBV
                          gvc[0] += 1
                          k = n % 4
                          fw.dma('pool', gv[j][:], vb_d[l], reads=[ek] + vbk, writes=['gb%d' % j],
                                 indirect=bass.IndirectOffsetOnAxis(ap=eidx2[:, i % 2, n:n + 1], axis=0))
                          fw.op('act', lambda e, k=k, n=n: e.activation(out=dg[k][:], in_=identf, func=AF.Copy, scale=wgt[:, n:n + 1]),
                                ['cst', 'wgt'], ['dg%d' % k])
                          MMG([(PB[:, 1024:1536], dg[k][:], gv[j][:, 0:512], n == 0, n == 127),
                               (PB[:, 1536:2048], dg[k][:], gv[j][:, 512:1024], n == 0, n == 127)],
                              ['dg%d' % k, 'gb%d' % j], ['PB2', 'PB3'])

                  def v_fin(i):
                      xk = 'x%d' % i
                      TT('dve', acc[:], PB[:, 1024:2048], gt2[:], ALU.mult, ['PB2', 'PB3', 'gt2'], ['acc'])
                      TT('dve', xs[:, i, :], xs[:, i, :], acc[:], ALU.add, [xk, 'acc'], [xk])

                  idx_a(0)
                  idx_b(0)
                  for i in range(NT):
                      u_phase(i)
                      if i + 1 < NT:
                          idx_a(i + 1)
                      v_phase(i)
                      if i + 1 < NT:
                          idx_b(i + 1)
                      v_fin(i)
                  barrier()
                  fw.ctx = octx
              barrier()
          except _Stop:
            fw.ctx = ctx
            barrier()
            break

        fw.mute = False
        barrier()
        with ExitStack() as fctx:
            fw.ctx = fctx
            gf = fw.sb("gf", [128, 1024], F32)
            ss = fw.sb("ss", [128, NT], F32)
            junk = fw.sb("junk", [128, 1024], BF16)
            ob = [fw.sb("ob%d" % j, [128, 1024], F32) for j in range(2)]
            fw.dma('sync', gf[:], gfin_d[0:1, :].to_broadcast([128, 1024]), writes=['gf'])
            for i in range(NT):
                ACT(junk[:], xs[:, i, :], AF.Square, ['x%d' % i], ['junk', 'ss'], accum=ss[:, i:i + 1])
            rstd_from_ss(ss[:], float(D), 'ss')
            outk = []
            for i in range(NT):
                j = i % 2
                STT(ob[j][:], xs[:, i, :], ss[:, i:i + 1], gf[:], ALU.mult, ALU.mult, ['x%d' % i, 'ss', 'gf'], ['ob%d' % j])
                fw.dma('sync', out_d[i * 128:(i + 1) * 128, :], ob[j][:], reads=['ob%d' % j], writes=['out%d' % i])
                outk.append('out%d' % i)
            fw.finish(outk)
    return nc


def _consts():
    c = np.zeros((128, CSTW), np.float32)
    c[:, 0:128] = np.eye(128, dtype=np.float32)
    c[:, 128:256] = 1.0
    s = np.arange(128)[:, None]
    t = np.arange(128)[None, :]
    c[:, 256:384] = ((s // 32 == t // 32) & (s <= t)).astype(np.float32)
    c[:, 384] = np.pi / 2
    c[:, 385] = EPS
    c[:, 386] = 1.0
    c[:, 400:416] = np.arange(16, dtype=np.float32)[None, :]
    c[:, 416:928] = np.arange(1, 513, dtype=np.float32)[None, :]
    rm = np.ones(2048, np.float32)
    rm[::32] = 0.0
    c[:, 928:928 + 2048] = rm[None, :]
    return c


def _layouts(inp):
    f = lambda k: np.asarray(inp[k], dtype=np.float32)
    small = np.zeros((4, 128, NSMALL), np.float32)
    crt = np.zeros((4, 128, 8, 128), np.float32)
    cit = np.zeros((4, 128, 8, 128), np.float32)
    wa = np.zeros((4, 128, 2, 128), np.float32)
    wx = np.zeros((4, 128, 2, 128), np.float32)
    lbl = f('hgrn_lb_logits').reshape(4, 4, 128).transpose(2, 1, 0).reshape(128, 16)
    st = lambda a: a.reshape(8, 2, 64).transpose(1, 2, 0).reshape(128, 8)
    rep = lambda a: np.broadcast_to(a.reshape(2, 8, 1, 64), (2, 8, 16, 64)).transpose(1, 2, 0, 3).reshape(128, 128)
    col2 = lambda v: v.reshape(2, 128).T
    rmk = np.zeros((128, 8), np.float32)
    for gl in range(8):
        rmk[gl * 16:(gl + 1) * 16, gl] = 1.0
    for l in range(4):
        small[l, :, O_LBL:O_LBL + 16] = lbl
        small[l, :, O_ARS:O_ARS + 8] = st(f('s5_a_re')[l])
        small[l, :, O_AIS:O_AIS + 8] = st(f('s5_a_im')[l])
        ld = f('s5_log_dt')[l]
        small[l, :, O_LDS:O_LDS + 8] = st(np.broadcast_to(ld[:, None], (16, 64)).copy())
        small[l, :, O_ARR:O_ARR + 128] = rep(f('s5_a_re')[l])
        small[l, :, O_AIR:O_AIR + 128] = rep(f('s5_a_im')[l])
        small[l, :, O_BTR:O_BTR + 128] = f('s5_b_re')[l].reshape(2, 8, 64, 16).transpose(1, 3, 0, 2).reshape(128, 128)
        small[l, :, O_BTI:O_BTI + 128] = f('s5_b_im')[l].reshape(2, 8, 64, 16).transpose(1, 3, 0, 2).reshape(128, 128)
        small[l, :, O_LDR:O_LDR + 2] = np.broadcast_to(ld.reshape(2, 8, 1), (2, 8, 16)).transpose(1, 2, 0).reshape(128, 2)
        small[l, :, O_S5D:O_S5D + 2] = col2(f('s5_d')[l])
        small[l, :, O_BGL:O_BGL + 2] = col2(f('s5_b_glu')[l])
        small[l, :, O_CW:O_CW + 8] = f('lru_conv_w')[l].reshape(4, 2, 128).transpose(2, 1, 0).reshape(128, 8)
        small[l, :, O_CB:O_CB + 2] = col2(f('lru_conv_b')[l])
        small[l, :, O_BA:O_BA + 2] = col2(f('lru_b_a')[l])
        small[l, :, O_BX:O_BX + 2] = col2(f('lru_b_x')[l])
        small[l, :, O_LAM:O_LAM + 2] = col2(f('lru_lambda')[l])
        small[l, :, O_GBR:O_GBR + 8] = f('g_branch')[l].reshape(8, 128).T
        small[l, :, O_RMK:O_RMK + 8] = rmk
        for g in range(16):
            gp, gi, gl = g // 2, g % 2, g % 8
            crt[l, gi * 64:(gi + 1) * 64, gp, gl * 16:(gl + 1) * 16] = f('s5_c_re')[l, g].T
            cit[l, gi * 64:(gi + 1) * 64, gp, gl * 16:(gl + 1) * 16] = f('s5_c_im')[l, g].T
        for h in range(4):
            hc, o = h // 2, (h % 2) * 64
            wa[l, o:o + 64, hc, o:o + 64] = f('lru_w_a')[l, h]
            wx[l, o:o + 64, hc, o:o + 64] = f('lru_w_x')[l, h]
    skt = np.ascontiguousarray(f('peer_sub_keys').transpose(0, 3, 1, 2))
    return dict(small=small, crt=crt, cit=cit, wa_bd=wa, wx_bd=wx, skt=skt)


def make_in_maps(inp, cores):
    f = lambda k: np.ascontiguousarray(np.asarray(inp[k], dtype=np.float32))
    lay = _layouts(inp)
    shared = dict(w_mod=f('w_mod'), b_mod=f('b_mod'), g_mix=f('g_mix'), w_in=f('w_in'), w_glu=f('s5_w_glu'),
                  g_branch=f('g_branch'), w_out=f('w_out'), g_ffn=f('g_ffn'), peer_w_q=f('peer_w_q'),
                  g_final=f('g_final').reshape(1, D), consts=_consts())
    shared.update(lay)
    pu, pv = f('peer_u'), f('peer_v')
    for l_ in range(4):
        shared['peer_u%d' % l_] = pu[l_]
        shared['peer_v%d' % l_] = pv[l_]
    x = f('x')
    c = f('c')
    maps = []
    for b in cores:
        m = dict(shared)
        m['x'] = np.ascontiguousarray(x[b])
        m['c'] = np.ascontiguousarray(c[b].reshape(128, 8))
        maps.append(m)
    return maps


def kernel(**inputs):
    nc = build()
    maps = make_in_maps(inputs, list(range(8)))
    res = run_bass_kernel_spmd(nc, maps, core_ids=list(range(8)))
    return np.stack([np.asarray(r['out'], dtype=np.float32) for r in res.results], axis=0)
```

```python
import numpy as np
from contextlib import ExitStack
import concourse.bass as bass
import concourse.mybir as mybir
from concourse.bass_utils import run_bass_kernel_spmd

F32 = mybir.dt.float32
BF16 = mybir.dt.bfloat16
U32 = mybir.dt.uint32
I32 = mybir.dt.int32
AF = mybir.ActivationFunctionType
ALU = mybir.AluOpType
AX = mybir.AxisListType
F32R = mybir.dt.float32r

ENG = ['sync', 'act', 'pool', 'pe', 'dve']


class FW:
    def __init__(self, nc, ctx, slots=None):
        self.nc = nc
        self.ctx = ctx
        self.q = {e: [] for e in ENG}
        self.sems = {}
        self.cnt = {}
        for e in ENG:
            self.sems[e] = ctx.enter_context(nc.semaphore('s_' + e))
            self.cnt[e] = 0
        slots = slots or {'sync': 8, 'act': 4, 'pool': 8}
        self.slots = {}
        self.rr = {}
        for e, n in slots.items():
            self.slots[e] = []
            self.rr[e] = 0
            for i in range(n):
                nm = 'd_%s%d' % (e, i)
                self.sems[nm] = ctx.enter_context(nc.semaphore(nm))
                self.cnt[nm] = 0
                self.slots[e].append(nm)
        self.seen = {e: {} for e in ENG}
        self.lastw = {}
        self.readers = {}
        self.nins = {e: 0 for e in ENG}

    def sb(self, name, shape, dtype):
        self.uid = getattr(self, 'uid', 0) + 1
        if not hasattr(self, 'names'):
            self.names = {}
        self.names.setdefault(name, []).append('%s_u%d' % (name, self.uid))
        return self.ctx.enter_context(self.nc.sbuf_tensor('%s_u%d' % (name, self.uid), list(shape), dtype))

    def ps(self, name, shape, dtype):
        return self.ctx.enter_context(self.nc.psum_tensor(name, list(shape), dtype))

    def _wait(self, eng, s, v):
        if self.mute:
            return
        if self.seen[eng].get(s, 0) < v:
            self.seen[eng][s] = v
            sem = self.sems[s]
            self.q[eng].append(lambda e, sem=sem, v=v: e.wait_ge(sem, v))

    def _wait_deps(self, eng, reads, writes):
        deps = {}
        for k in list(reads) + list(writes):
            ev = self.lastw.get(k)
            if ev is not None and deps.get(ev[0], 0) < ev[1]:
                deps[ev[0]] = ev[1]
        for k in writes:
            for s, v in self.readers.get(k, {}).items():
                if deps.get(s, 0) < v:
                    deps[s] = v
        for s, v in deps.items():
            self._wait(eng, s, v)

    def _record(self, ev, reads, writes):
        s, v = ev
        ws = set(writes)
        for k in ws:
            self.lastw[k] = ev
            self.readers[k] = {}
        for k in reads:
            if k in ws:
                continue
            r = self.readers.setdefault(k, {})
            if r.get(s, 0) < v:
                r[s] = v

    mute = False

    def op(self, eng, fn, reads=(), writes=()):
        if self.mute:
            return
        self._wait_deps(eng, reads, writes)
        self.cnt[eng] += 1
        v = self.cnt[eng]
        sem = self.sems[eng]
        self.q[eng].append(lambda e, fn=fn, sem=sem: fn(e).then_inc(sem, 1))
        self.nins[eng] += 1
        self._record((eng, v), reads, writes)

    def dma(self, eng, out, in_, reads=(), writes=(), indirect=None, **kw):
        if self.mute:
            return
        self._wait_deps(eng, reads, writes)
        sl = self.slots[eng]
        slot = sl[self.rr[eng] % len(sl)]
        self.rr[eng] += 1
        if self.cnt[slot] > 0 and not kw.pop('noslotwait', False):
            self._wait(eng, slot, self.cnt[slot])
        kw.pop('noslotwait', None)
        self.cnt[slot] += 16
        v = self.cnt[slot]
        sem = self.sems[slot]
        if indirect is None:
            self.q[eng].append(lambda e, out=out, in_=in_, sem=sem, kw=kw:
                               e.dma_start(out=out, in_=in_, **kw).then_inc(sem, 16))
        else:
            self.q[eng].append(lambda e, out=out, in_=in_, sem=sem, ind=indirect, kw=kw:
                               e.indirect_dma_start(out=out, out_offset=None, in_=in_, in_offset=ind, **kw).then_inc(sem, 16))
        self.nins[eng] += 1
        self._record((slot, v), reads, writes)

    def finish(self, out_keys):
        self._wait_deps('sync', out_keys, [])
        q = self.q
        with self.nc.Block() as block:
            @block.sync
            def _(e):
                for f in q['sync']:
                    f(e)

            @block.scalar
            def _(e):
                for f in q['act']:
                    f(e)

            @block.gpsimd
            def _(e):
                for f in q['pool']:
                    f(e)

            @block.tensor
            def _(e):
                for f in q['pe']:
                    f(e)

            @block.vector
            def _(e):
                for f in q['dve']:
                    f(e)


import os
PIPE = 1
JF = 1
T = 2048
D = 1024
NT = 16
EPS = 1e-6
TWO_PI = float(2 * np.pi)
CW1 = 6.28125
CW2 = 0.0019353071795864769
GK = 1.5957691216057308

CSTW = 384 + 32 + 512 + 2048
NSMALL = 16 + 24 + 128 * 4 + 2 + 2 + 2 + 8 + 2 + 2 + 2 + 2 + 8 + 8
O_LBL = 0
O_ARS = 16
O_AIS = 24
O_LDS = 32
O_ARR = 40
O_AIR = 168
O_BTR = 296
O_BTI = 424
O_LDR = 552
O_S5D = 554
O_BGL = 556
O_CW = 558
O_CB = 566
O_BA = 568
O_BX = 570
O_LAM = 572
O_GBR = 574
O_RMK = 582


class _Stop(Exception):
    pass


def build(n_layers=4, dbg=None, do_peer=True, stop=None):
    nc = bass.Bass("TRN2", target_bir_lowering=False)

    def din(name, shape, dt=F32):
        return nc.dram_tensor(name, list(shape), dt, kind="ExternalInput").ap()

    x_d = din("x", [T, D])
    c_d = din("c", [128, 8])
    wmod_d = din("w_mod", [4, D, 6 * D])
    bmod_d = din("b_mod", [4, 6 * D])
    gmix_d = din("g_mix", [4, D])
    win_d = din("w_in", [4, D, 2816])
    small_d = din("small", [4, 128, NSMALL])
    crt_d = din("crt", [4, 128, 8, 128])
    cit_d = din("cit", [4, 128, 8, 128])
    wglu_d = din("w_glu", [4, 256, 256])
    wa_d = din("wa_bd", [4, 128, 2, 128])
    wx_d = din("wx_bd", [4, 128, 2, 128])
    gbrow_d = din("g_branch", [4, D])
    wout_d = din("w_out", [4, D, D])
    gffn_d = din("g_ffn", [4, D])
    wq_d = din("peer_w_q", [4, D, 2048])
    skt_d = din("skt", [4, 128, 2, 128])
    pu_d = [din("peer_u%d" % l_, [16384, D]) for l_ in range(4)]
    pv_d = [din("peer_v%d" % l_, [16384, D]) for l_ in range(4)]
    gfin_d = din("g_final", [1, D])
    consts_d = din("consts", [128, CSTW])
    out_d = nc.dram_tensor("out", [T, D], F32, kind="ExternalOutput").ap()
    ub_d = [nc.dram_tensor("ub_scr%d" % l_, [16384, D], BF16, kind="Internal").ap() for l_ in range(4)]
    vb_d = [nc.dram_tensor("vb_scr%d" % l_, [16384, D], BF16, kind="Internal").ap() for l_ in range(4)]

    with ExitStack() as ctx:
        fw = FW(nc, ctx, slots={'sync': 8, 'act': 2, 'pool': 20})
        build.fw = fw

        def TS(eng, out, in0, s1, s2, op0, op1, r, w):
            fw.op(eng, lambda e: e.tensor_scalar(out=out, in0=in0, scalar1=s1, scalar2=s2, op0=op0, op1=op1), r, w)

        def TT(eng, out, in0, in1, op, r, w):
            fw.op(eng, lambda e: e.tensor_tensor(out=out, in0=in0, in1=in1, op=op), r, w)

        def STT(out, in0, scalar, in1, op0, op1, r, w):
            fw.op('dve', lambda e: e.scalar_tensor_tensor(out=out, in0=in0, scalar=scalar, in1=in1, op0=op0, op1=op1), r, w)

        def ACT(out, in_, func, r, w, scale=1.0, bias=0.0, accum=None):
            r = list(r) + ['cst', 'sm']
            if accum is None:
                fw.op('act', lambda e: e.activation(out=out, in_=in_, func=func, scale=scale, bias=bias), r, w)
            else:
                fw.op('act', lambda e: e.activation(out=out, in_=in_, func=func, scale=scale, bias=bias, accum_out=accum), r, w)

        def CP(eng, out, in_, r, w):
            if eng == 'act':
                fw.op(eng, lambda e: e.activation(out=out, in_=in_, func=AF.Copy), r, w)
            else:
                fw.op(eng, lambda e: e.tensor_copy(out=out, in_=in_), r, w)

        def SCAN(out, d0, d1, init, r, w):
            fw.op('dve', lambda e: e.tensor_tensor_scan(out=out, data0=d0, data1=d1, initial=init, op0=ALU.mult, op1=ALU.add), r, w)

        def MMG(mms, r, w):
            def f(e, mms=mms):
                ins = None
                for mm in mms:
                    (o, l, rh, st, sp) = mm[:5]
                    if len(mm) > 5:
                        ins = e.matmul(o, l, rh, start=st, stop=sp, skip_group_check=True)
                    else:
                        ins = e.matmul(o, l, rh, start=st, stop=sp)
                return ins
            fw.op('pe', f, r, w)

        def TRG(trs, ident, r, w):
            def f(e, trs=trs):
                ins = None
                for (o, i) in trs:
                    ins = e.transpose(o, i, ident)
                return ins
            fw.op('pe', f, r, w)

        def barrier():
            allsems = list(fw.cnt.items())
            for e in ENG:
                for s, v in allsems:
                    if v > 0:
                        fw._wait(e, s, v)

        xs = fw.sb("xs", [128, NT, D], F32)
        cst = fw.sb("cst", [128, 928], F32)
        identf = cst[:, 0:128]
        onesf = cst[:, 128:256]
        maskbd = cst[:, 256:384]
        halfpi = cst[:, 384:385]
        epscol = cst[:, 385:386]
        onecol = cst[:, 386:387]
        io16 = cst[:, 400:416]
        tau = cst[:, 416:416 + 512]
        rmask_t = fw.sb("rmask", [128, 2048], BF16)
        rmask = rmask_t[:]
        identb = fw.sb("identb", [128, 128], BF16)
        condr = fw.sb("condr", [128, 8, 128], F32)
        cond = fw.sb("cond", [128, 8], F32)
        PA = fw.ps("PA", [128, 2048], F32)
        PB = fw.ps("PB", [128, 2048], F32)
        PAk = ['PA0', 'PA1', 'PA2', 'PA3']
        PBk = ['PB0', 'PB1', 'PB2', 'PB3']

        fw.dma('sync', cst[:, 0:928], consts_d[:, 0:928], writes=['cst'])
        fw.dma('pool', rmask_t[:], consts_d[:, 928:928 + 2048], writes=['cst'])
        for i in range(NT):
            fw.dma('sync', xs[:, i, :], x_d[i * 128:(i + 1) * 128, :], writes=['x%d' % i])
        fw.dma('sync', cond[:], c_d[:, :], writes=['cond'])
        CP('dve', identb[:], identf, ['cst'], ['identb'])
        ACT(cond[:], cond[:], AF.Silu, ['cond'], ['cond'])
        CP('dve', condr[:], cond[:, :].unsqueeze(2).to_broadcast([128, 8, 128]), ['cond'], ['condr'])

        def mod_tile(l, j, out_tile, okey, stg, bm):
            wv = wmod_d[l].rearrange("(p kc) n -> p kc n", kc=8)
            fw.dma('sync', bm[:], bmod_d[l:l + 1, j * 1024:(j + 1) * 1024].to_broadcast([128, 1024]), writes=['bm'])
            for half in range(2):
                c0 = j * 1024 + half * 512
                fw.dma('sync', stg[:], wv[:, :, c0:c0 + 512], writes=['stg'])
                MMG([(PA[:, 0:512], condr[:, kc, :], stg[:, kc, :], kc == 0, kc == 7) for kc in range(8)],
                    ['condr', 'stg'], ['PA0'])
                TT('dve', out_tile[:, half * 512:(half + 1) * 512], PA[:, 0:512], bm[:, half * 512:(half + 1) * 512], ALU.add,
                   ['PA0', 'bm'], [okey])

        def gelu_inplace(eng_t, t, tmp, key, tkey):
            ACT(tmp, t, AF.Square, [key], [tkey])
            TS('dve', tmp, tmp, 0.044715, 1.0, ALU.mult, ALU.add, [tkey], [tkey])
            TT('dve', tmp, tmp, t, ALU.mult, [tkey, key], [tkey])
            ACT(tmp, tmp, AF.Sigmoid, [tkey], [tkey], scale=GK)
            TT('dve', t, t, tmp, ALU.mult, [key, tkey], [key])

        def rstd_from_ss(ss_ap, n, key):
            ACT(ss_ap, ss_ap, AF.Sqrt, [key], [key], scale=1.0 / n, bias=epscol)
            fw.op('dve', lambda e: e.reciprocal(out=ss_ap, in_=ss_ap), [key], [key])

        def norm_to_T(A, B, hT, hkeys, keep=None):
            pass

        for l in range(n_layers):
          try:
              with ExitStack() as actx:
                  octx = fw.ctx
                  fw.ctx = actx
                  hT = fw.sb("hT", [128, 8, T], BF16)
                  yT = fw.sb("yT", [128, 8, T], BF16)
                  sm = fw.sb("sm", [128, NSMALL], F32)
                  rsd = fw.sb("rsd", [128, 3, NT], F32)
                  fw.dma('sync', sm[:], small_d[l], writes=['sm'])
                  if do_peer:
                      for (src, dst, key) in [(pu_d[l], ub_d[l], 'ubd'), (pv_d[l], vb_d[l], 'vbd')]:
                          for c in range(16):
                              fw.dma('pool', dst[c * 1024:(c + 1) * 1024, :].rearrange("(p r) d -> p r d", p=128),
                                     src[c * 1024:(c + 1) * 1024, :].rearrange("(p r) d -> p r d", p=128), writes=[key + str(c)])
                  with ExitStack() as sctx:
                      fw.ctx = sctx
                      stg = fw.sb("stg", [128, 8, 512], F32)
                      bm = fw.sb("bm", [128, 1024], F32)
                      A1 = fw.sb("A1", [128, 1024], F32)
                      B1 = fw.sb("B1", [128, 1024], F32)
                      gm = fw.sb("gm", [128, 1024], F32)
                      ss = fw.sb("ss", [128, NT], F32)
                      junk = fw.sb("junk", [128, 1024], BF16)
                      tmpf = fw.sb("tmpf", [128, 1024], F32)
                      hb = fw.sb("hb", [128, 1024], BF16)
                      mod_tile(l, 0, B1, 'B1', stg, bm)
                      mod_tile(l, 1, A1, 'A1', stg, bm)
                      fw.dma('sync', gm[:], gmix_d[l:l + 1, :].to_broadcast([128, 1024]), writes=['gm'])
                      STT(A1[:], A1[:], 1.0, gm[:], ALU.add, ALU.mult, ['A1', 'gm'], ['A1'])
                      for i in range(NT):
                          ACT(junk[:], xs[:, i, :], AF.Square, ['x%d' % i], ['junk', 'ss'], accum=ss[:, i:i + 1])
                      rstd_from_ss(ss[:], float(D), 'ss')
                      PAb = PA[:, 0:512].bitcast(BF16)
                      for i in range(NT):
                          STT(tmpf[:], xs[:, i, :], ss[:, i:i + 1], A1[:], ALU.mult, ALU.mult, ['x%d' % i, 'ss', 'A1'], ['tmpf'])
                          TT('dve', hb[:], tmpf[:], B1[:], ALU.add, ['tmpf', 'B1'], ['hb'])
                          TRG([(PAb[:, kc * 128:(kc + 1) * 128], hb[:, kc * 128:(kc + 1) * 128]) for kc in range(8)],
                              identb[:], ['hb', 'identb'], ['PA0'])
                          CP('act', hT[:, :, i * 128:(i + 1) * 128], PAb.rearrange("p (k t) -> p k t", k=8), ['PA0'], ['hT'])
                      barrier()
                  fw.ctx = actx
                  if stop == 'norm':
                      fw.mute = True
                  lbt = fw.sb("lbt", [128, 16], F32)
                  lbz = fw.sb("lbz", [128, 4], F32)
                  lb = fw.sb("lb", [128, 4], F32)
                  oml = fw.sb("oml", [128, 4], F32)
                  ACT(lbt[:], sm[:, O_LBL:O_LBL + 16], AF.Exp, ['sm'], ['lbt'])
                  lbv = lbt[:].rearrange("p (h l) -> p h l", h=4)
                  fw.op('dve', lambda e: e.tensor_reduce(out=lbz[:], in_=lbv, op=ALU.add, axis=AX.X), ['lbt'], ['lbz'])
                  fw.op('dve', lambda e: e.reciprocal(out=lbz[:], in_=lbz[:]), ['lbz'], ['lbz'])
                  fw.op('dve', lambda e: e.memset(lb[:], 0.0), [], ['lb'])
                  for j in range(1, l + 1):
                      TT('dve', lb[:], lb[:], lbv[:, :, j], ALU.add, ['lb', 'lbt'], ['lb'])
                  TT('dve', lb[:], lb[:], lbz[:], ALU.mult, ['lb', 'lbz'], ['lb'])
                  TS('dve', oml[:], lb[:], -1.0, 1.0, ALU.mult, ALU.add, ['lb'], ['oml'])
                  winv = win_d[l].rearrange("(kc p) n -> p kc n", p=128)

                  with ExitStack() as sctx:
                      fw.ctx = sctx
                      wh = fw.sb("wh", [128, 8, 512], BF16)
                      t1 = fw.sb("t1", [128, T], F32)
                      t2 = fw.sb("t2", [128, T], F32)
                      t3 = fw.sb("t3", [128, T], F32)
                      kt = fw.sb("kt", [128, T], BF16)
                      kh = fw.sb("kh", [128, T], BF16)
                      qt = fw.sb("qt", [128, T], BF16)
                      khk = fw.sb("khk", [128, NT, 128], BF16)
                      khkz = fw.sb("khkz", [128, NT, 128], BF16)
                      qz = fw.sb("qz", [128, T], BF16)
                      fw.op('dve', lambda e: e.memset(qz[:], 0.0), [], ['qz'])
                      zer = fw.sb("zer", [128, 128], BF16)
                      fw.op('dve', lambda e: e.memset(zer[:], 0.0), [], ['zer'])
                      el = fw.sb("el", [128, 64], F32)
                      S = fw.sb("S", [128, 128], F32)
                      Sb = fw.sb("Sb", [128, 128], BF16)
                      vb = fw.sb("vb", [128, 128], BF16)
                      gs = fw.sb("gs", [128, 128], F32)
                      scm = fw.sb("scm", [128, 128], BF16)
                      yh = fw.sb("yh", [128, 128], F32)
                      yhb = fw.sb("yhb", [128, 128], BF16)
                      jk = fw.sb("jk", [128, 128], F32)
                      ssh = fw.sb("ssh", [128, 2], F32)
                      ssa = fw.sb("ssa", [128, 4, NT], F32)
                      gbb = fw.sb("gbb", [128, 512], F32)
                      fw.dma('sync', gbb[:], gbrow_d[l:l + 1, 0:512].to_broadcast([128, 512]), writes=['gbb'])
                      PAb = PA[:, 0:1024].bitcast(BF16)
                      for h in range(4):
                          for j, c0 in enumerate([h * 128, 512 + h * 128, 1024 + h * 128, 1536 + h * 128]):
                              fw.dma('pool', wh[:, :, j * 128:(j + 1) * 128], winv[:, :, c0:c0 + 128], writes=['wh'])
                          for tb in range(4):
                              MMG([(PA[:, tb * 512:(tb + 1) * 512], wh[:, kc, 128:256], hT[:, kc, tb * 512:(tb + 1) * 512], kc == 0, kc == 7)
                                   for kc in range(8)], ['wh', 'hT'], [PAk[tb]])
                              ACT(t1[:, tb * 512:(tb + 1) * 512], PA[:, tb * 512:(tb + 1) * 512], AF.Sigmoid, [PAk[tb]], ['t1'])
                          TS('dve', t1[:], t1[:], oml[:, h:h + 1], lb[:, h:h + 1], ALU.mult, ALU.add, ['t1', 'oml', 'lb'], ['t1'])
                          if stop == 'hg_a':
                              fw.mute = True
                          ACT(t2[:], t1[:], AF.Ln, ['t1'], ['t2'])
                          SCAN(t3[:], rmask, t2[:], 0.0, ['cst', 't2'], ['t3'])
                          if stop == 'hg_b':
                              fw.mute = True
                          TS('dve', t1[:], t1[:], -1.0, 1.0, ALU.mult, ALU.add, ['t1'], ['t1'])
                          ACT(t2[:], t3[:], AF.Exp, ['t3'], ['t2'], scale=-1.0)
                          TT('dve', kt[:], t1[:], t2[:], ALU.mult, ['t1', 't2'], ['kt'])
                          b3 = t3[:].rearrange("p (c s) -> p c s", s=32)
                          TT('dve', t2[:].rearrange("p (c s) -> p c s", s=32), b3[:, :, 31:32].to_broadcast([128, 64, 32]), b3, ALU.subtract,
                             ['t3'], ['t2'])
                          ACT(t2[:], t2[:], AF.Exp, ['t2'], ['t2'])
                          TT('dve', kh[:], t1[:], t2[:], ALU.mult, ['t1', 't2'], ['kh'])
                          ACT(t3[:], t3[:], AF.Exp, ['t3'], ['t3'])
                          CP('dve', el[:], t3[:].rearrange("p (c s) -> p c s", s=32)[:, :, 31], ['t3'], ['el'])
                          if stop == 'hg_c':
                              fw.mute = True
                          for tb in range(4):
                              MMG([(PB[:, tb * 512:(tb + 1) * 512], wh[:, kc, 0:128], hT[:, kc, tb * 512:(tb + 1) * 512], kc == 0, kc == 7)
                                   for kc in range(8)], ['wh', 'hT'], [PBk[tb]])
                              ACT(t1[:, tb * 512:(tb + 1) * 512], PB[:, tb * 512:(tb + 1) * 512], AF.Silu, [PBk[tb]], ['t1'])
                          TT('dve', qt[:], t1[:], t3[:], ALU.mult, ['t1', 't3'], ['qt'])
                          if stop == 'hg_c1':
                              fw.mute = True
                          CP('dve', qz[:].rearrange("p (i t) -> p i t", t=128)[:, :, 96:128], qt[:].rearrange("p (i t) -> p i t", t=128)[:, :, 96:128],
                             ['qt'], ['qz'])
                          if stop == 'hg_c2':
                              fw.mute = True
                          for half in range(2):
                              TRG([(PAb[:, j * 128:(j + 1) * 128], kh[:, (half * 8 + j) * 128:(half * 8 + j + 1) * 128]) for j in range(8)],
                                  identb[:], ['kh', 'identb'], ['PA0'])
                              CP('act', khk[:, half * 8:(half + 1) * 8, :], PAb[:, 0:1024].rearrange("p (j d) -> p j d", j=8), ['PA0'], ['khk'])
                              if stop == 'hg_c3':
                                  fw.mute = True
                              CP('act', khkz[64:128, half * 8:(half + 1) * 8, :], PAb[64:128, 0:1024].rearrange("p (j d) -> p j d", j=8), ['PA0'], ['khkz'])
                              if stop == 'hg_c4':
                                  fw.mute = True
                              fw.op('dve', lambda e, half=half: e.memset(khkz[64:96, half * 8:(half + 1) * 8, :], 0.0), [], ['khkz'])
                              if stop == 'hg_c5':
                                  fw.mute = True
                          fw.op('dve', lambda e: e.memset(S[:], 0.0), [], ['S'])
                          if stop == 'hg_c7':
                              fw.mute = True
                          fw.op('dve', lambda e: e.memset(Sb[:], 0.0), [], ['Sb'])
                          if stop == 'hg_d':
                              fw.mute = True
                          for i in range(NT):
                              tsl = slice(i * 128, (i + 1) * 128)
                              MMG([(PB[:, 0:256], hT[:, kc, tsl], wh[:, kc, 256:512], kc == 0, kc == 7) for kc in range(8)],
                                  ['hT', 'wh'], ['PB0'])
                              CP('act', vb[:], PB[:, 0:128], ['PB0'], ['vb'])
                              ACT(gs[:], PB[:, 128:256], AF.Silu, ['PB0'], ['gs'])
                              MMG([(PB[:, 512:640], kt[:, tsl], qt[:, tsl], True, True)], ['kt', 'qt'], ['PB1'])
                              TT('dve', scm[:], PB[:, 512:640], maskbd, ALU.mult, ['PB1', 'cst'], ['scm'])
                              if stop == 'hg_e':
                                  fw.mute = True
                              MMG([(PB[:, 1024:1152], scm[:], vb[:], True, False)], ['scm', 'vb'], ['PB2'])
                              for j in range(4):
                                  rs_ = slice(32 * j, 32 * j + 32)
                                  if j < 3:
                                      MMG([(PB[rs_, 1024:1152], qt[:, i * 128 + 32 * j:i * 128 + 32 * j + 32], Sb[:], False, False, 1)],
                                          ['qt', 'Sb'], ['PB2'])
                                      MMG([(PB[:, 1536:1664], khk[rs_, i, :], vb[rs_, :], True, True)], ['khk', 'vb'], ['PB3'])
                                  else:
                                      MMG([(PB[64:128, 1024:1152], qz[:, i * 128 + 64:i * 128 + 128], Sb[:], False, False, 1)],
                                          ['qz', 'Sb'], ['PB2'])
                                      MMG([(PB[:, 1536:1664], khkz[64:128, i, :], vb[64:128, :], True, True)], ['khkz', 'vb'], ['PB3'])
                                  STT(S[:], S[:], el[:, 4 * i + j:4 * i + j + 1], PB[:, 1536:1664], ALU.mult, ALU.add, ['S', 'el', 'PB3'], ['S'])
                                  CP('act', Sb[:], S[:], ['S'], ['Sb'])
                              MMG([(PB[:, 1024:1152], zer[:], vb[:], False, True)], ['zer', 'vb'], ['PB2'])
                              if stop == 'hg_f':
                                  fw.mute = True
                              ACT(jk[:], PB[:, 1024:1152], AF.Square, ['PB2'], ['jk', 'ssh'], accum=ssh[:, 0:1])
                              rstd_from_ss(ssh[:, 0:1], 128.0, 'ssh')
                              STT(yh[:], PB[:, 1024:1152], ssh[:, 0:1], gs[:], ALU.mult, ALU.mult, ['PB2', 'ssh', 'gs'], ['yh'])
                              ACT(jk[:], yh[:], AF.Square, ['yh'], ['jk', 'ssa'], accum=ssa[:, h, i:i + 1])
                              TT('dve', yhb[:], yh[:], gbb[:, h * 128:(h + 1) * 128], ALU.mult, ['yh', 'gbb'], ['yhb'])
                              TRG([(PAb[:, 1024:1152], yhb[:])], identb[:], ['yhb', 'identb'], ['PA1'])
                              CP('act', yT[:, h, tsl], PAb[:, 1024:1152], ['PA1'], ['yT%d' % h])
                      TT('dve', ssa[:, 0, :], ssa[:, 0, :], ssa[:, 1, :], ALU.add, ['ssa'], ['ssa'])
                      TT('dve', ssa[:, 2, :], ssa[:, 2, :], ssa[:, 3, :], ALU.add, ['ssa'], ['ssa'])
                      TT('dve', rsd[:, 0, :], ssa[:, 0, :], ssa[:, 2, :], ALU.add, ['ssa'], ['rsd0'])
                      rstd_from_ss(rsd[:, 0, :], 512.0, 'rsd0')
                      barrier()
                  fw.ctx = actx
                  if stop == 'hgrn':
                      fw.mute = True

                  with ExitStack() as sctx:
                      fw.ctx = sctx
                      wl = fw.sb("wl", [128, 8, 256], BF16)
                      wab = fw.sb("wab", [128, 2, 128], BF16)
                      wxb = fw.sb("wxb", [128, 2, 128], BF16)
                      xraw = fw.sb("xraw", [128, 3 + T], F32)
                      xc = fw.sb("xc", [128, T], F32)
                      xcb = fw.sb("xcb", [128, T], BF16)
                      ta = fw.sb("ta", [128, T], F32)
                      tb_ = fw.sb("tb_", [128, T], F32)
                      tr = fw.sb("tr", [128, T], F32)
                      ti = fw.sb("ti", [128, T], F32)
                      c8 = fw.sb("c8", [128, 2], F32)
                      c16 = fw.sb("c16", [128, 2], F32)
                      ssc = fw.sb("ssc", [128, 2, NT], F32)
                      fw.dma('pool', wab[:], wa_d[l], writes=['wab'])
                      fw.dma('pool', wxb[:], wx_d[l], writes=['wxb'])
                      ACT(c8[:], sm[:, O_LAM:O_LAM + 2], AF.Exp, ['sm'], ['c8'], scale=-1.0)
                      ACT(c8[:], c8[:], AF.Ln, ['c8'], ['c8'], bias=onecol)
                      TS('dve', c16[:], c8[:], -16.0, None, ALU.mult, ALU.bypass, ['c8'], ['c16'])
                      TS('dve', c8[:], c8[:], -8.0, None, ALU.mult, ALU.bypass, ['c8'], ['c8'])
                      fw.op('dve', lambda e: e.memset(xraw[:, 0:3], 0.0), [], ['xraw'])
                      for hc in range(2):
                          for j, c0 in enumerate([2304 + hc * 128, 2560 + hc * 128]):
                              fw.dma('pool', wl[:, :, j * 128:(j + 1) * 128], winv[:, :, c0:c0 + 128], writes=['wl'])
                          for tb in range(4):
                              MMG([(PA[:, tb * 512:(tb + 1) * 512], wl[:, kc, 0:128], hT[:, kc, tb * 512:(tb + 1) * 512], kc == 0, kc == 7)
                                   for kc in range(8)], ['wl', 'hT'], [PAk[tb]])
                              CP('act', xraw[:, 3 + tb * 512:3 + (tb + 1) * 512], PA[:, tb * 512:(tb + 1) * 512], [PAk[tb]], ['xraw'])
                          cw = sm[:, O_CW + hc * 4:O_CW + hc * 4 + 4]
                          TS('dve', xc[:], xraw[:, 3:3 + T], cw[:, 3:4], sm[:, O_CB + hc:O_CB + hc + 1], ALU.mult, ALU.add, ['xraw', 'sm'], ['xc'])
                          for w_ in range(3):
                              STT(xc[:], xraw[:, w_:w_ + T], cw[:, w_:w_ + 1], xc[:], ALU.mult, ALU.add, ['xraw', 'sm', 'xc'], ['xc'])
                          CP('act', xcb[:], xc[:], ['xc'], ['xcb'])
                          for tb in range(4):
                              sl = slice(tb * 512, (tb + 1) * 512)
                              MMG([(PB[:, sl], wab[:, hc, :], xcb[:, sl], True, True)], ['wab', 'xcb'], [PBk[tb]])
                              ACT(tr[:, sl], PB[:, sl], AF.Sigmoid, [PBk[tb]], ['tr'], bias=sm[:, O_BA + hc:O_BA + hc + 1])
                          for tb in range(4):
                              sl = slice(tb * 512, (tb + 1) * 512)
                              MMG([(PA[:, sl], wxb[:, hc, :], xcb[:, sl], True, True)], ['wxb', 'xcb'], [PAk[tb]])
                              ACT(ti[:, sl], PA[:, sl], AF.Sigmoid, [PAk[tb]], ['ti'], bias=sm[:, O_BX + hc:O_BX + hc + 1])
                          ACT(ta[:], tr[:], AF.Exp, ['tr', 'c8'], ['ta'], scale=c8[:, hc:hc + 1])
                          ACT(tb_[:], tr[:], AF.Exp, ['tr', 'c16'], ['tb_'], scale=c16[:, hc:hc + 1])
                          TS('dve', tb_[:], tb_[:], -1.0, 1.0, ALU.mult, ALU.add, ['tb_'], ['tb_'])
                          ACT(tb_[:], tb_[:], AF.Sqrt, ['tb_'], ['tb_'])
                          TT('dve', tb_[:], tb_[:], ti[:], ALU.mult, ['tb_', 'ti'], ['tb_'])
                          TT('dve', tb_[:], tb_[:], xc[:], ALU.mult, ['tb_', 'xc'], ['tb_'])
                          SCAN(tr[:], ta[:], tb_[:], 0.0, ['ta', 'tb_'], ['tr'])
                          for tb in range(4):
                              sl = slice(tb * 512, (tb + 1) * 512)
                              MMG([(PB[:, sl], wl[:, kc, 128:256], hT[:, kc, sl], kc == 0, kc == 7) for kc in range(8)],
                                  ['wl', 'hT'], [PBk[tb]])
                              CP('act', ti[:, sl], PB[:, sl], [PBk[tb]], ['ti'])
                          gelu_inplace('dve', ti[:], ta[:], 'ti', 'ta')
                          TT('dve', tb_[:], tr[:], ti[:], ALU.mult, ['tr', 'ti'], ['tb_'])
                          TS('dve', yT[:, 6 + hc, :], tb_[:], sm[:, O_GBR + 6 + hc:O_GBR + 7 + hc], None, ALU.mult, ALU.bypass,
                             ['tb_', 'sm'], ['yT%d' % (6 + hc)])
                          ACT(ta[:], tb_[:], AF.Square, ['tb_'], ['ta'])
                          MMG([(PA[:, 2 * i:2 * i + 2], ta[:, i * 128:(i + 1) * 128], onesf[:, 0:2], True, True) for i in range(NT)],
                              ['ta', 'cst'], ['PA0'])
                          CP('dve', ssc[:, hc, :], PA[:, 0:2 * NT].rearrange("p (i two) -> p i two", two=2)[:, :, 0], ['PA0'], ['ssc'])
                      TT('dve', rsd[:, 2, :], ssc[:, 0, :], ssc[:, 1, :], ALU.add, ['ssc'], ['rsd2'])
                      rstd_from_ss(rsd[:, 2, :], 256.0, 'rsd2')
                      barrier()
                  fw.ctx = actx
                  if stop == 'lru':
                      fw.mute = True

                  with ExitStack() as sctx:
                      fw.ctx = sctx
                      LP = 512
                      NP_ = T // LP
                      ws5 = fw.sb("ws5", [128, 8, 256], BF16)
                      crt = fw.sb("crt", [128, 8, 128], BF16)
                      cit = fw.sb("cit", [128, 8, 128], BF16)
                      wglu = fw.sb("wglu", [128, 2, 256], BF16)
                      ub = fw.sb("ub", [128, 2, T], BF16)
                      zb = fw.sb("zb", [128, 2, T], BF16)
                      lhb = fw.sb("lhb", [128, 8, 2, 128], BF16)
                      pst = fw.sb("pst", [128, 5, 8], F32)
                      prp = fw.sb("prp", [128, 12, 128], F32)
                      pri = fw.sb("pri", [128, 128], I32)
                      car = fw.sb("car", [128, 8, 2], F32)
                      tcos = fw.sb("tcos", [128, LP], F32)
                      tsin = fw.sb("tsin", [128, LP], F32)
                      tki = fw.sb("tki", [128, LP], I32)
                      s1 = fw.sb("s1", [128, LP], F32)
                      s2 = fw.sb("s2", [128, LP], F32)
                      swr = fw.sb("swr", [128, LP], F32)
                      swi = fw.sb("swi", [128, LP], F32)
                      szr = fw.sb("szr", [128, LP], F32)
                      szi = fw.sb("szi", [128, LP], F32)
                      xrb = fw.sb("xrb", [128, LP], BF16)
                      xib = fw.sb("xib", [128, LP], BF16)
                      yb = fw.sb("yb", [128, 512], F32)
                      yb2 = fw.sb("yb2", [128, 512], F32)
                      ssb = fw.sb("ssb", [128, 2, NT], F32)
                      fw.dma('pool', crt[:], crt_d[l], writes=['crt'])
                      fw.dma('pool', cit[:], cit_d[l], writes=['cit'])
                      fw.dma('pool', wglu[:], wglu_d[l].rearrange("(kc p) n -> p kc n", p=128), writes=['wglu'])
                      fw.dma('pool', ws5[:], winv[:, :, 2048:2304], writes=['ws5'])

                      def sincos(ang, ki, sin_o, cos_o, tmp, keys):
                          ka, kk, ks, kc_, kt_ = keys
                          TS('dve', ki, ang, 1.0 / TWO_PI, None, ALU.mult, ALU.bypass, [ka], [kk])
                          STT(ang, ki, -CW1, ang, ALU.mult, ALU.add, [kk, ka], [ka])
                          STT(ang, ki, -CW2, ang, ALU.mult, ALU.add, [kk, ka], [ka])
                          TS('dve', ang, ang, -3.14159, 3.14159, ALU.max, ALU.min, [ka], [ka])
                          ACT(sin_o, ang, AF.Sin, [ka], [ks])
                          STT(tmp, ang, -1.0, ang, ALU.mult, ALU.max, [ka], [kt_])
                          ACT(cos_o, tmp, AF.Sin, [kt_, 'cst'], [kc_], scale=-1.0, bias=halfpi)

                      lamre, dts, rmag, theta = pst[:, 0, :], pst[:, 1, :], pst[:, 2, :], pst[:, 3, :]
                      TS('dve', lamre, sm[:, O_ARS:O_ARS + 8], -1e-4, None, ALU.min, ALU.bypass, ['sm'], ['pst'])
                      ACT(dts, sm[:, O_LDS:O_LDS + 8], AF.Exp, ['sm'], ['pst'])
                      TT('dve', rmag, lamre, dts, ALU.mult, ['pst'], ['pst'])
                      ACT(rmag, rmag, AF.Exp, ['pst'], ['pst'])
                      TT('dve', theta, sm[:, O_AIS:O_AIS + 8], dts, ALU.mult, ['sm', 'pst'], ['pst'])
                      R = lambda k: prp[:, k, :]
                      lam_r, lam_i = R(0), R(1)
                      TS('dve', lam_r, sm[:, O_ARR:O_ARR + 128], -1e-4, None, ALU.min, ALU.bypass, ['sm'], ['prp'])
                      CP('dve', lam_i, sm[:, O_AIR:O_AIR + 128], ['sm'], ['prp'])
                      ACT(prp[:, 11, 0:2], sm[:, O_LDR:O_LDR + 2], AF.Exp, ['sm'], ['prp'])
                      for hg in range(2):
                          cs = slice(hg * 64, (hg + 1) * 64)
                          TS('dve', prp[:, 2, cs], prp[:, 0, cs], prp[:, 11, hg:hg + 1], None, ALU.mult, ALU.bypass, ['prp'], ['prp'])
                          TS('dve', prp[:, 3, cs], prp[:, 1, cs], prp[:, 11, hg:hg + 1], None, ALU.mult, ALU.bypass, ['prp'], ['prp'])
                      ACT(R(2), R(2), AF.Exp, ['prp'], ['prp'])
                      sincos(R(3), pri[:], R(4), R(5), R(6), ['prp', 'pri', 'prp', 'prp', 'prp'])
                      TT('dve', R(5), R(5), R(2), ALU.mult, ['prp'], ['prp'])
                      TT('dve', R(4), R(4), R(2), ALU.mult, ['prp'], ['prp'])
                      TS('dve', R(5), R(5), -1.0, None, ALU.add, ALU.bypass, ['prp'], ['prp'])
                      TT('dve', R(2), lam_r, lam_r, ALU.mult, ['prp'], ['prp'])
                      TT('dve', R(3), lam_i, lam_i, ALU.mult, ['prp'], ['prp'])
                      TT('dve', R(2), R(2), R(3), ALU.add, ['prp'], ['prp'])
                      fw.op('dve', lambda e: e.reciprocal(out=R(2), in_=R(2)), ['prp'], ['prp'])
                      TT('dve', R(6), R(5), lam_r, ALU.mult, ['prp'], ['prp'])
                      TT('dve', R(7), R(4), lam_i, ALU.mult, ['prp'], ['prp'])
                      TT('dve', R(6), R(6), R(7), ALU.add, ['prp'], ['prp'])
                      TT('dve', R(6), R(6), R(2), ALU.mult, ['prp'], ['prp'])
                      TT('dve', R(7), R(4), lam_r, ALU.mult, ['prp'], ['prp'])
                      TT('dve', R(8), R(5), lam_i, ALU.mult, ['prp'], ['prp'])
                      TT('dve', R(7), R(7), R(8), ALU.subtract, ['prp'], ['prp'])
                      TT('dve', R(7), R(7), R(2), ALU.mult, ['prp'], ['prp'])
                      btr, bti = sm[:, O_BTR:O_BTR + 128], sm[:, O_BTI:O_BTI + 128]
                      TT('dve', R(8), R(6), btr, ALU.mult, ['prp', 'sm'], ['prp'])
                      TT('dve', R(9), R(7), bti, ALU.mult, ['prp', 'sm'], ['prp'])
                      TT('dve', R(8), R(8), R(9), ALU.subtract, ['prp'], ['prp'])
                      TT('dve', R(9), R(6), bti, ALU.mult, ['prp', 'sm'], ['prp'])
                      TT('dve', R(10), R(7), btr, ALU.mult, ['prp', 'sm'], ['prp'])
                      TT('dve', R(9), R(9), R(10), ALU.add, ['prp'], ['prp'])
                      for gp in range(8):
                          hc = gp // 4
                          for gi in range(2):
                              gl = (2 * gp + gi) % 8
                              for ri, src in enumerate([8, 9]):
                                  TS('dve', lhb[:, gp, ri, gi * 64:(gi + 1) * 64], prp[:, src, hc * 64:(hc + 1) * 64],
                                     sm[:, O_RMK + gl:O_RMK + gl + 1], None, ALU.mult, ALU.bypass, ['prp', 'sm'], ['lhb'])
                      for hc in range(2):
                          for tb in range(4):
                              sl = slice(tb * 512, (tb + 1) * 512)
                              MMG([(PA[:, sl], ws5[:, kc, hc * 128:(hc + 1) * 128], hT[:, kc, sl], kc == 0, kc == 7) for kc in range(8)],
                                  ['ws5', 'hT'], [PAk[tb]])
                              CP('act', ub[:, hc, sl], PA[:, sl], [PAk[tb]], ['ub'])
                      fw.op('dve', lambda e: e.memset(car[:], 0.0), [], ['car'])
                      for hc in range(2):
                          for pc in range(NP_):
                              psl = slice(pc * LP, (pc + 1) * LP)
                              for gq in range(4):
                                  gp = hc * 4 + gq
                                  if True:
                                      TS('dve', s1[:], tau[:, 0:LP], theta[:, gp:gp + 1], None, ALU.mult, ALU.bypass, ['cst', 'pst'], ['s1'])
                                      sincos(s1[:], tki[:], tsin[:], tcos[:], s2[:], ['s1', 'tki', 'tsin', 'tcos', 's2'])
                                  MMG([(PA[:, 0:LP], lhb[:, gp, 0, :], ub[:, hc, psl], True, True),
                                       (PA[:, 512:512 + LP], lhb[:, gp, 1, :], ub[:, hc, psl], True, True)], ['lhb', 'ub'], ['PA0', 'PA1'])
                                  bur, bui = PA[:, 0:LP], PA[:, 512:512 + LP]
                                  TT('dve', s1[:], bur, tcos[:], ALU.mult, ['PA0', 'tcos'], ['s1'])
                                  TT('dve', s2[:], bui, tsin[:], ALU.mult, ['PA1', 'tsin'], ['s2'])
                                  TT('dve', swr[:], s1[:], s2[:], ALU.add, ['s1', 's2'], ['swr'])
                                  TT('dve', s1[:], bui, tcos[:], ALU.mult, ['PA1', 'tcos'], ['s1'])
                                  TT('dve', s2[:], bur, tsin[:], ALU.mult, ['PA0', 'tsin'], ['s2'])
                                  TT('dve', swi[:], s1[:], s2[:], ALU.subtract, ['s1', 's2'], ['swi'])
                                  rb_ = rmag[:, gp:gp + 1].to_broadcast([128, LP])
                                  SCAN(szr[:], rb_, swr[:], car[:, gp, 0:1], ['pst', 'swr', 'car'], ['szr'])
                                  SCAN(szi[:], rb_, swi[:], car[:, gp, 1:2], ['pst', 'swi', 'car'], ['szi'])
                                  TT('dve', s1[:], szr[:], tcos[:], ALU.mult, ['szr', 'tcos'], ['s1'])
                                  TT('dve', s2[:], szi[:], tsin[:], ALU.mult, ['szi', 'tsin'], ['s2'])
                                  TT('dve', swr[:], s1[:], s2[:], ALU.subtract, ['s1', 's2'], ['swr'])
                                  TT('dve', s1[:], szr[:], tsin[:], ALU.mult, ['szr', 'tsin'], ['s1'])
                                  TT('dve', s2[:], szi[:], tcos[:], ALU.mult, ['szi', 'tcos'], ['s2'])
                                  TT('dve', swi[:], s1[:], s2[:], ALU.add, ['s1', 's2'], ['swi'])
                                  CP('act', car[:, gp, 0:1], swr[:, LP - 1:LP], ['swr'], ['car'])
                                  CP('act', car[:, gp, 1:2], swi[:, LP - 1:LP], ['swi'], ['car'])
                                  CP('act', xrb[:], swr[:], ['swr'], ['xrb'])
                                  ACT(xib[:], swi[:], AF.Copy, ['swi'], ['xib'], scale=-1.0)
                                  MMG([(PB[:, 0:LP], crt[:, gp, :], xrb[:], gq == 0, False),
                                       (PB[:, 0:LP], cit[:, gp, :], xib[:], False, gq == 3)], ['crt', 'cit', 'xrb', 'xib'], ['PB0'])
                              STT(yb[:], ub[:, hc, psl], sm[:, O_S5D + hc:O_S5D + hc + 1], PB[:, 0:LP], ALU.mult, ALU.add,
                                  ['ub', 'sm', 'PB0'], ['yb'])
                              gelu_inplace('dve', yb[:], yb2[:], 'yb', 'yb2')
                              CP('act', zb[:, hc, psl], yb[:], ['yb'], ['zb'])
                      for oc in range(2):
                          for tb in range(4):
                              sl = slice(tb * 512, (tb + 1) * 512)
                              MMG([(PA[:, sl], wglu[:, kc, oc * 128:(oc + 1) * 128], zb[:, kc, sl], kc == 0, kc == 1) for kc in range(2)],
                                  ['wglu', 'zb'], [PAk[tb]])
                              ACT(yb[:], PA[:, sl], AF.Sigmoid, [PAk[tb]], ['yb'], bias=sm[:, O_BGL + oc:O_BGL + oc + 1])
                              TT('dve', yb[:], yb[:], zb[:, oc, sl], ALU.mult, ['yb', 'zb'], ['yb'])
                              TS('dve', yT[:, 4 + oc, sl], yb[:], sm[:, O_GBR + 4 + oc:O_GBR + 5 + oc], None, ALU.mult, ALU.bypass,
                                 ['yb', 'sm'], ['yT%d' % (4 + oc)])
                              ACT(yb2[:], yb[:], AF.Square, ['yb'], ['yb2'])
                              MMG([(PB[:, 2 * (tb * 4 + j):2 * (tb * 4 + j) + 2], yb2[:, j * 128:(j + 1) * 128], onesf[:, 0:2], True, True)
                                   for j in range(4)], ['yb2', 'cst'], ['PB0'])
                          CP('dve', ssb[:, oc, :], PB[:, 0:2 * NT].rearrange("p (i two) -> p i two", two=2)[:, :, 0], ['PB0'], ['ssb'])
                      TT('dve', rsd[:, 1, :], ssb[:, 0, :], ssb[:, 1, :], ALU.add, ['ssb'], ['rsd1'])
                      rstd_from_ss(rsd[:, 1, :], 256.0, 'rsd1')
                      barrier()
                  fw.ctx = actx
                  if stop == 's5':
                      fw.mute = True

                  with ExitStack() as sctx:
                      fw.ctx = sctx
                      stg = fw.sb("stg", [128, 8, 512], F32)
                      bm = fw.sb("bm", [128, 1024], F32)
                      gt1 = fw.sb("gt1", [128, 1024], F32)
                      wo = fw.sb("wo", [128, 8, 1024], BF16)
                      tmpo = fw.sb("tmpo", [128, 1024], F32)
                      wov = wout_d[l].rearrange("(kc p) n -> p kc n", p=128)
                      for kc in range(8):
                          fw.dma('pool', wo[:, kc, :], wov[:, kc, :], writes=['wo'])
                      mod_tile(l, 2, gt1, 'gt1', stg, bm)
                      for i in range(NT):
                          tsl = slice(i * 128, (i + 1) * 128)
                          for (P_, off, kcs, keys) in [(PA, 0, [0, 1, 2, 3], ['PA0', 'PA1']), (PA, 1024, [4, 5], ['PA2', 'PA3']),
                                                       (PB, 0, [6, 7], ['PB0', 'PB1'])]:
                              for nh in range(2):
                                  MMG([(P_[:, off + nh * 512:off + (nh + 1) * 512], yT[:, kc, tsl], wo[:, kc, nh * 512:(nh + 1) * 512],
                                        kc == kcs[0], kc == kcs[-1]) for kc in kcs],
                                      ['yT%d' % kc for kc in kcs] + ['wo'], [keys[nh]])
                          TS('dve', tmpo[:], PA[:, 0:1024], rsd[:, 0, i:i + 1], None, ALU.mult, ALU.bypass, ['PA0', 'PA1', 'rsd0'], ['tmpo'])
                          STT(tmpo[:], PA[:, 1024:2048], rsd[:, 1, i:i + 1], tmpo[:], ALU.mult, ALU.add, ['PA2', 'PA3', 'rsd1', 'tmpo'], ['tmpo'])
                          STT(tmpo[:], PB[:, 0:1024], rsd[:, 2, i:i + 1], tmpo[:], ALU.mult, ALU.add, ['PB0', 'PB1', 'rsd2', 'tmpo'], ['tmpo'])
                          TT('dve', tmpo[:], tmpo[:], gt1[:], ALU.mult, ['tmpo', 'gt1'], ['tmpo'])
                          TT('dve', xs[:, i, :], xs[:, i, :], tmpo[:], ALU.add, ['x%d' % i, 'tmpo'], ['x%d' % i])
                      barrier()
                  fw.ctx = octx
              barrier()
              if not do_peer:
                  continue
              with ExitStack() as bctx:
                  octx = fw.ctx
                  fw.ctx = bctx
                  A2 = fw.sb("A2", [128, 1024], F32)
                  B2 = fw.sb("B2", [128, 1024], F32)
                  gt2 = fw.sb("gt2", [128, 1024], F32)
                  with ExitStack() as mctx:
                      fw.ctx = mctx
                      stg = fw.sb("stg", [128, 8, 512], F32)
                      bm = fw.sb("bm", [128, 1024], F32)
                      gm = fw.sb("gm", [128, 1024], F32)
                      mod_tile(l, 3, B2, 'B2', stg, bm)
                      mod_tile(l, 4, A2, 'A2', stg, bm)
                      mod_tile(l, 5, gt2, 'gt2', stg, bm)
                      fw.dma('sync', gm[:], gffn_d[l:l + 1, :].to_broadcast([128, 1024]), writes=['gm'])
                      STT(A2[:], A2[:], 1.0, gm[:], ALU.add, ALU.mult, ['A2', 'gm'], ['A2'])
                      barrier()
                  fw.ctx = bctx
                  NB = 16
                  ss = fw.sb("ss", [128, NT], F32)
                  junk = fw.sb("junk", [128, 1024], BF16)
                  wq = fw.sb("wq", [128, 8, 2048], BF16)
                  skt = fw.sb("skt", [128, 2, 128], BF16)
                  h2 = fw.sb("h2", [128, 1024], F32)
                  h2b = fw.sb("h2b", [128, 1024], BF16)
                  h2T = fw.sb("h2T", [128, 8, 128], BF16)
                  qTb = fw.sb("qTb", [128, 16, 128], BF16)
                  big = fw.sb("big", [128, 2048], F32)
                  sc = big[:].rearrange("p (a b) -> p a b", a=16)
                  cand = big[:].rearrange("p (h c) -> p h c", h=8)
                  eq = big[:].rearrange("p (h k a) -> p h k a", h=8, k=16)
                  top = fw.sb("top", [128, 16, 16], F32)
                  tix = fw.sb("tix", [128, 16, 16], U32)
                  tixf = fw.sb("tixf", [128, 16, 16], F32)
                  best = fw.sb("best", [128, 8, 16], F32)
                  pos = fw.sb("pos", [128, 8, 16], U32)
                  posf = fw.sb("posf", [128, 8, 16], F32)
                  ai = fw.sb("ai", [128, 8, 16], I32)
                  af = fw.sb("af", [128, 8, 16], F32)
                  bf = fw.sb("bf", [128, 8, 16], F32)
                  isel = fw.sb("isel", [128, 8, 16], F32)
                  jsel = fw.sb("jsel", [128, 8, 16], F32)
                  eidx2 = fw.sb("eidx", [128, 2, 128], I32)
                  gat = fw.sb("gat", [128, 8, 16], F32)
                  gz_ = fw.sb("gz_", [128, 8], F32)
                  actp = fw.sb("actp", [128, 128], F32)
                  actt = fw.sb("actt", [128, 128], F32)
                  wgt = fw.sb("wgt", [128, 128], F32)
                  acc = fw.sb("acc", [128, 1024], F32)
                  jf = fw.sb("jf", [128, 1024], F32) if JF else acc
                  gb = [fw.sb("gb%d" % j, [128, 1024], BF16) for j in range(NB)]
                  gv = gb
                  gvc = gbc = [0]
                  NBV = NB
                  wqv = wq_d[l].rearrange("(kc p) n -> p kc n", p=128)
                  for kc in range(8):
                      fw.dma('pool', wq[:, kc, :], wqv[:, kc, :], writes=['wq'])
                  fw.dma('pool', skt[:], skt_d[l], writes=['skt'])
                  for i in range(NT):
                      ACT(junk[:], xs[:, i, :], AF.Square, ['x%d' % i], ['junk', 'ss'], accum=ss[:, i:i + 1])
                  rstd_from_ss(ss[:], float(D), 'ss')
                  PAb = PA[:, 0:512].bitcast(BF16)
                  topv = top[:].rearrange("p (h two) k -> p h two k", two=2)
                  tixv = tixf[:].rearrange("p (h two) k -> p h two k", two=2)
                  dg = [fw.sb("dg%d" % k, [128, 128], BF16) for k in range(4)]
                  ubk = ['ubd%d' % c for c in range(16)]
                  vbk = ['vbd%d' % c for c in range(16)]

                  def idx_a(i):
                      xk = 'x%d' % i
                      ek = 'eidx%d' % (i % 2)
                      eidx = eidx2[:, i % 2, :]
                      STT(jf[:], xs[:, i, :], ss[:, i:i + 1], A2[:], ALU.mult, ALU.mult, [xk, 'ss', 'A2'], ['acc', 'jf'])
                      TT('dve', h2[:], jf[:], B2[:], ALU.add, ['acc', 'jf', 'B2'], ['h2'])
                      CP('act', h2b[:], h2[:], ['h2'], ['h2b'])
                      TRG([(PAb[:, kc * 128:(kc + 1) * 128], h2b[:, kc * 128:(kc + 1) * 128]) for kc in range(8)],
                          identb[:], ['h2b', 'identb'], ['PA0'])
                      CP('act', h2T[:], PAb.rearrange("p (k t) -> p k t", k=8), ['PA0'], ['h2T'])
                      for half in range(2):
                          for hq2 in range(2):
                              hq = half * 2 + hq2
                              MMG([(PB[:, hq2 * 512 + j * 128:hq2 * 512 + (j + 1) * 128], wq[:, kc, (hq * 4 + j) * 128:(hq * 4 + j + 1) * 128], h2T[:, kc, :],
                                    kc == 0, kc == 7) for j in range(4) for kc in range(8)], ['wq', 'h2T'], [PBk[hq2]])
                          CP('act', qTb[:, half * 8:(half + 1) * 8, :].rearrange("p a b -> p (a b)"), PB[:, 0:1024], PBk[0:2], ['qTb'])
                      for hq in range(4):
                          MMG([(PA[:, hq * 512 + j * 128:hq * 512 + (j + 1) * 128], qTb[:, hq * 4 + j, :], skt[:, (hq * 4 + j) % 2, :], True, True)
                               for j in range(4)], ['qTb', 'skt'], [PAk[hq]])
                      CP('dve', big[:], PA[:, :], PAk, ['big'])

                  def idx_b(i):
                      ek = 'eidx%d' % (i % 2)
                      eidx = eidx2[:, i % 2, :]
                      for hp in range(16):
                          fw.op('dve', lambda e, hp=hp: e.max(out=top[:, hp, 0:8], in_=sc[:, hp, :]), ['big'], ['top'])
                          fw.op('dve', lambda e, hp=hp: e.max_index(out=tix[:, hp, 0:8], in_max=top[:, hp, 0:8], in_values=sc[:, hp, :]), ['big', 'top'], ['tix'])
                          fw.op('dve', lambda e, hp=hp: e.match_replace(out=sc[:, hp, :], in_to_replace=top[:, hp, 0:8], in_values=sc[:, hp, :], imm_value=-1e30),
                                ['big', 'top'], ['big'])
                          fw.op('dve', lambda e, hp=hp: e.max(out=top[:, hp, 8:16], in_=sc[:, hp, :]), ['big'], ['top'])
                          fw.op('dve', lambda e, hp=hp: e.max_index(out=tix[:, hp, 8:16], in_max=top[:, hp, 8:16], in_values=sc[:, hp, :]), ['big', 'top'], ['tix'])
                      CP('dve', tixf[:], tix[:], ['tix'], ['tixf'])
                      TT('dve', cand.rearrange("p h (a b) -> p h a b", a=16), topv[:, :, 0, :].unsqueeze(3).to_broadcast([128, 8, 16, 16]),
                         topv[:, :, 1, :].unsqueeze(2).to_broadcast([128, 8, 16, 16]), ALU.add, ['top'], ['big'])
                      for h in range(8):
                          fw.op('dve', lambda e, h=h: e.max(out=best[:, h, 0:8], in_=cand[:, h, :]), ['big'], ['best'])
                          fw.op('dve', lambda e, h=h: e.max_index(out=pos[:, h, 0:8], in_max=best[:, h, 0:8], in_values=cand[:, h, :]), ['big', 'best'], ['pos'])
                          fw.op('dve', lambda e, h=h: e.match_replace(out=cand[:, h, :], in_to_replace=best[:, h, 0:8], in_values=cand[:, h, :], imm_value=-1e30),
                                ['big', 'best'], ['big'])
                          fw.op('dve', lambda e, h=h: e.max(out=best[:, h, 8:16], in_=cand[:, h, :]), ['big'], ['best'])
                          fw.op('dve', lambda e, h=h: e.max_index(out=pos[:, h, 8:16], in_max=best[:, h, 8:16], in_values=cand[:, h, :]), ['big', 'best'], ['pos'])
                      CP('dve', posf[:], pos[:], ['pos'], ['posf'])
                      TS('dve', ai[:], posf[:], 1.0 / 16.0, -7.5 / 16.0, ALU.mult, ALU.add, ['posf'], ['ai'])
                      CP('dve', af[:], ai[:], ['ai'], ['af'])
                      STT(bf[:], af[:], -16.0, posf[:], ALU.mult, ALU.add, ['af', 'posf'], ['bf'])
                      io_b = io16.unsqueeze(1).unsqueeze(1).to_broadcast([128, 8, 16, 16])
                      for (src, tv, dst, dk) in [(af, 0, isel, 'isel'), (bf, 1, jsel, 'jsel')]:
                          TT('dve', eq, src[:].unsqueeze(3).to_broadcast([128, 8, 16, 16]), io_b, ALU.is_equal, ['af', 'bf', 'cst'], ['big'])
                          TT('dve', eq, eq, tixv[:, :, tv, :].unsqueeze(2).to_broadcast([128, 8, 16, 16]), ALU.mult, ['big', 'tixf'], ['big'])
                          fw.op('dve', lambda e, dst=dst: e.tensor_reduce(out=dst[:], in_=eq, axis=AX.X, op=ALU.add), ['big'], [dk])
                      STT(isel[:], isel[:], 128.0, jsel[:], ALU.mult, ALU.add, ['isel', 'jsel'], ['isel'])
                      CP('dve', eidx, isel[:].rearrange("p h k -> p (h k)"), ['isel'], [ek])

                  def u_phase(i):
                      ek = 'eidx%d' % (i % 2)
                      for n in range(128):
                          j = gbc[0] % NB
                          gbc[0] += 1
                          fw.dma('pool', gb[j][:], ub_d[l], reads=[ek] + ubk, writes=['gb%d' % j], noslotwait=(n >= 20 or i > 0),
                                 indirect=bass.IndirectOffsetOnAxis(ap=eidx2[:, i % 2, n:n + 1], axis=0))
                          fw.op('dve', lambda e, j=j, n=n: e.scalar_tensor_tensor(out=(jf[:] if JF else junk[:]), in0=gb[j][:], scalar=1.0, in1=h2[:],
                                                                                op0=ALU.mult, op1=ALU.mult, accum_out=actp[:, n:n + 1]),
                                ['gb%d' % j, 'h2'], ['junk', 'jf', 'actp'])
                      TT('dve', gat[:], best[:], best[:, :, 0:1].to_broadcast([128, 8, 16]), ALU.subtract, ['best'], ['gat'])
                      ACT(gat[:], gat[:], AF.Exp, ['gat'], ['gat'])
                      fw.op('dve', lambda e: e.tensor_reduce(out=gz_[:], in_=gat[:], axis=AX.X, op=ALU.add), ['gat'], ['gz_'])
                      fw.op('dve', lambda e: e.reciprocal(out=gz_[:], in_=gz_[:]), ['gz_'], ['gz_'])
                      TT('dve', gat[:], gat[:], gz_[:].unsqueeze(2).to_broadcast([128, 8, 16]), ALU.mult, ['gat', 'gz_'], ['gat'])

                      gelu_inplace('dve', actp[:], actt[:], 'actp', 'actt')
                      TT('dve', wgt[:], actp[:], gat[:].rearrange("p h k -> p (h k)"), ALU.mult, ['actp', 'gat'], ['wgt'])

                  def v_phase(i):
                      xk = 'x%d' % i
                      ek = 'eidx%d' % (i % 2)
                      for n in range(128):
                          j = gvc[0] % NBV
                          gvc[0] += 1
                          k = n % 4
                          fw.dma('pool', gv[j][:], vb_d[l], reads=[ek] + vbk, writes=['gb%d' % j], noslotwait=True,
                                 indirect=bass.IndirectOffsetOnAxis(ap=eidx2[:, i % 2, n:n + 1], axis=0))
                          fw.op('act', lambda e, k=k, n=n: e.activation(out=dg[k][:], in_=identf, func=AF.Copy, scale=wgt[:, n:n + 1]),
                                ['cst', 'wgt'], ['dg%d' % k])
                          MMG([(PB[:, 1024:1536], dg[k][:], gv[j][:, 0:512], n == 0, n == 127),
                               (PB[:, 1536:2048], dg[k][:], gv[j][:, 512:1024], n == 0, n == 127)],
                              ['dg%d' % k, 'gb%d' % j], ['PB2', 'PB3'])

                  def v_fin(i):
                      xk = 'x%d' % i
                      TT('dve', acc[:], PB[:, 1024:2048], gt2[:], ALU.mult, ['PB2', 'PB3', 'gt2'], ['acc'])
                      TT('dve', xs[:, i, :], xs[:, i, :], acc[:], ALU.add, [xk, 'acc'], [xk])

                  idx_a(0)
                  idx_b(0)
                  for i in range(NT):
                      u_phase(i)
                      if i + 1 < NT:
                          idx_a(i + 1)
                      v_phase(i)
                      if i + 1 < NT:
                          idx_b(i + 1)
                      v_fin(i)
                  barrier()
                  fw.ctx = octx
              barrier()
          except _Stop:
            fw.ctx = ctx
            barrier()
            break

        fw.mute = False
        barrier()
        with ExitStack() as fctx:
            fw.ctx = fctx
            gf = fw.sb("gf", [128, 1024], F32)
            ss = fw.sb("ss", [128, NT], F32)
            junk = fw.sb("junk", [128, 1024], BF16)
            ob = [fw.sb("ob%d" % j, [128, 1024], F32) for j in range(2)]
            fw.dma('sync', gf[:], gfin_d[0:1, :].to_broadcast([128, 1024]), writes=['gf'])
            for i in range(NT):
                ACT(junk[:], xs[:, i, :], AF.Square, ['x%d' % i], ['junk', 'ss'], accum=ss[:, i:i + 1])
            rstd_from_ss(ss[:], float(D), 'ss')
            outk = []
            for i in range(NT):
                j = i % 2
                STT(ob[j][:], xs[:, i, :], ss[:, i:i + 1], gf[:], ALU.mult, ALU.mult, ['x%d' % i, 'ss', 'gf'], ['ob%d' % j])
                fw.dma('sync', out_d[i * 128:(i + 1) * 128, :], ob[j][:], reads=['ob%d' % j], writes=['out%d' % i])
                outk.append('out%d' % i)
            fw.finish(outk)
    return nc


def _consts():
    c = np.zeros((128, CSTW), np.float32)
    c[:, 0:128] = np.eye(128, dtype=np.float32)
    c[:, 128:256] = 1.0
    s = np.arange(128)[:, None]
    t = np.arange(128)[None, :]
    c[:, 256:384] = ((s // 32 == t // 32) & (s <= t)).astype(np.float32)
    c[:, 384] = np.pi / 2
    c[:, 385] = EPS
    c[:, 386] = 1.0
    c[:, 400:416] = np.arange(16, dtype=np.float32)[None, :]
    c[:, 416:928] = np.arange(1, 513, dtype=np.float32)[None, :]
    rm = np.ones(2048, np.float32)
    rm[::32] = 0.0
    c[:, 928:928 + 2048] = rm[None, :]
    return c


def _layouts(inp):
    f = lambda k: np.asarray(inp[k], dtype=np.float32)
    small = np.zeros((4, 128, NSMALL), np.float32)
    crt = np.zeros((4, 128, 8, 128), np.float32)
    cit = np.zeros((4, 128, 8, 128), np.float32)
    wa = np.zeros((4, 128, 2, 128), np.float32)
    wx = np.zeros((4, 128, 2, 128), np.float32)
    lbl = f('hgrn_lb_logits').reshape(4, 4, 128).transpose(2, 1, 0).reshape(128, 16)
    st = lambda a: a.reshape(8, 2, 64).transpose(1, 2, 0).reshape(128, 8)
    rep = lambda a: np.broadcast_to(a.reshape(2, 8, 1, 64), (2, 8, 16, 64)).transpose(1, 2, 0, 3).reshape(128, 128)
    col2 = lambda v: v.reshape(2, 128).T
    rmk = np.zeros((128, 8), np.float32)
    for gl in range(8):
        rmk[gl * 16:(gl + 1) * 16, gl] = 1.0
    for l in range(4):
        small[l, :, O_LBL:O_LBL + 16] = lbl
        small[l, :, O_ARS:O_ARS + 8] = st(f('s5_a_re')[l])
        small[l, :, O_AIS:O_AIS + 8] = st(f('s5_a_im')[l])
        ld = f('s5_log_dt')[l]
        small[l, :, O_LDS:O_LDS + 8] = st(np.broadcast_to(ld[:, None], (16, 64)).copy())
        small[l, :, O_ARR:O_ARR + 128] = rep(f('s5_a_re')[l])
        small[l, :, O_AIR:O_AIR + 128] = rep(f('s5_a_im')[l])
        small[l, :, O_BTR:O_BTR + 128] = f('s5_b_re')[l].reshape(2, 8, 64, 16).transpose(1, 3, 0, 2).reshape(128, 128)
        small[l, :, O_BTI:O_BTI + 128] = f('s5_b_im')[l].reshape(2, 8, 64, 16).transpose(1, 3, 0, 2).reshape(128, 128)
        small[l, :, O_LDR:O_LDR + 2] = np.broadcast_to(ld.reshape(2, 8, 1), (2, 8, 16)).transpose(1, 2, 0).reshape(128, 2)
        small[l, :, O_S5D:O_S5D + 2] = col2(f('s5_d')[l])
        small[l, :, O_BGL:O_BGL + 2] = col2(f('s5_b_glu')[l])
        small[l, :, O_CW:O_CW + 8] = f('lru_conv_w')[l].reshape(4, 2, 128).transpose(2, 1, 0).reshape(128, 8)
        small[l, :, O_CB:O_CB + 2] = col2(f('lru_conv_b')[l])
        small[l, :, O_BA:O_BA + 2] = col2(f('lru_b_a')[l])
        small[l, :, O_BX:O_BX + 2] = col2(f('lru_b_x')[l])
        small[l, :, O_LAM:O_LAM + 2] = col2(f('lru_lambda')[l])
        small[l, :, O_GBR:O_GBR + 8] = f('g_branch')[l].reshape(8, 128).T
        small[l, :, O_RMK:O_RMK + 8] = rmk
        for g in range(16):
            gp, gi, gl = g // 2, g % 2, g % 8
            crt[l, gi * 64:(gi + 1) * 64, gp, gl * 16:(gl + 1) * 16] = f('s5_c_re')[l, g].T
            cit[l, gi * 64:(gi + 1) * 64, gp, gl * 16:(gl + 1) * 16] = f('s5_c_im')[l, g].T
        for h in range(4):
            hc, o = h // 2, (h % 2) * 64
            wa[l, o:o + 64, hc, o:o + 64] = f('lru_w_a')[l, h]
            wx[l, o:o + 64, hc, o:o + 64] = f('lru_w_x')[l, h]
    skt = np.ascontiguousarray(f('peer_sub_keys').transpose(0, 3, 1, 2))
    return dict(small=small, crt=crt, cit=cit, wa_bd=wa, wx_bd=wx, skt=skt)


def make_in_maps(inp, cores):
    f = lambda k: np.ascontiguousarray(np.asarray(inp[k], dtype=np.float32))
    lay = _layouts(inp)
    shared = dict(w_mod=f('w_mod'), b_mod=f('b_mod'), g_mix=f('g_mix'), w_in=f('w_in'), w_glu=f('s5_w_glu'),
                  g_branch=f('g_branch'), w_out=f('w_out'), g_ffn=f('g_ffn'), peer_w_q=f('peer_w_q'),
                  g_final=f('g_final').reshape(1, D), consts=_consts())
    shared.update(lay)
    pu, pv = f('peer_u'), f('peer_v')
    for l_ in range(4):
        shared['peer_u%d' % l_] = pu[l_]
        shared['peer_v%d' % l_] = pv[l_]
    x = f('x')
    c = f('c')
    maps = []
    for b in cores:
        m = dict(shared)
        m['x'] = np.ascontiguousarray(x[b])
        m['c'] = np.ascontiguousarray(c[b].reshape(128, 8))
        maps.append(m)
    return maps


def kernel(**inputs):
    nc = build()
    maps = make_in_maps(inputs, list(range(8)))
    res = run_bass_kernel_spmd(nc, maps, core_ids=list(range(8)))
    return np.stack([np.asarray(r['out'], dtype=np.float32) for r in res.results], axis=0)
```

```python
import numpy as np
from contextlib import ExitStack
import concourse.bass as bass
import concourse.mybir as mybir
from concourse.bass_utils import run_bass_kernel_spmd

F32 = mybir.dt.float32
BF16 = mybir.dt.bfloat16
U32 = mybir.dt.uint32
I32 = mybir.dt.int32
AF = mybir.ActivationFunctionType
ALU = mybir.AluOpType
AX = mybir.AxisListType
F32R = mybir.dt.float32r

ENG = ['sync', 'act', 'pool', 'pe', 'dve']


class FW:
    def __init__(self, nc, ctx, slots=None):
        self.nc = nc
        self.ctx = ctx
        self.q = {e: [] for e in ENG}
        self.sems = {}
        self.cnt = {}
        for e in ENG:
            self.sems[e] = ctx.enter_context(nc.semaphore('s_' + e))
            self.cnt[e] = 0
        slots = slots or {'sync': 8, 'act': 4, 'pool': 8}
        self.slots = {}
        self.rr = {}
        for e, n in slots.items():
            self.slots[e] = []
            self.rr[e] = 0
            for i in range(n):
                nm = 'd_%s%d' % (e, i)
                self.sems[nm] = ctx.enter_context(nc.semaphore(nm))
                self.cnt[nm] = 0
                self.slots[e].append(nm)
        self.seen = {e: {} for e in ENG}
        self.lastw = {}
        self.readers = {}
        self.nins = {e: 0 for e in ENG}

    def sb(self, name, shape, dtype):
        self.uid = getattr(self, 'uid', 0) + 1
        if not hasattr(self, 'names'):
            self.names = {}
        self.names.setdefault(name, []).append('%s_u%d' % (name, self.uid))
        return self.ctx.enter_context(self.nc.sbuf_tensor('%s_u%d' % (name, self.uid), list(shape), dtype))

    def ps(self, name, shape, dtype):
        return self.ctx.enter_context(self.nc.psum_tensor(name, list(shape), dtype))

    def _wait(self, eng, s, v):
        if self.mute:
            return
        if self.seen[eng].get(s, 0) < v:
            self.seen[eng][s] = v
            sem = self.sems[s]
            self.q[eng].append(lambda e, sem=sem, v=v: e.wait_ge(sem, v))

    def _wait_deps(self, eng, reads, writes):
        deps = {}
        for k in list(reads) + list(writes):
            ev = self.lastw.get(k)
            if ev is not None and deps.get(ev[0], 0) < ev[1]:
                deps[ev[0]] = ev[1]
        for k in writes:
            for s, v in self.readers.get(k, {}).items():
                if deps.get(s, 0) < v:
                    deps[s] = v
        for s, v in deps.items():
            self._wait(eng, s, v)

    def _record(self, ev, reads, writes):
        s, v = ev
        ws = set(writes)
        for k in ws:
            self.lastw[k] = ev
            self.readers[k] = {}
        for k in reads:
            if k in ws:
                continue
            r = self.readers.setdefault(k, {})
            if r.get(s, 0) < v:
                r[s] = v

    mute = False

    def op(self, eng, fn, reads=(), writes=()):
        if self.mute:
            return
        self._wait_deps(eng, reads, writes)
        self.cnt[eng] += 1
        v = self.cnt[eng]
        sem = self.sems[eng]
        self.q[eng].append(lambda e, fn=fn, sem=sem: fn(e).then_inc(sem, 1))
        self.nins[eng] += 1
        self._record((eng, v), reads, writes)

    def dma(self, eng, out, in_, reads=(), writes=(), indirect=None, **kw):
        if self.mute:
            return
        self._wait_deps(eng, reads, writes)
        sl = self.slots[eng]
        slot = sl[self.rr[eng] % len(sl)]
        self.rr[eng] += 1
        if self.cnt[slot] > 0 and not kw.pop('noslotwait', False):
            self._wait(eng, slot, self.cnt[slot])
        kw.pop('noslotwait', None)
        self.cnt[slot] += 16
        v = self.cnt[slot]
        sem = self.sems[slot]
        if indirect is None:
            self.q[eng].append(lambda e, out=out, in_=in_, sem=sem, kw=kw:
                               e.dma_start(out=out, in_=in_, **kw).then_inc(sem, 16))
        else:
            self.q[eng].append(lambda e, out=out, in_=in_, sem=sem, ind=indirect, kw=kw:
                               e.indirect_dma_start(out=out, out_offset=None, in_=in_, in_offset=ind, **kw).then_inc(sem, 16))
        self.nins[eng] += 1
        self._record((slot, v), reads, writes)

    def finish(self, out_keys):
        self._wait_deps('sync', out_keys, [])
        q = self.q
        with self.nc.Block() as block:
            @block.sync
            def _(e):
                for f in q['sync']:
                    f(e)

            @block.scalar
            def _(e):
                for f in q['act']:
                    f(e)

            @block.gpsimd
            def _(e):
                for f in q['pool']:
                    f(e)

            @block.tensor
            def _(e):
                for f in q['pe']:
                    f(e)

            @block.vector
            def _(e):
                for f in q['dve']:
                    f(e)


import os
PIPE = 1
JF = 1
T = 2048
D = 1024
NT = 16
EPS = 1e-6
TWO_PI = float(2 * np.pi)
CW1 = 6.28125
CW2 = 0.0019353071795864769
GK = 1.5957691216057308

CSTW = 384 + 32 + 512 + 2048
NSMALL = 16 + 24 + 128 * 4 + 2 + 2 + 2 + 8 + 2 + 2 + 2 + 2 + 8 + 8
O_LBL = 0
O_ARS = 16
O_AIS = 24
O_LDS = 32
O_ARR = 40
O_AIR = 168
O_BTR = 296
O_BTI = 424
O_LDR = 552
O_S5D = 554
O_BGL = 556
O_CW = 558
O_CB = 566
O_BA = 568
O_BX = 570
O_LAM = 572
O_GBR = 574
O_RMK = 582


class _Stop(Exception):
    pass


def build(n_layers=4, dbg=None, do_peer=True, stop=None):
    nc = bass.Bass("TRN2", target_bir_lowering=False)

    def din(name, shape, dt=F32):
        return nc.dram_tensor(name, list(shape), dt, kind="ExternalInput").ap()

    x_d = din("x", [T, D])
    c_d = din("c", [128, 8])
    wmod_d = din("w_mod", [4, D, 6 * D])
    bmod_d = din("b_mod", [4, 6 * D])
    gmix_d = din("g_mix", [4, D])
    win_d = din("w_in", [4, D, 2816])
    small_d = din("small", [4, 128, NSMALL])
    crt_d = din("crt", [4, 128, 8, 128])
    cit_d = din("cit", [4, 128, 8, 128])
    wglu_d = din("w_glu", [4, 256, 256])
    wa_d = din("wa_bd", [4, 128, 2, 128])
    wx_d = din("wx_bd", [4, 128, 2, 128])
    gbrow_d = din("g_branch", [4, D])
    wout_d = din("w_out", [4, D, D])
    gffn_d = din("g_ffn", [4, D])
    wq_d = din("peer_w_q", [4, D, 2048])
    skt_d = din("skt", [4, 128, 2, 128])
    pu_d = [din("peer_u%d" % l_, [16384, D]) for l_ in range(4)]
    pv_d = [din("peer_v%d" % l_, [16384, D]) for l_ in range(4)]
    gfin_d = din("g_final", [1, D])
    consts_d = din("consts", [128, CSTW])
    out_d = nc.dram_tensor("out", [T, D], F32, kind="ExternalOutput").ap()
    ub_d = [nc.dram_tensor("ub_scr%d" % l_, [16384, D], BF16, kind="Internal").ap() for l_ in range(4)]
    vb_d = [nc.dram_tensor("vb_scr%d" % l_, [16384, D], BF16, kind="Internal").ap() for l_ in range(4)]

    with ExitStack() as ctx:
        fw = FW(nc, ctx, slots={'sync': 8, 'act': 2, 'pool': 20})
        build.fw = fw

        def TS(eng, out, in0, s1, s2, op0, op1, r, w):
            fw.op(eng, lambda e: e.tensor_scalar(out=out, in0=in0, scalar1=s1, scalar2=s2, op0=op0, op1=op1), r, w)

        def TT(eng, out, in0, in1, op, r, w):
            fw.op(eng, lambda e: e.tensor_tensor(out=out, in0=in0, in1=in1, op=op), r, w)

        def STT(out, in0, scalar, in1, op0, op1, r, w):
            fw.op('dve', lambda e: e.scalar_tensor_tensor(out=out, in0=in0, scalar=scalar, in1=in1, op0=op0, op1=op1), r, w)

        def ACT(out, in_, func, r, w, scale=1.0, bias=0.0, accum=None):
            r = list(r) + ['cst', 'sm']
            if accum is None:
                fw.op('act', lambda e: e.activation(out=out, in_=in_, func=func, scale=scale, bias=bias), r, w)
            else:
                fw.op('act', lambda e: e.activation(out=out, in_=in_, func=func, scale=scale, bias=bias, accum_out=accum), r, w)

        def CP(eng, out, in_, r, w):
            if eng == 'act':
                fw.op(eng, lambda e: e.activation(out=out, in_=in_, func=AF.Copy), r, w)
            else:
                fw.op(eng, lambda e: e.tensor_copy(out=out, in_=in_), r, w)

        def SCAN(out, d0, d1, init, r, w):
            fw.op('dve', lambda e: e.tensor_tensor_scan(out=out, data0=d0, data1=d1, initial=init, op0=ALU.mult, op1=ALU.add), r, w)

        def MMG(mms, r, w):
            def f(e, mms=mms):
                ins = None
                for mm in mms:
                    (o, l, rh, st, sp) = mm[:5]
                    if len(mm) > 5:
                        ins = e.matmul(o, l, rh, start=st, stop=sp, skip_group_check=True)
                    else:
                        ins = e.matmul(o, l, rh, start=st, stop=sp)
                return ins
            fw.op('pe', f, r, w)

        def TRG(trs, ident, r, w):
            def f(e, trs=trs):
                ins = None
                for (o, i) in trs:
                    ins = e.transpose(o, i, ident)
                return ins
            fw.op('pe', f, r, w)

        def barrier():
            allsems = list(fw.cnt.items())
            for e in ENG:
                for s, v in allsems:
                    if v > 0:
                        fw._wait(e, s, v)

        xs = fw.sb("xs", [128, NT, D], F32)
        cst = fw.sb("cst", [128, 928], F32)
        identf = cst[:, 0:128]
        onesf = cst[:, 128:256]
        maskbd = cst[:, 256:384]
        halfpi = cst[:, 384:385]
        epscol = cst[:, 385:386]
        onecol = cst[:, 386:387]
        io16 = cst[:, 400:416]
        tau = cst[:, 416:416 + 512]
        rmask_t = fw.sb("rmask", [128, 2048], BF16)
        rmask = rmask_t[:]
        identb = fw.sb("identb", [128, 128], BF16)
        condr = fw.sb("condr", [128, 8, 128], F32)
        cond = fw.sb("cond", [128, 8], F32)
        PA = fw.ps("PA", [128, 2048], F32)
        PB = fw.ps("PB", [128, 2048], F32)
        PAk = ['PA0', 'PA1', 'PA2', 'PA3']
        PBk = ['PB0', 'PB1', 'PB2', 'PB3']

        fw.dma('sync', cst[:, 0:928], consts_d[:, 0:928], writes=['cst'])
        fw.dma('pool', rmask_t[:], consts_d[:, 928:928 + 2048], writes=['cst'])
        for i in range(NT):
            fw.dma('sync', xs[:, i, :], x_d[i * 128:(i + 1) * 128, :], writes=['x%d' % i])
        fw.dma('sync', cond[:], c_d[:, :], writes=['cond'])
        CP('dve', identb[:], identf, ['cst'], ['identb'])
        ACT(cond[:], cond[:], AF.Silu, ['cond'], ['cond'])
        CP('dve', condr[:], cond[:, :].unsqueeze(2).to_broadcast([128, 8, 128]), ['cond'], ['condr'])

        def mod_tile(l, j, out_tile, okey, stg, bm):
            wv = wmod_d[l].rearrange("(p kc) n -> p kc n", kc=8)
            fw.dma('sync', bm[:], bmod_d[l:l + 1, j * 1024:(j + 1) * 1024].to_broadcast([128, 1024]), writes=['bm'])
            for half in range(2):
                c0 = j * 1024 + half * 512
                fw.dma('sync', stg[:], wv[:, :, c0:c0 + 512], writes=['stg'])
                MMG([(PA[:, 0:512], condr[:, kc, :], stg[:, kc, :], kc == 0, kc == 7) for kc in range(8)],
                    ['condr', 'stg'], ['PA0'])
                TT('dve', out_tile[:, half * 512:(half + 1) * 512], PA[:, 0:512], bm[:, half * 512:(half + 1) * 512], ALU.add,
                   ['PA0', 'bm'], [okey])

        def gelu_inplace(eng_t, t, tmp, key, tkey):
            ACT(tmp, t, AF.Square, [key], [tkey])
            TS('dve', tmp, tmp, 0.044715, 1.0, ALU.mult, ALU.add, [tkey], [tkey])
            TT('dve', tmp, tmp, t, ALU.mult, [tkey, key], [tkey])
            ACT(tmp, tmp, AF.Sigmoid, [tkey], [tkey], scale=GK)
            TT('dve', t, t, tmp, ALU.mult, [key, tkey], [key])

        def rstd_from_ss(ss_ap, n, key):
            ACT(ss_ap, ss_ap, AF.Sqrt, [key], [key], scale=1.0 / n, bias=epscol)
            fw.op('dve', lambda e: e.reciprocal(out=ss_ap, in_=ss_ap), [key], [key])

        def norm_to_T(A, B, hT, hkeys, keep=None):
            pass

        for l in range(n_layers):
          try:
              with ExitStack() as actx:
                  octx = fw.ctx
                  fw.ctx = actx
                  hT = fw.sb("hT", [128, 8, T], BF16)
                  yT = fw.sb("yT", [128, 8, T], BF16)
                  sm = fw.sb("sm", [128, NSMALL], F32)
                  rsd = fw.sb("rsd", [128, 3, NT], F32)
                  fw.dma('sync', sm[:], small_d[l], writes=['sm'])
                  if do_peer:
                      for (src, dst, key) in [(pu_d[l], ub_d[l], 'ubd'), (pv_d[l], vb_d[l], 'vbd')]:
                          for c in range(16):
                              fw.dma('pool', dst[c * 1024:(c + 1) * 1024, :].rearrange("(p r) d -> p r d", p=128),
                                     src[c * 1024:(c + 1) * 1024, :].rearrange("(p r) d -> p r d", p=128), writes=[key + str(c)])
                  with ExitStack() as sctx:
                      fw.ctx = sctx
                      stg = fw.sb("stg", [128, 8, 512], F32)
                      bm = fw.sb("bm", [128, 1024], F32)
                      A1 = fw.sb("A1", [128, 1024], F32)
                      B1 = fw.sb("B1", [128, 1024], F32)
                      gm = fw.sb("gm", [128, 1024], F32)
                      ss = fw.sb("ss", [128, NT], F32)
                      junk = fw.sb("junk", [128, 1024], BF16)
                      tmpf = fw.sb("tmpf", [128, 1024], F32)
                      hb = fw.sb("hb", [128, 1024], BF16)
                      mod_tile(l, 0, B1, 'B1', stg, bm)
                      mod_tile(l, 1, A1, 'A1', stg, bm)
                      fw.dma('sync', gm[:], gmix_d[l:l + 1, :].to_broadcast([128, 1024]), writes=['gm'])
                      STT(A1[:], A1[:], 1.0, gm[:], ALU.add, ALU.mult, ['A1', 'gm'], ['A1'])
                      for i in range(NT):
                          ACT(junk[:], xs[:, i, :], AF.Square, ['x%d' % i], ['junk', 'ss'], accum=ss[:, i:i + 1])
                      rstd_from_ss(ss[:], float(D), 'ss')
                      PAb = PA[:, 0:512].bitcast(BF16)
                      for i in range(NT):
                          STT(tmpf[:], xs[:, i, :], ss[:, i:i + 1], A1[:], ALU.mult, ALU.mult, ['x%d' % i, 'ss', 'A1'], ['tmpf'])
                          TT('dve', hb[:], tmpf[:], B1[:], ALU.add, ['tmpf', 'B1'], ['hb'])
                          TRG([(PAb[:, kc * 128:(kc + 1) * 128], hb[:, kc * 128:(kc + 1) * 128]) for kc in range(8)],
                              identb[:], ['hb', 'identb'], ['PA0'])
                          CP('act', hT[:, :, i * 128:(i + 1) * 128], PAb.rearrange("p (k t) -> p k t", k=8), ['PA0'], ['hT'])
                      barrier()
                  fw.ctx = actx
                  if stop == 'norm':
                      fw.mute = True
                  lbt = fw.sb("lbt", [128, 16], F32)
                  lbz = fw.sb("lbz", [128, 4], F32)
                  lb = fw.sb("lb", [128, 4], F32)
                  oml = fw.sb("oml", [128, 4], F32)
                  ACT(lbt[:], sm[:, O_LBL:O_LBL + 16], AF.Exp, ['sm'], ['lbt'])
                  lbv = lbt[:].rearrange("p (h l) -> p h l", h=4)
                  fw.op('dve', lambda e: e.tensor_reduce(out=lbz[:], in_=lbv, op=ALU.add, axis=AX.X), ['lbt'], ['lbz'])
                  fw.op('dve', lambda e: e.reciprocal(out=lbz[:], in_=lbz[:]), ['lbz'], ['lbz'])
                  fw.op('dve', lambda e: e.memset(lb[:], 0.0), [], ['lb'])
                  for j in range(1, l + 1):
                      TT('dve', lb[:], lb[:], lbv[:, :, j], ALU.add, ['lb', 'lbt'], ['lb'])
                  TT('dve', lb[:], lb[:], lbz[:], ALU.mult, ['lb', 'lbz'], ['lb'])
                  TS('dve', oml[:], lb[:], -1.0, 1.0, ALU.mult, ALU.add, ['lb'], ['oml'])
                  winv = win_d[l].rearrange("(kc p) n -> p kc n", p=128)

                  with ExitStack() as sctx:
                      fw.ctx = sctx
                      wh = fw.sb("wh", [128, 8, 512], BF16)
                      t1 = fw.sb("t1", [128, T], F32)
                      t2 = fw.sb("t2", [128, T], F32)
                      t3 = fw.sb("t3", [128, T], F32)
                      kt = fw.sb("kt", [128, T], BF16)
                      kh = fw.sb("kh", [128, T], BF16)
                      qt = fw.sb("qt", [128, T], BF16)
                      khk = fw.sb("khk", [128, NT, 128], BF16)
                      khkz = fw.sb("khkz", [128, NT, 128], BF16)
                      qz = fw.sb("qz", [128, T], BF16)
                      fw.op('dve', lambda e: e.memset(qz[:], 0.0), [], ['qz'])
                      zer = fw.sb("zer", [128, 128], BF16)
                      fw.op('dve', lambda e: e.memset(zer[:], 0.0), [], ['zer'])
                      el = fw.sb("el", [128, 64], F32)
                      S = fw.sb("S", [128, 128], F32)
                      Sb = fw.sb("Sb", [128, 128], BF16)
                      vb = fw.sb("vb", [128, 128], BF16)
                      gs = fw.sb("gs", [128, 128], F32)
                      scm = fw.sb("scm", [128, 128], BF16)
                      yh = fw.sb("yh", [128, 128], F32)
                      yhb = fw.sb("yhb", [128, 128], BF16)
                      jk = fw.sb("jk", [128, 128], F32)
                      ssh = fw.sb("ssh", [128, 2], F32)
                      ssa = fw.sb("ssa", [128, 4, NT], F32)
                      gbb = fw.sb("gbb", [128, 512], F32)
                      fw.dma('sync', gbb[:], gbrow_d[l:l + 1, 0:512].to_broadcast([128, 512]), writes=['gbb'])
                      PAb = PA[:, 0:1024].bitcast(BF16)
                      for h in range(4):
                          for j, c0 in enumerate([h * 128, 512 + h * 128, 1024 + h * 128, 1536 + h * 128]):
                              fw.dma('pool', wh[:, :, j * 128:(j + 1) * 128], winv[:, :, c0:c0 + 128], writes=['wh'])
                          for tb in range(4):
                              MMG([(PA[:, tb * 512:(tb + 1) * 512], wh[:, kc, 128:256], hT[:, kc, tb * 512:(tb + 1) * 512], kc == 0, kc == 7)
                                   for kc in range(8)], ['wh', 'hT'], [PAk[tb]])
                              ACT(t1[:, tb * 512:(tb + 1) * 512], PA[:, tb * 512:(tb + 1) * 512], AF.Sigmoid, [PAk[tb]], ['t1'])
                          TS('dve', t1[:], t1[:], oml[:, h:h + 1], lb[:, h:h + 1], ALU.mult, ALU.add, ['t1', 'oml', 'lb'], ['t1'])
                          if stop == 'hg_a':
                              fw.mute = True
                          ACT(t2[:], t1[:], AF.Ln, ['t1'], ['t2'])
                          SCAN(t3[:], rmask, t2[:], 0.0, ['cst', 't2'], ['t3'])
                          if stop == 'hg_b':
                              fw.mute = True
                          TS('dve', t1[:], t1[:], -1.0, 1.0, ALU.mult, ALU.add, ['t1'], ['t1'])
                          ACT(t2[:], t3[:], AF.Exp, ['t3'], ['t2'], scale=-1.0)
                          TT('dve', kt[:], t1[:], t2[:], ALU.mult, ['t1', 't2'], ['kt'])
                          b3 = t3[:].rearrange("p (c s) -> p c s", s=32)
                          TT('dve', t2[:].rearrange("p (c s) -> p c s", s=32), b3[:, :, 31:32].to_broadcast([128, 64, 32]), b3, ALU.subtract,
                             ['t3'], ['t2'])
                          ACT(t2[:], t2[:], AF.Exp, ['t2'], ['t2'])
                          TT('dve', kh[:], t1[:], t2[:], ALU.mult, ['t1', 't2'], ['kh'])
                          ACT(t3[:], t3[:], AF.Exp, ['t3'], ['t3'])
                          CP('dve', el[:], t3[:].rearrange("p (c s) -> p c s", s=32)[:, :, 31], ['t3'], ['el'])
                          if stop == 'hg_c':
                              fw.mute = True
                          for tb in range(4):
                              MMG([(PB[:, tb * 512:(tb + 1) * 512], wh[:, kc, 0:128], hT[:, kc, tb * 512:(tb + 1) * 512], kc == 0, kc == 7)
                                   for kc in range(8)], ['wh', 'hT'], [PBk[tb]])
                              ACT(t1[:, tb * 512:(tb + 1) * 512], PB[:, tb * 512:(tb + 1) * 512], AF.Silu, [PBk[tb]], ['t1'])
                          TT('dve', qt[:], t1[:], t3[:], ALU.mult, ['t1', 't3'], ['qt'])
                          if stop == 'hg_c1':
                              fw.mute = True
                          CP('dve', qz[:].rearrange("p (i t) -> p i t", t=128)[:, :, 96:128], qt[:].rearrange("p (i t) -> p i t", t=128)[:, :, 96:128],
                             ['qt'], ['qz'])
                          if stop == 'hg_c2':
                              fw.mute = True
                          for half in range(2):
                              TRG([(PAb[:, j * 128:(j + 1) * 128], kh[:, (half * 8 + j) * 128:(half * 8 + j + 1) * 128]) for j in range(8)],
                                  identb[:], ['kh', 'identb'], ['PA0'])
                              CP('act', khk[:, half * 8:(half + 1) * 8, :], PAb[:, 0:1024].rearrange("p (j d) -> p j d", j=8), ['PA0'], ['khk'])
                              if stop == 'hg_c3':
                                  fw.mute = True
                              CP('act', khkz[64:128, half * 8:(half + 1) * 8, :], PAb[64:128, 0:1024].rearrange("p (j d) -> p j d", j=8), ['PA0'], ['khkz'])
                              if stop == 'hg_c4':
                                  fw.mute = True
                              fw.op('dve', lambda e, half=half: e.memset(khkz[64:96, half * 8:(half + 1) * 8, :], 0.0), [], ['khkz'])
                              if stop == 'hg_c5':
                                  fw.mute = True
                          fw.op('dve', lambda e: e.memset(S[:], 0.0), [], ['S'])
                          if stop == 'hg_c7':
                              fw.mute = True
                          fw.op('dve', lambda e: e.memset(Sb[:], 0.0), [], ['Sb'])
                          if stop == 'hg_d':
                              fw.mute = True
                          for i in range(NT):
                              tsl = slice(i * 128, (i + 1) * 128)
                              MMG([(PB[:, 0:256], hT[:, kc, tsl], wh[:, kc, 256:512], kc == 0, kc == 7) for kc in range(8)],
                                  ['hT', 'wh'], ['PB0'])
                              CP('act', vb[:], PB[:, 0:128], ['PB0'], ['vb'])
                              ACT(gs[:], PB[:, 128:256], AF.Silu, ['PB0'], ['gs'])
                              MMG([(PB[:, 512:640], kt[:, tsl], qt[:, tsl], True, True)], ['kt', 'qt'], ['PB1'])
                              TT('dve', scm[:], PB[:, 512:640], maskbd, ALU.mult, ['PB1', 'cst'], ['scm'])
                              if stop == 'hg_e':
                                  fw.mute = True
                              MMG([(PB[:, 1024:1152], scm[:], vb[:], True, False)], ['scm', 'vb'], ['PB2'])
                              for j in range(4):
                                  rs_ = slice(32 * j, 32 * j + 32)
                                  if j < 3:
                                      MMG([(PB[rs_, 1024:1152], qt[:, i * 128 + 32 * j:i * 128 + 32 * j + 32], Sb[:], False, False, 1)],
                                          ['qt', 'Sb'], ['PB2'])
                                      MMG([(PB[:, 1536:1664], khk[rs_, i, :], vb[rs_, :], True, True)], ['khk', 'vb'], ['PB3'])
                                  else:
                                      MMG([(PB[64:128, 1024:1152], qz[:, i * 128 + 64:i * 128 + 128], Sb[:], False, False, 1)],
                                          ['qz', 'Sb'], ['PB2'])
                                      MMG([(PB[:, 1536:1664], khkz[64:128, i, :], vb[64:128, :], True, True)], ['khkz', 'vb'], ['PB3'])
                                  STT(Sb[:], S[:], el[:, 4 * i + j:4 * i + j + 1], PB[:, 1536:1664], ALU.mult, ALU.add, ['S', 'el', 'PB3'], ['Sb'])
                                  STT(S[:], S[:], el[:, 4 * i + j:4 * i + j + 1], PB[:, 1536:1664], ALU.mult, ALU.add, ['S', 'el', 'PB3'], ['S'])
                              MMG([(PB[:, 1024:1152], zer[:], vb[:], False, True)], ['zer', 'vb'], ['PB2'])
                              if stop == 'hg_f':
                                  fw.mute = True
                              ACT(jk[:], PB[:, 1024:1152], AF.Square, ['PB2'], ['jk', 'ssh'], accum=ssh[:, 0:1])
                              rstd_from_ss(ssh[:, 0:1], 128.0, 'ssh')
                              STT(yh[:], PB[:, 1024:1152], ssh[:, 0:1], gs[:], ALU.mult, ALU.mult, ['PB2', 'ssh', 'gs'], ['yh'])
                              ACT(jk[:], yh[:], AF.Square, ['yh'], ['jk', 'ssa'], accum=ssa[:, h, i:i + 1])
                              TT('dve', yhb[:], yh[:], gbb[:, h * 128:(h + 1) * 128], ALU.mult, ['yh', 'gbb'], ['yhb'])
                              TRG([(PAb[:, 1024:1152], yhb[:])], identb[:], ['yhb', 'identb'], ['PA1'])
                              CP('act', yT[:, h, tsl], PAb[:, 1024:1152], ['PA1'], ['yT%d' % h])
                      TT('dve', ssa[:, 0, :], ssa[:, 0, :], ssa[:, 1, :], ALU.add, ['ssa'], ['ssa'])
                      TT('dve', ssa[:, 2, :], ssa[:, 2, :], ssa[:, 3, :], ALU.add, ['ssa'], ['ssa'])
                      TT('dve', rsd[:, 0, :], ssa[:, 0, :], ssa[:, 2, :], ALU.add, ['ssa'], ['rsd0'])
                      rstd_from_ss(rsd[:, 0, :], 512.0, 'rsd0')
                      barrier()
                  fw.ctx = actx
                  if stop == 'hgrn':
                      fw.mute = True

                  with ExitStack() as sctx:
                      fw.ctx = sctx
                      wl = fw.sb("wl", [128, 8, 256], BF16)
                      wab = fw.sb("wab", [128, 2, 128], BF16)
                      wxb = fw.sb("wxb", [128, 2, 128], BF16)
                      xraw = fw.sb("xraw", [128, 3 + T], F32)
                      xc = fw.sb("xc", [128, T], F32)
                      xcb = fw.sb("xcb", [128, T], BF16)
                      ta = fw.sb("ta", [128, T], F32)
                      tb_ = fw.sb("tb_", [128, T], F32)
                      tr = fw.sb("tr", [128, T], F32)
                      ti = fw.sb("ti", [128, T], F32)
                      c8 = fw.sb("c8", [128, 2], F32)
                      c16 = fw.sb("c16", [128, 2], F32)
                      ssc = fw.sb("ssc", [128, 2, NT], F32)
                      fw.dma('pool', wab[:], wa_d[l], writes=['wab'])
                      fw.dma('pool', wxb[:], wx_d[l], writes=['wxb'])
                      ACT(c8[:], sm[:, O_LAM:O_LAM + 2], AF.Exp, ['sm'], ['c8'], scale=-1.0)
                      ACT(c8[:], c8[:], AF.Ln, ['c8'], ['c8'], bias=onecol)
                      TS('dve', c16[:], c8[:], -16.0, None, ALU.mult, ALU.bypass, ['c8'], ['c16'])
                      TS('dve', c8[:], c8[:], -8.0, None, ALU.mult, ALU.bypass, ['c8'], ['c8'])
                      fw.op('dve', lambda e: e.memset(xraw[:, 0:3], 0.0), [], ['xraw'])
                      for hc in range(2):
                          for j, c0 in enumerate([2304 + hc * 128, 2560 + hc * 128]):
                              fw.dma('pool', wl[:, :, j * 128:(j + 1) * 128], winv[:, :, c0:c0 + 128], writes=['wl'])
                          for tb in range(4):
                              MMG([(PA[:, tb * 512:(tb + 1) * 512], wl[:, kc, 0:128], hT[:, kc, tb * 512:(tb + 1) * 512], kc == 0, kc == 7)
                                   for kc in range(8)], ['wl', 'hT'], [PAk[tb]])
                              CP('act', xraw[:, 3 + tb * 512:3 + (tb + 1) * 512], PA[:, tb * 512:(tb + 1) * 512], [PAk[tb]], ['xraw'])
                          cw = sm[:, O_CW + hc * 4:O_CW + hc * 4 + 4]
                          TS('dve', xc[:], xraw[:, 3:3 + T], cw[:, 3:4], sm[:, O_CB + hc:O_CB + hc + 1], ALU.mult, ALU.add, ['xraw', 'sm'], ['xc'])
                          for w_ in range(3):
                              STT(xc[:], xraw[:, w_:w_ + T], cw[:, w_:w_ + 1], xc[:], ALU.mult, ALU.add, ['xraw', 'sm', 'xc'], ['xc'])
                          CP('act', xcb[:], xc[:], ['xc'], ['xcb'])
                          for tb in range(4):
                              sl = slice(tb * 512, (tb + 1) * 512)
                              MMG([(PB[:, sl], wab[:, hc, :], xcb[:, sl], True, True)], ['wab', 'xcb'], [PBk[tb]])
                              ACT(tr[:, sl], PB[:, sl], AF.Sigmoid, [PBk[tb]], ['tr'], bias=sm[:, O_BA + hc:O_BA + hc + 1])
                          for tb in range(4):
                              sl = slice(tb * 512, (tb + 1) * 512)
                              MMG([(PA[:, sl], wxb[:, hc, :], xcb[:, sl], True, True)], ['wxb', 'xcb'], [PAk[tb]])
                              ACT(ti[:, sl], PA[:, sl], AF.Sigmoid, [PAk[tb]], ['ti'], bias=sm[:, O_BX + hc:O_BX + hc + 1])
                          ACT(ta[:], tr[:], AF.Exp, ['tr', 'c8'], ['ta'], scale=c8[:, hc:hc + 1])
                          ACT(tb_[:], tr[:], AF.Exp, ['tr', 'c16'], ['tb_'], scale=c16[:, hc:hc + 1])
                          TS('dve', tb_[:], tb_[:], -1.0, 1.0, ALU.mult, ALU.add, ['tb_'], ['tb_'])
                          ACT(tb_[:], tb_[:], AF.Sqrt, ['tb_'], ['tb_'])
                          TT('dve', tb_[:], tb_[:], ti[:], ALU.mult, ['tb_', 'ti'], ['tb_'])
                          TT('dve', tb_[:], tb_[:], xc[:], ALU.mult, ['tb_', 'xc'], ['tb_'])
                          SCAN(tr[:], ta[:], tb_[:], 0.0, ['ta', 'tb_'], ['tr'])
                          for tb in range(4):
                              sl = slice(tb * 512, (tb + 1) * 512)
                              MMG([(PB[:, sl], wl[:, kc, 128:256], hT[:, kc, sl], kc == 0, kc == 7) for kc in range(8)],
                                  ['wl', 'hT'], [PBk[tb]])
                              CP('act', ti[:, sl], PB[:, sl], [PBk[tb]], ['ti'])
                          gelu_inplace('dve', ti[:], ta[:], 'ti', 'ta')
                          TT('dve', tb_[:], tr[:], ti[:], ALU.mult, ['tr', 'ti'], ['tb_'])
                          TS('dve', yT[:, 6 + hc, :], tb_[:], sm[:, O_GBR + 6 + hc:O_GBR + 7 + hc], None, ALU.mult, ALU.bypass,
                             ['tb_', 'sm'], ['yT%d' % (6 + hc)])
                          ACT(ta[:], tb_[:], AF.Square, ['tb_'], ['ta'])
                          MMG([(PA[:, 2 * i:2 * i + 2], ta[:, i * 128:(i + 1) * 128], onesf[:, 0:2], True, True) for i in range(NT)],
                              ['ta', 'cst'], ['PA0'])
                          CP('dve', ssc[:, hc, :], PA[:, 0:2 * NT].rearrange("p (i two) -> p i two", two=2)[:, :, 0], ['PA0'], ['ssc'])
                      TT('dve', rsd[:, 2, :], ssc[:, 0, :], ssc[:, 1, :], ALU.add, ['ssc'], ['rsd2'])
                      rstd_from_ss(rsd[:, 2, :], 256.0, 'rsd2')
                      barrier()
                  fw.ctx = actx
                  if stop == 'lru':
                      fw.mute = True

                  with ExitStack() as sctx:
                      fw.ctx = sctx
                      LP = 512
                      NP_ = T // LP
                      ws5 = fw.sb("ws5", [128, 8, 256], BF16)
                      crt = fw.sb("crt", [128, 8, 128], BF16)
                      cit = fw.sb("cit", [128, 8, 128], BF16)
                      wglu = fw.sb("wglu", [128, 2, 256], BF16)
                      ub = fw.sb("ub", [128, 2, T], BF16)
                      zb = fw.sb("zb", [128, 2, T], BF16)
                      lhb = fw.sb("lhb", [128, 8, 2, 128], BF16)
                      pst = fw.sb("pst", [128, 5, 8], F32)
                      prp = fw.sb("prp", [128, 12, 128], F32)
                      pri = fw.sb("pri", [128, 128], I32)
                      car = fw.sb("car", [128, 8, 2], F32)
                      tcos = fw.sb("tcos", [128, LP], F32)
                      tsin = fw.sb("tsin", [128, LP], F32)
                      tki = fw.sb("tki", [128, LP], I32)
                      s1 = fw.sb("s1", [128, LP], F32)
                      s2 = fw.sb("s2", [128, LP], F32)
                      swr = fw.sb("swr", [128, LP], F32)
                      swi = fw.sb("swi", [128, LP], F32)
                      szr = fw.sb("szr", [128, LP], F32)
                      szi = fw.sb("szi", [128, LP], F32)
                      xrb = fw.sb("xrb", [128, LP], BF16)
                      xib = fw.sb("xib", [128, LP], BF16)
                      yb = fw.sb("yb", [128, 512], F32)
                      yb2 = fw.sb("yb2", [128, 512], F32)
                      ssb = fw.sb("ssb", [128, 2, NT], F32)
                      fw.dma('pool', crt[:], crt_d[l], writes=['crt'])
                      fw.dma('pool', cit[:], cit_d[l], writes=['cit'])
                      fw.dma('pool', wglu[:], wglu_d[l].rearrange("(kc p) n -> p kc n", p=128), writes=['wglu'])
                      fw.dma('pool', ws5[:], winv[:, :, 2048:2304], writes=['ws5'])

                      def sincos(ang, ki, sin_o, cos_o, tmp, keys):
                          ka, kk, ks, kc_, kt_ = keys
                          TS('dve', ki, ang, 1.0 / TWO_PI, None, ALU.mult, ALU.bypass, [ka], [kk])
                          STT(ang, ki, -CW1, ang, ALU.mult, ALU.add, [kk, ka], [ka])
                          STT(ang, ki, -CW2, ang, ALU.mult, ALU.add, [kk, ka], [ka])
                          TS('dve', ang, ang, -3.14159, 3.14159, ALU.max, ALU.min, [ka], [ka])
                          ACT(sin_o, ang, AF.Sin, [ka], [ks])
                          STT(tmp, ang, -1.0, ang, ALU.mult, ALU.max, [ka], [kt_])
                          ACT(cos_o, tmp, AF.Sin, [kt_, 'cst'], [kc_], scale=-1.0, bias=halfpi)

                      lamre, dts, rmag, theta = pst[:, 0, :], pst[:, 1, :], pst[:, 2, :], pst[:, 3, :]
                      TS('dve', lamre, sm[:, O_ARS:O_ARS + 8], -1e-4, None, ALU.min, ALU.bypass, ['sm'], ['pst'])
                      ACT(dts, sm[:, O_LDS:O_LDS + 8], AF.Exp, ['sm'], ['pst'])
                      TT('dve', rmag, lamre, dts, ALU.mult, ['pst'], ['pst'])
                      ACT(rmag, rmag, AF.Exp, ['pst'], ['pst'])
                      TT('dve', theta, sm[:, O_AIS:O_AIS + 8], dts, ALU.mult, ['sm', 'pst'], ['pst'])
                      R = lambda k: prp[:, k, :]
                      lam_r, lam_i = R(0), R(1)
                      TS('dve', lam_r, sm[:, O_ARR:O_ARR + 128], -1e-4, None, ALU.min, ALU.bypass, ['sm'], ['prp'])
                      CP('dve', lam_i, sm[:, O_AIR:O_AIR + 128], ['sm'], ['prp'])
                      ACT(prp[:, 11, 0:2], sm[:, O_LDR:O_LDR + 2], AF.Exp, ['sm'], ['prp'])
                      for hg in range(2):
                          cs = slice(hg * 64, (hg + 1) * 64)
                          TS('dve', prp[:, 2, cs], prp[:, 0, cs], prp[:, 11, hg:hg + 1], None, ALU.mult, ALU.bypass, ['prp'], ['prp'])
                          TS('dve', prp[:, 3, cs], prp[:, 1, cs], prp[:, 11, hg:hg + 1], None, ALU.mult, ALU.bypass, ['prp'], ['prp'])
                      ACT(R(2), R(2), AF.Exp, ['prp'], ['prp'])
                      sincos(R(3), pri[:], R(4), R(5), R(6), ['prp', 'pri', 'prp', 'prp', 'prp'])
                      TT('dve', R(5), R(5), R(2), ALU.mult, ['prp'], ['prp'])
                      TT('dve', R(4), R(4), R(2), ALU.mult, ['prp'], ['prp'])
                      TS('dve', R(5), R(5), -1.0, None, ALU.add, ALU.bypass, ['prp'], ['prp'])
                      TT('dve', R(2), lam_r, lam_r, ALU.mult, ['prp'], ['prp'])
                      TT('dve', R(3), lam_i, lam_i, ALU.mult, ['prp'], ['prp'])
                      TT('dve', R(2), R(2), R(3), ALU.add, ['prp'], ['prp'])
                      fw.op('dve', lambda e: e.reciprocal(out=R(2), in_=R(2)), ['prp'], ['prp'])
                      TT('dve', R(6), R(5), lam_r, ALU.mult, ['prp'], ['prp'])
                      TT('dve', R(7), R(4), lam_i, ALU.mult, ['prp'], ['prp'])
                      TT('dve', R(6), R(6), R(7), ALU.add, ['prp'], ['prp'])
                      TT('dve', R(6), R(6), R(2), ALU.mult, ['prp'], ['prp'])
                      TT('dve', R(7), R(4), lam_r, ALU.mult, ['prp'], ['prp'])
                      TT('dve', R(8), R(5), lam_i, ALU.mult, ['prp'], ['prp'])
                      TT('dve', R(7), R(7), R(8), ALU.subtract, ['prp'], ['prp'])
                      TT('dve', R(7), R(7), R(2), ALU.mult, ['prp'], ['prp'])
                      btr, bti = sm[:, O_BTR:O_BTR + 128], sm[:, O_BTI:O_BTI + 128]
                      TT('dve', R(8), R(6), btr, ALU.mult, ['prp', 'sm'], ['prp'])
                      TT('dve', R(9), R(7), bti, ALU.mult, ['prp', 'sm'], ['prp'])
                      TT('dve', R(8), R(8), R(9), ALU.subtract, ['prp'], ['prp'])
                      TT('dve', R(9), R(6), bti, ALU.mult, ['prp', 'sm'], ['prp'])
                      TT('dve', R(10), R(7), btr, ALU.mult, ['prp', 'sm'], ['prp'])
                      TT('dve', R(9), R(9), R(10), ALU.add, ['prp'], ['prp'])
                      for gp in range(8):
                          hc = gp // 4
                          for gi in range(2):
                              gl = (2 * gp + gi) % 8
                              for ri, src in enumerate([8, 9]):
                                  TS('dve', lhb[:, gp, ri, gi * 64:(gi + 1) * 64], prp[:, src, hc * 64:(hc + 1) * 64],
                                     sm[:, O_RMK + gl:O_RMK + gl + 1], None, ALU.mult, ALU.bypass, ['prp', 'sm'], ['lhb'])
                      for hc in range(2):
                          for tb in range(4):
                              sl = slice(tb * 512, (tb + 1) * 512)
                              MMG([(PA[:, sl], ws5[:, kc, hc * 128:(hc + 1) * 128], hT[:, kc, sl], kc == 0, kc == 7) for kc in range(8)],
                                  ['ws5', 'hT'], [PAk[tb]])
                              CP('act', ub[:, hc, sl], PA[:, sl], [PAk[tb]], ['ub'])
                      fw.op('dve', lambda e: e.memset(car[:], 0.0), [], ['car'])
                      for hc in range(2):
                          for pc in range(NP_):
                              psl = slice(pc * LP, (pc + 1) * LP)
                              for gq in range(4):
                                  gp = hc * 4 + gq
                                  if True:
                                      TS('dve', s1[:], tau[:, 0:LP], theta[:, gp:gp + 1], None, ALU.mult, ALU.bypass, ['cst', 'pst'], ['s1'])
                                      sincos(s1[:], tki[:], tsin[:], tcos[:], s2[:], ['s1', 'tki', 'tsin', 'tcos', 's2'])
                                  MMG([(PA[:, 0:LP], lhb[:, gp, 0, :], ub[:, hc, psl], True, True),
                                       (PA[:, 512:512 + LP], lhb[:, gp, 1, :], ub[:, hc, psl], True, True)], ['lhb', 'ub'], ['PA0', 'PA1'])
                                  bur, bui = PA[:, 0:LP], PA[:, 512:512 + LP]
                                  TT('dve', s1[:], bur, tcos[:], ALU.mult, ['PA0', 'tcos'], ['s1'])
                                  TT('dve', s2[:], bui, tsin[:], ALU.mult, ['PA1', 'tsin'], ['s2'])
                                  TT('dve', swr[:], s1[:], s2[:], ALU.add, ['s1', 's2'], ['swr'])
                                  TT('dve', s1[:], bui, tcos[:], ALU.mult, ['PA1', 'tcos'], ['s1'])
                                  TT('dve', s2[:], bur, tsin[:], ALU.mult, ['PA0', 'tsin'], ['s2'])
                                  TT('dve', swi[:], s1[:], s2[:], ALU.subtract, ['s1', 's2'], ['swi'])
                                  rb_ = rmag[:, gp:gp + 1].to_broadcast([128, LP])
                                  SCAN(szr[:], rb_, swr[:], car[:, gp, 0:1], ['pst', 'swr', 'car'], ['szr'])
                                  SCAN(szi[:], rb_, swi[:], car[:, gp, 1:2], ['pst', 'swi', 'car'], ['szi'])
                                  TT('dve', s1[:], szr[:], tcos[:], ALU.mult, ['szr', 'tcos'], ['s1'])
                                  TT('dve', s2[:], szi[:], tsin[:], ALU.mult, ['szi', 'tsin'], ['s2'])
                                  TT('dve', swr[:], s1[:], s2[:], ALU.subtract, ['s1', 's2'], ['swr'])
                                  TT('dve', s1[:], szr[:], tsin[:], ALU.mult, ['szr', 'tsin'], ['s1'])
                                  TT('dve', s2[:], szi[:], tcos[:], ALU.mult, ['szi', 'tcos'], ['s2'])
                                  TT('dve', swi[:], s1[:], s2[:], ALU.add, ['s1', 's2'], ['swi'])
                                  CP('act', car[:, gp, 0:1], swr[:, LP - 1:LP], ['swr'], ['car'])
                                  CP('act', car[:, gp, 1:2], swi[:, LP - 1:LP], ['swi'], ['car'])
                                  CP('act', xrb[:], swr[:], ['swr'], ['xrb'])
                                  ACT(xib[:], swi[:], AF.Copy, ['swi'], ['xib'], scale=-1.0)
                                  MMG([(PB[:, 0:LP], crt[:, gp, :], xrb[:], gq == 0, False),
                                       (PB[:, 0:LP], cit[:, gp, :], xib[:], False, gq == 3)], ['crt', 'cit', 'xrb', 'xib'], ['PB0'])
                              STT(yb[:], ub[:, hc, psl], sm[:, O_S5D + hc:O_S5D + hc + 1], PB[:, 0:LP], ALU.mult, ALU.add,
                                  ['ub', 'sm', 'PB0'], ['yb'])
                              gelu_inplace('dve', yb[:], yb2[:], 'yb', 'yb2')
                              CP('act', zb[:, hc, psl], yb[:], ['yb'], ['zb'])
                      for oc in range(2):
                          for tb in range(4):
                              sl = slice(tb * 512, (tb + 1) * 512)
                              MMG([(PA[:, sl], wglu[:, kc, oc * 128:(oc + 1) * 128], zb[:, kc, sl], kc == 0, kc == 1) for kc in range(2)],
                                  ['wglu', 'zb'], [PAk[tb]])
                              ACT(yb[:], PA[:, sl], AF.Sigmoid, [PAk[tb]], ['yb'], bias=sm[:, O_BGL + oc:O_BGL + oc + 1])
                              TT('dve', yb[:], yb[:], zb[:, oc, sl], ALU.mult, ['yb', 'zb'], ['yb'])
                              TS('dve', yT[:, 4 + oc, sl], yb[:], sm[:, O_GBR + 4 + oc:O_GBR + 5 + oc], None, ALU.mult, ALU.bypass,
                                 ['yb', 'sm'], ['yT%d' % (4 + oc)])
                              ACT(yb2[:], yb[:], AF.Square, ['yb'], ['yb2'])
                              MMG([(PB[:, 2 * (tb * 4 + j):2 * (tb * 4 + j) + 2], yb2[:, j * 128:(j + 1) * 128], onesf[:, 0:2], True, True)
                                   for j in range(4)], ['yb2', 'cst'], ['PB0'])
                          CP('dve', ssb[:, oc, :], PB[:, 0:2 * NT].rearrange("p (i two) -> p i two", two=2)[:, :, 0], ['PB0'], ['ssb'])
                      TT('dve', rsd[:, 1, :], ssb[:, 0, :], ssb[:, 1, :], ALU.add, ['ssb'], ['rsd1'])
                      rstd_from_ss(rsd[:, 1, :], 256.0, 'rsd1')
                      barrier()
                  fw.ctx = actx
                  if stop == 's5':
                      fw.mute = True

                  with ExitStack() as sctx:
                      fw.ctx = sctx
                      stg = fw.sb("stg", [128, 8, 512], F32)
                      bm = fw.sb("bm", [128, 1024], F32)
                      gt1 = fw.sb("gt1", [128, 1024], F32)
                      wo = fw.sb("wo", [128, 8, 1024], BF16)
                      tmpo = fw.sb("tmpo", [128, 1024], F32)
                      wov = wout_d[l].rearrange("(kc p) n -> p kc n", p=128)
                      for kc in range(8):
                          fw.dma('pool', wo[:, kc, :], wov[:, kc, :], writes=['wo'])
                      mod_tile(l, 2, gt1, 'gt1', stg, bm)
                      for i in range(NT):
                          tsl = slice(i * 128, (i + 1) * 128)
                          for (P_, off, kcs, keys) in [(PA, 0, [0, 1, 2, 3], ['PA0', 'PA1']), (PA, 1024, [4, 5], ['PA2', 'PA3']),
                                                       (PB, 0, [6, 7], ['PB0', 'PB1'])]:
                              for nh in range(2):
                                  MMG([(P_[:, off + nh * 512:off + (nh + 1) * 512], yT[:, kc, tsl], wo[:, kc, nh * 512:(nh + 1) * 512],
                                        kc == kcs[0], kc == kcs[-1]) for kc in kcs],
                                      ['yT%d' % kc for kc in kcs] + ['wo'], [keys[nh]])
                          TS('dve', tmpo[:], PA[:, 0:1024], rsd[:, 0, i:i + 1], None, ALU.mult, ALU.bypass, ['PA0', 'PA1', 'rsd0'], ['tmpo'])
                          STT(tmpo[:], PA[:, 1024:2048], rsd[:, 1, i:i + 1], tmpo[:], ALU.mult, ALU.add, ['PA2', 'PA3', 'rsd1', 'tmpo'], ['tmpo'])
                          STT(tmpo[:], PB[:, 0:1024], rsd[:, 2, i:i + 1], tmpo[:], ALU.mult, ALU.add, ['PB0', 'PB1', 'rsd2', 'tmpo'], ['tmpo'])
                          TT('dve', tmpo[:], tmpo[:], gt1[:], ALU.mult, ['tmpo', 'gt1'], ['tmpo'])
                          TT('dve', xs[:, i, :], xs[:, i, :], tmpo[:], ALU.add, ['x%d' % i, 'tmpo'], ['x%d' % i])
                      barrier()
                  fw.ctx = octx
              barrier()
              if not do_peer:
                  continue
              with ExitStack() as bctx:
                  octx = fw.ctx
                  fw.ctx = bctx
                  A2 = fw.sb("A2", [128, 1024], F32)
                  B2 = fw.sb("B2", [128, 1024], F32)
                  gt2 = fw.sb("gt2", [128, 1024], F32)
                  with ExitStack() as mctx:
                      fw.ctx = mctx
                      stg = fw.sb("stg", [128, 8, 512], F32)
                      bm = fw.sb("bm", [128, 1024], F32)
                      gm = fw.sb("gm", [128, 1024], F32)
                      mod_tile(l, 3, B2, 'B2', stg, bm)
                      mod_tile(l, 4, A2, 'A2', stg, bm)
                      mod_tile(l, 5, gt2, 'gt2', stg, bm)
                      fw.dma('sync', gm[:], gffn_d[l:l + 1, :].to_broadcast([128, 1024]), writes=['gm'])
                      STT(A2[:], A2[:], 1.0, gm[:], ALU.add, ALU.mult, ['A2', 'gm'], ['A2'])
                      barrier()
                  fw.ctx = bctx
                  NB = 16
                  ss = fw.sb("ss", [128, NT], F32)
                  junk = fw.sb("junk", [128, 1024], BF16)
                  wq = fw.sb("wq", [128, 8, 2048], BF16)
                  skt = fw.sb("skt", [128, 2, 128], BF16)
                  h2 = fw.sb("h2", [128, 1024], F32)
                  h2b = fw.sb("h2b", [128, 1024], BF16)
                  h2T = fw.sb("h2T", [128, 8, 128], BF16)
                  qTb = fw.sb("qTb", [128, 16, 128], BF16)
                  big = fw.sb("big", [128, 2048], F32)
                  sc = big[:].rearrange("p (a b) -> p a b", a=16)
                  cand = big[:].rearrange("p (h c) -> p h c", h=8)
                  eq = big[:].rearrange("p (h k a) -> p h k a", h=8, k=16)
                  top = fw.sb("top", [128, 16, 16], F32)
                  tix = fw.sb("tix", [128, 16, 16], U32)
                  tixf = fw.sb("tixf", [128, 16, 16], F32)
                  best = fw.sb("best", [128, 8, 16], F32)
                  pos = fw.sb("pos", [128, 8, 16], U32)
                  posf = fw.sb("posf", [128, 8, 16], F32)
                  ai = fw.sb("ai", [128, 8, 16], I32)
                  af = fw.sb("af", [128, 8, 16], F32)
                  bf = fw.sb("bf", [128, 8, 16], F32)
                  isel = fw.sb("isel", [128, 8, 16], F32)
                  jsel = fw.sb("jsel", [128, 8, 16], F32)
                  eidx2 = fw.sb("eidx", [128, 2, 128], I32)
                  gat = fw.sb("gat", [128, 8, 16], F32)
                  gz_ = fw.sb("gz_", [128, 8], F32)
                  actp = fw.sb("actp", [128, 128], F32)
                  actt = fw.sb("actt", [128, 128], F32)
                  wgt = fw.sb("wgt", [128, 128], F32)
                  acc = fw.sb("acc", [128, 1024], F32)
                  jf = fw.sb("jf", [128, 1024], F32) if JF else acc
                  gb = [fw.sb("gb%d" % j, [128, 1024], BF16) for j in range(NB)]
                  gv = gb
                  gvc = gbc = [0]
                  NBV = NB
                  wqv = wq_d[l].rearrange("(kc p) n -> p kc n", p=128)
                  for kc in range(8):
                      fw.dma('pool', wq[:, kc, :], wqv[:, kc, :], writes=['wq'])
                  fw.dma('pool', skt[:], skt_d[l], writes=['skt'])
                  for i in range(NT):
                      ACT(junk[:], xs[:, i, :], AF.Square, ['x%d' % i], ['junk', 'ss'], accum=ss[:, i:i + 1])
                  rstd_from_ss(ss[:], float(D), 'ss')
                  PAb = PA[:, 0:512].bitcast(BF16)
                  topv = top[:].rearrange("p (h two) k -> p h two k", two=2)
                  tixv = tixf[:].rearrange("p (h two) k -> p h two k", two=2)
                  dg = [fw.sb("dg%d" % k, [128, 128], BF16) for k in range(4)]
                  ubk = ['ubd%d' % c for c in range(16)]
                  vbk = ['vbd%d' % c for c in range(16)]

                  def idx_a(i):
                      xk = 'x%d' % i
                      ek = 'eidx%d' % (i % 2)
                      eidx = eidx2[:, i % 2, :]
                      STT(jf[:], xs[:, i, :], ss[:, i:i + 1], A2[:], ALU.mult, ALU.mult, [xk, 'ss', 'A2'], ['acc', 'jf'])
                      TT('dve', h2[:], jf[:], B2[:], ALU.add, ['acc', 'jf', 'B2'], ['h2'])
                      CP('act', h2b[:], h2[:], ['h2'], ['h2b'])
                      TRG([(PAb[:, kc * 128:(kc + 1) * 128], h2b[:, kc * 128:(kc + 1) * 128]) for kc in range(8)],
                          identb[:], ['h2b', 'identb'], ['PA0'])
                      CP('act', h2T[:], PAb.rearrange("p (k t) -> p k t", k=8), ['PA0'], ['h2T'])
                      for half in range(2):
                          for hq2 in range(2):
                              hq = half * 2 + hq2
                              MMG([(PB[:, hq2 * 512 + j * 128:hq2 * 512 + (j + 1) * 128], wq[:, kc, (hq * 4 + j) * 128:(hq * 4 + j + 1) * 128], h2T[:, kc, :],
                                    kc == 0, kc == 7) for j in range(4) for kc in range(8)], ['wq', 'h2T'], [PBk[hq2]])
                          CP('act', qTb[:, half * 8:(half + 1) * 8, :].rearrange("p a b -> p (a b)"), PB[:, 0:1024], PBk[0:2], ['qTb'])
                      for hq in range(4):
                          MMG([(PA[:, hq * 512 + j * 128:hq * 512 + (j + 1) * 128], qTb[:, hq * 4 + j, :], skt[:, (hq * 4 + j) % 2, :], True, True)
                               for j in range(4)], ['qTb', 'skt'], [PAk[hq]])
                      CP('dve', big[:], PA[:, :], PAk, ['big'])

                  def idx_b(i):
                      ek = 'eidx%d' % (i % 2)
                      eidx = eidx2[:, i % 2, :]
                      for hp in range(16):
                          fw.op('dve', lambda e, hp=hp: e.max(out=top[:, hp, 0:8], in_=sc[:, hp, :]), ['big'], ['top'])
                          fw.op('dve', lambda e, hp=hp: e.max_index(out=tix[:, hp, 0:8], in_max=top[:, hp, 0:8], in_values=sc[:, hp, :]), ['big', 'top'], ['tix'])
                          fw.op('dve', lambda e, hp=hp: e.match_replace(out=sc[:, hp, :], in_to_replace=top[:, hp, 0:8], in_values=sc[:, hp, :], imm_value=-1e30),
                                ['big', 'top'], ['big'])
                          fw.op('dve', lambda e, hp=hp: e.max(out=top[:, hp, 8:16], in_=sc[:, hp, :]), ['big'], ['top'])
                          fw.op('dve', lambda e, hp=hp: e.max_index(out=tix[:, hp, 8:16], in_max=top[:, hp, 8:16], in_values=sc[:, hp, :]), ['big', 'top'], ['tix'])
                      CP('dve', tixf[:], tix[:], ['tix'], ['tixf'])
                      TT('dve', cand.rearrange("p h (a b) -> p h a b", a=16), topv[:, :, 0, :].unsqueeze(3).to_broadcast([128, 8, 16, 16]),
                         topv[:, :, 1, :].unsqueeze(2).to_broadcast([128, 8, 16, 16]), ALU.add, ['top'], ['big'])
                      for h in range(8):
                          fw.op('dve', lambda e, h=h: e.max(out=best[:, h, 0:8], in_=cand[:, h, :]), ['big'], ['best'])
                          fw.op('dve', lambda e, h=h: e.max_index(out=pos[:, h, 0:8], in_max=best[:, h, 0:8], in_values=cand[:, h, :]), ['big', 'best'], ['pos'])
                          fw.op('dve', lambda e, h=h: e.match_replace(out=cand[:, h, :], in_to_replace=best[:, h, 0:8], in_values=cand[:, h, :], imm_value=-1e30),
                                ['big', 'best'], ['big'])
                          fw.op('dve', lambda e, h=h: e.max(out=best[:, h, 8:16], in_=cand[:, h, :]), ['big'], ['best'])
                          fw.op('dve', lambda e, h=h: e.max_index(out=pos[:, h, 8:16], in_max=best[:, h, 8:16], in_values=cand[:, h, :]), ['big', 'best'], ['pos'])
                      CP('dve', posf[:], pos[:], ['pos'], ['posf'])
                      TS('dve', ai[:], posf[:], 1.0 / 16.0, -7.5 / 16.0, ALU.mult, ALU.add, ['posf'], ['ai'])
                      CP('dve', af[:], ai[:], ['ai'], ['af'])
                      STT(bf[:], af[:], -16.0, posf[:], ALU.mult, ALU.add, ['af', 'posf'], ['bf'])
                      io_b = io16.unsqueeze(1).unsqueeze(1).to_broadcast([128, 8, 16, 16])
                      for (src, tv, dst, dk) in [(af, 0, isel, 'isel'), (bf, 1, jsel, 'jsel')]:
                          TT('dve', eq, src[:].unsqueeze(3).to_broadcast([128, 8, 16, 16]), io_b, ALU.is_equal, ['af', 'bf', 'cst'], ['big'])
                          TT('dve', eq, eq, tixv[:, :, tv, :].unsqueeze(2).to_broadcast([128, 8, 16, 16]), ALU.mult, ['big', 'tixf'], ['big'])
                          fw.op('dve', lambda e, dst=dst: e.tensor_reduce(out=dst[:], in_=eq, axis=AX.X, op=ALU.add), ['big'], [dk])
                      STT(isel[:], isel[:], 128.0, jsel[:], ALU.mult, ALU.add, ['isel', 'jsel'], ['isel'])
                      CP('dve', eidx, isel[:].rearrange("p h k -> p (h k)"), ['isel'], [ek])

                  def u_phase(i):
                      ek = 'eidx%d' % (i % 2)
                      for n in range(128):
                          j = gbc[0] % NB
                          gbc[0] += 1
                          fw.dma('pool', gb[j][:], ub_d[l], reads=[ek] + ubk, writes=['gb%d' % j], noslotwait=(n >= 20 or i > 0),
                                 indirect=bass.IndirectOffsetOnAxis(ap=eidx2[:, i % 2, n:n + 1], axis=0))
                          fw.op('dve', lambda e, j=j, n=n: e.scalar_tensor_tensor(out=(jf[:] if JF else junk[:]), in0=gb[j][:], scalar=1.0, in1=h2[:],
                                                                                op0=ALU.mult, op1=ALU.mult, accum_out=actp[:, n:n + 1]),
                                ['gb%d' % j, 'h2'], ['junk', 'jf', 'actp'])
                      TT('dve', gat[:], best[:], best[:, :, 0:1].to_broadcast([128, 8, 16]), ALU.subtract, ['best'], ['gat'])
                      ACT(gat[:], gat[:], AF.Exp, ['gat'], ['gat'])
                      fw.op('dve', lambda e: e.tensor_reduce(out=gz_[:], in_=gat[:], axis=AX.X, op=ALU.add), ['gat'], ['gz_'])
                      fw.op('dve', lambda e: e.reciprocal(out=gz_[:], in_=gz_[:]), ['gz_'], ['gz_'])
                      TT('dve', gat[:], gat[:], gz_[:].unsqueeze(2).to_broadcast([128, 8, 16]), ALU.mult, ['gat', 'gz_'], ['gat'])

                      gelu_inplace('dve', actp[:], actt[:], 'actp', 'actt')
                      TT('dve', wgt[:], actp[:], gat[:].rearrange("p h k -> p (h k)"), ALU.mult, ['actp', 'gat'], ['wgt'])

                  def v_phase(i):
                      xk = 'x%d' % i
                      ek = 'eidx%d' % (i % 2)
                      for n in range(128):
                          j = gvc[0] % NBV
                          gvc[0] += 1
                          k = n % 4
                          fw.dma('pool', gv[j][:], vb_d[l], reads=[ek] + vbk, writes=['gb%d' % j], noslotwait=True,
                                 indirect=bass.IndirectOffsetOnAxis(ap=eidx2[:, i % 2, n:n + 1], axis=0))
                          fw.op('act', lambda e, k=k, n=n: e.activation(out=dg[k][:], in_=identf, func=AF.Copy, scale=wgt[:, n:n + 1]),
                                ['cst', 'wgt'], ['dg%d' % k])
                          MMG([(PB[:, 1024:1536], dg[k][:], gv[j][:, 0:512], n == 0, n == 127),
                               (PB[:, 1536:2048], dg[k][:], gv[j][:, 512:1024], n == 0, n == 127)],
                              ['dg%d' % k, 'gb%d' % j], ['PB2', 'PB3'])

                  def v_fin(i):
                      xk = 'x%d' % i
                      TT('dve', acc[:], PB[:, 1024:2048], gt2[:], ALU.mult, ['PB2', 'PB3', 'gt2'], ['acc'])
                      TT('dve', xs[:, i, :], xs[:, i, :], acc[:], ALU.add, [xk, 'acc'], [xk])

                  idx_a(0)
                  idx_b(0)
                  for i in range(NT):
                      u_phase(i)
                      if i + 1 < NT:
                          idx_a(i + 1)
                      v_phase(i)
                      if i + 1 < NT:
                          idx_b(i + 1)
                      v_fin(i)
                  barrier()
                  fw.ctx = octx
              barrier()
          except _Stop:
            fw.ctx = ctx
            barrier()
            break

        fw.mute = False
        barrier()
        with ExitStack() as fctx:
            fw.ctx = fctx
            gf = fw.sb("gf", [128, 1024], F32)
            ss = fw.sb("ss", [128, NT], F32)
            junk = fw.sb("junk", [128, 1024], BF16)
            ob = [fw.sb("ob%d" % j, [128, 1024], F32) for j in range(2)]
            fw.dma('sync', gf[:], gfin_d[0:1, :].to_broadcast([128, 1024]), writes=['gf'])
            for i in range(NT):
                ACT(junk[:], xs[:, i, :], AF.Square, ['x%d' % i], ['junk', 'ss'], accum=ss[:, i:i + 1])
            rstd_from_ss(ss[:], float(D), 'ss')
            outk = []
            for i in range(NT):
                j = i % 2
                STT(ob[j][:], xs[:, i, :], ss[:, i:i + 1], gf[:], ALU.mult, ALU.mult, ['x%d' % i, 'ss', 'gf'], ['ob%d' % j])
                fw.dma('sync', out_d[i * 128:(i + 1) * 128, :], ob[j][:], reads=['ob%d' % j], writes=['out%d' % i])
                outk.append('out%d' % i)
            fw.finish(outk)
    return nc


def _consts():
    c = np.zeros((128, CSTW), np.float32)
    c[:, 0:128] = np.eye(128, dtype=np.float32)
    c[:, 128:256] = 1.0
    s = np.arange(128)[:, None]
    t = np.arange(128)[None, :]
    c[:, 256:384] = ((s // 32 == t // 32) & (s <= t)).astype(np.float32)
    c[:, 384] = np.pi / 2
    c[:, 385] = EPS
    c[:, 386] = 1.0
    c[:, 400:416] = np.arange(16, dtype=np.float32)[None, :]
    c[:, 416:928] = np.arange(1, 513, dtype=np.float32)[None, :]
    rm = np.ones(2048, np.float32)
    rm[::32] = 0.0
    c[:, 928:928 + 2048] = rm[None, :]
    return c


def _layouts(inp):
    f = lambda k: np.asarray(inp[k], dtype=np.float32)
    small = np.zeros((4, 128, NSMALL), np.float32)
    crt = np.zeros((4, 128, 8, 128), np.float32)
    cit = np.zeros((4, 128, 8, 128), np.float32)
    wa = np.zeros((4, 128, 2, 128), np.float32)
    wx = np.zeros((4, 128, 2, 128), np.float32)
    lbl = f('hgrn_lb_logits').reshape(4, 4, 128).transpose(2, 1, 0).reshape(128, 16)
    st = lambda a: a.reshape(8, 2, 64).transpose(1, 2, 0).reshape(128, 8)
    rep = lambda a: np.broadcast_to(a.reshape(2, 8, 1, 64), (2, 8, 16, 64)).transpose(1, 2, 0, 3).reshape(128, 128)
    col2 = lambda v: v.reshape(2, 128).T
    rmk = np.zeros((128, 8), np.float32)
    for gl in range(8):
        rmk[gl * 16:(gl + 1) * 16, gl] = 1.0
    for l in range(4):
        small[l, :, O_LBL:O_LBL + 16] = lbl
        small[l, :, O_ARS:O_ARS + 8] = st(f('s5_a_re')[l])
        small[l, :, O_AIS:O_AIS + 8] = st(f('s5_a_im')[l])
        ld = f('s5_log_dt')[l]
        small[l, :, O_LDS:O_LDS + 8] = st(np.broadcast_to(ld[:, None], (16, 64)).copy())
        small[l, :, O_ARR:O_ARR + 128] = rep(f('s5_a_re')[l])
        small[l, :, O_AIR:O_AIR + 128] = rep(f('s5_a_im')[l])
        small[l, :, O_BTR:O_BTR + 128] = f('s5_b_re')[l].reshape(2, 8, 64, 16).transpose(1, 3, 0, 2).reshape(128, 128)
        small[l, :, O_BTI:O_BTI + 128] = f('s5_b_im')[l].reshape(2, 8, 64, 16).transpose(1, 3, 0, 2).reshape(128, 128)
        small[l, :, O_LDR:O_LDR + 2] = np.broadcast_to(ld.reshape(2, 8, 1), (2, 8, 16)).transpose(1, 2, 0).reshape(128, 2)
        small[l, :, O_S5D:O_S5D + 2] = col2(f('s5_d')[l])
        small[l, :, O_BGL:O_BGL + 2] = col2(f('s5_b_glu')[l])
        small[l, :, O_CW:O_CW + 8] = f('lru_conv_w')[l].reshape(4, 2, 128).transpose(2, 1, 0).reshape(128, 8)
        small[l, :, O_CB:O_CB + 2] = col2(f('lru_conv_b')[l])
        small[l, :, O_BA:O_BA + 2] = col2(f('lru_b_a')[l])
        small[l, :, O_BX:O_BX + 2] = col2(f('lru_b_x')[l])
        small[l, :, O_LAM:O_LAM + 2] = col2(f('lru_lambda')[l])
        small[l, :, O_GBR:O_GBR + 8] = f('g_branch')[l].reshape(8, 128).T
        small[l, :, O_RMK:O_RMK + 8] = rmk
        for g in range(16):
            gp, gi, gl = g // 2, g % 2, g % 8
            crt[l, gi * 64:(gi + 1) * 64, gp, gl * 16:(gl + 1) * 16] = f('s5_c_re')[l, g].T
            cit[l, gi * 64:(gi + 1) * 64, gp, gl * 16:(gl + 1) * 16] = f('s5_c_im')[l, g].T
        for h in range(4):
            hc, o = h // 2, (h % 2) * 64
            wa[l, o:o + 64, hc, o:o + 64] = f('lru_w_a')[l, h]
            wx[l, o:o + 64, hc, o:o + 64] = f('lru_w_x')[l, h]
    skt = np.ascontiguousarray(f('peer_sub_keys').transpose(0, 3, 1, 2))
    return dict(small=small, crt=crt, cit=cit, wa_bd=wa, wx_bd=wx, skt=skt)


def make_in_maps(inp, cores):
    f = lambda k: np.ascontiguousarray(np.asarray(inp[k], dtype=np.float32))
    lay = _layouts(inp)
    shared = dict(w_mod=f('w_mod'), b_mod=f('b_mod'), g_mix=f('g_mix'), w_in=f('w_in'), w_glu=f('s5_w_glu'),
                  g_branch=f('g_branch'), w_out=f('w_out'), g_ffn=f('g_ffn'), peer_w_q=f('peer_w_q'),
                  g_final=f('g_final').reshape(1, D), consts=_consts())
    shared.update(lay)
    pu, pv = f('peer_u'), f('peer_v')
    for l_ in range(4):
        shared['peer_u%d' % l_] = pu[l_]
        shared['peer_v%d' % l_] = pv[l_]
    x = f('x')
    c = f('c')
    maps = []
    for b in cores:
        m = dict(shared)
        m['x'] = np.ascontiguousarray(x[b])
        m['c'] = np.ascontiguousarray(c[b].reshape(128, 8))
        maps.append(m)
    return maps


def kernel(**inputs):
    nc = build()
    maps = make_in_maps(inputs, list(range(8)))
    res = run_bass_kernel_spmd(nc, maps, core_ids=list(range(8)))
    return np.stack([np.asarray(r['out'], dtype=np.float32) for r in res.results], axis=0)
```

```python
import numpy as np
from contextlib import ExitStack
import concourse.bass as bass
import concourse.mybir as mybir
from concourse.bass_utils import run_bass_kernel_spmd

F32 = mybir.dt.float32
BF16 = mybir.dt.bfloat16
U32 = mybir.dt.uint32
I32 = mybir.dt.int32
AF = mybir.ActivationFunctionType
ALU = mybir.AluOpType
AX = mybir.AxisListType
F32R = mybir.dt.float32r

ENG = ['sync', 'act', 'pool', 'pe', 'dve']


class FW:
    def __init__(self, nc, ctx, slots=None):
        self.nc = nc
        self.ctx = ctx
        self.q = {e: [] for e in ENG}
        self.sems = {}
        self.cnt = {}
        for e in ENG:
            self.sems[e] = ctx.enter_context(nc.semaphore('s_' + e))
            self.cnt[e] = 0
        slots = slots or {'sync': 8, 'act': 4, 'pool': 8}
        self.slots = {}
        self.rr = {}
        for e, n in slots.items():
            self.slots[e] = []
            self.rr[e] = 0
            for i in range(n):
                nm = 'd_%s%d' % (e, i)
                self.sems[nm] = ctx.enter_context(nc.semaphore(nm))
                self.cnt[nm] = 0
                self.slots[e].append(nm)
        self.seen = {e: {} for e in ENG}
        self.lastw = {}
        self.readers = {}
        self.nins = {e: 0 for e in ENG}

    def sb(self, name, shape, dtype):
        self.uid = getattr(self, 'uid', 0) + 1
        if not hasattr(self, 'names'):
            self.names = {}
        self.names.setdefault(name, []).append('%s_u%d' % (name, self.uid))
        return self.ctx.enter_context(self.nc.sbuf_tensor('%s_u%d' % (name, self.uid), list(shape), dtype))

    def ps(self, name, shape, dtype):
        return self.ctx.enter_context(self.nc.psum_tensor(name, list(shape), dtype))

    def _wait(self, eng, s, v):
        if self.mute:
            return
        if self.seen[eng].get(s, 0) < v:
            self.seen[eng][s] = v
            sem = self.sems[s]
            self.q[eng].append(lambda e, sem=sem, v=v: e.wait_ge(sem, v))

    def _wait_deps(self, eng, reads, writes):
        deps = {}
        for k in list(reads) + list(writes):
            ev = self.lastw.get(k)
            if ev is not None and deps.get(ev[0], 0) < ev[1]:
                deps[ev[0]] = ev[1]
        for k in writes:
            for s, v in self.readers.get(k, {}).items():
                if deps.get(s, 0) < v:
                    deps[s] = v
        for s, v in deps.items():
            self._wait(eng, s, v)

    def _record(self, ev, reads, writes):
        s, v = ev
        ws = set(writes)
        for k in ws:
            self.lastw[k] = ev
            self.readers[k] = {}
        for k in reads:
            if k in ws:
                continue
            r = self.readers.setdefault(k, {})
            if r.get(s, 0) < v:
                r[s] = v

    mute = False

    def op(self, eng, fn, reads=(), writes=()):
        if self.mute:
            return
        self._wait_deps(eng, reads, writes)
        self.cnt[eng] += 1
        v = self.cnt[eng]
        sem = self.sems[eng]
        self.q[eng].append(lambda e, fn=fn, sem=sem: fn(e).then_inc(sem, 1))
        self.nins[eng] += 1
        self._record((eng, v), reads, writes)

    def dma(self, eng, out, in_, reads=(), writes=(), indirect=None, **kw):
        if self.mute:
            return
        self._wait_deps(eng, reads, writes)
        sl = self.slots[eng]
        slot = sl[self.rr[eng] % len(sl)]
        self.rr[eng] += 1
        if self.cnt[slot] > 0 and not kw.pop('noslotwait', False):
            self._wait(eng, slot, self.cnt[slot])
        kw.pop('noslotwait', None)
        self.cnt[slot] += 16
        v = self.cnt[slot]
        sem = self.sems[slot]
        if indirect is None:
            self.q[eng].append(lambda e, out=out, in_=in_, sem=sem, kw=kw:
                               e.dma_start(out=out, in_=in_, **kw).then_inc(sem, 16))
        else:
            self.q[eng].append(lambda e, out=out, in_=in_, sem=sem, ind=indirect, kw=kw:
                               e.indirect_dma_start(out=out, out_offset=None, in_=in_, in_offset=ind, **kw).then_inc(sem, 16))
        self.nins[eng] += 1
        self._record((slot, v), reads, writes)

    def finish(self, out_keys):
        self._wait_deps('sync', out_keys, [])
        q = self.q
        with self.nc.Block() as block:
            @block.sync
            def _(e):
                for f in q['sync']:
                    f(e)

            @block.scalar
            def _(e):
                for f in q['act']:
                    f(e)

            @block.gpsimd
            def _(e):
                for f in q['pool']:
                    f(e)

            @block.tensor
            def _(e):
                for f in q['pe']:
                    f(e)

            @block.vector
            def _(e):
                for f in q['dve']:
                    f(e)


import os
PIPE = 1
JF = 1
T = 2048
D = 1024
NT = 16
EPS = 1e-6
TWO_PI = float(2 * np.pi)
CW1 = 6.28125
CW2 = 0.0019353071795864769
GK = 1.5957691216057308

CSTW = 384 + 32 + 512 + 2048
NSMALL = 16 + 24 + 128 * 4 + 2 + 2 + 2 + 8 + 2 + 2 + 2 + 2 + 8 + 8
O_LBL = 0
O_ARS = 16
O_AIS = 24
O_LDS = 32
O_ARR = 40
O_AIR = 168
O_BTR = 296
O_BTI = 424
O_LDR = 552
O_S5D = 554
O_BGL = 556
O_CW = 558
O_CB = 566
O_BA = 568
O_BX = 570
O_LAM = 572
O_GBR = 574
O_RMK = 582


class _Stop(Exception):
    pass


def build(n_layers=4, dbg=None, do_peer=True, stop=None):
    nc = bass.Bass("TRN2", target_bir_lowering=False)

    def din(name, shape, dt=F32):
        return nc.dram_tensor(name, list(shape), dt, kind="ExternalInput").ap()

    x_d = din("x", [T, D])
    c_d = din("c", [128, 8])
    wmod_d = din("w_mod", [4, D, 6 * D])
    bmod_d = din("b_mod", [4, 6 * D])
    gmix_d = din("g_mix", [4, D])
    win_d = din("w_in", [4, D, 2816])
    small_d = din("small", [4, 128, NSMALL])
    crt_d = din("crt", [4, 128, 8, 128])
    cit_d = din("cit", [4, 128, 8, 128])
    wglu_d = din("w_glu", [4, 256, 256])
    wa_d = din("wa_bd", [4, 128, 2, 128])
    wx_d = din("wx_bd", [4, 128, 2, 128])
    gbrow_d = din("g_branch", [4, D])
    wout_d = din("w_out", [4, D, D])
    gffn_d = din("g_ffn", [4, D])
    wq_d = din("peer_w_q", [4, D, 2048])
    skt_d = din("skt", [4, 128, 2, 128])
    pu_d = [din("peer_u%d" % l_, [16384, D]) for l_ in range(4)]
    pv_d = [din("peer_v%d" % l_, [16384, D]) for l_ in range(4)]
    gfin_d = din("g_final", [1, D])
    consts_d = din("consts", [128, CSTW])
    out_d = nc.dram_tensor("out", [T, D], F32, kind="ExternalOutput").ap()
    ub_d = [nc.dram_tensor("ub_scr%d" % l_, [16384, D], BF16, kind="Internal").ap() for l_ in range(4)]
    vb_d = [nc.dram_tensor("vb_scr%d" % l_, [16384, D], BF16, kind="Internal").ap() for l_ in range(4)]

    with ExitStack() as ctx:
        fw = FW(nc, ctx, slots={'sync': 8, 'act': 2, 'pool': 20})
        build.fw = fw

        def TS(eng, out, in0, s1, s2, op0, op1, r, w):
            fw.op(eng, lambda e: e.tensor_scalar(out=out, in0=in0, scalar1=s1, scalar2=s2, op0=op0, op1=op1), r, w)

        def TT(eng, out, in0, in1, op, r, w):
            fw.op(eng, lambda e: e.tensor_tensor(out=out, in0=in0, in1=in1, op=op), r, w)

        def STT(out, in0, scalar, in1, op0, op1, r, w):
            fw.op('dve', lambda e: e.scalar_tensor_tensor(out=out, in0=in0, scalar=scalar, in1=in1, op0=op0, op1=op1), r, w)

        def ACT(out, in_, func, r, w, scale=1.0, bias=0.0, accum=None):
            r = list(r) + ['cst', 'sm']
            if accum is None:
                fw.op('act', lambda e: e.activation(out=out, in_=in_, func=func, scale=scale, bias=bias), r, w)
            else:
                fw.op('act', lambda e: e.activation(out=out, in_=in_, func=func, scale=scale, bias=bias, accum_out=accum), r, w)

        def CP(eng, out, in_, r, w):
            if eng == 'act':
                fw.op(eng, lambda e: e.activation(out=out, in_=in_, func=AF.Copy), r, w)
            else:
                fw.op(eng, lambda e: e.tensor_copy(out=out, in_=in_), r, w)

        def SCAN(out, d0, d1, init, r, w):
            fw.op('dve', lambda e: e.tensor_tensor_scan(out=out, data0=d0, data1=d1, initial=init, op0=ALU.mult, op1=ALU.add), r, w)

        def MMG(mms, r, w):
            def f(e, mms=mms):
                ins = None
                for mm in mms:
                    (o, l, rh, st, sp) = mm[:5]
                    if len(mm) > 5:
                        ins = e.matmul(o, l, rh, start=st, stop=sp, skip_group_check=True)
                    else:
                        ins = e.matmul(o, l, rh, start=st, stop=sp)
                return ins
            fw.op('pe', f, r, w)

        def TRG(trs, ident, r, w):
            def f(e, trs=trs):
                ins = None
                for (o, i) in trs:
                    ins = e.transpose(o, i, ident)
                return ins
            fw.op('pe', f, r, w)

        def barrier():
            allsems = list(fw.cnt.items())
            for e in ENG:
                for s, v in allsems:
                    if v > 0:
                        fw._wait(e, s, v)

        xs = fw.sb("xs", [128, NT, D], F32)
        cst = fw.sb("cst", [128, 928], F32)
        identf = cst[:, 0:128]
        onesf = cst[:, 128:256]
        maskbd = cst[:, 256:384]
        halfpi = cst[:, 384:385]
        epscol = cst[:, 385:386]
        onecol = cst[:, 386:387]
        io16 = cst[:, 400:416]
        tau = cst[:, 416:416 + 512]
        rmask_t = fw.sb("rmask", [128, 2048], BF16)
        rmask = rmask_t[:]
        identb = fw.sb("identb", [128, 128], BF16)
        condr = fw.sb("condr", [128, 8, 128], F32)
        cond = fw.sb("cond", [128, 8], F32)
        PA = fw.ps("PA", [128, 2048], F32)
        PB = fw.ps("PB", [128, 2048], F32)
        PAk = ['PA0', 'PA1', 'PA2', 'PA3']
        PBk = ['PB0', 'PB1', 'PB2', 'PB3']

        fw.dma('sync', cst[:, 0:928], consts_d[:, 0:928], writes=['cst'])
        fw.dma('pool', rmask_t[:], consts_d[:, 928:928 + 2048], writes=['cst'])
        for i in range(NT):
            fw.dma('sync', xs[:, i, :], x_d[i * 128:(i + 1) * 128, :], writes=['x%d' % i])
        fw.dma('sync', cond[:], c_d[:, :], writes=['cond'])
        CP('dve', identb[:], identf, ['cst'], ['identb'])
        ACT(cond[:], cond[:], AF.Silu, ['cond'], ['cond'])
        CP('dve', condr[:], cond[:, :].unsqueeze(2).to_broadcast([128, 8, 128]), ['cond'], ['condr'])

        def mod_tile(l, j, out_tile, okey, stg, bm):
            wv = wmod_d[l].rearrange("(p kc) n -> p kc n", kc=8)
            fw.dma('sync', bm[:], bmod_d[l:l + 1, j * 1024:(j + 1) * 1024].to_broadcast([128, 1024]), writes=['bm'])
            for half in range(2):
                c0 = j * 1024 + half * 512
                fw.dma('sync', stg[:], wv[:, :, c0:c0 + 512], writes=['stg'])
                MMG([(PA[:, 0:512], condr[:, kc, :], stg[:, kc, :], kc == 0, kc == 7) for kc in range(8)],
                    ['condr', 'stg'], ['PA0'])
                TT('dve', out_tile[:, half * 512:(half + 1) * 512], PA[:, 0:512], bm[:, half * 512:(half + 1) * 512], ALU.add,
                   ['PA0', 'bm'], [okey])

        def gelu_inplace(eng_t, t, tmp, key, tkey):
            ACT(tmp, t, AF.Square, [key], [tkey])
            TS('dve', tmp, tmp, 0.044715, 1.0, ALU.mult, ALU.add, [tkey], [tkey])
            TT('dve', tmp, tmp, t, ALU.mult, [tkey, key], [tkey])
            ACT(tmp, tmp, AF.Sigmoid, [tkey], [tkey], scale=GK)
            TT('dve', t, t, tmp, ALU.mult, [key, tkey], [key])

        def rstd_from_ss(ss_ap, n, key):
            ACT(ss_ap, ss_ap, AF.Sqrt, [key], [key], scale=1.0 / n, bias=epscol)
            fw.op('dve', lambda e: e.reciprocal(out=ss_ap, in_=ss_ap), [key], [key])

        def norm_to_T(A, B, hT, hkeys, keep=None):
            pass

        for l in range(n_layers):
          try:
              with ExitStack() as actx:
                  octx = fw.ctx
                  fw.ctx = actx
                  hT = fw.sb("hT", [128, 8, T], BF16)
                  yT = fw.sb("yT", [128, 8, T], BF16)
                  sm = fw.sb("sm", [128, NSMALL], F32)
                  rsd = fw.sb("rsd", [128, 3, NT], F32)
                  fw.dma('sync', sm[:], small_d[l], writes=['sm'])
                  with ExitStack() as sctx:
                      fw.ctx = sctx
                      stg = fw.sb("stg", [128, 8, 512], F32)
                      bm = fw.sb("bm", [128, 1024], F32)
                      A1 = fw.sb("A1", [128, 1024], F32)
                      B1 = fw.sb("B1", [128, 1024], F32)
                      gm = fw.sb("gm", [128, 1024], F32)
                      ss = fw.sb("ss", [128, NT], F32)
                      junk = fw.sb("junk", [128, 1024], BF16)
                      tmpf = fw.sb("tmpf", [128, 1024], F32)
                      hb = fw.sb("hb", [128, 1024], BF16)
                      mod_tile(l, 0, B1, 'B1', stg, bm)
                      mod_tile(l, 1, A1, 'A1', stg, bm)
                      fw.dma('sync', gm[:], gmix_d[l:l + 1, :].to_broadcast([128, 1024]), writes=['gm'])
                      STT(A1[:], A1[:], 1.0, gm[:], ALU.add, ALU.mult, ['A1', 'gm'], ['A1'])
                      for i in range(NT):
                          ACT(junk[:], xs[:, i, :], AF.Square, ['x%d' % i], ['junk', 'ss'], accum=ss[:, i:i + 1])
                      rstd_from_ss(ss[:], float(D), 'ss')
                      PAb = PA[:, 0:512].bitcast(BF16)
                      for i in range(NT):
                          STT(tmpf[:], xs[:, i, :], ss[:, i:i + 1], A1[:], ALU.mult, ALU.mult, ['x%d' % i, 'ss', 'A1'], ['tmpf'])
                          TT('dve', hb[:], tmpf[:], B1[:], ALU.add, ['tmpf', 'B1'], ['hb'])
                          TRG([(PAb[:, kc * 128:(kc + 1) * 128], hb[:, kc * 128:(kc + 1) * 128]) for kc in range(8)],
                              identb[:], ['hb', 'identb'], ['PA0'])
                          CP('act', hT[:, :, i * 128:(i + 1) * 128], PAb.rearrange("p (k t) -> p k t", k=8), ['PA0'], ['hT'])
                      barrier()
                  fw.ctx = actx
                  if stop == 'norm':
                      fw.mute = True
                  lbt = fw.sb("lbt", [128, 16], F32)
                  lbz = fw.sb("lbz", [128, 4], F32)
                  lb = fw.sb("lb", [128, 4], F32)
                  oml = fw.sb("oml", [128, 4], F32)
                  ACT(lbt[:], sm[:, O_LBL:O_LBL + 16], AF.Exp, ['sm'], ['lbt'])
                  lbv = lbt[:].rearrange("p (h l) -> p h l", h=4)
                  fw.op('dve', lambda e: e.tensor_reduce(out=lbz[:], in_=lbv, op=ALU.add, axis=AX.X), ['lbt'], ['lbz'])
                  fw.op('dve', lambda e: e.reciprocal(out=lbz[:], in_=lbz[:]), ['lbz'], ['lbz'])
                  fw.op('dve', lambda e: e.memset(lb[:], 0.0), [], ['lb'])
                  for j in range(1, l + 1):
                      TT('dve', lb[:], lb[:], lbv[:, :, j], ALU.add, ['lb', 'lbt'], ['lb'])
                  TT('dve', lb[:], lb[:], lbz[:], ALU.mult, ['lb', 'lbz'], ['lb'])
                  TS('dve', oml[:], lb[:], -1.0, 1.0, ALU.mult, ALU.add, ['lb'], ['oml'])
                  winv = win_d[l].rearrange("(kc p) n -> p kc n", p=128)

                  with ExitStack() as sctx:
                      fw.ctx = sctx
                      wh = fw.sb("wh", [128, 8, 512], BF16)
                      t1 = fw.sb("t1", [128, T], F32)
                      t2 = fw.sb("t2", [128, T], F32)
                      t3 = fw.sb("t3", [128, T], F32)
                      kt = fw.sb("kt", [128, T], BF16)
                      kh = fw.sb("kh", [128, T], BF16)
                      qt = fw.sb("qt", [128, T], BF16)
                      khk = fw.sb("khk", [128, NT, 128], BF16)
                      khkz = fw.sb("khkz", [128, NT, 128], BF16)
                      qz = fw.sb("qz", [128, T], BF16)
                      fw.op('dve', lambda e: e.memset(qz[:], 0.0), [], ['qz'])
                      zer = fw.sb("zer", [128, 128], BF16)
                      fw.op('dve', lambda e: e.memset(zer[:], 0.0), [], ['zer'])
                      el = fw.sb("el", [128, 64], F32)
                      S = fw.sb("S", [128, 128], F32)
                      Sb = fw.sb("Sb", [128, 128], BF16)
                      vb = fw.sb("vb", [128, 128], BF16)
                      gs = fw.sb("gs", [128, 128], F32)
                      scm = fw.sb("scm", [128, 128], BF16)
                      yh = fw.sb("yh", [128, 128], F32)
                      yhb = fw.sb("yhb", [128, 128], BF16)
                      jk = fw.sb("jk", [128, 128], F32)
                      ssh = fw.sb("ssh", [128, 2], F32)
                      ssa = fw.sb("ssa", [128, 4, NT], F32)
                      gbb = fw.sb("gbb", [128, 512], F32)
                      fw.dma('sync', gbb[:], gbrow_d[l:l + 1, 0:512].to_broadcast([128, 512]), writes=['gbb'])
                      PAb = PA[:, 0:1024].bitcast(BF16)
                      for h in range(4):
                          for j, c0 in enumerate([h * 128, 512 + h * 128, 1024 + h * 128, 1536 + h * 128]):
                              fw.dma('pool', wh[:, :, j * 128:(j + 1) * 128], winv[:, :, c0:c0 + 128], writes=['wh'])
                          if do_peer:
                              for (src, dst, key) in [(pu_d[l], ub_d[l], 'ubd'), (pv_d[l], vb_d[l], 'vbd')]:
                                  for c in range(4 * h, 4 * h + 4):
                                      fw.dma('pool', dst[c * 1024:(c + 1) * 1024, :].rearrange("(p r) d -> p r d", p=128),
                                             src[c * 1024:(c + 1) * 1024, :].rearrange("(p r) d -> p r d", p=128), writes=[key + str(c)])
                          for tb in range(4):
                              MMG([(PA[:, tb * 512:(tb + 1) * 512], wh[:, kc, 128:256], hT[:, kc, tb * 512:(tb + 1) * 512], kc == 0, kc == 7)
                                   for kc in range(8)], ['wh', 'hT'], [PAk[tb]])
                              ACT(t1[:, tb * 512:(tb + 1) * 512], PA[:, tb * 512:(tb + 1) * 512], AF.Sigmoid, [PAk[tb]], ['t1'])
                          TS('dve', t1[:], t1[:], oml[:, h:h + 1], lb[:, h:h + 1], ALU.mult, ALU.add, ['t1', 'oml', 'lb'], ['t1'])
                          if stop == 'hg_a':
                              fw.mute = True
                          ACT(t2[:], t1[:], AF.Ln, ['t1'], ['t2'])
                          SCAN(t3[:], rmask, t2[:], 0.0, ['cst', 't2'], ['t3'])
                          if stop == 'hg_b':
                              fw.mute = True
                          TS('dve', t1[:], t1[:], -1.0, 1.0, ALU.mult, ALU.add, ['t1'], ['t1'])
                          ACT(t2[:], t3[:], AF.Exp, ['t3'], ['t2'], scale=-1.0)
                          TT('dve', kt[:], t1[:], t2[:], ALU.mult, ['t1', 't2'], ['kt'])
                          b3 = t3[:].rearrange("p (c s) -> p c s", s=32)
                          TT('dve', t2[:].rearrange("p (c s) -> p c s", s=32), b3[:, :, 31:32].to_broadcast([128, 64, 32]), b3, ALU.subtract,
                             ['t3'], ['t2'])
                          ACT(t2[:], t2[:], AF.Exp, ['t2'], ['t2'])
                          TT('dve', kh[:], t1[:], t2[:], ALU.mult, ['t1', 't2'], ['kh'])
                          ACT(t3[:], t3[:], AF.Exp, ['t3'], ['t3'])
                          CP('dve', el[:], t3[:].rearrange("p (c s) -> p c s", s=32)[:, :, 31], ['t3'], ['el'])
                          if stop == 'hg_c':
                              fw.mute = True
                          for tb in range(4):
                              MMG([(PB[:, tb * 512:(tb + 1) * 512], wh[:, kc, 0:128], hT[:, kc, tb * 512:(tb + 1) * 512], kc == 0, kc == 7)
                                   for kc in range(8)], ['wh', 'hT'], [PBk[tb]])
                              ACT(t1[:, tb * 512:(tb + 1) * 512], PB[:, tb * 512:(tb + 1) * 512], AF.Silu, [PBk[tb]], ['t1'])
                          TT('dve', qt[:], t1[:], t3[:], ALU.mult, ['t1', 't3'], ['qt'])
                          if stop == 'hg_c1':
                              fw.mute = True
                          CP('dve', qz[:].rearrange("p (i t) -> p i t", t=128)[:, :, 96:128], qt[:].rearrange("p (i t) -> p i t", t=128)[:, :, 96:128],
                             ['qt'], ['qz'])
                          if stop == 'hg_c2':
                              fw.mute = True
                          for half in range(2):
                              TRG([(PAb[:, j * 128:(j + 1) * 128], kh[:, (half * 8 + j) * 128:(half * 8 + j + 1) * 128]) for j in range(8)],
                                  identb[:], ['kh', 'identb'], ['PA0'])
                              CP('act', khk[:, half * 8:(half + 1) * 8, :], PAb[:, 0:1024].rearrange("p (j d) -> p j d", j=8), ['PA0'], ['khk'])
                              if stop == 'hg_c3':
                                  fw.mute = True
                              CP('act', khkz[64:128, half * 8:(half + 1) * 8, :], PAb[64:128, 0:1024].rearrange("p (j d) -> p j d", j=8), ['PA0'], ['khkz'])
                              if stop == 'hg_c4':
                                  fw.mute = True
                              fw.op('dve', lambda e, half=half: e.memset(khkz[64:96, half * 8:(half + 1) * 8, :], 0.0), [], ['khkz'])
                              if stop == 'hg_c5':
                                  fw.mute = True
                          fw.op('dve', lambda e: e.memset(S[:], 0.0), [], ['S'])
                          if stop == 'hg_c7':
                              fw.mute = True
                          fw.op('dve', lambda e: e.memset(Sb[:], 0.0), [], ['Sb'])
                          if stop == 'hg_d':
                              fw.mute = True
                          for i in range(NT):
                              tsl = slice(i * 128, (i + 1) * 128)
                              MMG([(PB[:, 0:256], hT[:, kc, tsl], wh[:, kc, 256:512], kc == 0, kc == 7) for kc in range(8)],
                                  ['hT', 'wh'], ['PB0'])
                              CP('act', vb[:], PB[:, 0:128], ['PB0'], ['vb'])
                              ACT(gs[:], PB[:, 128:256], AF.Silu, ['PB0'], ['gs'])
                              MMG([(PB[:, 512:640], kt[:, tsl], qt[:, tsl], True, True)], ['kt', 'qt'], ['PB1'])
                              TT('dve', scm[:], PB[:, 512:640], maskbd, ALU.mult, ['PB1', 'cst'], ['scm'])
                              if stop == 'hg_e':
                                  fw.mute = True
                              MMG([(PB[:, 1024:1152], scm[:], vb[:], True, False)], ['scm', 'vb'], ['PB2'])
                              for j in range(4):
                                  rs_ = slice(32 * j, 32 * j + 32)
                                  if j < 3:
                                      MMG([(PB[rs_, 1024:1152], qt[:, i * 128 + 32 * j:i * 128 + 32 * j + 32], Sb[:], False, False, 1)],
                                          ['qt', 'Sb'], ['PB2'])
                                      MMG([(PB[:, 1536:1664], khk[rs_, i, :], vb[rs_, :], True, True)], ['khk', 'vb'], ['PB3'])
                                  else:
                                      MMG([(PB[64:128, 1024:1152], qz[:, i * 128 + 64:i * 128 + 128], Sb[:], False, False, 1)],
                                          ['qz', 'Sb'], ['PB2'])
                                      MMG([(PB[:, 1536:1664], khkz[64:128, i, :], vb[64:128, :], True, True)], ['khkz', 'vb'], ['PB3'])
                                  STT(Sb[:], S[:], el[:, 4 * i + j:4 * i + j + 1], PB[:, 1536:1664], ALU.mult, ALU.add, ['S', 'el', 'PB3'], ['Sb'])
                                  STT(S[:], S[:], el[:, 4 * i + j:4 * i + j + 1], PB[:, 1536:1664], ALU.mult, ALU.add, ['S', 'el', 'PB3'], ['S'])
                              MMG([(PB[:, 1024:1152], zer[:], vb[:], False, True)], ['zer', 'vb'], ['PB2'])
                              if stop == 'hg_f':
                                  fw.mute = True
                              ACT(jk[:], PB[:, 1024:1152], AF.Square, ['PB2'], ['jk', 'ssh'], accum=ssh[:, 0:1])
                              rstd_from_ss(ssh[:, 0:1], 128.0, 'ssh')
                              STT(yh[:], PB[:, 1024:1152], ssh[:, 0:1], gs[:], ALU.mult, ALU.mult, ['PB2', 'ssh', 'gs'], ['yh'])
                              ACT(jk[:], yh[:], AF.Square, ['yh'], ['jk', 'ssa'], accum=ssa[:, h, i:i + 1])
                              TT('dve', yhb[:], yh[:], gbb[:, h * 128:(h + 1) * 128], ALU.mult, ['yh', 'gbb'], ['yhb'])
                              TRG([(PAb[:, 1024:1152], yhb[:])], identb[:], ['yhb', 'identb'], ['PA1'])
                              CP('act', yT[:, h, tsl], PAb[:, 1024:1152], ['PA1'], ['yT%d' % h])
                      TT('dve', ssa[:, 0, :], ssa[:, 0, :], ssa[:, 1, :], ALU.add, ['ssa'], ['ssa'])
                      TT('dve', ssa[:, 2, :], ssa[:, 2, :], ssa[:, 3, :], ALU.add, ['ssa'], ['ssa'])
                      TT('dve', rsd[:, 0, :], ssa[:, 0, :], ssa[:, 2, :], ALU.add, ['ssa'], ['rsd0'])
                      rstd_from_ss(rsd[:, 0, :], 512.0, 'rsd0')
                      barrier()
                  fw.ctx = actx
                  if stop == 'hgrn':
                      fw.mute = True

                  with ExitStack() as sctx:
                      fw.ctx = sctx
                      wl = fw.sb("wl", [128, 8, 256], BF16)
                      wab = fw.sb("wab", [128, 2, 128], BF16)
                      wxb = fw.sb("wxb", [128, 2, 128], BF16)
                      xraw = fw.sb("xraw", [128, 3 + T], F32)
                      xc = fw.sb("xc", [128, T], F32)
                      xcb = fw.sb("xcb", [128, T], BF16)
                      ta = fw.sb("ta", [128, T], F32)
                      tb_ = fw.sb("tb_", [128, T], F32)
                      tr = fw.sb("tr", [128, T], F32)
                      ti = fw.sb("ti", [128, T], F32)
                      c8 = fw.sb("c8", [128, 2], F32)
                      c16 = fw.sb("c16", [128, 2], F32)
                      ssc = fw.sb("ssc", [128, 2, NT], F32)
                      fw.dma('pool', wab[:], wa_d[l], writes=['wab'])
                      fw.dma('pool', wxb[:], wx_d[l], writes=['wxb'])
                      ACT(c8[:], sm[:, O_LAM:O_LAM + 2], AF.Exp, ['sm'], ['c8'], scale=-1.0)
                      ACT(c8[:], c8[:], AF.Ln, ['c8'], ['c8'], bias=onecol)
                      TS('dve', c16[:], c8[:], -16.0, None, ALU.mult, ALU.bypass, ['c8'], ['c16'])
                      TS('dve', c8[:], c8[:], -8.0, None, ALU.mult, ALU.bypass, ['c8'], ['c8'])
                      fw.op('dve', lambda e: e.memset(xraw[:, 0:3], 0.0), [], ['xraw'])
                      for hc in range(2):
                          for j, c0 in enumerate([2304 + hc * 128, 2560 + hc * 128]):
                              fw.dma('pool', wl[:, :, j * 128:(j + 1) * 128], winv[:, :, c0:c0 + 128], writes=['wl'])
                          for tb in range(4):
                              MMG([(PA[:, tb * 512:(tb + 1) * 512], wl[:, kc, 0:128], hT[:, kc, tb * 512:(tb + 1) * 512], kc == 0, kc == 7)
                                   for kc in range(8)], ['wl', 'hT'], [PAk[tb]])
                              CP('act', xraw[:, 3 + tb * 512:3 + (tb + 1) * 512], PA[:, tb * 512:(tb + 1) * 512], [PAk[tb]], ['xraw'])
                          cw = sm[:, O_CW + hc * 4:O_CW + hc * 4 + 4]
                          TS('dve', xc[:], xraw[:, 3:3 + T], cw[:, 3:4], sm[:, O_CB + hc:O_CB + hc + 1], ALU.mult, ALU.add, ['xraw', 'sm'], ['xc'])
                          for w_ in range(3):
                              STT(xc[:], xraw[:, w_:w_ + T], cw[:, w_:w_ + 1], xc[:], ALU.mult, ALU.add, ['xraw', 'sm', 'xc'], ['xc'])
                          CP('act', xcb[:], xc[:], ['xc'], ['xcb'])
                          for tb in range(4):
                              sl = slice(tb * 512, (tb + 1) * 512)
                              MMG([(PB[:, sl], wab[:, hc, :], xcb[:, sl], True, True)], ['wab', 'xcb'], [PBk[tb]])
                              ACT(tr[:, sl], PB[:, sl], AF.Sigmoid, [PBk[tb]], ['tr'], bias=sm[:, O_BA + hc:O_BA + hc + 1])
                          for tb in range(4):
                              sl = slice(tb * 512, (tb + 1) * 512)
                              MMG([(PA[:, sl], wxb[:, hc, :], xcb[:, sl], True, True)], ['wxb', 'xcb'], [PAk[tb]])
                              ACT(ti[:, sl], PA[:, sl], AF.Sigmoid, [PAk[tb]], ['ti'], bias=sm[:, O_BX + hc:O_BX + hc + 1])
                          ACT(ta[:], tr[:], AF.Exp, ['tr', 'c8'], ['ta'], scale=c8[:, hc:hc + 1])
                          ACT(tb_[:], tr[:], AF.Exp, ['tr', 'c16'], ['tb_'], scale=c16[:, hc:hc + 1])
                          TS('dve', tb_[:], tb_[:], -1.0, 1.0, ALU.mult, ALU.add, ['tb_'], ['tb_'])
                          ACT(tb_[:], tb_[:], AF.Sqrt, ['tb_'], ['tb_'])
                          TT('dve', tb_[:], tb_[:], ti[:], ALU.mult, ['tb_', 'ti'], ['tb_'])
                          TT('dve', tb_[:], tb_[:], xc[:], ALU.mult, ['tb_', 'xc'], ['tb_'])
                          SCAN(tr[:], ta[:], tb_[:], 0.0, ['ta', 'tb_'], ['tr'])
                          for tb in range(4):
                              sl = slice(tb * 512, (tb + 1) * 512)
                              MMG([(PB[:, sl], wl[:, kc, 128:256], hT[:, kc, sl], kc == 0, kc == 7) for kc in range(8)],
                                  ['wl', 'hT'], [PBk[tb]])
                              CP('act', ti[:, sl], PB[:, sl], [PBk[tb]], ['ti'])
                          gelu_inplace('dve', ti[:], ta[:], 'ti', 'ta')
                          TT('dve', tb_[:], tr[:], ti[:], ALU.mult, ['tr', 'ti'], ['tb_'])
                          TS('dve', yT[:, 6 + hc, :], tb_[:], sm[:, O_GBR + 6 + hc:O_GBR + 7 + hc], None, ALU.mult, ALU.bypass,
                             ['tb_', 'sm'], ['yT%d' % (6 + hc)])
                          ACT(ta[:], tb_[:], AF.Square, ['tb_'], ['ta'])
                          MMG([(PA[:, 2 * i:2 * i + 2], ta[:, i * 128:(i + 1) * 128], onesf[:, 0:2], True, True) for i in range(NT)],
                              ['ta', 'cst'], ['PA0'])
                          CP('dve', ssc[:, hc, :], PA[:, 0:2 * NT].rearrange("p (i two) -> p i two", two=2)[:, :, 0], ['PA0'], ['ssc'])
                      TT('dve', rsd[:, 2, :], ssc[:, 0, :], ssc[:, 1, :], ALU.add, ['ssc'], ['rsd2'])
                      rstd_from_ss(rsd[:, 2, :], 256.0, 'rsd2')
                      barrier()
                  fw.ctx = actx
                  if stop == 'lru':
                      fw.mute = True

                  with ExitStack() as sctx:
                      fw.ctx = sctx
                      LP = 512
                      NP_ = T // LP
                      ws5 = fw.sb("ws5", [128, 8, 256], BF16)
                      crt = fw.sb("crt", [128, 8, 128], BF16)
                      cit = fw.sb("cit", [128, 8, 128], BF16)
                      wglu = fw.sb("wglu", [128, 2, 256], BF16)
                      ub = fw.sb("ub", [128, 2, T], BF16)
                      zb = fw.sb("zb", [128, 2, T], BF16)
                      lhb = fw.sb("lhb", [128, 8, 2, 128], BF16)
                      pst = fw.sb("pst", [128, 5, 8], F32)
                      prp = fw.sb("prp", [128, 12, 128], F32)
                      pri = fw.sb("pri", [128, 128], I32)
                      car = fw.sb("car", [128, 8, 2], F32)
                      tcos = fw.sb("tcos", [128, LP], F32)
                      tsin = fw.sb("tsin", [128, LP], F32)
                      tki = fw.sb("tki", [128, LP], I32)
                      s1 = fw.sb("s1", [128, LP], F32)
                      s2 = fw.sb("s2", [128, LP], F32)
                      swr = fw.sb("swr", [128, LP], F32)
                      swi = fw.sb("swi", [128, LP], F32)
                      szr = fw.sb("szr", [128, LP], F32)
                      szi = fw.sb("szi", [128, LP], F32)
                      xrb = fw.sb("xrb", [128, LP], BF16)
                      xib = fw.sb("xib", [128, LP], BF16)
                      yb = fw.sb("yb", [128, 512], F32)
                      yb2 = fw.sb("yb2", [128, 512], F32)
                      ssb = fw.sb("ssb", [128, 2, NT], F32)
                      fw.dma('pool', crt[:], crt_d[l], writes=['crt'])
                      fw.dma('pool', cit[:], cit_d[l], writes=['cit'])
                      fw.dma('pool', wglu[:], wglu_d[l].rearrange("(kc p) n -> p kc n", p=128), writes=['wglu'])
                      fw.dma('pool', ws5[:], winv[:, :, 2048:2304], writes=['ws5'])

                      def sincos(ang, ki, sin_o, cos_o, tmp, keys):
                          ka, kk, ks, kc_, kt_ = keys
                          TS('dve', ki, ang, 1.0 / TWO_PI, None, ALU.mult, ALU.bypass, [ka], [kk])
                          STT(ang, ki, -CW1, ang, ALU.mult, ALU.add, [kk, ka], [ka])
                          STT(ang, ki, -CW2, ang, ALU.mult, ALU.add, [kk, ka], [ka])
                          TS('dve', ang, ang, -3.14159, 3.14159, ALU.max, ALU.min, [ka], [ka])
                          ACT(sin_o, ang, AF.Sin, [ka], [ks])
                          STT(tmp, ang, -1.0, ang, ALU.mult, ALU.max, [ka], [kt_])
                          ACT(cos_o, tmp, AF.Sin, [kt_, 'cst'], [kc_], scale=-1.0, bias=halfpi)

                      lamre, dts, rmag, theta = pst[:, 0, :], pst[:, 1, :], pst[:, 2, :], pst[:, 3, :]
                      TS('dve', lamre, sm[:, O_ARS:O_ARS + 8], -1e-4, None, ALU.min, ALU.bypass, ['sm'], ['pst'])
                      ACT(dts, sm[:, O_LDS:O_LDS + 8], AF.Exp, ['sm'], ['pst'])
                      TT('dve', rmag, lamre, dts, ALU.mult, ['pst'], ['pst'])
                      ACT(rmag, rmag, AF.Exp, ['pst'], ['pst'])
                      TT('dve', theta, sm[:, O_AIS:O_AIS + 8], dts, ALU.mult, ['sm', 'pst'], ['pst'])
                      R = lambda k: prp[:, k, :]
                      lam_r, lam_i = R(0), R(1)
                      TS('dve', lam_r, sm[:, O_ARR:O_ARR + 128], -1e-4, None, ALU.min, ALU.bypass, ['sm'], ['prp'])
                      CP('dve', lam_i, sm[:, O_AIR:O_AIR + 128], ['sm'], ['prp'])
                      ACT(prp[:, 11, 0:2], sm[:, O_LDR:O_LDR + 2], AF.Exp, ['sm'], ['prp'])
                      for hg in range(2):
                          cs = slice(hg * 64, (hg + 1) * 64)
                          TS('dve', prp[:, 2, cs], prp[:, 0, cs], prp[:, 11, hg:hg + 1], None, ALU.mult, ALU.bypass, ['prp'], ['prp'])
                          TS('dve', prp[:, 3, cs], prp[:, 1, cs], prp[:, 11, hg:hg + 1], None, ALU.mult, ALU.bypass, ['prp'], ['prp'])
                      ACT(R(2), R(2), AF.Exp, ['prp'], ['prp'])
                      sincos(R(3), pri[:], R(4), R(5), R(6), ['prp', 'pri', 'prp', 'prp', 'prp'])
                      TT('dve', R(5), R(5), R(2), ALU.mult, ['prp'], ['prp'])
                      TT('dve', R(4), R(4), R(2), ALU.mult, ['prp'], ['prp'])
                      TS('dve', R(5), R(5), -1.0, None, ALU.add, ALU.bypass, ['prp'], ['prp'])
                      TT('dve', R(2), lam_r, lam_r, ALU.mult, ['prp'], ['prp'])
                      TT('dve', R(3), lam_i, lam_i, ALU.mult, ['prp'], ['prp'])
                      TT('dve', R(2), R(2), R(3), ALU.add, ['prp'], ['prp'])
                      fw.op('dve', lambda e: e.reciprocal(out=R(2), in_=R(2)), ['prp'], ['prp'])
                      TT('dve', R(6), R(5), lam_r, ALU.mult, ['prp'], ['prp'])
                      TT('dve', R(7), R(4), lam_i, ALU.mult, ['prp'], ['prp'])
                      TT('dve', R(6), R(6), R(7), ALU.add, ['prp'], ['prp'])
                      TT('dve', R(6), R(6), R(2), ALU.mult, ['prp'], ['prp'])
                      TT('dve', R(7), R(4), lam_r, ALU.mult, ['prp'], ['prp'])
                      TT('dve', R(8), R(5), lam_i, ALU.mult, ['prp'], ['prp'])
                      TT('dve', R(7), R(7), R(8), ALU.subtract, ['prp'], ['prp'])
                      TT('dve', R(7), R(7), R(2), ALU.mult, ['prp'], ['prp'])
                      btr, bti = sm[:, O_BTR:O_BTR + 128], sm[:, O_BTI:O_BTI + 128]
                      TT('dve', R(8), R(6), btr, ALU.mult, ['prp', 'sm'], ['prp'])
                      TT('dve', R(9), R(7), bti, ALU.mult, ['prp', 'sm'], ['prp'])
                      TT('dve', R(8), R(8), R(9), ALU.subtract, ['prp'], ['prp'])
                      TT('dve', R(9), R(6), bti, ALU.mult, ['prp', 'sm'], ['prp'])
                      TT('dve', R(10), R(7), btr, ALU.mult, ['prp', 'sm'], ['prp'])
                      TT('dve', R(9), R(9), R(10), ALU.add, ['prp'], ['prp'])
                      for gp in range(8):
                          hc = gp // 4
                          for gi in range(2):
                              gl = (2 * gp + gi) % 8
                              for ri, src in enumerate([8, 9]):
                                  TS('dve', lhb[:, gp, ri, gi * 64:(gi + 1) * 64], prp[:, src, hc * 64:(hc + 1) * 64],
                                     sm[:, O_RMK + gl:O_RMK + gl + 1], None, ALU.mult, ALU.bypass, ['prp', 'sm'], ['lhb'])
                      for hc in range(2):
                          for tb in range(4):
                              sl = slice(tb * 512, (tb + 1) * 512)
                              MMG([(PA[:, sl], ws5[:, kc, hc * 128:(hc + 1) * 128], hT[:, kc, sl], kc == 0, kc == 7) for kc in range(8)],
                                  ['ws5', 'hT'], [PAk[tb]])
                              CP('act', ub[:, hc, sl], PA[:, sl], [PAk[tb]], ['ub'])
                      fw.op('dve', lambda e: e.memset(car[:], 0.0), [], ['car'])
                      for hc in range(2):
                          for pc in range(NP_):
                              psl = slice(pc * LP, (pc + 1) * LP)
                              for gq in range(4):
                                  gp = hc * 4 + gq
                                  if True:
                                      TS('dve', s1[:], tau[:, 0:LP], theta[:, gp:gp + 1], None, ALU.mult, ALU.bypass, ['cst', 'pst'], ['s1'])
                                      sincos(s1[:], tki[:], tsin[:], tcos[:], s2[:], ['s1', 'tki', 'tsin', 'tcos', 's2'])
                                  MMG([(PA[:, 0:LP], lhb[:, gp, 0, :], ub[:, hc, psl], True, True),
                                       (PA[:, 512:512 + LP], lhb[:, gp, 1, :], ub[:, hc, psl], True, True)], ['lhb', 'ub'], ['PA0', 'PA1'])
                                  bur, bui = PA[:, 0:LP], PA[:, 512:512 + LP]
                                  TT('dve', s1[:], bur, tcos[:], ALU.mult, ['PA0', 'tcos'], ['s1'])
                                  TT('dve', s2[:], bui, tsin[:], ALU.mult, ['PA1', 'tsin'], ['s2'])
                                  TT('dve', swr[:], s1[:], s2[:], ALU.add, ['s1', 's2'], ['swr'])
                                  TT('dve', s1[:], bui, tcos[:], ALU.mult, ['PA1', 'tcos'], ['s1'])
                                  TT('dve', s2[:], bur, tsin[:], ALU.mult, ['PA0', 'tsin'], ['s2'])
                                  TT('dve', swi[:], s1[:], s2[:], ALU.subtract, ['s1', 's2'], ['swi'])
                                  rb_ = rmag[:, gp:gp + 1].to_broadcast([128, LP])
                                  SCAN(szr[:], rb_, swr[:], car[:, gp, 0:1], ['pst', 'swr', 'car'], ['szr'])
                                  SCAN(szi[:], rb_, swi[:], car[:, gp, 1:2], ['pst', 'swi', 'car'], ['szi'])
                                  TT('dve', s1[:], szr[:], tcos[:], ALU.mult, ['szr', 'tcos'], ['s1'])
                                  TT('dve', s2[:], szi[:], tsin[:], ALU.mult, ['szi', 'tsin'], ['s2'])
                                  TT('dve', swr[:], s1[:], s2[:], ALU.subtract, ['s1', 's2'], ['swr'])
                                  TT('dve', s1[:], szr[:], tsin[:], ALU.mult, ['szr', 'tsin'], ['s1'])
                                  TT('dve', s2[:], szi[:], tcos[:], ALU.mult, ['szi', 'tcos'], ['s2'])
                                  TT('dve', swi[:], s1[:], s2[:], ALU.add, ['s1', 's2'], ['swi'])
                                  CP('act', car[:, gp, 0:1], swr[:, LP - 1:LP], ['swr'], ['car'])
                                  CP('act', car[:, gp, 1:2], swi[:, LP - 1:LP], ['swi'], ['car'])
                                  CP('act', xrb[:], swr[:], ['swr'], ['xrb'])
                                  ACT(xib[:], swi[:], AF.Copy, ['swi'], ['xib'], scale=-1.0)
                                  MMG([(PB[:, 0:LP], crt[:, gp, :], xrb[:], gq == 0, False),
                                       (PB[:, 0:LP], cit[:, gp, :], xib[:], False, gq == 3)], ['crt', 'cit', 'xrb', 'xib'], ['PB0'])
                              STT(yb[:], ub[:, hc, psl], sm[:, O_S5D + hc:O_S5D + hc + 1], PB[:, 0:LP], ALU.mult, ALU.add,
                                  ['ub', 'sm', 'PB0'], ['yb'])
                              gelu_inplace('dve', yb[:], yb2[:], 'yb', 'yb2')
                              CP('act', zb[:, hc, psl], yb[:], ['yb'], ['zb'])
                      for oc in range(2):
                          for tb in range(4):
                              sl = slice(tb * 512, (tb + 1) * 512)
                              MMG([(PA[:, sl], wglu[:, kc, oc * 128:(oc + 1) * 128], zb[:, kc, sl], kc == 0, kc == 1) for kc in range(2)],
                                  ['wglu', 'zb'], [PAk[tb]])
                              ACT(yb[:], PA[:, sl], AF.Sigmoid, [PAk[tb]], ['yb'], bias=sm[:, O_BGL + oc:O_BGL + oc + 1])
                              TT('dve', yb[:], yb[:], zb[:, oc, sl], ALU.mult, ['yb', 'zb'], ['yb'])
                              TS('dve', yT[:, 4 + oc, sl], yb[:], sm[:, O_GBR + 4 + oc:O_GBR + 5 + oc], None, ALU.mult, ALU.bypass,
                                 ['yb', 'sm'], ['yT%d' % (4 + oc)])
                              ACT(yb2[:], yb[:], AF.Square, ['yb'], ['yb2'])
                              MMG([(PB[:, 2 * (tb * 4 + j):2 * (tb * 4 + j) + 2], yb2[:, j * 128:(j + 1) * 128], onesf[:, 0:2], True, True)
                                   for j in range(4)], ['yb2', 'cst'], ['PB0'])
                          CP('dve', ssb[:, oc, :], PB[:, 0:2 * NT].rearrange("p (i two) -> p i two", two=2)[:, :, 0], ['PB0'], ['ssb'])
                      TT('dve', rsd[:, 1, :], ssb[:, 0, :], ssb[:, 1, :], ALU.add, ['ssb'], ['rsd1'])
                      rstd_from_ss(rsd[:, 1, :], 256.0, 'rsd1')
                      barrier()
                  fw.ctx = actx
                  if stop == 's5':
                      fw.mute = True

                  with ExitStack() as sctx:
                      fw.ctx = sctx
                      stg = fw.sb("stg", [128, 8, 512], F32)
                      bm = fw.sb("bm", [128, 1024], F32)
                      gt1 = fw.sb("gt1", [128, 1024], F32)
                      wo = fw.sb("wo", [128, 8, 1024], BF16)
                      tmpo = fw.sb("tmpo", [128, 1024], F32)
                      wov = wout_d[l].rearrange("(kc p) n -> p kc n", p=128)
                      for kc in range(8):
                          fw.dma('pool', wo[:, kc, :], wov[:, kc, :], writes=['wo'])
                      mod_tile(l, 2, gt1, 'gt1', stg, bm)
                      for i in range(NT):
                          tsl = slice(i * 128, (i + 1) * 128)
                          for (P_, off, kcs, keys) in [(PA, 0, [0, 1, 2, 3], ['PA0', 'PA1']), (PA, 1024, [4, 5], ['PA2', 'PA3']),
                                                       (PB, 0, [6, 7], ['PB0', 'PB1'])]:
                              for nh in range(2):
                                  MMG([(P_[:, off + nh * 512:off + (nh + 1) * 512], yT[:, kc, tsl], wo[:, kc, nh * 512:(nh + 1) * 512],
                                        kc == kcs[0], kc == kcs[-1]) for kc in kcs],
                                      ['yT%d' % kc for kc in kcs] + ['wo'], [keys[nh]])
                          TS('dve', tmpo[:], PA[:, 0:1024], rsd[:, 0, i:i + 1], None, ALU.mult, ALU.bypass, ['PA0', 'PA1', 'rsd0'], ['tmpo'])
                          STT(tmpo[:], PA[:, 1024:2048], rsd[:, 1, i:i + 1], tmpo[:], ALU.mult, ALU.add, ['PA2', 'PA3', 'rsd1', 'tmpo'], ['tmpo'])
                          STT(tmpo[:], PB[:, 0:1024], rsd[:, 2, i:i + 1], tmpo[:], ALU.mult, ALU.add, ['PB0', 'PB1', 'rsd2', 'tmpo'], ['tmpo'])
                          TT('dve', tmpo[:], tmpo[:], gt1[:], ALU.mult, ['tmpo', 'gt1'], ['tmpo'])
                          TT('dve', xs[:, i, :], xs[:, i, :], tmpo[:], ALU.add, ['x%d' % i, 'tmpo'], ['x%d' % i])
                      barrier()
                  fw.ctx = octx
              barrier()
              if not do_peer:
                  continue
              with ExitStack() as bctx:
                  octx = fw.ctx
                  fw.ctx = bctx
                  A2 = fw.sb("A2", [128, 1024], F32)
                  B2 = fw.sb("B2", [128, 1024], F32)
                  gt2 = fw.sb("gt2", [128, 1024], F32)
                  with ExitStack() as mctx:
                      fw.ctx = mctx
                      stg = fw.sb("stg", [128, 8, 512], F32)
                      bm = fw.sb("bm", [128, 1024], F32)
                      gm = fw.sb("gm", [128, 1024], F32)
                      mod_tile(l, 3, B2, 'B2', stg, bm)
                      mod_tile(l, 4, A2, 'A2', stg, bm)
                      mod_tile(l, 5, gt2, 'gt2', stg, bm)
                      fw.dma('sync', gm[:], gffn_d[l:l + 1, :].to_broadcast([128, 1024]), writes=['gm'])
                      STT(A2[:], A2[:], 1.0, gm[:], ALU.add, ALU.mult, ['A2', 'gm'], ['A2'])
                      barrier()
                  fw.ctx = bctx
                  NB = 16
                  ss = fw.sb("ss", [128, NT], F32)
                  junk = fw.sb("junk", [128, 1024], BF16)
                  wq = fw.sb("wq", [128, 8, 2048], BF16)
                  skt = fw.sb("skt", [128, 2, 128], BF16)
                  h2 = fw.sb("h2", [128, 1024], F32)
                  h2b = fw.sb("h2b", [128, 1024], BF16)
                  h2T = fw.sb("h2T", [128, 8, 128], BF16)
                  qTb = fw.sb("qTb", [128, 16, 128], BF16)
                  big = fw.sb("big", [128, 2048], F32)
                  sc = big[:].rearrange("p (a b) -> p a b", a=16)
                  cand = big[:].rearrange("p (h c) -> p h c", h=8)
                  eq = big[:].rearrange("p (h k a) -> p h k a", h=8, k=16)
                  top = fw.sb("top", [128, 16, 16], F32)
                  tix = fw.sb("tix", [128, 16, 16], U32)
                  tixf = fw.sb("tixf", [128, 16, 16], F32)
                  best = fw.sb("best", [128, 8, 16], F32)
                  pos = fw.sb("pos", [128, 8, 16], U32)
                  posf = fw.sb("posf", [128, 8, 16], F32)
                  ai = fw.sb("ai", [128, 8, 16], I32)
                  af = fw.sb("af", [128, 8, 16], F32)
                  bf = fw.sb("bf", [128, 8, 16], F32)
                  isel = fw.sb("isel", [128, 8, 16], F32)
                  jsel = fw.sb("jsel", [128, 8, 16], F32)
                  eidx2 = fw.sb("eidx", [128, 2, 128], I32)
                  gat = fw.sb("gat", [128, 8, 16], F32)
                  gz_ = fw.sb("gz_", [128, 8], F32)
                  actp = fw.sb("actp", [128, 128], F32)
                  actt = fw.sb("actt", [128, 128], F32)
                  wgt = fw.sb("wgt", [128, 128], F32)
                  acc = fw.sb("acc", [128, 1024], F32)
                  jf = fw.sb("jf", [128, 1024], F32) if JF else acc
                  gb = [fw.sb("gb%d" % j, [128, 1024], BF16) for j in range(NB)]
                  gv = gb
                  gvc = gbc = [0]
                  NBV = NB
                  wqv = wq_d[l].rearrange("(kc p) n -> p kc n", p=128)
                  for kc in range(8):
                      fw.dma('pool', wq[:, kc, :], wqv[:, kc, :], writes=['wq'])
                  fw.dma('pool', skt[:], skt_d[l], writes=['skt'])
                  for i in range(NT):
                      ACT(junk[:], xs[:, i, :], AF.Square, ['x%d' % i], ['junk', 'ss'], accum=ss[:, i:i + 1])
                  rstd_from_ss(ss[:], float(D), 'ss')
                  PAb = PA[:, 0:512].bitcast(BF16)
                  topv = top[:].rearrange("p (h two) k -> p h two k", two=2)
                  tixv = tixf[:].rearrange("p (h two) k -> p h two k", two=2)
                  dg = [fw.sb("dg%d" % k, [128, 128], BF16) for k in range(4)]
                  ubk = ['ubd%d' % c for c in range(16)]
                  vbk = ['vbd%d' % c for c in range(16)]

                  def idx_a(i):
                      xk = 'x%d' % i
                      ek = 'eidx%d' % (i % 2)
                      eidx = eidx2[:, i % 2, :]
                      STT(jf[:], xs[:, i, :], ss[:, i:i + 1], A2[:], ALU.mult, ALU.mult, [xk, 'ss', 'A2'], ['acc', 'jf'])
                      TT('dve', h2[:], jf[:], B2[:], ALU.add, ['acc', 'jf', 'B2'], ['h2'])
                      CP('act', h2b[:], h2[:], ['h2'], ['h2b'])
                      TRG([(PAb[:, kc * 128:(kc + 1) * 128], h2b[:, kc * 128:(kc + 1) * 128]) for kc in range(8)],
                          identb[:], ['h2b', 'identb'], ['PA0'])
                      CP('act', h2T[:], PAb.rearrange("p (k t) -> p k t", k=8), ['PA0'], ['h2T'])
                      for half in range(2):
                          for hq2 in range(2):
                              hq = half * 2 + hq2
                              MMG([(PB[:, hq2 * 512 + j * 128:hq2 * 512 + (j + 1) * 128], wq[:, kc, (hq * 4 + j) * 128:(hq * 4 + j + 1) * 128], h2T[:, kc, :],
                                    kc == 0, kc == 7) for j in range(4) for kc in range(8)], ['wq', 'h2T'], [PBk[hq2]])
                          CP('act', qTb[:, half * 8:(half + 1) * 8, :].rearrange("p a b -> p (a b)"), PB[:, 0:1024], PBk[0:2], ['qTb'])
                      for hq in range(4):
                          MMG([(PA[:, hq * 512 + j * 128:hq * 512 + (j + 1) * 128], qTb[:, hq * 4 + j, :], skt[:, (hq * 4 + j) % 2, :], True, True)
                               for j in range(4)], ['qTb', 'skt'], [PAk[hq]])
                      CP('dve', big[:], PA[:, :], PAk, ['big'])

                  def idx_b(i):
                      ek = 'eidx%d' % (i % 2)
                      eidx = eidx2[:, i % 2, :]
                      for hp in range(16):
                          fw.op('dve', lambda e, hp=hp: e.max(out=top[:, hp, 0:8], in_=sc[:, hp, :]), ['big'], ['top'])
                          fw.op('dve', lambda e, hp=hp: e.max_index(out=tix[:, hp, 0:8], in_max=top[:, hp, 0:8], in_values=sc[:, hp, :]), ['big', 'top'], ['tix'])
                          fw.op('dve', lambda e, hp=hp: e.match_replace(out=sc[:, hp, :], in_to_replace=top[:, hp, 0:8], in_values=sc[:, hp, :], imm_value=-1e30),
                                ['big', 'top'], ['big'])
                          fw.op('dve', lambda e, hp=hp: e.max(out=top[:, hp, 8:16], in_=sc[:, hp, :]), ['big'], ['top'])
                          fw.op('dve', lambda e, hp=hp: e.max_index(out=tix[:, hp, 8:16], in_max=top[:, hp, 8:16], in_values=sc[:, hp, :]), ['big', 'top'], ['tix'])
                      CP('dve', tixf[:], tix[:], ['tix'], ['tixf'])
                      TT('dve', cand.rearrange("p h (a b) -> p h a b", a=16), topv[:, :, 0, :].unsqueeze(3).to_broadcast([128, 8, 16, 16]),
                         topv[:, :, 1, :].unsqueeze(2).to_broadcast([128, 8, 16, 16]), ALU.add, ['top'], ['big'])
                      for h in range(8):
                          fw.op('dve', lambda e, h=h: e.max(out=best[:, h, 0:8], in_=cand[:, h, :]), ['big'], ['best'])
                          fw.op('dve', lambda e, h=h: e.max_index(out=pos[:, h, 0:8], in_max=best[:, h, 0:8], in_values=cand[:, h, :]), ['big', 'best'], ['pos'])
                          fw.op('dve', lambda e, h=h: e.match_replace(out=cand[:, h, :], in_to_replace=best[:, h, 0:8], in_values=cand[:, h, :], imm_value=-1e30),
                                ['big', 'best'], ['big'])
                          fw.op('dve', lambda e, h=h: e.max(out=best[:, h, 8:16], in_=cand[:, h, :]), ['big'], ['best'])
                          fw.op('dve', lambda e, h=h: e.max_index(out=pos[:, h, 8:16], in_max=best[:, h, 8:16], in_values=cand[:, h, :]), ['big', 'best'], ['pos'])
                      CP('dve', posf[:], pos[:], ['pos'], ['posf'])
                      TS('dve', ai[:], posf[:], 1.0 / 16.0, -7.5 / 16.0, ALU.mult, ALU.add, ['posf'], ['ai'])
                      CP('dve', af[:], ai[:], ['ai'], ['af'])
                      STT(bf[:], af[:], -16.0, posf[:], ALU.mult, ALU.add, ['af', 'posf'], ['bf'])
                      io_b = io16.unsqueeze(1).unsqueeze(1).to_broadcast([128, 8, 16, 16])
                      for (src, tv, dst, dk) in [(af, 0, isel, 'isel'), (bf, 1, jsel, 'jsel')]:
                          TT('dve', eq, src[:].unsqueeze(3).to_broadcast([128, 8, 16, 16]), io_b, ALU.is_equal, ['af', 'bf', 'cst'], ['big'])
                          TT('dve', eq, eq, tixv[:, :, tv, :].unsqueeze(2).to_broadcast([128, 8, 16, 16]), ALU.mult, ['big', 'tixf'], ['big'])
                          fw.op('dve', lambda e, dst=dst: e.tensor_reduce(out=dst[:], in_=eq, axis=AX.X, op=ALU.add), ['big'], [dk])
                      STT(isel[:], isel[:], 128.0, jsel[:], ALU.mult, ALU.add, ['isel', 'jsel'], ['isel'])
                      CP('dve', eidx, isel[:].rearrange("p h k -> p (h k)"), ['isel'], [ek])

                  def u_phase(i):
                      ek = 'eidx%d' % (i % 2)
                      for n in range(128):
                          j = gbc[0] % NB
                          gbc[0] += 1
                          fw.dma('pool', gb[j][:], ub_d[l], reads=[ek] + ubk, writes=['gb%d' % j], noslotwait=(n >= 20 or i > 0),
                                 indirect=bass.IndirectOffsetOnAxis(ap=eidx2[:, i % 2, n:n + 1], axis=0))
                          fw.op('dve', lambda e, j=j, n=n: e.scalar_tensor_tensor(out=(jf[:] if JF else junk[:]), in0=gb[j][:], scalar=1.0, in1=h2[:],
                                                                                op0=ALU.mult, op1=ALU.mult, accum_out=actp[:, n:n + 1]),
                                ['gb%d' % j, 'h2'], ['junk', 'jf', 'actp'])
                      TT('dve', gat[:], best[:], best[:, :, 0:1].to_broadcast([128, 8, 16]), ALU.subtract, ['best'], ['gat'])
                      ACT(gat[:], gat[:], AF.Exp, ['gat'], ['gat'])
                      fw.op('dve', lambda e: e.tensor_reduce(out=gz_[:], in_=gat[:], axis=AX.X, op=ALU.add), ['gat'], ['gz_'])
                      fw.op('dve', lambda e: e.reciprocal(out=gz_[:], in_=gz_[:]), ['gz_'], ['gz_'])
                      TT('dve', gat[:], gat[:], gz_[:].unsqueeze(2).to_broadcast([128, 8, 16]), ALU.mult, ['gat', 'gz_'], ['gat'])

                      gelu_inplace('dve', actp[:], actt[:], 'actp', 'actt')
                      TT('dve', wgt[:], actp[:], gat[:].rearrange("p h k -> p (h k)"), ALU.mult, ['actp', 'gat'], ['wgt'])

                  def v_phase(i):
                      xk = 'x%d' % i
                      ek = 'eidx%d' % (i % 2)
                      for n in range(128):
                          j = gvc[0] % NBV
                          gvc[0] += 1
                          k = n % 4
                          fw.dma('pool', gv[j][:], vb_d[l], reads=[ek] + vbk, writes=['gb%d' % j], noslotwait=True,
                                 indirect=bass.IndirectOffsetOnAxis(ap=eidx2[:, i % 2, n:n + 1], axis=0))
                          fw.op('act', lambda e, k=k, n=n: e.activation(out=dg[k][:], in_=identf, func=AF.Copy, scale=wgt[:, n:n + 1]),
                                ['cst', 'wgt'], ['dg%d' % k])
                          MMG([(PB[:, 1024:1536], dg[k][:], gv[j][:, 0:512], n == 0, n == 127),
                               (PB[:, 1536:2048], dg[k][:], gv[j][:, 512:1024], n == 0, n == 127)],
                              ['dg%d' % k, 'gb%d' % j], ['PB2', 'PB3'])

                  def v_fin(i):
                      xk = 'x%d' % i
                      TT('dve', acc[:], PB[:, 1024:2048], gt2[:], ALU.mult, ['PB2', 'PB3', 'gt2'], ['acc'])
                      TT('dve', xs[:, i, :], xs[:, i, :], acc[:], ALU.add, [xk, 'acc'], [xk])

                  idx_a(0)
                  idx_b(0)
                  for i in range(NT):
                      u_phase(i)
                      if i + 1 < NT:
                          idx_a(i + 1)
                      v_phase(i)
                      if i + 1 < NT:
                          idx_b(i + 1)
                      v_fin(i)
                  barrier()
                  fw.ctx = octx
              barrier()
          except _Stop:
            fw.ctx = ctx
            barrier()
            break

        fw.mute = False
        barrier()
        with ExitStack() as fctx:
            fw.ctx = fctx
            gf = fw.sb("gf", [128, 1024], F32)
            ss = fw.sb("ss", [128, NT], F32)
            junk = fw.sb("junk", [128, 1024], BF16)
            ob = [fw.sb("ob%d" % j, [128, 1024], F32) for j in range(2)]
            fw.dma('sync', gf[:], gfin_d[0:1, :].to_broadcast([128, 1024]), writes=['gf'])
            for i in range(NT):
                ACT(junk[:], xs[:, i, :], AF.Square, ['x%d' % i], ['junk', 'ss'], accum=ss[:, i:i + 1])
            rstd_from_ss(ss[:], float(D), 'ss')
            outk = []
            for i in range(NT):
                j = i % 2
                STT(ob[j][:], xs[:, i, :], ss[:, i:i + 1], gf[:], ALU.mult, ALU.mult, ['x%d' % i, 'ss', 'gf'], ['ob%d' % j])
                fw.dma('sync', out_d[i * 128:(i + 1) * 128, :], ob[j][:], reads=['ob%d' % j], writes=['out%d' % i])
                outk.append('out%d' % i)
            fw.finish(outk)
    return nc


def _consts():
    c = np.zeros((128, CSTW), np.float32)
    c[:, 0:128] = np.eye(128, dtype=np.float32)
    c[:, 128:256] = 1.0
    s = np.arange(128)[:, None]
    t = np.arange(128)[None, :]
    c[:, 256:384] = ((s // 32 == t // 32) & (s <= t)).astype(np.float32)
    c[:, 384] = np.pi / 2
    c[:, 385] = EPS
    c[:, 386] = 1.0
    c[:, 400:416] = np.arange(16, dtype=np.float32)[None, :]
    c[:, 416:928] = np.arange(1, 513, dtype=np.float32)[None, :]
    rm = np.ones(2048, np.float32)
    rm[::32] = 0.0
    c[:, 928:928 + 2048] = rm[None, :]
    return c


def _layouts(inp):
    f = lambda k: np.asarray(inp[k], dtype=np.float32)
    small = np.zeros((4, 128, NSMALL), np.float32)
    crt = np.zeros((4, 128, 8, 128), np.float32)
    cit = np.zeros((4, 128, 8, 128), np.float32)
    wa = np.zeros((4, 128, 2, 128), np.float32)
    wx = np.zeros((4, 128, 2, 128), np.float32)
    lbl = f('hgrn_lb_logits').reshape(4, 4, 128).transpose(2, 1, 0).reshape(128, 16)
    st = lambda a: a.reshape(8, 2, 64).transpose(1, 2, 0).reshape(128, 8)
    rep = lambda a: np.broadcast_to(a.reshape(2, 8, 1, 64), (2, 8, 16, 64)).transpose(1, 2, 0, 3).reshape(128, 128)
    col2 = lambda v: v.reshape(2, 128).T
    rmk = np.zeros((128, 8), np.float32)
    for gl in range(8):
        rmk[gl * 16:(gl + 1) * 16, gl] = 1.0
    for l in range(4):
        small[l, :, O_LBL:O_LBL + 16] = lbl
        small[l, :, O_ARS:O_ARS + 8] = st(f('s5_a_re')[l])
        small[l, :, O_AIS:O_AIS + 8] = st(f('s5_a_im')[l])
        ld = f('s5_log_dt')[l]
        small[l, :, O_LDS:O_LDS + 8] = st(np.broadcast_to(ld[:, None], (16, 64)).copy())
        small[l, :, O_ARR:O_ARR + 128] = rep(f('s5_a_re')[l])
        small[l, :, O_AIR:O_AIR + 128] = rep(f('s5_a_im')[l])
        small[l, :, O_BTR:O_BTR + 128] = f('s5_b_re')[l].reshape(2, 8, 64, 16).transpose(1, 3, 0, 2).reshape(128, 128)
        small[l, :, O_BTI:O_BTI + 128] = f('s5_b_im')[l].reshape(2, 8, 64, 16).transpose(1, 3, 0, 2).reshape(128, 128)
        small[l, :, O_LDR:O_LDR + 2] = np.broadcast_to(ld.reshape(2, 8, 1), (2, 8, 16)).transpose(1, 2, 0).reshape(128, 2)
        small[l, :, O_S5D:O_S5D + 2] = col2(f('s5_d')[l])
        small[l, :, O_BGL:O_BGL + 2] = col2(f('s5_b_glu')[l])
        small[l, :, O_CW:O_CW + 8] = f('lru_conv_w')[l].reshape(4, 2, 128).transpose(2, 1, 0).reshape(128, 8)
        small[l, :, O_CB:O_CB + 2] = col2(f('lru_conv_b')[l])
        small[l, :, O_BA:O_BA + 2] = col2(f('lru_b_a')[l])
        small[l, :, O_BX:O_BX + 2] = col2(f('lru_b_x')[l])
        small[l, :, O_LAM:O_LAM + 2] = col2(f('lru_lambda')[l])
        small[l, :, O_GBR:O_GBR + 8] = f('g_branch')[l].reshape(8, 128).T
        small[l, :, O_RMK:O_RMK + 8] = rmk
        for g in range(16):
            gp, gi, gl = g // 2, g % 2, g % 8
            crt[l, gi * 64:(gi + 1) * 64, gp, gl * 16:(gl + 1) * 16] = f('s5_c_re')[l, g].T
            cit[l, gi * 64:(gi + 1) * 64, gp, gl * 16:(gl + 1) * 16] = f('s5_c_im')[l, g].T
        for h in range(4):
            hc, o = h // 2, (h % 2) * 64
            wa[l, o:o + 64, hc, o:o + 64] = f('lru_w_a')[l, h]
            wx[l, o:o + 64, hc, o:o + 64] = f('lru_w_x')[l, h]
    skt = np.ascontiguousarray(f('peer_sub_keys').transpose(0, 3, 1, 2))
    return dict(small=small, crt=crt, cit=cit, wa_bd=wa, wx_bd=wx, skt=skt)


def make_in_maps(inp, cores):
    f = lambda k: np.ascontiguousarray(np.asarray(inp[k], dtype=np.float32))
    lay = _layouts(inp)
    shared = dict(w_mod=f('w_mod'), b_mod=f('b_mod'), g_mix=f('g_mix'), w_in=f('w_in'), w_glu=f('s5_w_glu'),
                  g_branch=f('g_branch'), w_out=f('w_out'), g_ffn=f('g_ffn'), peer_w_q=f('peer_w_q'),
                  g_final=f('g_final').reshape(1, D), consts=_consts())
    shared.update(lay)
    pu, pv = f('peer_u'), f('peer_v')
    for l_ in range(4):
        shared['peer_u%d' % l_] = pu[l_]
        shared['peer_v%d' % l_] = pv[l_]
    x = f('x')
    c = f('c')
    maps = []
    for b in cores:
        m = dict(shared)
        m['x'] = np.ascontiguousarray(x[b])
        m['c'] = np.ascontiguousarray(c[b].reshape(128, 8))
        maps.append(m)
    return maps


def kernel(**inputs):
    nc = build()
    maps = make_in_maps(inputs, list(range(8)))
    res = run_bass_kernel_spmd(nc, maps, core_ids=list(range(8)))
    return np.stack([np.asarray(r['out'], dtype=np.float32) for r in res.results], axis=0)
```

```python
import numpy as np
from contextlib import ExitStack
import concourse.bass as bass
import concourse.mybir as mybir
from concourse.bass_utils import run_bass_kernel_spmd

F32 = mybir.dt.float32
BF16 = mybir.dt.bfloat16
U32 = mybir.dt.uint32
I32 = mybir.dt.int32
AF = mybir.ActivationFunctionType
ALU = mybir.AluOpType
AX = mybir.AxisListType
F32R = mybir.dt.float32r

ENG = ['sync', 'act', 'pool', 'pe', 'dve']


class FW:
    def __init__(self, nc, ctx, slots=None):
        self.nc = nc
        self.ctx = ctx
        self.q = {e: [] for e in ENG}
        self.sems = {}
        self.cnt = {}
        for e in ENG:
            self.sems[e] = ctx.enter_context(nc.semaphore('s_' + e))
            self.cnt[e] = 0
        slots = slots or {'sync': 8, 'act': 4, 'pool': 8}
        self.slots = {}
        self.rr = {}
        for e, n in slots.items():
            self.slots[e] = []
            self.rr[e] = 0
            for i in range(n):
                nm = 'd_%s%d' % (e, i)
                self.sems[nm] = ctx.enter_context(nc.semaphore(nm))
                self.cnt[nm] = 0
                self.slots[e].append(nm)
        self.seen = {e: {} for e in ENG}
        self.lastw = {}
        self.readers = {}
        self.nins = {e: 0 for e in ENG}

    def sb(self, name, shape, dtype):
        self.uid = getattr(self, 'uid', 0) + 1
        if not hasattr(self, 'names'):
            self.names = {}
        self.names.setdefault(name, []).append('%s_u%d' % (name, self.uid))
        return self.ctx.enter_context(self.nc.sbuf_tensor('%s_u%d' % (name, self.uid), list(shape), dtype))

    def ps(self, name, shape, dtype):
        return self.ctx.enter_context(self.nc.psum_tensor(name, list(shape), dtype))

    def _wait(self, eng, s, v):
        if self.mute:
            return
        if self.seen[eng].get(s, 0) < v:
            self.seen[eng][s] = v
            sem = self.sems[s]
            self.q[eng].append(lambda e, sem=sem, v=v: e.wait_ge(sem, v))

    def _wait_deps(self, eng, reads, writes):
        deps = {}
        for k in list(reads) + list(writes):
            ev = self.lastw.get(k)
            if ev is not None and deps.get(ev[0], 0) < ev[1]:
                deps[ev[0]] = ev[1]
        for k in writes:
            for s, v in self.readers.get(k, {}).items():
                if deps.get(s, 0) < v:
                    deps[s] = v
        for s, v in deps.items():
            self._wait(eng, s, v)

    def _record(self, ev, reads, writes):
        s, v = ev
        ws = set(writes)
        for k in ws:
            self.lastw[k] = ev
            self.readers[k] = {}
        for k in reads:
            if k in ws:
                continue
            r = self.readers.setdefault(k, {})
            if r.get(s, 0) < v:
                r[s] = v

    mute = False

    def op(self, eng, fn, reads=(), writes=()):
        if self.mute:
            return
        self._wait_deps(eng, reads, writes)
        self.cnt[eng] += 1
        v = self.cnt[eng]
        sem = self.sems[eng]
        self.q[eng].append(lambda e, fn=fn, sem=sem: fn(e).then_inc(sem, 1))
        self.nins[eng] += 1
        self._record((eng, v), reads, writes)

    def dma(self, eng, out, in_, reads=(), writes=(), indirect=None, **kw):
        if self.mute:
            return
        self._wait_deps(eng, reads, writes)
        sl = self.slots[eng]
        slot = sl[self.rr[eng] % len(sl)]
        self.rr[eng] += 1
        if self.cnt[slot] > 0 and not kw.pop('noslotwait', False):
            self._wait(eng, slot, self.cnt[slot])
        kw.pop('noslotwait', None)
        self.cnt[slot] += 16
        v = self.cnt[slot]
        sem = self.sems[slot]
        if indirect is None:
            self.q[eng].append(lambda e, out=out, in_=in_, sem=sem, kw=kw:
                               e.dma_start(out=out, in_=in_, **kw).then_inc(sem, 16))
        else:
            self.q[eng].append(lambda e, out=out, in_=in_, sem=sem, ind=indirect, kw=kw:
                               e.indirect_dma_start(out=out, out_offset=None, in_=in_, in_offset=ind, **kw).then_inc(sem, 16))
        self.nins[eng] += 1
        self._record((slot, v), reads, writes)

    def finish(self, out_keys):
        self._wait_deps('sync', out_keys, [])
        q = self.q
        with self.nc.Block() as block:
            @block.sync
            def _(e):
                for f in q['sync']:
                    f(e)

            @block.scalar
            def _(e):
                for f in q['act']:
                    f(e)

            @block.gpsimd
            def _(e):
                for f in q['pool']:
                    f(e)

            @block.tensor
            def _(e):
                for f in q['pe']:
                    f(e)

            @block.vector
            def _(e):
                for f in q['dve']:
                    f(e)


import os
PIPE = 1
JF = 1
T = 2048
D = 1024
NT = 16
EPS = 1e-6
TWO_PI = float(2 * np.pi)
CW1 = 6.28125
CW2 = 0.0019353071795864769
GK = 1.5957691216057308

CSTW = 384 + 32 + 512 + 2048
NSMALL = 16 + 24 + 128 * 4 + 2 + 2 + 2 + 8 + 2 + 2 + 2 + 2 + 8 + 8
O_LBL = 0
O_ARS = 16
O_AIS = 24
O_LDS = 32
O_ARR = 40
O_AIR = 168
O_BTR = 296
O_BTI = 424
O_LDR = 552
O_S5D = 554
O_BGL = 556
O_CW = 558
O_CB = 566
O_BA = 568
O_BX = 570
O_LAM = 572
O_GBR = 574
O_RMK = 582


class _Stop(Exception):
    pass


def build(n_layers=4, dbg=None, do_peer=True, stop=None):
    nc = bass.Bass("TRN2", target_bir_lowering=False)

    def din(name, shape, dt=F32):
        return nc.dram_tensor(name, list(shape), dt, kind="ExternalInput").ap()

    x_d = din("x", [T, D])
    c_d = din("c", [128, 8])
    wmod_d = din("w_mod", [4, D, 6 * D])
    bmod_d = din("b_mod", [4, 6 * D])
    gmix_d = din("g_mix", [4, D])
    win_d = din("w_in", [4, D, 2816])
    small_d = din("small", [4, 128, NSMALL])
    crt_d = din("crt", [4, 128, 8, 128])
    cit_d = din("cit", [4, 128, 8, 128])
    wglu_d = din("w_glu", [4, 256, 256])
    wa_d = din("wa_bd", [4, 128, 2, 128])
    wx_d = din("wx_bd", [4, 128, 2, 128])
    gbrow_d = din("g_branch", [4, D])
    wout_d = din("w_out", [4, D, D])
    gffn_d = din("g_ffn", [4, D])
    wq_d = din("peer_w_q", [4, D, 2048])
    skt_d = din("skt", [4, 128, 2, 128])
    pu_d = [din("peer_u%d" % l_, [16384, D]) for l_ in range(4)]
    pv_d = [din("peer_v%d" % l_, [16384, D]) for l_ in range(4)]
    gfin_d = din("g_final", [1, D])
    consts_d = din("consts", [128, CSTW])
    out_d = nc.dram_tensor("out", [T, D], F32, kind="ExternalOutput").ap()
    ub_d = [nc.dram_tensor("ub_scr%d" % l_, [16384, D], BF16, kind="Internal").ap() for l_ in range(4)]
    vb_d = [nc.dram_tensor("vb_scr%d" % l_, [16384, D], BF16, kind="Internal").ap() for l_ in range(4)]

    with ExitStack() as ctx:
        fw = FW(nc, ctx, slots={'sync': 8, 'act': 2, 'pool': 20})
        build.fw = fw

        def TS(eng, out, in0, s1, s2, op0, op1, r, w):
            fw.op(eng, lambda e: e.tensor_scalar(out=out, in0=in0, scalar1=s1, scalar2=s2, op0=op0, op1=op1), r, w)

        def TT(eng, out, in0, in1, op, r, w):
            fw.op(eng, lambda e: e.tensor_tensor(out=out, in0=in0, in1=in1, op=op), r, w)

        def STT(out, in0, scalar, in1, op0, op1, r, w):
            fw.op('dve', lambda e: e.scalar_tensor_tensor(out=out, in0=in0, scalar=scalar, in1=in1, op0=op0, op1=op1), r, w)

        def ACT(out, in_, func, r, w, scale=1.0, bias=0.0, accum=None):
            r = list(r) + ['cst', 'sm']
            if accum is None:
                fw.op('act', lambda e: e.activation(out=out, in_=in_, func=func, scale=scale, bias=bias), r, w)
            else:
                fw.op('act', lambda e: e.activation(out=out, in_=in_, func=func, scale=scale, bias=bias, accum_out=accum), r, w)

        def CP(eng, out, in_, r, w):
            if eng == 'act':
                fw.op(eng, lambda e: e.activation(out=out, in_=in_, func=AF.Copy), r, w)
            else:
                fw.op(eng, lambda e: e.tensor_copy(out=out, in_=in_), r, w)

        def SCAN(out, d0, d1, init, r, w):
            fw.op('dve', lambda e: e.tensor_tensor_scan(out=out, data0=d0, data1=d1, initial=init, op0=ALU.mult, op1=ALU.add), r, w)

        def MMG(mms, r, w):
            def f(e, mms=mms):
                ins = None
                for mm in mms:
                    (o, l, rh, st, sp) = mm[:5]
                    if len(mm) > 5:
                        ins = e.matmul(o, l, rh, start=st, stop=sp, skip_group_check=True)
                    else:
                        ins = e.matmul(o, l, rh, start=st, stop=sp)
                return ins
            fw.op('pe', f, r, w)

        def TRG(trs, ident, r, w):
            def f(e, trs=trs):
                ins = None
                for (o, i) in trs:
                    ins = e.transpose(o, i, ident)
                return ins
            fw.op('pe', f, r, w)

        def barrier():
            allsems = list(fw.cnt.items())
            for e in ENG:
                for s, v in allsems:
                    if v > 0:
                        fw._wait(e, s, v)

        xs = fw.sb("xs", [128, NT, D], F32)
        cst = fw.sb("cst", [128, 928], F32)
        identf = cst[:, 0:128]
        onesf = cst[:, 128:256]
        maskbd = cst[:, 256:384]
        halfpi = cst[:, 384:385]
        epscol = cst[:, 385:386]
        onecol = cst[:, 386:387]
        io16 = cst[:, 400:416]
        tau = cst[:, 416:416 + 512]
        rmask_t = fw.sb("rmask", [128, 2048], BF16)
        rmask = rmask_t[:]
        identb = fw.sb("identb", [128, 128], BF16)
        condr = fw.sb("condr", [128, 8, 128], F32)
        cond = fw.sb("cond", [128, 8], F32)
        PA = fw.ps("PA", [128, 2048], F32)
        PB = fw.ps("PB", [128, 2048], F32)
        PAk = ['PA0', 'PA1', 'PA2', 'PA3']
        PBk = ['PB0', 'PB1', 'PB2', 'PB3']

        fw.dma('sync', cst[:, 0:928], consts_d[:, 0:928], writes=['cst'])
        fw.dma('pool', rmask_t[:], consts_d[:, 928:928 + 2048], writes=['cst'])
        for i in range(NT):
            fw.dma('sync', xs[:, i, :], x_d[i * 128:(i + 1) * 128, :], writes=['x%d' % i])
        fw.dma('sync', cond[:], c_d[:, :], writes=['cond'])
        CP('dve', identb[:], identf, ['cst'], ['identb'])
        ACT(cond[:], cond[:], AF.Silu, ['cond'], ['cond'])
        CP('dve', condr[:], cond[:, :].unsqueeze(2).to_broadcast([128, 8, 128]), ['cond'], ['condr'])

        def mod_tile(l, j, out_tile, okey, stg, bm):
            wv = wmod_d[l].rearrange("(p kc) n -> p kc n", kc=8)
            fw.dma('sync', bm[:], bmod_d[l:l + 1, j * 1024:(j + 1) * 1024].to_broadcast([128, 1024]), writes=['bm'])
            stgs = stg if isinstance(stg, (list, tuple)) else [stg]
            for half in range(2):
                c0 = j * 1024 + half * 512
                sg = stgs[half % len(stgs)]
                sk = 'stg%d' % (half % len(stgs))
                pk = ['PA0', 'PA1'][half]
                po = PA[:, half * 512:(half + 1) * 512]
                fw.dma('sync', sg[:], wv[:, :, c0:c0 + 512], writes=[sk])
                MMG([(po, condr[:, kc, :], sg[:, kc, :], kc == 0, kc == 7) for kc in range(8)],
                    ['condr', sk], [pk])
                TT('dve', out_tile[:, half * 512:(half + 1) * 512], po, bm[:, half * 512:(half + 1) * 512], ALU.add,
                   [pk, 'bm'], [okey])

        def gelu_inplace(eng_t, t, tmp, key, tkey):
            ACT(tmp, t, AF.Square, [key], [tkey])
            TS('dve', tmp, tmp, 0.044715, 1.0, ALU.mult, ALU.add, [tkey], [tkey])
            TT('dve', tmp, tmp, t, ALU.mult, [tkey, key], [tkey])
            ACT(tmp, tmp, AF.Sigmoid, [tkey], [tkey], scale=GK)
            TT('dve', t, t, tmp, ALU.mult, [key, tkey], [key])

        def rstd_from_ss(ss_ap, n, key):
            ACT(ss_ap, ss_ap, AF.Sqrt, [key], [key], scale=1.0 / n, bias=epscol)
            fw.op('dve', lambda e: e.reciprocal(out=ss_ap, in_=ss_ap), [key], [key])

        def norm_to_T(A, B, hT, hkeys, keep=None):
            pass

        for l in range(n_layers):
          try:
              with ExitStack() as actx:
                  octx = fw.ctx
                  fw.ctx = actx
                  hT = fw.sb("hT", [128, 8, T], BF16)
                  yT = fw.sb("yT", [128, 8, T], BF16)
                  sm = fw.sb("sm", [128, NSMALL], F32)
                  rsd = fw.sb("rsd", [128, 3, NT], F32)
                  fw.dma('sync', sm[:], small_d[l], writes=['sm'])
                  with ExitStack() as sctx:
                      fw.ctx = sctx
                      stg = [fw.sb("stg", [128, 8, 512], F32), fw.sb("stgb", [128, 8, 512], F32)]
                      bm = fw.sb("bm", [128, 1024], F32)
                      A1 = fw.sb("A1", [128, 1024], F32)
                      B1 = fw.sb("B1", [128, 1024], F32)
                      gm = fw.sb("gm", [128, 1024], F32)
                      ss = fw.sb("ss", [128, NT], F32)
                      junk = fw.sb("junk", [128, 1024], BF16)
                      tmpf = fw.sb("tmpf", [128, 1024], F32)
                      hb = fw.sb("hb", [128, 1024], BF16)
                      mod_tile(l, 0, B1, 'B1', stg, bm)
                      mod_tile(l, 1, A1, 'A1', stg, bm)
                      fw.dma('sync', gm[:], gmix_d[l:l + 1, :].to_broadcast([128, 1024]), writes=['gm'])
                      STT(A1[:], A1[:], 1.0, gm[:], ALU.add, ALU.mult, ['A1', 'gm'], ['A1'])
                      for i in range(NT):
                          ACT(junk[:], xs[:, i, :], AF.Square, ['x%d' % i], ['junk', 'ss'], accum=ss[:, i:i + 1])
                      rstd_from_ss(ss[:], float(D), 'ss')
                      PAb = PA[:, 0:512].bitcast(BF16)
                      for i in range(NT):
                          STT(tmpf[:], xs[:, i, :], ss[:, i:i + 1], A1[:], ALU.mult, ALU.mult, ['x%d' % i, 'ss', 'A1'], ['tmpf'])
                          TT('dve', hb[:], tmpf[:], B1[:], ALU.add, ['tmpf', 'B1'], ['hb'])
                          TRG([(PAb[:, kc * 128:(kc + 1) * 128], hb[:, kc * 128:(kc + 1) * 128]) for kc in range(8)],
                              identb[:], ['hb', 'identb'], ['PA0'])
                          CP('act', hT[:, :, i * 128:(i + 1) * 128], PAb.rearrange("p (k t) -> p k t", k=8), ['PA0'], ['hT'])
                      barrier()
                  fw.ctx = actx
                  if stop == 'norm':
                      fw.mute = True
                  lbt = fw.sb("lbt", [128, 16], F32)
                  lbz = fw.sb("lbz", [128, 4], F32)
                  lb = fw.sb("lb", [128, 4], F32)
                  oml = fw.sb("oml", [128, 4], F32)
                  ACT(lbt[:], sm[:, O_LBL:O_LBL + 16], AF.Exp, ['sm'], ['lbt'])
                  lbv = lbt[:].rearrange("p (h l) -> p h l", h=4)
                  fw.op('dve', lambda e: e.tensor_reduce(out=lbz[:], in_=lbv, op=ALU.add, axis=AX.X), ['lbt'], ['lbz'])
                  fw.op('dve', lambda e: e.reciprocal(out=lbz[:], in_=lbz[:]), ['lbz'], ['lbz'])
                  fw.op('dve', lambda e: e.memset(lb[:], 0.0), [], ['lb'])
                  for j in range(1, l + 1):
                      TT('dve', lb[:], lb[:], lbv[:, :, j], ALU.add, ['lb', 'lbt'], ['lb'])
                  TT('dve', lb[:], lb[:], lbz[:], ALU.mult, ['lb', 'lbz'], ['lb'])
                  TS('dve', oml[:], lb[:], -1.0, 1.0, ALU.mult, ALU.add, ['lb'], ['oml'])
                  winv = win_d[l].rearrange("(kc p) n -> p kc n", p=128)

                  with ExitStack() as sctx:
                      fw.ctx = sctx
                      wh = fw.sb("wh", [128, 8, 512], BF16)
                      t1 = fw.sb("t1", [128, T], F32)
                      t2 = fw.sb("t2", [128, T], F32)
                      t3 = fw.sb("t3", [128, T], F32)
                      kt = fw.sb("kt", [128, T], BF16)
                      kh = fw.sb("kh", [128, T], BF16)
                      qt = fw.sb("qt", [128, T], BF16)
                      khk = fw.sb("khk", [128, NT, 128], BF16)
                      khkz = fw.sb("khkz", [128, NT, 128], BF16)
                      qz = fw.sb("qz", [128, T], BF16)
                      fw.op('dve', lambda e: e.memset(qz[:], 0.0), [], ['qz'])
                      zer = fw.sb("zer", [128, 128], BF16)
                      fw.op('dve', lambda e: e.memset(zer[:], 0.0), [], ['zer'])
                      el = fw.sb("el", [128, 64], F32)
                      S = fw.sb("S", [128, 128], F32)
                      Sb = fw.sb("Sb", [128, 128], BF16)
                      vb = fw.sb("vb", [128, 128], BF16)
                      gs = fw.sb("gs", [128, 128], F32)
                      scm = fw.sb("scm", [128, 128], BF16)
                      yh = fw.sb("yh", [128, 128], F32)
                      yhb = fw.sb("yhb", [128, 128], BF16)
                      jk = fw.sb("jk", [128, 128], F32)
                      ssh = fw.sb("ssh", [128, 2], F32)
                      ssa = fw.sb("ssa", [128, 4, NT], F32)
                      gbb = fw.sb("gbb", [128, 512], F32)
                      fw.dma('sync', gbb[:], gbrow_d[l:l + 1, 0:512].to_broadcast([128, 512]), writes=['gbb'])
                      PAb = PA[:, 0:1024].bitcast(BF16)
                      for h in range(4):
                          for j, c0 in enumerate([h * 128, 512 + h * 128, 1024 + h * 128, 1536 + h * 128]):
                              fw.dma('pool', wh[:, :, j * 128:(j + 1) * 128], winv[:, :, c0:c0 + 128], writes=['wh'])
                          if do_peer:
                              for (src, dst, key) in [(pu_d[l], ub_d[l], 'ubd'), (pv_d[l], vb_d[l], 'vbd')]:
                                  for c in range(4 * h, 4 * h + 4):
                                      fw.dma('pool', dst[c * 1024:(c + 1) * 1024, :].rearrange("(p r) d -> p r d", p=128),
                                             src[c * 1024:(c + 1) * 1024, :].rearrange("(p r) d -> p r d", p=128), writes=[key + str(c)])
                          for tb in range(4):
                              MMG([(PA[:, tb * 512:(tb + 1) * 512], wh[:, kc, 128:256], hT[:, kc, tb * 512:(tb + 1) * 512], kc == 0, kc == 7)
                                   for kc in range(8)], ['wh', 'hT'], [PAk[tb]])
                              ACT(t1[:, tb * 512:(tb + 1) * 512], PA[:, tb * 512:(tb + 1) * 512], AF.Sigmoid, [PAk[tb]], ['t1'])
                          TS('dve', t1[:], t1[:], oml[:, h:h + 1], lb[:, h:h + 1], ALU.mult, ALU.add, ['t1', 'oml', 'lb'], ['t1'])
                          if stop == 'hg_a':
                              fw.mute = True
                          ACT(t2[:], t1[:], AF.Ln, ['t1'], ['t2'])
                          SCAN(t3[:], rmask, t2[:], 0.0, ['cst', 't2'], ['t3'])
                          if stop == 'hg_b':
                              fw.mute = True
                          TS('dve', t1[:], t1[:], -1.0, 1.0, ALU.mult, ALU.add, ['t1'], ['t1'])
                          ACT(t2[:], t3[:], AF.Exp, ['t3'], ['t2'], scale=-1.0)
                          TT('dve', kt[:], t1[:], t2[:], ALU.mult, ['t1', 't2'], ['kt'])
                          b3 = t3[:].rearrange("p (c s) -> p c s", s=32)
                          TT('dve', t2[:].rearrange("p (c s) -> p c s", s=32), b3[:, :, 31:32].to_broadcast([128, 64, 32]), b3, ALU.subtract,
                             ['t3'], ['t2'])
                          ACT(t2[:], t2[:], AF.Exp, ['t2'], ['t2'])
                          TT('dve', kh[:], t1[:], t2[:], ALU.mult, ['t1', 't2'], ['kh'])
                          ACT(t3[:], t3[:], AF.Exp, ['t3'], ['t3'])
                          CP('dve', el[:], t3[:].rearrange("p (c s) -> p c s", s=32)[:, :, 31], ['t3'], ['el'])
                          if stop == 'hg_c':
                              fw.mute = True
                          for tb in range(4):
                              MMG([(PB[:, tb * 512:(tb + 1) * 512], wh[:, kc, 0:128], hT[:, kc, tb * 512:(tb + 1) * 512], kc == 0, kc == 7)
                                   for kc in range(8)], ['wh', 'hT'], [PBk[tb]])
                              ACT(t1[:, tb * 512:(tb + 1) * 512], PB[:, tb * 512:(tb + 1) * 512], AF.Silu, [PBk[tb]], ['t1'])
                          TT('dve', qt[:], t1[:], t3[:], ALU.mult, ['t1', 't3'], ['qt'])
                          if stop == 'hg_c1':
                              fw.mute = True
                          CP('dve', qz[:].rearrange("p (i t) -> p i t", t=128)[:, :, 96:128], qt[:].rearrange("p (i t) -> p i t", t=128)[:, :, 96:128],
                             ['qt'], ['qz'])
                          if stop == 'hg_c2':
                              fw.mute = True
                          for half in range(2):
                              TRG([(PAb[:, j * 128:(j + 1) * 128], kh[:, (half * 8 + j) * 128:(half * 8 + j + 1) * 128]) for j in range(8)],
                                  identb[:], ['kh', 'identb'], ['PA0'])
                              CP('act', khk[:, half * 8:(half + 1) * 8, :], PAb[:, 0:1024].rearrange("p (j d) -> p j d", j=8), ['PA0'], ['khk'])
                              if stop == 'hg_c3':
                                  fw.mute = True
                              CP('act', khkz[64:128, half * 8:(half + 1) * 8, :], PAb[64:128, 0:1024].rearrange("p (j d) -> p j d", j=8), ['PA0'], ['khkz'])
                              if stop == 'hg_c4':
                                  fw.mute = True
                              fw.op('dve', lambda e, half=half: e.memset(khkz[64:96, half * 8:(half + 1) * 8, :], 0.0), [], ['khkz'])
                              if stop == 'hg_c5':
                                  fw.mute = True
                          fw.op('dve', lambda e: e.memset(S[:], 0.0), [], ['S'])
                          if stop == 'hg_c7':
                              fw.mute = True
                          fw.op('dve', lambda e: e.memset(Sb[:], 0.0), [], ['Sb'])
                          if stop == 'hg_d':
                              fw.mute = True
                          for i in range(NT):
                              tsl = slice(i * 128, (i + 1) * 128)
                              MMG([(PB[:, 0:256], hT[:, kc, tsl], wh[:, kc, 256:512], kc == 0, kc == 7) for kc in range(8)],
                                  ['hT', 'wh'], ['PB0'])
                              CP('act', vb[:], PB[:, 0:128], ['PB0'], ['vb'])
                              ACT(gs[:], PB[:, 128:256], AF.Silu, ['PB0'], ['gs'])
                              MMG([(PB[:, 512:640], kt[:, tsl], qt[:, tsl], True, True)], ['kt', 'qt'], ['PB1'])
                              TT('dve', scm[:], PB[:, 512:640], maskbd, ALU.mult, ['PB1', 'cst'], ['scm'])
                              if stop == 'hg_e':
                                  fw.mute = True
                              MMG([(PB[:, 1024:1152], scm[:], vb[:], True, False)], ['scm', 'vb'], ['PB2'])
                              for j in range(4):
                                  rs_ = slice(32 * j, 32 * j + 32)
                                  if j < 3:
                                      MMG([(PB[rs_, 1024:1152], qt[:, i * 128 + 32 * j:i * 128 + 32 * j + 32], Sb[:], False, False, 1)],
                                          ['qt', 'Sb'], ['PB2'])
                                      MMG([(PB[:, 1536:1664], khk[rs_, i, :], vb[rs_, :], True, True)], ['khk', 'vb'], ['PB3'])
                                  else:
                                      MMG([(PB[64:128, 1024:1152], qz[:, i * 128 + 64:i * 128 + 128], Sb[:], False, False, 1)],
                                          ['qz', 'Sb'], ['PB2'])
                                      MMG([(PB[:, 1536:1664], khkz[64:128, i, :], vb[64:128, :], True, True)], ['khkz', 'vb'], ['PB3'])
                                  STT(Sb[:], S[:], el[:, 4 * i + j:4 * i + j + 1], PB[:, 1536:1664], ALU.mult, ALU.add, ['S', 'el', 'PB3'], ['Sb'])
                                  STT(S[:], S[:], el[:, 4 * i + j:4 * i + j + 1], PB[:, 1536:1664], ALU.mult, ALU.add, ['S', 'el', 'PB3'], ['S'])
                              MMG([(PB[:, 1024:1152], zer[:], vb[:], False, True)], ['zer', 'vb'], ['PB2'])
                              if stop == 'hg_f':
                                  fw.mute = True
                              ACT(jk[:], PB[:, 1024:1152], AF.Square, ['PB2'], ['jk', 'ssh'], accum=ssh[:, 0:1])
                              rstd_from_ss(ssh[:, 0:1], 128.0, 'ssh')
                              STT(yh[:], PB[:, 1024:1152], ssh[:, 0:1], gs[:], ALU.mult, ALU.mult, ['PB2', 'ssh', 'gs'], ['yh'])
                              ACT(jk[:], yh[:], AF.Square, ['yh'], ['jk', 'ssa'], accum=ssa[:, h, i:i + 1])
                              TT('dve', yhb[:], yh[:], gbb[:, h * 128:(h + 1) * 128], ALU.mult, ['yh', 'gbb'], ['yhb'])
                              TRG([(PAb[:, 1024:1152], yhb[:])], identb[:], ['yhb', 'identb'], ['PA1'])
                              CP('act', yT[:, h, tsl], PAb[:, 1024:1152], ['PA1'], ['yT%d' % h])
                      TT('dve', ssa[:, 0, :], ssa[:, 0, :], ssa[:, 1, :], ALU.add, ['ssa'], ['ssa'])
                      TT('dve', ssa[:, 2, :], ssa[:, 2, :], ssa[:, 3, :], ALU.add, ['ssa'], ['ssa'])
                      TT('dve', rsd[:, 0, :], ssa[:, 0, :], ssa[:, 2, :], ALU.add, ['ssa'], ['rsd0'])
                      rstd_from_ss(rsd[:, 0, :], 512.0, 'rsd0')
                      barrier()
                  fw.ctx = actx
                  if stop == 'hgrn':
                      fw.mute = True

                  with ExitStack() as sctx:
                      fw.ctx = sctx
                      wl = fw.sb("wl", [128, 8, 256], BF16)
                      wab = fw.sb("wab", [128, 2, 128], BF16)
                      wxb = fw.sb("wxb", [128, 2, 128], BF16)
                      xraw = fw.sb("xraw", [128, 3 + T], F32)
                      xc = fw.sb("xc", [128, T], F32)
                      xcb = fw.sb("xcb", [128, T], BF16)
                      ta = fw.sb("ta", [128, T], F32)
                      tb_ = fw.sb("tb_", [128, T], F32)
                      tr = fw.sb("tr", [128, T], F32)
                      ti = fw.sb("ti", [128, T], F32)
                      c8 = fw.sb("c8", [128, 2], F32)
                      c16 = fw.sb("c16", [128, 2], F32)
                      ssc = fw.sb("ssc", [128, 2, NT], F32)
                      fw.dma('pool', wab[:], wa_d[l], writes=['wab'])
                      fw.dma('pool', wxb[:], wx_d[l], writes=['wxb'])
                      ACT(c8[:], sm[:, O_LAM:O_LAM + 2], AF.Exp, ['sm'], ['c8'], scale=-1.0)
                      ACT(c8[:], c8[:], AF.Ln, ['c8'], ['c8'], bias=onecol)
                      TS('dve', c16[:], c8[:], -16.0, None, ALU.mult, ALU.bypass, ['c8'], ['c16'])
                      TS('dve', c8[:], c8[:], -8.0, None, ALU.mult, ALU.bypass, ['c8'], ['c8'])
                      fw.op('dve', lambda e: e.memset(xraw[:, 0:3], 0.0), [], ['xraw'])
                      for hc in range(2):
                          for j, c0 in enumerate([2304 + hc * 128, 2560 + hc * 128]):
                              fw.dma('pool', wl[:, :, j * 128:(j + 1) * 128], winv[:, :, c0:c0 + 128], writes=['wl'])
                          for tb in range(4):
                              MMG([(PA[:, tb * 512:(tb + 1) * 512], wl[:, kc, 0:128], hT[:, kc, tb * 512:(tb + 1) * 512], kc == 0, kc == 7)
                                   for kc in range(8)], ['wl', 'hT'], [PAk[tb]])
                              CP('act', xraw[:, 3 + tb * 512:3 + (tb + 1) * 512], PA[:, tb * 512:(tb + 1) * 512], [PAk[tb]], ['xraw'])
                          cw = sm[:, O_CW + hc * 4:O_CW + hc * 4 + 4]
                          TS('dve', xc[:], xraw[:, 3:3 + T], cw[:, 3:4], sm[:, O_CB + hc:O_CB + hc + 1], ALU.mult, ALU.add, ['xraw', 'sm'], ['xc'])
                          for w_ in range(3):
                              STT(xc[:], xraw[:, w_:w_ + T], cw[:, w_:w_ + 1], xc[:], ALU.mult, ALU.add, ['xraw', 'sm', 'xc'], ['xc'])
                          CP('act', xcb[:], xc[:], ['xc'], ['xcb'])
                          for tb in range(4):
                              sl = slice(tb * 512, (tb + 1) * 512)
                              MMG([(PB[:, sl], wab[:, hc, :], xcb[:, sl], True, True)], ['wab', 'xcb'], [PBk[tb]])
                              ACT(tr[:, sl], PB[:, sl], AF.Sigmoid, [PBk[tb]], ['tr'], bias=sm[:, O_BA + hc:O_BA + hc + 1])
                          for tb in range(4):
                              sl = slice(tb * 512, (tb + 1) * 512)
                              MMG([(PA[:, sl], wxb[:, hc, :], xcb[:, sl], True, True)], ['wxb', 'xcb'], [PAk[tb]])
                              ACT(ti[:, sl], PA[:, sl], AF.Sigmoid, [PAk[tb]], ['ti'], bias=sm[:, O_BX + hc:O_BX + hc + 1])
                          ACT(ta[:], tr[:], AF.Exp, ['tr', 'c8'], ['ta'], scale=c8[:, hc:hc + 1])
                          ACT(tb_[:], tr[:], AF.Exp, ['tr', 'c16'], ['tb_'], scale=c16[:, hc:hc + 1])
                          TS('dve', tb_[:], tb_[:], -1.0, 1.0, ALU.mult, ALU.add, ['tb_'], ['tb_'])
                          ACT(tb_[:], tb_[:], AF.Sqrt, ['tb_'], ['tb_'])
                          TT('dve', tb_[:], tb_[:], ti[:], ALU.mult, ['tb_', 'ti'], ['tb_'])
                          TT('dve', tb_[:], tb_[:], xc[:], ALU.mult, ['tb_', 'xc'], ['tb_'])
                          SCAN(tr[:], ta[:], tb_[:], 0.0, ['ta', 'tb_'], ['tr'])
                          for tb in range(4):
                              sl = slice(tb * 512, (tb + 1) * 512)
                              MMG([(PB[:, sl], wl[:, kc, 128:256], hT[:, kc, sl], kc == 0, kc == 7) for kc in range(8)],
                                  ['wl', 'hT'], [PBk[tb]])
                              CP('act', ti[:, sl], PB[:, sl], [PBk[tb]], ['ti'])
                          gelu_inplace('dve', ti[:], ta[:], 'ti', 'ta')
                          TT('dve', tb_[:], tr[:], ti[:], ALU.mult, ['tr', 'ti'], ['tb_'])
                          TS('dve', yT[:, 6 + hc, :], tb_[:], sm[:, O_GBR + 6 + hc:O_GBR + 7 + hc], None, ALU.mult, ALU.bypass,
                             ['tb_', 'sm'], ['yT%d' % (6 + hc)])
                          ACT(ta[:], tb_[:], AF.Square, ['tb_'], ['ta'])
                          MMG([(PA[:, 2 * i:2 * i + 2], ta[:, i * 128:(i + 1) * 128], onesf[:, 0:2], True, True) for i in range(NT)],
                              ['ta', 'cst'], ['PA0'])
                          CP('dve', ssc[:, hc, :], PA[:, 0:2 * NT].rearrange("p (i two) -> p i two", two=2)[:, :, 0], ['PA0'], ['ssc'])
                      TT('dve', rsd[:, 2, :], ssc[:, 0, :], ssc[:, 1, :], ALU.add, ['ssc'], ['rsd2'])
                      rstd_from_ss(rsd[:, 2, :], 256.0, 'rsd2')
                      barrier()
                  fw.ctx = actx
                  if stop == 'lru':
                      fw.mute = True

                  with ExitStack() as sctx:
                      fw.ctx = sctx
                      LP = 512
                      NP_ = T // LP
                      ws5 = fw.sb("ws5", [128, 8, 256], BF16)
                      crt = fw.sb("crt", [128, 8, 128], BF16)
                      cit = fw.sb("cit", [128, 8, 128], BF16)
                      wglu = fw.sb("wglu", [128, 2, 256], BF16)
                      ub = fw.sb("ub", [128, 2, T], BF16)
                      zb = fw.sb("zb", [128, 2, T], BF16)
                      lhb = fw.sb("lhb", [128, 8, 2, 128], BF16)
                      pst = fw.sb("pst", [128, 5, 8], F32)
                      prp = fw.sb("prp", [128, 12, 128], F32)
                      pri = fw.sb("pri", [128, 128], I32)
                      car = fw.sb("car", [128, 8, 2], F32)
                      tcos = fw.sb("tcos", [128, LP], F32)
                      tsin = fw.sb("tsin", [128, LP], F32)
                      tki = fw.sb("tki", [128, LP], I32)
                      s1 = fw.sb("s1", [128, LP], F32)
                      s2 = fw.sb("s2", [128, LP], F32)
                      swr = fw.sb("swr", [128, LP], F32)
                      swi = fw.sb("swi", [128, LP], F32)
                      szr = fw.sb("szr", [128, LP], F32)
                      szi = fw.sb("szi", [128, LP], F32)
                      xrb = fw.sb("xrb", [128, LP], BF16)
                      xib = fw.sb("xib", [128, LP], BF16)
                      yb = fw.sb("yb", [128, 512], F32)
                      yb2 = fw.sb("yb2", [128, 512], F32)
                      ssb = fw.sb("ssb", [128, 2, NT], F32)
                      fw.dma('pool', crt[:], crt_d[l], writes=['crt'])
                      fw.dma('pool', cit[:], cit_d[l], writes=['cit'])
                      fw.dma('pool', wglu[:], wglu_d[l].rearrange("(kc p) n -> p kc n", p=128), writes=['wglu'])
                      fw.dma('pool', ws5[:], winv[:, :, 2048:2304], writes=['ws5'])

                      def sincos(ang, ki, sin_o, cos_o, tmp, keys):
                          ka, kk, ks, kc_, kt_ = keys
                          TS('dve', ki, ang, 1.0 / TWO_PI, None, ALU.mult, ALU.bypass, [ka], [kk])
                          STT(ang, ki, -CW1, ang, ALU.mult, ALU.add, [kk, ka], [ka])
                          STT(ang, ki, -CW2, ang, ALU.mult, ALU.add, [kk, ka], [ka])
                          TS('dve', ang, ang, -3.14159, 3.14159, ALU.max, ALU.min, [ka], [ka])
                          ACT(sin_o, ang, AF.Sin, [ka], [ks])
                          STT(tmp, ang, -1.0, ang, ALU.mult, ALU.max, [ka], [kt_])
                          ACT(cos_o, tmp, AF.Sin, [kt_, 'cst'], [kc_], scale=-1.0, bias=halfpi)

                      lamre, dts, rmag, theta = pst[:, 0, :], pst[:, 1, :], pst[:, 2, :], pst[:, 3, :]
                      TS('dve', lamre, sm[:, O_ARS:O_ARS + 8], -1e-4, None, ALU.min, ALU.bypass, ['sm'], ['pst'])
                      ACT(dts, sm[:, O_LDS:O_LDS + 8], AF.Exp, ['sm'], ['pst'])
                      TT('dve', rmag, lamre, dts, ALU.mult, ['pst'], ['pst'])
                      ACT(rmag, rmag, AF.Exp, ['pst'], ['pst'])
                      TT('dve', theta, sm[:, O_AIS:O_AIS + 8], dts, ALU.mult, ['sm', 'pst'], ['pst'])
                      R = lambda k: prp[:, k, :]
                      lam_r, lam_i = R(0), R(1)
                      TS('dve', lam_r, sm[:, O_ARR:O_ARR + 128], -1e-4, None, ALU.min, ALU.bypass, ['sm'], ['prp'])
                      CP('dve', lam_i, sm[:, O_AIR:O_AIR + 128], ['sm'], ['prp'])
                      ACT(prp[:, 11, 0:2], sm[:, O_LDR:O_LDR + 2], AF.Exp, ['sm'], ['prp'])
                      for hg in range(2):
                          cs = slice(hg * 64, (hg + 1) * 64)
                          TS('dve', prp[:, 2, cs], prp[:, 0, cs], prp[:, 11, hg:hg + 1], None, ALU.mult, ALU.bypass, ['prp'], ['prp'])
                          TS('dve', prp[:, 3, cs], prp[:, 1, cs], prp[:, 11, hg:hg + 1], None, ALU.mult, ALU.bypass, ['prp'], ['prp'])
                      ACT(R(2), R(2), AF.Exp, ['prp'], ['prp'])
                      sincos(R(3), pri[:], R(4), R(5), R(6), ['prp', 'pri', 'prp', 'prp', 'prp'])
                      TT('dve', R(5), R(5), R(2), ALU.mult, ['prp'], ['prp'])
                      TT('dve', R(4), R(4), R(2), ALU.mult, ['prp'], ['prp'])
                      TS('dve', R(5), R(5), -1.0, None, ALU.add, ALU.bypass, ['prp'], ['prp'])
                      TT('dve', R(2), lam_r, lam_r, ALU.mult, ['prp'], ['prp'])
                      TT('dve', R(3), lam_i, lam_i, ALU.mult, ['prp'], ['prp'])
                      TT('dve', R(2), R(2), R(3), ALU.add, ['prp'], ['prp'])
                      fw.op('dve', lambda e: e.reciprocal(out=R(2), in_=R(2)), ['prp'], ['prp'])
                      TT('dve', R(6), R(5), lam_r, ALU.mult, ['prp'], ['prp'])
                      TT('dve', R(7), R(4), lam_i, ALU.mult, ['prp'], ['prp'])
                      TT('dve', R(6), R(6), R(7), ALU.add, ['prp'], ['prp'])
                      TT('dve', R(6), R(6), R(2), ALU.mult, ['prp'], ['prp'])
                      TT('dve', R(7), R(4), lam_r, ALU.mult, ['prp'], ['prp'])
                      TT('dve', R(8), R(5), lam_i, ALU.mult, ['prp'], ['prp'])
                      TT('dve', R(7), R(7), R(8), ALU.subtract, ['prp'], ['prp'])
                      TT('dve', R(7), R(7), R(2), ALU.mult, ['prp'], ['prp'])
                      btr, bti = sm[:, O_BTR:O_BTR + 128], sm[:, O_BTI:O_BTI + 128]
                      TT('dve', R(8), R(6), btr, ALU.mult, ['prp', 'sm'], ['prp'])
                      TT('dve', R(9), R(7), bti, ALU.mult, ['prp', 'sm'], ['prp'])
                      TT('dve', R(8), R(8), R(9), ALU.subtract, ['prp'], ['prp'])
                      TT('dve', R(9), R(6), bti, ALU.mult, ['prp', 'sm'], ['prp'])
                      TT('dve', R(10), R(7), btr, ALU.mult, ['prp', 'sm'], ['prp'])
                      TT('dve', R(9), R(9), R(10), ALU.add, ['prp'], ['prp'])
                      for gp in range(8):
                          hc = gp // 4
                          for gi in range(2):
                              gl = (2 * gp + gi) % 8
                              for ri, src in enumerate([8, 9]):
                                  TS('dve', lhb[:, gp, ri, gi * 64:(gi + 1) * 64], prp[:, src, hc * 64:(hc + 1) * 64],
                                     sm[:, O_RMK + gl:O_RMK + gl + 1], None, ALU.mult, ALU.bypass, ['prp', 'sm'], ['lhb'])
                      for hc in range(2):
                          for tb in range(4):
                              sl = slice(tb * 512, (tb + 1) * 512)
                              MMG([(PA[:, sl], ws5[:, kc, hc * 128:(hc + 1) * 128], hT[:, kc, sl], kc == 0, kc == 7) for kc in range(8)],
                                  ['ws5', 'hT'], [PAk[tb]])
                              CP('act', ub[:, hc, sl], PA[:, sl], [PAk[tb]], ['ub'])
                      fw.op('dve', lambda e: e.memset(car[:], 0.0), [], ['car'])
                      for hc in range(2):
                          for pc in range(NP_):
                              psl = slice(pc * LP, (pc + 1) * LP)
                              for gq in range(4):
                                  gp = hc * 4 + gq
                                  if True:
                                      TS('dve', s1[:], tau[:, 0:LP], theta[:, gp:gp + 1], None, ALU.mult, ALU.bypass, ['cst', 'pst'], ['s1'])
                                      sincos(s1[:], tki[:], tsin[:], tcos[:], s2[:], ['s1', 'tki', 'tsin', 'tcos', 's2'])
                                  MMG([(PA[:, 0:LP], lhb[:, gp, 0, :], ub[:, hc, psl], True, True),
                                       (PA[:, 512:512 + LP], lhb[:, gp, 1, :], ub[:, hc, psl], True, True)], ['lhb', 'ub'], ['PA0', 'PA1'])
                                  bur, bui = PA[:, 0:LP], PA[:, 512:512 + LP]
                                  TT('dve', s1[:], bur, tcos[:], ALU.mult, ['PA0', 'tcos'], ['s1'])
                                  TT('dve', s2[:], bui, tsin[:], ALU.mult, ['PA1', 'tsin'], ['s2'])
                                  TT('dve', swr[:], s1[:], s2[:], ALU.add, ['s1', 's2'], ['swr'])
                                  TT('dve', s1[:], bui, tcos[:], ALU.mult, ['PA1', 'tcos'], ['s1'])
                                  TT('dve', s2[:], bur, tsin[:], ALU.mult, ['PA0', 'tsin'], ['s2'])
                                  TT('dve', swi[:], s1[:], s2[:], ALU.subtract, ['s1', 's2'], ['swi'])
                                  rb_ = rmag[:, gp:gp + 1].to_broadcast([128, LP])
                                  SCAN(szr[:], rb_, swr[:], car[:, gp, 0:1], ['pst', 'swr', 'car'], ['szr'])
                                  SCAN(szi[:], rb_, swi[:], car[:, gp, 1:2], ['pst', 'swi', 'car'], ['szi'])
                                  TT('dve', s1[:], szr[:], tcos[:], ALU.mult, ['szr', 'tcos'], ['s1'])
                                  TT('dve', s2[:], szi[:], tsin[:], ALU.mult, ['szi', 'tsin'], ['s2'])
                                  TT('dve', swr[:], s1[:], s2[:], ALU.subtract, ['s1', 's2'], ['swr'])
                                  TT('dve', s1[:], szr[:], tsin[:], ALU.mult, ['szr', 'tsin'], ['s1'])
                                  TT('dve', s2[:], szi[:], tcos[:], ALU.mult, ['szi', 'tcos'], ['s2'])
                                  TT('dve', swi[:], s1[:], s2[:], ALU.add, ['s1', 's2'], ['swi'])
                                  CP('act', car[:, gp, 0:1], swr[:, LP - 1:LP], ['swr'], ['car'])
                                  CP('act', car[:, gp, 1:2], swi[:, LP - 1:LP], ['swi'], ['car'])
                                  CP('act', xrb[:], swr[:], ['swr'], ['xrb'])
                                  ACT(xib[:], swi[:], AF.Copy, ['swi'], ['xib'], scale=-1.0)
                                  MMG([(PB[:, 0:LP], crt[:, gp, :], xrb[:], gq == 0, False),
                                       (PB[:, 0:LP], cit[:, gp, :], xib[:], False, gq == 3)], ['crt', 'cit', 'xrb', 'xib'], ['PB0'])
                              STT(yb[:], ub[:, hc, psl], sm[:, O_S5D + hc:O_S5D + hc + 1], PB[:, 0:LP], ALU.mult, ALU.add,
                                  ['ub', 'sm', 'PB0'], ['yb'])
                              gelu_inplace('dve', yb[:], yb2[:], 'yb', 'yb2')
                              CP('act', zb[:, hc, psl], yb[:], ['yb'], ['zb'])
                      for oc in range(2):
                          for tb in range(4):
                              sl = slice(tb * 512, (tb + 1) * 512)
                              MMG([(PA[:, sl], wglu[:, kc, oc * 128:(oc + 1) * 128], zb[:, kc, sl], kc == 0, kc == 1) for kc in range(2)],
                                  ['wglu', 'zb'], [PAk[tb]])
                              ACT(yb[:], PA[:, sl], AF.Sigmoid, [PAk[tb]], ['yb'], bias=sm[:, O_BGL + oc:O_BGL + oc + 1])
                              TT('dve', yb[:], yb[:], zb[:, oc, sl], ALU.mult, ['yb', 'zb'], ['yb'])
                              TS('dve', yT[:, 4 + oc, sl], yb[:], sm[:, O_GBR + 4 + oc:O_GBR + 5 + oc], None, ALU.mult, ALU.bypass,
                                 ['yb', 'sm'], ['yT%d' % (4 + oc)])
                              ACT(yb2[:], yb[:], AF.Square, ['yb'], ['yb2'])
                              MMG([(PB[:, 2 * (tb * 4 + j):2 * (tb * 4 + j) + 2], yb2[:, j * 128:(j + 1) * 128], onesf[:, 0:2], True, True)
                                   for j in range(4)], ['yb2', 'cst'], ['PB0'])
                          CP('dve', ssb[:, oc, :], PB[:, 0:2 * NT].rearrange("p (i two) -> p i two", two=2)[:, :, 0], ['PB0'], ['ssb'])
                      TT('dve', rsd[:, 1, :], ssb[:, 0, :], ssb[:, 1, :], ALU.add, ['ssb'], ['rsd1'])
                      rstd_from_ss(rsd[:, 1, :], 256.0, 'rsd1')
                      barrier()
                  fw.ctx = actx
                  if stop == 's5':
                      fw.mute = True

                  with ExitStack() as sctx:
                      fw.ctx = sctx
                      stg = [fw.sb("stg", [128, 8, 512], F32), fw.sb("stgb", [128, 8, 512], F32)]
                      bm = fw.sb("bm", [128, 1024], F32)
                      gt1 = fw.sb("gt1", [128, 1024], F32)
                      wo = fw.sb("wo", [128, 8, 1024], BF16)
                      tmpo = fw.sb("tmpo", [128, 1024], F32)
                      wov = wout_d[l].rearrange("(kc p) n -> p kc n", p=128)
                      for kc in range(8):
                          fw.dma('pool', wo[:, kc, :], wov[:, kc, :], writes=['wo'])
                      mod_tile(l, 2, gt1, 'gt1', stg, bm)
                      for i in range(NT):
                          tsl = slice(i * 128, (i + 1) * 128)
                          for (P_, off, kcs, keys) in [(PA, 0, [0, 1, 2, 3], ['PA0', 'PA1']), (PA, 1024, [4, 5], ['PA2', 'PA3']),
                                                       (PB, 0, [6, 7], ['PB0', 'PB1'])]:
                              for nh in range(2):
                                  MMG([(P_[:, off + nh * 512:off + (nh + 1) * 512], yT[:, kc, tsl], wo[:, kc, nh * 512:(nh + 1) * 512],
                                        kc == kcs[0], kc == kcs[-1]) for kc in kcs],
                                      ['yT%d' % kc for kc in kcs] + ['wo'], [keys[nh]])
                          TS('dve', tmpo[:], PA[:, 0:1024], rsd[:, 0, i:i + 1], None, ALU.mult, ALU.bypass, ['PA0', 'PA1', 'rsd0'], ['tmpo'])
                          STT(tmpo[:], PA[:, 1024:2048], rsd[:, 1, i:i + 1], tmpo[:], ALU.mult, ALU.add, ['PA2', 'PA3', 'rsd1', 'tmpo'], ['tmpo'])
                          STT(tmpo[:], PB[:, 0:1024], rsd[:, 2, i:i + 1], tmpo[:], ALU.mult, ALU.add, ['PB0', 'PB1', 'rsd2', 'tmpo'], ['tmpo'])
                          TT('dve', tmpo[:], tmpo[:], gt1[:], ALU.mult, ['tmpo', 'gt1'], ['tmpo'])
                          TT('dve', xs[:, i, :], xs[:, i, :], tmpo[:], ALU.add, ['x%d' % i, 'tmpo'], ['x%d' % i])
                      barrier()
                  fw.ctx = octx
              barrier()
              if not do_peer:
                  continue
              with ExitStack() as bctx:
                  octx = fw.ctx
                  fw.ctx = bctx
                  A2 = fw.sb("A2", [128, 1024], F32)
                  B2 = fw.sb("B2", [128, 1024], F32)
                  gt2 = fw.sb("gt2", [128, 1024], F32)
                  with ExitStack() as mctx:
                      fw.ctx = mctx
                      stg = [fw.sb("stg", [128, 8, 512], F32), fw.sb("stgb", [128, 8, 512], F32)]
                      bm = fw.sb("bm", [128, 1024], F32)
                      gm = fw.sb("gm", [128, 1024], F32)
                      mod_tile(l, 3, B2, 'B2', stg, bm)
                      mod_tile(l, 4, A2, 'A2', stg, bm)
                      mod_tile(l, 5, gt2, 'gt2', stg, bm)
                      fw.dma('sync', gm[:], gffn_d[l:l + 1, :].to_broadcast([128, 1024]), writes=['gm'])
                      STT(A2[:], A2[:], 1.0, gm[:], ALU.add, ALU.mult, ['A2', 'gm'], ['A2'])
                      barrier()
                  fw.ctx = bctx
                  NB = 16
                  ss = fw.sb("ss", [128, NT], F32)
                  junk = fw.sb("junk", [128, 1024], BF16)
                  wq = fw.sb("wq", [128, 8, 2048], BF16)
                  skt = fw.sb("skt", [128, 2, 128], BF16)
                  h2 = fw.sb("h2", [128, 1024], F32)
                  h2b = fw.sb("h2b", [128, 1024], BF16)
                  h2T = fw.sb("h2T", [128, 8, 128], BF16)
                  qTb = fw.sb("qTb", [128, 16, 128], BF16)
                  big = fw.sb("big", [128, 2048], F32)
                  sc = big[:].rearrange("p (a b) -> p a b", a=16)
                  cand = big[:].rearrange("p (h c) -> p h c", h=8)
                  eq = big[:].rearrange("p (h k a) -> p h k a", h=8, k=16)
                  top = fw.sb("top", [128, 16, 16], F32)
                  tix = fw.sb("tix", [128, 16, 16], U32)
                  tixf = fw.sb("tixf", [128, 16, 16], F32)
                  best = fw.sb("best", [128, 8, 16], F32)
                  pos = fw.sb("pos", [128, 8, 16], U32)
                  posf = fw.sb("posf", [128, 8, 16], F32)
                  ai = fw.sb("ai", [128, 8, 16], I32)
                  af = fw.sb("af", [128, 8, 16], F32)
                  bf = fw.sb("bf", [128, 8, 16], F32)
                  isel = fw.sb("isel", [128, 8, 16], F32)
                  jsel = fw.sb("jsel", [128, 8, 16], F32)
                  eidx2 = fw.sb("eidx", [128, 2, 128], I32)
                  gat = fw.sb("gat", [128, 8, 16], F32)
                  gz_ = fw.sb("gz_", [128, 8], F32)
                  actp = fw.sb("actp", [128, 128], F32)
                  actt = fw.sb("actt", [128, 128], F32)
                  wgt = fw.sb("wgt", [128, 128], F32)
                  acc = fw.sb("acc", [128, 1024], F32)
                  jf = fw.sb("jf", [128, 1024], F32) if JF else acc
                  gb = [fw.sb("gb%d" % j, [128, 1024], BF16) for j in range(NB)]
                  gv = gb
                  gvc = gbc = [0]
                  NBV = NB
                  wqv = wq_d[l].rearrange("(kc p) n -> p kc n", p=128)
                  for kc in range(8):
                      fw.dma('pool', wq[:, kc, :], wqv[:, kc, :], writes=['wq'])
                  fw.dma('pool', skt[:], skt_d[l], writes=['skt'])
                  for i in range(NT):
                      ACT(junk[:], xs[:, i, :], AF.Square, ['x%d' % i], ['junk', 'ss'], accum=ss[:, i:i + 1])
                  rstd_from_ss(ss[:], float(D), 'ss')
                  PAb = PA[:, 0:512].bitcast(BF16)
                  topv = top[:].rearrange("p (h two) k -> p h two k", two=2)
                  tixv = tixf[:].rearrange("p (h two) k -> p h two k", two=2)
                  dg = [fw.sb("dg%d" % k, [128, 128], BF16) for k in range(4)]
                  ubk = ['ubd%d' % c for c in range(16)]
                  vbk = ['vbd%d' % c for c in range(16)]

                  def idx_a(i):
                      xk = 'x%d' % i
                      ek = 'eidx%d' % (i % 2)
                      eidx = eidx2[:, i % 2, :]
                      STT(jf[:], xs[:, i, :], ss[:, i:i + 1], A2[:], ALU.mult, ALU.mult, [xk, 'ss', 'A2'], ['acc', 'jf'])
                      TT('dve', h2[:], jf[:], B2[:], ALU.add, ['acc', 'jf', 'B2'], ['h2'])
                      CP('act', h2b[:], h2[:], ['h2'], ['h2b'])
                      TRG([(PAb[:, kc * 128:(kc + 1) * 128], h2b[:, kc * 128:(kc + 1) * 128]) for kc in range(8)],
                          identb[:], ['h2b', 'identb'], ['PA0'])
                      CP('act', h2T[:], PAb.rearrange("p (k t) -> p k t", k=8), ['PA0'], ['h2T'])
                      for half in range(2):
                          for hq2 in range(2):
                              hq = half * 2 + hq2
                              MMG([(PB[:, hq2 * 512 + j * 128:hq2 * 512 + (j + 1) * 128], wq[:, kc, (hq * 4 + j) * 128:(hq * 4 + j + 1) * 128], h2T[:, kc, :],
                                    kc == 0, kc == 7) for j in range(4) for kc in range(8)], ['wq', 'h2T'], [PBk[hq2]])
                          CP('act', qTb[:, half * 8:(half + 1) * 8, :].rearrange("p a b -> p (a b)"), PB[:, 0:1024], PBk[0:2], ['qTb'])
                      for hq in range(4):
                          MMG([(PA[:, hq * 512 + j * 128:hq * 512 + (j + 1) * 128], qTb[:, hq * 4 + j, :], skt[:, (hq * 4 + j) % 2, :], True, True)
                               for j in range(4)], ['qTb', 'skt'], [PAk[hq]])
                      CP('dve', big[:], PA[:, :], PAk, ['big'])

                  def idx_b(i):
                      ek = 'eidx%d' % (i % 2)
                      eidx = eidx2[:, i % 2, :]
                      for hp in range(16):
                          fw.op('dve', lambda e, hp=hp: e.max(out=top[:, hp, 0:8], in_=sc[:, hp, :]), ['big'], ['top'])
                          fw.op('dve', lambda e, hp=hp: e.max_index(out=tix[:, hp, 0:8], in_max=top[:, hp, 0:8], in_values=sc[:, hp, :]), ['big', 'top'], ['tix'])
                          fw.op('dve', lambda e, hp=hp: e.match_replace(out=sc[:, hp, :], in_to_replace=top[:, hp, 0:8], in_values=sc[:, hp, :], imm_value=-1e30),
                                ['big', 'top'], ['big'])
                          fw.op('dve', lambda e, hp=hp: e.max(out=top[:, hp, 8:16], in_=sc[:, hp, :]), ['big'], ['top'])
                          fw.op('dve', lambda e, hp=hp: e.max_index(out=tix[:, hp, 8:16], in_max=top[:, hp, 8:16], in_values=sc[:, hp, :]), ['big', 'top'], ['tix'])
                      CP('dve', tixf[:], tix[:], ['tix'], ['tixf'])
                      TT('dve', cand.rearrange("p h (a b) -> p h a b", a=16), topv[:, :, 0, :].unsqueeze(3).to_broadcast([128, 8, 16, 16]),
                         topv[:, :, 1, :].unsqueeze(2).to_broadcast([128, 8, 16, 16]), ALU.add, ['top'], ['big'])
                      for h in range(8):
                          fw.op('dve', lambda e, h=h: e.max(out=best[:, h, 0:8], in_=cand[:, h, :]), ['big'], ['best'])
                          fw.op('dve', lambda e, h=h: e.max_index(out=pos[:, h, 0:8], in_max=best[:, h, 0:8], in_values=cand[:, h, :]), ['big', 'best'], ['pos'])
                          fw.op('dve', lambda e, h=h: e.match_replace(out=cand[:, h, :], in_to_replace=best[:, h, 0:8], in_values=cand[:, h, :], imm_value=-1e30),
                                ['big', 'best'], ['big'])
                          fw.op('dve', lambda e, h=h: e.max(out=best[:, h, 8:16], in_=cand[:, h, :]), ['big'], ['best'])
                          fw.op('dve', lambda e, h=h: e.max_index(out=pos[:, h, 8:16], in_max=best[:, h, 8:16], in_values=cand[:, h, :]), ['big', 'best'], ['pos'])
                      CP('dve', posf[:], pos[:], ['pos'], ['posf'])
                      TS('dve', ai[:], posf[:], 1.0 / 16.0, -7.5 / 16.0, ALU.mult, ALU.add, ['posf'], ['ai'])
                      CP('dve', af[:], ai[:], ['ai'], ['af'])
                      STT(bf[:], af[:], -16.0, posf[:], ALU.mult, ALU.add, ['af', 'posf'], ['bf'])
                      io_b = io16.unsqueeze(1).unsqueeze(1).to_broadcast([128, 8, 16, 16])
                      for (src, tv, dst, dk) in [(af, 0, isel, 'isel'), (bf, 1, jsel, 'jsel')]:
                          TT('dve', eq, src[:].unsqueeze(3).to_broadcast([128, 8, 16, 16]), io_b, ALU.is_equal, ['af', 'bf', 'cst'], ['big'])
                          TT('dve', eq, eq, tixv[:, :, tv, :].unsqueeze(2).to_broadcast([128, 8, 16, 16]), ALU.mult, ['big', 'tixf'], ['big'])
                          fw.op('dve', lambda e, dst=dst: e.tensor_reduce(out=dst[:], in_=eq, axis=AX.X, op=ALU.add), ['big'], [dk])
                      STT(isel[:], isel[:], 128.0, jsel[:], ALU.mult, ALU.add, ['isel', 'jsel'], ['isel'])
                      CP('dve', eidx, isel[:].rearrange("p h k -> p (h k)"), ['isel'], [ek])

                  def u_phase(i):
                      ek = 'eidx%d' % (i % 2)
                      TT('dve', gat[:], best[:], best[:, :, 0:1].to_broadcast([128, 8, 16]), ALU.subtract, ['best'], ['gat'])
                      ACT(gat[:], gat[:], AF.Exp, ['gat'], ['gat'])
                      fw.op('dve', lambda e: e.tensor_reduce(out=gz_[:], in_=gat[:], axis=AX.X, op=ALU.add), ['gat'], ['gz_'])
                      fw.op('dve', lambda e: e.reciprocal(out=gz_[:], in_=gz_[:]), ['gz_'], ['gz_'])
                      TT('dve', gat[:], gat[:], gz_[:].unsqueeze(2).to_broadcast([128, 8, 16]), ALU.mult, ['gat', 'gz_'], ['gat'])

                      for n in range(128):
                          j = gbc[0] % NB
                          gbc[0] += 1
                          fw.dma('pool', gb[j][:], ub_d[l], reads=[ek] + ubk, writes=['gb%d' % j], noslotwait=(n >= 20 or i > 0),
                                 indirect=bass.IndirectOffsetOnAxis(ap=eidx2[:, i % 2, n:n + 1], axis=0))
                          fw.op('dve', lambda e, j=j, n=n: e.scalar_tensor_tensor(out=(jf[:] if JF else junk[:]), in0=gb[j][:], scalar=1.0, in1=h2[:],
                                                                                op0=ALU.mult, op1=ALU.mult, accum_out=actp[:, n:n + 1]),
                                ['gb%d' % j, 'h2'], ['junk', 'jf', 'actp'])
                      gelu_inplace('dve', actp[:], actt[:], 'actp', 'actt')
                      TT('dve', wgt[:], actp[:], gat[:].rearrange("p h k -> p (h k)"), ALU.mult, ['actp', 'gat'], ['wgt'])

                  def v_phase(i):
                      xk = 'x%d' % i
                      ek = 'eidx%d' % (i % 2)
                      for n in range(128):
                          j = gvc[0] % NBV
                          gvc[0] += 1
                          k = n % 4
                          fw.dma('pool', gv[j][:], vb_d[l], reads=[ek] + vbk, writes=['gb%d' % j], noslotwait=True,
                                 indirect=bass.IndirectOffsetOnAxis(ap=eidx2[:, i % 2, n:n + 1], axis=0))
                          fw.op('act', lambda e, k=k, n=n: e.activation(out=dg[k][:], in_=identf, func=AF.Copy, scale=wgt[:, n:n + 1]),
                                ['cst', 'wgt'], ['dg%d' % k])
                          MMG([(PB[:, 1024:1536], dg[k][:], gv[j][:, 0:512], n == 0, n == 127),
                               (PB[:, 1536:2048], dg[k][:], gv[j][:, 512:1024], n == 0, n == 127)],
                              ['dg%d' % k, 'gb%d' % j], ['PB2', 'PB3'])

                  def v_fin(i):
                      xk = 'x%d' % i
                      TT('dve', acc[:], PB[:, 1024:2048], gt2[:], ALU.mult, ['PB2', 'PB3', 'gt2'], ['acc'])
                      TT('dve', xs[:, i, :], xs[:, i, :], acc[:], ALU.add, [xk, 'acc'], [xk])

                  idx_a(0)
                  idx_b(0)
                  for i in range(NT):
                      u_phase(i)
                      if i + 1 < NT:
                          idx_a(i + 1)
                      v_phase(i)
                      if i + 1 < NT:
                          idx_b(i + 1)
                      v_fin(i)
                  barrier()
                  fw.ctx = octx
              barrier()
          except _Stop:
            fw.ctx = ctx
            barrier()
            break

        fw.mute = False
        barrier()
        with ExitStack() as fctx:
            fw.ctx = fctx
            gf = fw.sb("gf", [128, 1024], F32)
            ss = fw.sb("ss", [128, NT], F32)
            junk = fw.sb("junk", [128, 1024], BF16)
            ob = [fw.sb("ob%d" % j, [128, 1024], F32) for j in range(2)]
            fw.dma('sync', gf[:], gfin_d[0:1, :].to_broadcast([128, 1024]), writes=['gf'])
            for i in range(NT):
                ACT(junk[:], xs[:, i, :], AF.Square, ['x%d' % i], ['junk', 'ss'], accum=ss[:, i:i + 1])
            rstd_from_ss(ss[:], float(D), 'ss')
            outk = []
            for i in range(NT):
                j = i % 2
                STT(ob[j][:], xs[:, i, :], ss[:, i:i + 1], gf[:], ALU.mult, ALU.mult, ['x%d' % i, 'ss', 'gf'], ['ob%d' % j])
                fw.dma('sync', out_d[i * 128:(i + 1) * 128, :], ob[j][:], reads=['ob%d' % j], writes=['out%d' % i])
                outk.append('out%d' % i)
            fw.finish(outk)
    return nc


def _consts():
    c = np.zeros((128, CSTW), np.float32)
    c[:, 0:128] = np.eye(128, dtype=np.float32)
    c[:, 128:256] = 1.0
    s = np.arange(128)[:, None]
    t = np.arange(128)[None, :]
    c[:, 256:384] = ((s // 32 == t // 32) & (s <= t)).astype(np.float32)
    c[:, 384] = np.pi / 2
    c[:, 385] = EPS
    c[:, 386] = 1.0
    c[:, 400:416] = np.arange(16, dtype=np.float32)[None, :]
    c[:, 416:928] = np.arange(1, 513, dtype=np.float32)[None, :]
    rm = np.ones(2048, np.float32)
    rm[::32] = 0.0
    c[:, 928:928 + 2048] = rm[None, :]
    return c


def _layouts(inp):
    f = lambda k: np.asarray(inp[k], dtype=np.float32)
    small = np.zeros((4, 128, NSMALL), np.float32)
    crt = np.zeros((4, 128, 8, 128), np.float32)
    cit = np.zeros((4, 128, 8, 128), np.float32)
    wa = np.zeros((4, 128, 2, 128), np.float32)
    wx = np.zeros((4, 128, 2, 128), np.float32)
    lbl = f('hgrn_lb_logits').reshape(4, 4, 128).transpose(2, 1, 0).reshape(128, 16)
    st = lambda a: a.reshape(8, 2, 64).transpose(1, 2, 0).reshape(128, 8)
    rep = lambda a: np.broadcast_to(a.reshape(2, 8, 1, 64), (2, 8, 16, 64)).transpose(1, 2, 0, 3).reshape(128, 128)
    col2 = lambda v: v.reshape(2, 128).T
    rmk = np.zeros((128, 8), np.float32)
    for gl in range(8):
        rmk[gl * 16:(gl + 1) * 16, gl] = 1.0
    for l in range(4):
        small[l, :, O_LBL:O_LBL + 16] = lbl
        small[l, :, O_ARS:O_ARS + 8] = st(f('s5_a_re')[l])
        small[l, :, O_AIS:O_AIS + 8] = st(f('s5_a_im')[l])
        ld = f('s5_log_dt')[l]
        small[l, :, O_LDS:O_LDS + 8] = st(np.broadcast_to(ld[:, None], (16, 64)).copy())
        small[l, :, O_ARR:O_ARR + 128] = rep(f('s5_a_re')[l])
        small[l, :, O_AIR:O_AIR + 128] = rep(f('s5_a_im')[l])
        small[l, :, O_BTR:O_BTR + 128] = f('s5_b_re')[l].reshape(2, 8, 64, 16).transpose(1, 3, 0, 2).reshape(128, 128)
        small[l, :, O_BTI:O_BTI + 128] = f('s5_b_im')[l].reshape(2, 8, 64, 16).transpose(1, 3, 0, 2).reshape(128, 128)
        small[l, :, O_LDR:O_LDR + 2] = np.broadcast_to(ld.reshape(2, 8, 1), (2, 8, 16)).transpose(1, 2, 0).reshape(128, 2)
        small[l, :, O_S5D:O_S5D + 2] = col2(f('s5_d')[l])
        small[l, :, O_BGL:O_BGL + 2] = col2(f('s5_b_glu')[l])
        small[l, :, O_CW:O_CW + 8] = f('lru_conv_w')[l].reshape(4, 2, 128).transpose(2, 1, 0).reshape(128, 8)
        small[l, :, O_CB:O_CB + 2] = col2(f('lru_conv_b')[l])
        small[l, :, O_BA:O_BA + 2] = col2(f('lru_b_a')[l])
        small[l, :, O_BX:O_BX + 2] = col2(f('lru_b_x')[l])
        small[l, :, O_LAM:O_LAM + 2] = col2(f('lru_lambda')[l])
        small[l, :, O_GBR:O_GBR + 8] = f('g_branch')[l].reshape(8, 128).T
        small[l, :, O_RMK:O_RMK + 8] = rmk
        for g in range(16):
            gp, gi, gl = g // 2, g % 2, g % 8
            crt[l, gi * 64:(gi + 1) * 64, gp, gl * 16:(gl + 1) * 16] = f('s5_c_re')[l, g].T
            cit[l, gi * 64:(gi + 1) * 64, gp, gl * 16:(gl + 1) * 16] = f('s5_c_im')[l, g].T
        for h in range(4):
            hc, o = h // 2, (h % 2) * 64
            wa[l, o:o + 64, hc, o:o + 64] = f('lru_w_a')[l, h]
            wx[l, o:o + 64, hc, o:o + 64] = f('lru_w_x')[l, h]
    skt = np.ascontiguousarray(f('peer_sub_keys').transpose(0, 3, 1, 2))
    return dict(small=small, crt=crt, cit=cit, wa_bd=wa, wx_bd=wx, skt=skt)


def make_in_maps(inp, cores):
    f = lambda k: np.ascontiguousarray(np.asarray(inp[k], dtype=np.float32))
    lay = _layouts(inp)
    shared = dict(w_mod=f('w_mod'), b_mod=f('b_mod'), g_mix=f('g_mix'), w_in=f('w_in'), w_glu=f('s5_w_glu'),
                  g_branch=f('g_branch'), w_out=f('w_out'), g_ffn=f('g_ffn'), peer_w_q=f('peer_w_q'),
                  g_final=f('g_final').reshape(1, D), consts=_consts())
    shared.update(lay)
    pu, pv = f('peer_u'), f('peer_v')
    for l_ in range(4):
        shared['peer_u%d' % l_] = pu[l_]
        shared['peer_v%d' % l_] = pv[l_]
    x = f('x')
    c = f('c')
    maps = []
    for b in cores:
        m = dict(shared)
        m['x'] = np.ascontiguousarray(x[b])
        m['c'] = np.ascontiguousarray(c[b].reshape(128, 8))
        maps.append(m)
    return maps


def kernel(**inputs):
    nc = build()
    maps = make_in_maps(inputs, list(range(8)))
    res = run_bass_kernel_spmd(nc, maps, core_ids=list(range(8)))
    return np.stack([np.asarray(r['out'], dtype=np.float32) for r in res.results], axis=0)
```
